# Optimizing a Trainium2 kernel written in Bass

```python
import math
import jax
import jax.numpy as jnp
from jax import lax
import numpy as np

D_MODEL = 1024
BATCH = 8
SEQ = 2048
DEPTH = 4

N_MEM = 256
ROPE_THETA = 500000.0
EPS = 1e-6
Q_BLOCK = 128

MLA_HEADS = 8
MLA_NOPE = 64
MLA_ROPE = 32
MLA_V = 64
MLA_Q_LORA = 384
MLA_KV_LORA = 256

SGU_GROUPS = 4
SGU_GROUP_DIM = 64
SGU_WIDTH = SGU_GROUPS * SGU_GROUP_DIM
SGU_CHUNK = 128

DIFF_HEADS = 4
DIFF_QK_DIM = 32
DIFF_V_DIM = 2 * DIFF_QK_DIM
DIFF_ROT = (2 * DIFF_QK_DIM) // 4 // 2

MIX_WIDTH = MLA_HEADS * MLA_V + SGU_WIDTH + DIFF_HEADS * DIFF_V_DIM

P_CQ = MLA_Q_LORA
P_CKV = MLA_KV_LORA
P_KR = MLA_ROPE
P_SGU = 2 * SGU_WIDTH
P_DQ = DIFF_HEADS * 2 * DIFF_QK_DIM
P_DK = DIFF_HEADS * 2 * DIFF_QK_DIM
P_DV = DIFF_HEADS * DIFF_V_DIM
IN_WIDTH = P_CQ + P_CKV + P_KR + P_SGU + P_DQ + P_DK + P_DV
IN_SPLIT = (P_CQ, P_CQ + P_CKV, P_CQ + P_CKV + P_KR, P_CQ + P_CKV + P_KR + P_SGU,
            P_CQ + P_CKV + P_KR + P_SGU + P_DQ, P_CQ + P_CKV + P_KR + P_SGU + P_DQ + P_DK)

X_HEADS = 4
X_HEAD_DIM = D_MODEL // X_HEADS

D_FF = 2816
CONV_WIDTH = 3

kernel_name = "hybrid_mla_gmlp_diffattn_encoder"


def rms_norm(x, g):
    xf = x.astype(jnp.float32)
    y = xf * lax.rsqrt(jnp.mean(xf * xf, axis=-1, keepdims=True) + EPS)
    return (y * g.astype(jnp.float32)).astype(x.dtype)


def layer_norm(x, g):
    xf = x.astype(jnp.float32)
    mu = jnp.mean(xf, axis=-1, keepdims=True)
    xc = xf - mu
    y = xc * lax.rsqrt(jnp.mean(xc * xc, axis=-1, keepdims=True) + EPS)
    return (y * g.astype(jnp.float32)).astype(x.dtype)


def rope_tables(positions, rot_dim):
    inv_freq = ROPE_THETA ** (-jnp.arange(0, rot_dim, 2, dtype=jnp.float32) / rot_dim)
    ang = positions.astype(jnp.float32)[..., None] * inv_freq
    return jnp.cos(ang), jnp.sin(ang)


def apply_rope(x, cos, sin):
    bshape = cos.shape[:2] + (1,) * (x.ndim - 3) + cos.shape[-1:]
    c = cos.reshape(bshape)
    s = sin.reshape(bshape)
    xf = x.astype(jnp.float32)
    x1, x2 = jnp.split(xf, 2, axis=-1)
    return jnp.concatenate([x1 * c - x2 * s, x2 * c + x1 * s], axis=-1).astype(x.dtype)


def partial_rope(x, cos, sin, rot):
    return jnp.concatenate([apply_rope(x[..., :rot], cos, sin), x[..., rot:]], axis=-1)


def to_query_blocks(q):
    b, s = q.shape[:2]
    return jnp.moveaxis(q.reshape((b, s // Q_BLOCK, Q_BLOCK) + q.shape[2:]), 1, 0)


def from_query_blocks(o):
    nb, b, qb = o.shape[:3]
    return jnp.moveaxis(o, 0, 1).reshape((b, nb * qb) + o.shape[3:])


def blocked_attention(q, k, v, scale):
    def block(qb):
        s = jnp.einsum('bqhd,bkhd->bhqk', qb, k).astype(jnp.float32) * scale
        p = jax.nn.softmax(s, axis=-1).astype(v.dtype)
        return jnp.einsum('bhqk,bkhd->bqhd', p, v)
    return from_query_blocks(lax.map(block, to_query_blocks(q)))


def mla_group(cq, ckv, kr, g_cq, g_ckv, w_uq, w_ukv, cos, sin):
    b, s, _ = cq.shape
    q = (rms_norm(cq, g_cq) @ w_uq).reshape(b, s, MLA_HEADS, MLA_NOPE + MLA_ROPE)
    q = jnp.concatenate([q[..., :MLA_NOPE], apply_rope(q[..., MLA_NOPE:], cos, sin)], axis=-1)
    kv = (rms_norm(ckv, g_ckv) @ w_ukv).reshape(b, s, MLA_HEADS, MLA_NOPE + MLA_V)
    k_nope, v = kv[..., :MLA_NOPE], kv[..., MLA_NOPE:]
    k_rope = apply_rope(kr[:, :, None, :], cos, sin)
    k = jnp.concatenate([k_nope, jnp.broadcast_to(k_rope, (b, s, MLA_HEADS, MLA_ROPE))], axis=-1)
    o = blocked_attention(q, k, v, (MLA_NOPE + MLA_ROPE) ** -0.5)
    return o.reshape(b, s, MLA_HEADS * MLA_V)


def sgu_group(z, g_sgu, w_s, b_s):
    b, s, _ = z.shape
    nc = s // SGU_CHUNK
    z = jax.nn.gelu(z, approximate=True)
    u, v = jnp.split(z, 2, axis=-1)
    u = u.reshape(b, nc, SGU_CHUNK, SGU_GROUPS, SGU_GROUP_DIM)
    v = layer_norm(v.reshape(b, nc, SGU_CHUNK, SGU_GROUPS, SGU_GROUP_DIM), g_sgu)
    mixed = jnp.einsum('gts,bnsgc->bntgc', w_s, v) + b_s.T[:, :, None]
    return (u * mixed).reshape(b, s, SGU_WIDTH)


def diff_group(dq, dk, dv, lam, lam_init, g_sub, cos, sin):
    b, s, _ = dq.shape
    q = partial_rope(dq.reshape(b, s, DIFF_HEADS, 2, DIFF_QK_DIM), cos, sin, 2 * DIFF_ROT)
    k = partial_rope(dk.reshape(b, s, DIFF_HEADS, 2, DIFF_QK_DIM), cos, sin, 2 * DIFF_ROT)
    v = dv.reshape(b, s, DIFF_HEADS, DIFF_V_DIM)
    scale = DIFF_QK_DIM ** -0.5

    def block(qb):
        sc = jnp.einsum('bqhcd,bkhcd->bhcqk', qb, k).astype(jnp.float32) * scale
        p = jax.nn.softmax(sc, axis=-1)
        a = (p[:, :, 0] - lam * p[:, :, 1]).astype(v.dtype)
        return jnp.einsum('bhqk,bkhe->bqhe', a, v)

    o = from_query_blocks(lax.map(block, to_query_blocks(q)))
    o = rms_norm(o, g_sub) * (1.0 - lam_init)
    return o.reshape(b, s, DIFF_HEADS * DIFF_V_DIM)


def memory_cross_attention(h, mem_n, w_q, w_kv, w_o):
    b, s, _ = h.shape
    m = mem_n.shape[1]
    q = (h @ w_q).reshape(b, s, X_HEADS, X_HEAD_DIM)
    kv = (mem_n @ w_kv).reshape(b, m, 2, X_HEADS, X_HEAD_DIM)
    k, v = kv[:, :, 0], kv[:, :, 1]
    sc = jnp.einsum('bqhd,bkhd->bhqk', q, k).astype(jnp.float32) * (X_HEAD_DIM ** -0.5)
    p = jax.nn.softmax(sc, axis=-1).astype(v.dtype)
    o = jnp.einsum('bhqk,bkhd->bqhd', p, v).reshape(b, s, D_MODEL)
    return o @ w_o


def conv_ffn(h, w_up, conv_w, conv_b, w_down):
    a = h @ w_up
    a = lax.conv_general_dilated(
        a, conv_w[:, None, :], window_strides=(1,),
        padding=((CONV_WIDTH // 2, CONV_WIDTH // 2),),
        dimension_numbers=('NWC', 'WIO', 'NWC'),
        feature_group_count=2 * D_FF) + conv_b
    g, u = jnp.split(a, 2, axis=-1)
    return (jax.nn.gelu(g, approximate=True) * u) @ w_down


def setup_inputs(seed: int = 0) -> dict:
    key = jax.random.key(seed)
    ks = iter(jax.random.split(key, 40))

    def nrm(shape, scale):
        return jax.random.normal(next(ks), shape, jnp.float32) * scale

    def gain(dim_shape):
        return 1.0 + nrm((DEPTH,) + dim_shape, 0.05)

    x = nrm((BATCH, SEQ, D_MODEL), 1.0)
    mem = nrm((BATCH, N_MEM, D_MODEL), 1.0)
    offsets = jax.random.randint(next(ks), (BATCH, 1), 0, 4096, dtype=jnp.int32)
    positions = offsets + jnp.arange(SEQ, dtype=jnp.int32)[None, :]
    return {
        'x': x, 'mem': mem, 'positions': positions,
        'mix_pre_g': gain((D_MODEL,)),
        'mix_post_g': gain((D_MODEL,)),
        'w_in': nrm((DEPTH, D_MODEL, IN_WIDTH), D_MODEL ** -0.5),
        'mla_cq_g': gain((MLA_Q_LORA,)),
        'mla_ckv_g': gain((MLA_KV_LORA,)),
        'mla_w_uq': nrm((DEPTH, MLA_Q_LORA, MLA_HEADS * (MLA_NOPE + MLA_ROPE)), MLA_Q_LORA ** -0.5),
        'mla_w_ukv': nrm((DEPTH, MLA_KV_LORA, MLA_HEADS * (MLA_NOPE + MLA_V)), MLA_KV_LORA ** -0.5),
        'sgu_norm_g': gain((SGU_GROUPS, SGU_GROUP_DIM)),
        'sgu_w_s': nrm((DEPTH, SGU_GROUPS, SGU_CHUNK, SGU_CHUNK), SGU_CHUNK ** -0.5),
        'sgu_b_s': gain((SGU_GROUPS, SGU_CHUNK)),
        'diff_lam_q1': nrm((DEPTH, DIFF_QK_DIM), 0.1),
        'diff_lam_k1': nrm((DEPTH, DIFF_QK_DIM), 0.1),
        'diff_lam_q2': nrm((DEPTH, DIFF_QK_DIM), 0.1),
        'diff_lam_k2': nrm((DEPTH, DIFF_QK_DIM), 0.1),
        'diff_sub_g': gain((DIFF_V_DIM,)),
        'w_mix_out': nrm((DEPTH, MIX_WIDTH, D_MODEL), MIX_WIDTH ** -0.5),
        'mem_pre_g': gain((D_MODEL,)),
        'mem_post_g': gain((D_MODEL,)),
        'mem_kv_g': gain((D_MODEL,)),
        'mem_w_q': nrm((DEPTH, D_MODEL, D_MODEL), D_MODEL ** -0.5),
        'mem_w_kv': nrm((DEPTH, D_MODEL, 2 * D_MODEL), D_MODEL ** -0.5),
        'mem_w_o': nrm((DEPTH, D_MODEL, D_MODEL), D_MODEL ** -0.5),
        'ffn_pre_g': gain((D_MODEL,)),
        'ffn_post_g': gain((D_MODEL,)),
        'ffn_w_up': nrm((DEPTH, D_MODEL, 2 * D_FF), D_MODEL ** -0.5),
        'ffn_conv_w': nrm((DEPTH, CONV_WIDTH, 2 * D_FF), CONV_WIDTH ** -0.5),
        'ffn_conv_b': nrm((DEPTH, 2 * D_FF), 0.01),
        'ffn_w_down': nrm((DEPTH, D_FF, D_MODEL), D_FF ** -0.5),
    }


def reference(x, mem, positions,
              mix_pre_g, mix_post_g, w_in, mla_cq_g, mla_ckv_g, mla_w_uq, mla_w_ukv,
              sgu_norm_g, sgu_w_s, sgu_b_s,
              diff_lam_q1, diff_lam_k1, diff_lam_q2, diff_lam_k2, diff_sub_g, w_mix_out,
              mem_pre_g, mem_post_g, mem_kv_g, mem_w_q, mem_w_kv, mem_w_o,
              ffn_pre_g, ffn_post_g, ffn_w_up, ffn_conv_w, ffn_conv_b, ffn_w_down):
    cos_a, sin_a = rope_tables(positions, MLA_ROPE)
    cos_d, sin_d = rope_tables(positions, 2 * DIFF_ROT)
    f32 = jnp.float32
    for l in range(DEPTH):
        lam_init = 0.8 - 0.6 * math.exp(-0.3 * l)
        h = rms_norm(x, mix_pre_g[l])
        proj = h @ w_in[l]
        cq, ckv, kr, z, dq, dk, dv = jnp.split(proj, IN_SPLIT, axis=-1)
        out_a = mla_group(cq, ckv, kr, mla_cq_g[l], mla_ckv_g[l], mla_w_uq[l], mla_w_ukv[l], cos_a, sin_a)
        out_b = sgu_group(z, sgu_norm_g[l], sgu_w_s[l], sgu_b_s[l])
        lam = (jnp.exp(jnp.sum(diff_lam_q1[l].astype(f32) * diff_lam_k1[l].astype(f32)))
               - jnp.exp(jnp.sum(diff_lam_q2[l].astype(f32) * diff_lam_k2[l].astype(f32)))
               + lam_init)
        out_c = diff_group(dq, dk, dv, lam, lam_init, diff_sub_g[l], cos_d, sin_d)
        mix = jnp.concatenate([out_a, out_b, out_c], axis=-1) @ w_mix_out[l]
        x = x + rms_norm(mix, mix_post_g[l])
        h = rms_norm(x, mem_pre_g[l])
        mem_n = rms_norm(mem, mem_kv_g[l])
        x = x + rms_norm(memory_cross_attention(h, mem_n, mem_w_q[l], mem_w_kv[l], mem_w_o[l]), mem_post_g[l])
        h = rms_norm(x, ffn_pre_g[l])
        x = x + rms_norm(conv_ffn(h, ffn_w_up[l], ffn_conv_w[l], ffn_conv_b[l], ffn_w_down[l]), ffn_post_g[l])
    return x
```

```python
import math
from contextlib import ExitStack

import numpy as np
import concourse.bass as bass
import concourse.mybir as mybir
from concourse.bass_utils import run_bass_kernel_spmd

F32 = mybir.dt.float32
BF16 = mybir.dt.bfloat16
I32 = mybir.dt.int32
U8 = mybir.dt.uint8
AF = mybir.ActivationFunctionType
ALU = mybir.AluOpType
AX = mybir.AxisListType

D = 1024
S = 2048
NT = 16
DEPTH = 4
NMEM = 256
EPS = 1e-6
THETA = 500000.0
INW = 1952
DFF = 2816
NCH = 22

ENGS = ("pe", "act", "dve", "pool", "sp")
SEM_LIMIT = 1000
NDS = 12


class Res:
    __slots__ = ("lw", "rd", "rd_dma")

    def __init__(self):
        self.lw = None
        self.rd = {}
        self.rd_dma = []


class Op:
    __slots__ = ("eng", "meth", "args", "kw", "deps", "sig", "semk", "is_dma", "dsem", "dval", "idx")


class Sched:
    def __init__(self):
        self.ops = {e: [] for e in ENGS}
        self.ndma = {"sp": 0, "pool": 0}
        self.dma_hist = {"sp": [], "pool": []}
        self.n = 0

    def add(self, eng, meth, r, w, *args, is_dma=False, **kw):
        op = Op()
        op.eng, op.meth, op.args, op.kw = eng, meth, args, kw
        op.is_dma = is_dma
        op.sig = False
        op.semk = None
        op.idx = self.n
        op.dsem = None
        op.dval = None
        self.n += 1
        deps = {}

        def need(d):
            if d is None or d is op:
                return
            if (not d.is_dma) and d.eng == "pe" and eng == "pe" and not is_dma:
                return
            deps[d.idx] = d

        wset = set(id(x) for x in w)
        for x in r:
            need(x.lw)
        for x in w:
            need(x.lw)
            for d in x.rd.values():
                need(d)
            for d in x.rd_dma:
                need(d)
        if is_dma:
            q = self.dma_hist[eng]
            op.dsem = len(q) % NDS
            op.dval = 16 * (len(q) // NDS + 1)
            if len(q) >= NDS:
                need(q[len(q) - NDS])
            q.append(op)
        best = {}
        out = []
        for d in deps.values():
            if d.is_dma:
                out.append(d)
            else:
                b = best.get(d.eng)
                if b is None or d.idx > b.idx:
                    best[d.eng] = d
        out.extend(best.values())
        for d in out:
            d.sig = True
        op.deps = out
        for x in w:
            x.lw = op
            x.rd = {}
            x.rd_dma = []
        for x in r:
            if id(x) in wset:
                continue
            if is_dma:
                x.rd_dma.append(op)
            else:
                x.rd[eng] = op
        self.ops[eng].append(op)
        return op

    def finalize(self):
        for e in ENGS:
            k = 0
            for op in self.ops[e]:
                if op.is_dma:
                    continue
                if op.sig:
                    op.semk = k
                    k += 1
        return {e: sum(1 for o in self.ops[e] if (not o.is_dma) and o.sig) for e in ENGS}


class Buf:
    __slots__ = ("ap", "res", "off", "size", "name", "owner")


class Arena:
    CH = 1024

    def __init__(self, t, nbytes, base=0, nextfit=False):
        self.t = t
        self.base = base
        self.nextfit = nextfit
        self.ptr = 0
        self.nbytes = nbytes
        self.nch = nbytes // self.CH
        self.res = [Res() for _ in range(self.nch)]
        self.used = [False] * self.nch
        self.peak = 0

    def alloc(self, name, shape, dt):
        esz = 4 if dt in (F32, I32) else 2
        n = 1
        for s in shape[1:]:
            n *= s
        nb = n * esz
        k = (nb + self.CH - 1) // self.CH
        start = None
        order = [0]
        if self.nextfit:
            order = [self.ptr, 0]
        for s0 in order:
            run = 0
            for i in range(s0, self.nch):
                if not self.used[i]:
                    run += 1
                    if run == k:
                        start = i - k + 1
                        break
                else:
                    run = 0
            if start is not None:
                break
        if start is not None:
            self.ptr = start + k
        if start is None:
            raise RuntimeError("arena OOM for %s (%d B); used=%d" % (name, nb, sum(self.used)))
        for i in range(start, start + k):
            self.used[i] = True
        self.peak = max(self.peak, max(i for i in range(self.nch) if self.used[i]) + 1)
        b = Buf()
        b.name = name
        b.owner = self
        b.off = start
        b.size = k
        o = self.base + start * self.CH
        ap = self.t[:, o:o + nb].bitcast(dt)
        if len(shape) == 3:
            b.ap = ap.rearrange("p (a b) -> p a b", b=shape[2])
        else:
            b.ap = ap
        b.res = self.res[start:start + k]
        return b

    def free(self, *bufs):
        for b in bufs:
            for i in range(b.off, b.off + b.size):
                b.owner.used[i] = False


TRANS = {"hn", "hT", "ssq", "rsq", "cn", "rt", "zg", "st", "vsq", "vn", "qk", "r6", "pt", "rc", "o1", "s2", "tmp",
         "cT", "aT", "mn", "y", "wu", "ptm", "ltmp"}


class Arenas:
    def __init__(self, main, trans):
        self.main = main
        self.trans = trans

    def alloc(self, name, shape, dt):
        if name in TRANS:
            return self.trans.alloc(name, shape, dt)
        return self.main.alloc(name, shape, dt)

    def free(self, *bufs):
        self.main.free(*bufs)

    @property
    def peak(self):
        return (self.main.peak, self.trans.peak)


def build(n_layers=DEPTH, stop_after=None):
    nc = bass.Bass("TRN2", target_bir_lowering=False)

    def din(name, shape, dt=F32):
        return nc.dram_tensor(name, list(shape), dt, kind="ExternalInput").ap()

    WL = n_layers
    x_d = din("x", [S, D])
    mem_d = din("mem", [NMEM, D])
    pos_d = din("pos", [NT, 128], I32)
    w_in_d = din("w_in", [WL, D, INW])
    w_uq_d = din("w_uq", [WL, 384, 768])
    w_ukv_d = din("w_ukv", [WL, 256, 1024])
    wsT_d = din("wsT", [WL, 128, 512])
    w_mo_d = din("w_mo", [WL, D, D])
    w_q_d = din("w_q", [WL, D, D])
    w_kv_d = din("w_kv", [WL, D, 2 * D])
    w_o_d = din("w_o", [WL, D, D])
    w_up_d = din("w_up", [WL, D, 2 * DFF])
    w_dn_d = din("w_dn", [WL, DFF, D])
    NPK = 217
    pk_d = din("pk", [128, DEPTH, NPK])
    NBC = 3520
    bc_d = din("bc", [DEPTH, NBC])
    out_d = nc.dram_tensor("out", [S, D], F32, kind="ExternalOutput").ap()

    sc = Sched()
    es = ExitStack()
    xs = es.enter_context(nc.sbuf_tensor("xs", [128, NT, D], F32))
    x_res = [Res() for _ in range(NT)]
    MAIN_B = 114 * 1024
    TRANS_B = 26 * 1024
    ar_t = es.enter_context(nc.sbuf_tensor("arena", [128, MAIN_B + TRANS_B], U8))
    ar = Arenas(Arena(ar_t, MAIN_B), Arena(ar_t, TRANS_B, base=MAIN_B, nextfit=True))
    banks = [es.enter_context(nc.psum_tensor("bank%d" % i, [128, 512], F32)) for i in range(8)]
    bank_res = [Res() for _ in range(8)]
    bank_ids = set(id(b) for b in bank_res)
    out_res = [Res() for _ in range(NT)]

    def A(eng, meth, r, w, *a, **k):
        rr = []
        for x in r:
            rr.extend(x if isinstance(x, (list, tuple)) else [x])
        ww = []
        for x in w:
            ww.extend(x if isinstance(x, (list, tuple)) else [x])
        for x in rr:
            if id(x) in bank_ids:
                ww.append(x)
        return sc.add(eng, meth, rr, ww, *a, **k)

    tp_i = [0]

    def tp_bank():
        tp_i[0] ^= 1
        return tp_i[0]

    mm_i = [0]

    def mm_bank():
        mm_i[0] = (mm_i[0] + 1) % 6
        return 2 + mm_i[0]

    cp_i = [0]

    def copy_eng():
        cp_i[0] ^= 1
        return "act" if cp_i[0] else "dve"

    def copy(eng, r, w, out, in_):
        if eng == "act":
            A("act", "activation", r, w, out=out, in_=in_, func=AF.Copy)
        elif eng == "dve":
            A("dve", "tensor_copy", r, w, out=out, in_=in_)
        else:
            A("pool", "tensor_copy", r, w, out=out, in_=in_)

    ident = ar.alloc("ident", [128, 128], BF16)
    identf = ar.alloc("identf", [128, 128], F32)
    A("pool", "memset", [], [identf.res], identf.ap, 0.0)
    A("pool", "iota", [], [identf.res], identf.ap, pattern=[[1, 128]], base=0, channel_multiplier=-1,
      allow_small_or_imprecise_dtypes=True)
    A("dve", "tensor_scalar", [identf.res], [ident.res], out=ident.ap, in0=identf.ap, scalar1=0.0, scalar2=None,
      op0=ALU.is_equal)
    epsb = ar.alloc("eps", [128, 1], F32)
    A("dve", "memset", [], [epsb.res], epsb.ap, EPS)
    zerob = ar.alloc("zero", [128, 1], F32)
    A("dve", "memset", [], [zerob.res], zerob.ap, 0.0)
    junk = ar.alloc("junk", [128, 1024], BF16)

    xv = x_d.rearrange("(t p) d -> p t d", p=128)
    for t in range(NT):
        A("sp", "dma_start", [], [x_res[t]], is_dma=True, out=xs[:, t, :], in_=xv[:, t, :])

    pk = ar.alloc("pk", [128, DEPTH, NPK], F32)
    A("sp", "dma_start", [], [pk.res], is_dma=True, out=pk.ap, in_=pk_d)

    posi = ar.alloc("posi", [128, NT], I32)
    for t in range(NT):
        A("sp", "dma_start", [], [posi.res], is_dma=True, out=posi.ap[:, t:t + 1],
          in_=pos_d[t:t + 1, :].rearrange("a p -> p a"))
    posf = ar.alloc("posf", [128, NT], F32)
    A("dve", "tensor_copy", [posi.res], [posf.res], out=posf.ap, in_=posi.ap)
    ang = ar.alloc("ang", [128, NT, 16], F32)
    for f in range(16):
        A("dve", "tensor_scalar", [posf.res], [ang.res], out=ang.ap[:, :, f], in0=posf.ap,
          scalar1=float(THETA ** (-2.0 * f / 32.0)), scalar2=None, op0=ALU.mult)
    cosA = ar.alloc("cosA", [128, NT, 16], F32)
    sinA = ar.alloc("sinA", [128, NT, 16], F32)
    TWO_PI = 2.0 * math.pi
    C1 = 6.28125
    C2 = TWO_PI - C1
    tA = ar.alloc("tA", [128, NT, 16], F32)
    tK = ar.alloc("tK", [128, NT, 16], I32)
    tKf = ar.alloc("tKf", [128, NT, 16], F32)
    tM = ar.alloc("tM", [128, NT, 16], F32)
    for (dst, shift) in ((sinA, 0.0), (cosA, math.pi / 2)):
        A("dve", "tensor_scalar", [ang.res], [tA.res], out=tA.ap, in0=ang.ap, scalar1=shift, scalar2=None,
          op0=ALU.add)
        A("dve", "tensor_scalar", [tA.res], [tM.res], out=tM.ap, in0=tA.ap, scalar1=1.0 / TWO_PI, scalar2=None,
          op0=ALU.mult)
        A("dve", "tensor_copy", [tM.res], [tK.res], out=tK.ap, in_=tM.ap)
        A("dve", "tensor_copy", [tK.res], [tKf.res], out=tKf.ap, in_=tK.ap)
        A("dve", "scalar_tensor_tensor", [tKf.res, tA.res], [tM.res], out=tM.ap, in0=tKf.ap, scalar=-C1,
          in1=tA.ap, op0=ALU.mult, op1=ALU.add)
        A("dve", "scalar_tensor_tensor", [tKf.res, tM.res], [tA.res], out=tA.ap, in0=tKf.ap, scalar=-C2,
          in1=tM.ap, op0=ALU.mult, op1=ALU.add)
        A("dve", "tensor_scalar", [tA.res], [tM.res], out=tM.ap, in0=tA.ap, scalar1=math.pi, scalar2=-TWO_PI,
          op0=ALU.is_gt, op1=ALU.mult)
        A("dve", "tensor_tensor", [tA.res, tM.res], [tKf.res], out=tKf.ap, in0=tA.ap, in1=tM.ap, op=ALU.add)
        A("dve", "tensor_scalar", [tKf.res], [tM.res], out=tM.ap, in0=tKf.ap, scalar1=-math.pi, scalar2=TWO_PI,
          op0=ALU.is_lt, op1=ALU.mult)
        A("dve", "tensor_tensor", [tKf.res, tM.res], [tA.res], out=tA.ap, in0=tKf.ap, in1=tM.ap, op=ALU.add)
        A("dve", "tensor_scalar", [tA.res], [tM.res], out=tM.ap, in0=tA.ap, scalar1=math.pi, scalar2=-math.pi,
          op0=ALU.min, op1=ALU.max)
        A("act", "activation", [tM.res], [dst.res], out=dst.ap, in_=tM.ap, func=AF.Sin)
    ar.free(tA, tK, tKf, tM, ang, posi, posf, identf)

    def rstd_from(ss, ncol, dim, dst):
        A("act", "activation", [ss[1], epsb.res], [dst[1]], out=dst[0], in_=ss[0], func=AF.Ln, scale=1.0 / dim,
          bias=epsb.ap[:, 0:1])
        A("act", "activation", [dst[1]], [dst[1]], out=dst[0], in_=dst[0], func=AF.Exp, scale=-0.5)

    def load_w(name, shape, src, scale_cols=None, l=0, chunk_cols=None):
        b = ar.alloc(name, shape, BF16)
        kc, n = shape[1], shape[2]
        step = 2048
        for c in range(kc):
            for n0 in range(0, n, step):
                n1 = min(n, n0 + step)
                A("pool", "dma_start", [], [b.res], is_dma=True, out=b.ap[:, c, n0:n1], in_=src[:, c, n0:n1])
            if scale_cols is not None:
                A("pool", "tensor_scalar", [b.res, pk.res], [b.res], out=b.ap[:, c, :], in0=b.ap[:, c, :],
                  scalar1=pk.ap[:, l, scale_cols + c:scale_cols + c + 1], scalar2=1.0, op0=ALU.mult, op1=ALU.mult)
        return b

    def transposes(srcs, src_res, dst_ap, dst_res, rows=128):
        bk = tp_bank()
        pv = banks[bk][:].bitcast(BF16).rearrange("p (a b) -> p a b", b=128)
        for i, s_ap in enumerate(srcs):
            A("pe", "transpose", [src_res, ident.res], [bank_res[bk]], out=pv[0:rows, i, :], in_=s_ap,
              identity=ident.ap)
        copy(copy_eng(), [bank_res[bk]], [dst_res], dst_ap, pv[0:rows, 0:len(srcs), :])

    def attention(QT, KT, krows, roff, V_of_kc, v_res, nkc, dv, scale, evac):
        for qb in range(4):
            accb = [mm_bank() for _ in range(4)]
            for kc in range(nkc):
                sb_ = tp_bank()
                A("pe", "matmul", [QT[1], KT[1]], [bank_res[sb_]], banks[sb_][:, :],
                  lhsT=KT[0][roff:roff + krows, kc * 128:(kc + 1) * 128],
                  rhs=QT[0][roff:roff + krows, qb * 512:(qb + 1) * 512], start=True, stop=True,
                  **({"tile_position": (roff, 0)} if krows == 32 else {}))
                pt = ar.alloc("pt", [128, 512], BF16)
                A("act", "activation", [bank_res[sb_]], [pt.res], out=pt.ap, in_=banks[sb_][:, :], func=AF.Exp,
                  scale=scale)
                for j in range(4):
                    A("pe", "matmul", [pt.res, v_res], [bank_res[accb[j]]], banks[accb[j]][:, 0:dv + 1],
                      lhsT=pt.ap[:, j * 128:(j + 1) * 128], rhs=V_of_kc(kc), start=(kc == 0), stop=(kc == nkc - 1))
                ar.free(pt)
            for j in range(4):
                evac(qb * 4 + j, banks[accb[j]], bank_res[accb[j]])

    for l in range(n_layers):
        if stop_after == "setup":
            break
        lam_init = 0.8 - 0.6 * math.exp(-0.3 * l)
        bcs = ar.alloc("bcs", [128, 448], F32)
        A("sp", "dma_start", [], [bcs.res], is_dma=True, out=bcs.ap, in_=bc_d[l:l + 1, 3072:3520].partition_broadcast(128))
        lamt = ar.alloc("lamt", [128, 8], F32)
        ltmp = ar.alloc("ltmp", [128, 64], F32)
        A("dve", "tensor_tensor", [bcs.res], [ltmp.res], out=ltmp.ap[:, 0:32], in0=bcs.ap[:, 320:352],
          in1=bcs.ap[:, 352:384], op=ALU.mult)
        A("dve", "tensor_tensor", [bcs.res], [ltmp.res], out=ltmp.ap[:, 32:64], in0=bcs.ap[:, 384:416],
          in1=bcs.ap[:, 416:448], op=ALU.mult)
        A("dve", "tensor_reduce", [ltmp.res], [lamt.res], out=lamt.ap[:, 0:2],
          in_=ltmp.ap.rearrange("p (a b) -> p a b", b=32), axis=AX.X, op=ALU.add)
        A("act", "activation", [lamt.res], [lamt.res], out=lamt.ap[:, 2:4], in_=lamt.ap[:, 0:2], func=AF.Exp)
        A("dve", "tensor_tensor", [lamt.res], [lamt.res], out=lamt.ap[:, 4:5], in0=lamt.ap[:, 3:4], in1=lamt.ap[:, 2:3],
          op=ALU.subtract)
        A("dve", "tensor_scalar", [lamt.res], [lamt.res], out=lamt.ap[:, 5:6], in0=lamt.ap[:, 4:5], scalar1=-lam_init,
          scalar2=None, op0=ALU.add)
        neglam = lamt.ap[:, 5:6]
        A("dve", "tensor_scalar", [bcs.res], [bcs.res], out=bcs.ap[:, 256:320], in0=bcs.ap[:, 256:320],
          scalar1=1.0 - lam_init, scalar2=None, op0=ALU.mult)
        ar.free(ltmp)

        w_in_t = load_w("w_in", [128, 8, INW], w_in_d[l].rearrange("(c p) n -> p c n", p=128), scale_cols=0, l=l)
        wsT = load_w("wsT", [128, 1, 512], wsT_d[l].rearrange("p (a n) -> p a n", a=1))

        ss1 = ar.alloc("ss1", [128, NT], F32)
        rs1 = ar.alloc("rs1", [128, NT], F32)
        for t in range(NT):
            A("act", "activation", [x_res[t]], [junk.res, ss1.res], out=junk.ap, in_=xs[:, t, :], func=AF.Square,
              accum_out=ss1.ap[:, t:t + 1])
        rstd_from((ss1.ap, ss1.res), NT, D, (rs1.ap, rs1.res))
        cqnT = ar.alloc("cqnT", [128, 3, S], BF16)
        ckvnT = ar.alloc("ckvnT", [128, 2, S], BF16)
        krtm = ar.alloc("krtm", [128, NT, 32], BF16)
        dqT = ar.alloc("dqT", [128, 2, S], BF16)
        dkT = ar.alloc("dkT", [128, 2, S], BF16)
        dV = ar.alloc("dV", [128, NT * 4, 65], BF16)
        A("pool", "memset", [], [dV.res], dV.ap[:, :, 64:65], 1.0)
        cb = ar.alloc("cb", [128, NT, 256], BF16)
        COLS = ((0, 384), (384, 672), (672, 1184), (1184, 1696), (1696, 1952))
        for t in range(NT):
            tc_ = slice(t * 128, (t + 1) * 128)
            hn = ar.alloc("hn", [128, D], BF16)
            A("dve", "tensor_scalar", [x_res[t], rs1.res], [hn.res], out=hn.ap, in0=xs[:, t, :],
              scalar1=rs1.ap[:, t:t + 1], scalar2=None, op0=ALU.mult)
            hT = ar.alloc("hT", [128, 8, 128], BF16)
            transposes([hn.ap[:, c * 128:(c + 1) * 128] for c in range(8)], hn.res, hT.ap, hT.res)
            ar.free(hn)
            pj = [mm_bank() for _ in range(5)]
            for gi, (c0, c1) in enumerate(COLS):
                for c in range(8):
                    A("pe", "matmul", [hT.res, w_in_t.res], [bank_res[pj[gi]]], banks[pj[gi]][:, 0:c1 - c0],
                      lhsT=hT.ap[:, c, :], rhs=w_in_t.ap[:, c, c0:c1], start=(c == 0), stop=(c == 7))
            ar.free(hT)
            ssq = ar.alloc("ssq", [128, 2], F32)
            rsq = ar.alloc("rsq", [128, 2], F32)
            A("act", "activation", [bank_res[pj[0]]], [junk.res, ssq.res], out=junk.ap[:, 0:384],
              in_=banks[pj[0]][:, 0:384], func=AF.Square, accum_out=ssq.ap[:, 0:1])
            A("act", "activation", [bank_res[pj[1]]], [junk.res, ssq.res], out=junk.ap[:, 0:256],
              in_=banks[pj[1]][:, 0:256], func=AF.Square, accum_out=ssq.ap[:, 1:2])
            A("act", "activation", [ssq.res, epsb.res], [rsq.res], out=rsq.ap[:, 0:1], in_=ssq.ap[:, 0:1], func=AF.Ln,
              scale=1.0 / 384, bias=epsb.ap[:, 0:1])
            A("act", "activation", [ssq.res, epsb.res], [rsq.res], out=rsq.ap[:, 1:2], in_=ssq.ap[:, 1:2], func=AF.Ln,
              scale=1.0 / 256, bias=epsb.ap[:, 0:1])
            A("act", "activation", [rsq.res], [rsq.res], out=rsq.ap, in_=rsq.ap, func=AF.Exp, scale=-0.5)
            cn = ar.alloc("cn", [128, 640], BF16)
            A("dve", "tensor_scalar", [bank_res[pj[0]], rsq.res], [cn.res], out=cn.ap[:, 0:384],
              in0=banks[pj[0]][:, 0:384], scalar1=rsq.ap[:, 0:1], scalar2=None, op0=ALU.mult)
            A("dve", "tensor_scalar", [bank_res[pj[1]], rsq.res], [cn.res], out=cn.ap[:, 384:640],
              in0=banks[pj[1]][:, 0:256], scalar1=rsq.ap[:, 1:2], scalar2=None, op0=ALU.mult)
            transposes([cn.ap[:, c * 128:(c + 1) * 128] for c in range(3)], cn.res, cqnT.ap[:, :, tc_], cqnT.res)
            transposes([cn.ap[:, 384 + c * 128:384 + (c + 1) * 128] for c in range(2)], cn.res, ckvnT.ap[:, :, tc_],
                       ckvnT.res)
            ar.free(ssq, rsq, cn)
            rt = ar.alloc("rt", [128, 4, 16], F32)
            kb = banks[pj[1]]
            kr_res = bank_res[pj[1]]
            cA = cosA.ap[:, t, :]
            sA = sinA.ap[:, t, :]
            A("dve", "tensor_tensor", [kr_res, cosA.res], [rt.res], out=rt.ap[:, 0, :], in0=kb[:, 256:272], in1=cA,
              op=ALU.mult)
            A("dve", "tensor_tensor", [kr_res, sinA.res], [rt.res], out=rt.ap[:, 1, :], in0=kb[:, 272:288], in1=sA,
              op=ALU.mult)
            A("dve", "tensor_tensor", [kr_res, cosA.res], [rt.res], out=rt.ap[:, 2, :], in0=kb[:, 272:288], in1=cA,
              op=ALU.mult)
            A("dve", "tensor_tensor", [kr_res, sinA.res], [rt.res], out=rt.ap[:, 3, :], in0=kb[:, 256:272], in1=sA,
              op=ALU.mult)
            A("dve", "tensor_tensor", [rt.res], [krtm.res], out=krtm.ap[:, t, 0:16], in0=rt.ap[:, 0, :],
              in1=rt.ap[:, 1, :], op=ALU.subtract)
            A("dve", "tensor_tensor", [rt.res], [krtm.res], out=krtm.ap[:, t, 16:32], in0=rt.ap[:, 2, :],
              in1=rt.ap[:, 3, :], op=ALU.add)
            ar.free(rt)
            zg = ar.alloc("zg", [128, 512], F32)
            A("act", "activation", [bank_res[pj[2]]], [zg.res], out=zg.ap, in_=banks[pj[2]][:, :],
              func=AF.Gelu_apprx_tanh)
            st = ar.alloc("st", [128, 16], F32)
            vsq = ar.alloc("vsq", [128, 256], F32)
            v3 = zg.ap[:, 256:512].rearrange("p (g c) -> p g c", c=64)
            A("dve", "tensor_reduce", [zg.res], [st.res], out=st.ap[:, 0:4], in_=v3, axis=AX.X, op=ALU.add)
            A("act", "activation", [zg.res], [vsq.res], out=vsq.ap, in_=zg.ap[:, 256:512], func=AF.Square)
            A("dve", "tensor_reduce", [vsq.res], [st.res], out=st.ap[:, 4:8],
              in_=vsq.ap.rearrange("p (g c) -> p g c", c=64), axis=AX.X, op=ALU.add)
            A("dve", "tensor_scalar", [st.res], [st.res], out=st.ap[:, 8:12], in0=st.ap[:, 0:4], scalar1=1.0 / 64,
              scalar2=None, op0=ALU.mult)
            A("dve", "tensor_tensor", [st.res], [st.res], out=st.ap[:, 12:16], in0=st.ap[:, 8:12], in1=st.ap[:, 8:12],
              op=ALU.mult)
            A("dve", "scalar_tensor_tensor", [st.res], [st.res], out=st.ap[:, 4:8], in0=st.ap[:, 4:8], scalar=1.0 / 64,
              in1=st.ap[:, 12:16], op0=ALU.mult, op1=ALU.subtract)
            A("act", "activation", [st.res, epsb.res], [st.res], out=st.ap[:, 0:4], in_=st.ap[:, 4:8], func=AF.Ln,
              bias=epsb.ap[:, 0:1])
            A("act", "activation", [st.res], [st.res], out=st.ap[:, 0:4], in_=st.ap[:, 0:4], func=AF.Exp, scale=-0.5)
            vq3 = vsq.ap.rearrange("p (g c) -> p g c", c=64)
            A("dve", "tensor_tensor", [zg.res, st.res], [vsq.res], out=vq3, in0=v3,
              in1=st.ap[:, 8:12].unsqueeze(2).broadcast_to([128, 4, 64]), op=ALU.subtract)
            A("dve", "tensor_tensor", [vsq.res, st.res], [vsq.res], out=vq3, in0=vq3,
              in1=st.ap[:, 0:4].unsqueeze(2).broadcast_to([128, 4, 64]), op=ALU.mult)
            vn = ar.alloc("vn", [128, 256], BF16)
            A("dve", "tensor_tensor", [vsq.res, bcs.res], [vn.res], out=vn.ap, in0=vsq.ap, in1=bcs.ap[:, 0:256],
              op=ALU.mult)
            mb = mm_bank()
            for g in range(4):
                A("pe", "matmul", [vn.res, wsT.res], [bank_res[mb]], banks[mb][:, g * 64:(g + 1) * 64],
                  lhsT=wsT.ap[:, 0, g * 128:(g + 1) * 128], rhs=vn.ap[:, g * 64:(g + 1) * 64], start=True, stop=True)
            for g in range(4):
                A("dve", "scalar_tensor_tensor", [bank_res[mb], pk.res, zg.res], [cb.res],
                  out=cb.ap[:, t, g * 64:(g + 1) * 64], in0=banks[mb][:, g * 64:(g + 1) * 64],
                  scalar=pk.ap[:, l, 37 + g:38 + g], in1=zg.ap[:, g * 64:(g + 1) * 64], op0=ALU.add, op1=ALU.mult)
            ar.free(zg, st, vsq, vn)
            qk = ar.alloc("qk", [128, 16, 32], BF16)
            r6 = ar.alloc("r6", [128, 4 * 16, 8], F32)
            pq = banks[pj[3]][:, :].rearrange("p (m d) -> p m d", d=32)
            qres = bank_res[pj[3]]
            cD = cosA.ap[:, t, 0:16:2].unsqueeze(1).broadcast_to([128, 16, 8])
            sD = sinA.ap[:, t, 0:16:2].unsqueeze(1).broadcast_to([128, 16, 8])
            A("dve", "tensor_tensor", [qres, cosA.res], [r6.res], out=r6.ap[:, 0:16, :], in0=pq[:, :, 0:8], in1=cD,
              op=ALU.mult)
            A("dve", "tensor_tensor", [qres, sinA.res], [r6.res], out=r6.ap[:, 16:32, :], in0=pq[:, :, 8:16], in1=sD,
              op=ALU.mult)
            A("dve", "tensor_tensor", [qres, cosA.res], [r6.res], out=r6.ap[:, 32:48, :], in0=pq[:, :, 8:16], in1=cD,
              op=ALU.mult)
            A("dve", "tensor_tensor", [qres, sinA.res], [r6.res], out=r6.ap[:, 48:64, :], in0=pq[:, :, 0:8], in1=sD,
              op=ALU.mult)
            A("dve", "tensor_tensor", [r6.res], [qk.res], out=qk.ap[:, :, 0:8], in0=r6.ap[:, 0:16, :],
              in1=r6.ap[:, 16:32, :], op=ALU.subtract)
            A("dve", "tensor_tensor", [r6.res], [qk.res], out=qk.ap[:, :, 8:16], in0=r6.ap[:, 32:48, :],
              in1=r6.ap[:, 48:64, :], op=ALU.add)
            A("act", "activation", [qres], [qk.res], out=qk.ap[:, :, 16:32], in_=pq[:, :, 16:32], func=AF.Copy)
            qk2 = qk.ap.rearrange("p m d -> p (m d)")
            transposes([qk2[:, c * 128:(c + 1) * 128] for c in range(2)], qk.res, dqT.ap[:, :, tc_], dqT.res)
            transposes([qk2[:, 256 + c * 128:256 + (c + 1) * 128] for c in range(2)], qk.res, dkT.ap[:, :, tc_],
                       dkT.res)
            ar.free(qk, r6)
            A("act", "activation", [bank_res[pj[4]]], [dV.res], out=dV.ap[:, t * 4:(t + 1) * 4, 0:64],
              in_=banks[pj[4]][:, 0:256].rearrange("p (h e) -> p h e", e=64), func=AF.Copy)
        ar.free(ss1, rs1, w_in_t, wsT)
        if stop_after == "p1":
            break

        cc = ar.alloc("cc", [128, NT, 256], BF16)
        w_mo_t = load_w("w_mo", [128, 8, D], w_mo_d[l].rearrange("(c p) n -> p c n", p=128))
        w_uq_t = load_w("w_uq", [128, 3, 768], w_uq_d[l].rearrange("(c p) n -> p c n", p=128), scale_cols=32, l=l)
        w_ukv_t = load_w("w_ukv", [128, 2, 1024], w_ukv_d[l].rearrange("(c p) n -> p c n", p=128), scale_cols=35, l=l)
        for h in range(4):
            o0 = ar.alloc("o0", [128, NT, 64], F32)
            for c in range(2):
                m = 2 * h + c
                ci, ro = m // 4, 32 * (m % 4)

                def evac(qt, acc, acc_res, c=c, h=h, o0=o0):
                    rc = ar.alloc("rc", [128, 4], F32)
                    A("dve", "reciprocal", [acc_res], [rc.res], out=rc.ap[:, 0:1], in_=acc[:, 64:65])
                    if c == 0:
                        A("dve", "tensor_scalar", [acc_res, rc.res], [o0.res], out=o0.ap[:, qt, :], in0=acc[:, 0:64],
                          scalar1=rc.ap[:, 0:1], scalar2=None, op0=ALU.mult)
                    else:
                        o1 = ar.alloc("o1", [128, 128], F32)
                        A("dve", "tensor_scalar", [acc_res, rc.res, lamt.res], [o1.res], out=o1.ap[:, 0:64],
                          in0=acc[:, 0:64], scalar1=rc.ap[:, 0:1], scalar2=neglam, op0=ALU.mult, op1=ALU.mult)
                        A("dve", "tensor_tensor", [o1.res, o0.res], [o1.res], out=o1.ap[:, 0:64], in0=o1.ap[:, 0:64],
                          in1=o0.ap[:, qt, :], op=ALU.add)
                        A("act", "activation", [o1.res], [o1.res, rc.res], out=o1.ap[:, 64:128], in_=o1.ap[:, 0:64],
                          func=AF.Square, accum_out=rc.ap[:, 1:2])
                        A("act", "activation", [rc.res, epsb.res], [rc.res], out=rc.ap[:, 2:3], in_=rc.ap[:, 1:2],
                          func=AF.Ln, scale=1.0 / 64, bias=epsb.ap[:, 0:1])
                        A("act", "activation", [rc.res], [rc.res], out=rc.ap[:, 3:4], in_=rc.ap[:, 2:3], func=AF.Exp,
                          scale=-0.5)
                        A("dve", "scalar_tensor_tensor", [o1.res, rc.res, bcs.res], [cc.res],
                          out=cc.ap[:, qt, h * 64:(h + 1) * 64], in0=o1.ap[:, 0:64], scalar=rc.ap[:, 3:4],
                          in1=bcs.ap[:, 256:320], op0=ALU.mult, op1=ALU.mult)
                        ar.free(o1)
                    ar.free(rc)

                attention((dqT.ap[:, ci, :], dqT.res), (dkT.ap[:, ci, :], dkT.res), 32, ro,
                          lambda kc, h=h: dV.ap[:, kc * 4 + h, :], dV.res, NT, 64, 32 ** -0.5, evac)
            ar.free(o0)
        ar.free(dqT, dkT, dV)
        if stop_after == "p2":
            break

        ca = ar.alloc("ca", [128, NT, 512], BF16)
        for h in range(8):
            QKT = ar.alloc("QKT", [128, 2, S], BF16)
            Vh = ar.alloc("Vh", [128, NT, 65], BF16)
            A("pool", "memset", [], [Vh.res], Vh.ap[:, :, 64:65], 1.0)
            for t in range(NT):
                tc_ = slice(t * 128, (t + 1) * 128)
                mb = mm_bank()
                for c in range(3):
                    A("pe", "matmul", [cqnT.res, w_uq_t.res], [bank_res[mb]], banks[mb][:, 0:96],
                      lhsT=cqnT.ap[:, c, tc_], rhs=w_uq_t.ap[:, c, h * 96:(h + 1) * 96], start=(c == 0), stop=(c == 2))
                for c in range(2):
                    A("pe", "matmul", [ckvnT.res, w_ukv_t.res], [bank_res[mb]], banks[mb][:, 128:256],
                      lhsT=ckvnT.ap[:, c, tc_], rhs=w_ukv_t.ap[:, c, h * 128:(h + 1) * 128], start=(c == 0),
                      stop=(c == 1))
                qk = ar.alloc("qk", [128, 2, 128], BF16)
                A("pool", "memset", [], [qk.res], qk.ap[:, :, 96:128], 0.0)
                bk = banks[mb]
                br = bank_res[mb]
                A("act", "activation", [br], [qk.res], out=qk.ap[:, 0, 0:64], in_=bk[:, 0:64], func=AF.Copy)
                A("act", "activation", [br], [qk.res], out=qk.ap[:, 1, 0:64], in_=bk[:, 128:192], func=AF.Copy)
                A("act", "activation", [br], [Vh.res], out=Vh.ap[:, t, 0:64], in_=bk[:, 192:256], func=AF.Copy)
                rt = ar.alloc("rt", [128, 4, 16], F32)
                cA = cosA.ap[:, t, :]
                sA = sinA.ap[:, t, :]
                A("dve", "tensor_tensor", [br, cosA.res], [rt.res], out=rt.ap[:, 0, :], in0=bk[:, 64:80], in1=cA,
                  op=ALU.mult)
                A("dve", "tensor_tensor", [br, sinA.res], [rt.res], out=rt.ap[:, 1, :], in0=bk[:, 80:96], in1=sA,
                  op=ALU.mult)
                A("dve", "tensor_tensor", [br, cosA.res], [rt.res], out=rt.ap[:, 2, :], in0=bk[:, 80:96], in1=cA,
                  op=ALU.mult)
                A("dve", "tensor_tensor", [br, sinA.res], [rt.res], out=rt.ap[:, 3, :], in0=bk[:, 64:80], in1=sA,
                  op=ALU.mult)
                A("dve", "tensor_tensor", [rt.res], [qk.res], out=qk.ap[:, 0, 64:80], in0=rt.ap[:, 0, :],
                  in1=rt.ap[:, 1, :], op=ALU.subtract)
                A("dve", "tensor_tensor", [rt.res], [qk.res], out=qk.ap[:, 0, 80:96], in0=rt.ap[:, 2, :],
                  in1=rt.ap[:, 3, :], op=ALU.add)
                A("dve", "tensor_copy", [krtm.res], [qk.res], out=qk.ap[:, 1, 64:96], in_=krtm.ap[:, t, :])
                transposes([qk.ap[:, 0, :], qk.ap[:, 1, :]], qk.res, QKT.ap[:, :, tc_], QKT.res)
                ar.free(qk, rt)

            if stop_after == "p3a":
                break

            def evac_a(qt, acc, acc_res, h=h):
                rc = ar.alloc("rc", [128, 4], F32)
                A("dve", "reciprocal", [acc_res], [rc.res], out=rc.ap[:, 0:1], in_=acc[:, 64:65])
                A("dve", "tensor_scalar", [acc_res, rc.res], [ca.res], out=ca.ap[:, qt, h * 64:(h + 1) * 64],
                  in0=acc[:, 0:64], scalar1=rc.ap[:, 0:1], scalar2=None, op0=ALU.mult)
                ar.free(rc)

            attention((QKT.ap[:, 0, :], QKT.res), (QKT.ap[:, 1, :], QKT.res), 128, 0,
                      lambda kc, Vh=Vh: Vh.ap[:, kc, :], Vh.res, NT, 64, 96 ** -0.5, evac_a)
            ar.free(QKT, Vh)
            if stop_after == "p3b":
                break
        if stop_after in ("p3a", "p3b"):
            break
        ar.free(cqnT, ckvnT, krtm, w_uq_t, w_ukv_t)
        if stop_after == "p3":
            break

        def post_norm_add(t, bk2, gpost):
            s2 = ar.alloc("s2", [128, 4], F32)
            for i in range(2):
                A("act", "activation", [bank_res[bk2[i]]], [junk.res, s2.res], out=junk.ap[:, 0:512],
                  in_=banks[bk2[i]][:, :], func=AF.Square, accum_out=s2.ap[:, i:i + 1])
            A("dve", "tensor_tensor", [s2.res], [s2.res], out=s2.ap[:, 2:3], in0=s2.ap[:, 0:1], in1=s2.ap[:, 1:2],
              op=ALU.add)
            A("act", "activation", [s2.res, epsb.res], [s2.res], out=s2.ap[:, 3:4], in_=s2.ap[:, 2:3], func=AF.Ln,
              scale=1.0 / D, bias=epsb.ap[:, 0:1])
            A("act", "activation", [s2.res], [s2.res], out=s2.ap[:, 3:4], in_=s2.ap[:, 3:4], func=AF.Exp, scale=-0.5)
            tmp = ar.alloc("tmp", [128, D], F32)
            for i in range(2):
                A("dve", "scalar_tensor_tensor", [bank_res[bk2[i]], s2.res, gpost.res], [tmp.res],
                  out=tmp.ap[:, i * 512:(i + 1) * 512], in0=banks[bk2[i]][:, :], scalar=s2.ap[:, 3:4],
                  in1=gpost.ap[:, i * 512:(i + 1) * 512], op0=ALU.mult, op1=ALU.mult)
            A("pool", "tensor_tensor", [tmp.res, x_res[t]], [x_res[t]], out=xs[:, t, :], in0=xs[:, t, :], in1=tmp.ap,
              op=ALU.add)
            ar.free(s2, tmp)

        gpost = ar.alloc("gpost", [128, D], F32)
        A("sp", "dma_start", [], [gpost.res], is_dma=True, out=gpost.ap, in_=bc_d[l:l + 1, 0:1024].partition_broadcast(128))
        for t in range(NT):
            cT = ar.alloc("cT", [128, 8, 128], BF16)
            srcs = [ca.ap[:, t, c * 128:(c + 1) * 128] for c in range(4)] + \
                   [cb.ap[:, t, c * 128:(c + 1) * 128] for c in range(2)] + \
                   [cc.ap[:, t, c * 128:(c + 1) * 128] for c in range(2)]
            bk = tp_bank()
            pv = banks[bk][:].bitcast(BF16).rearrange("p (a b) -> p a b", b=128)
            for i, s_ap in enumerate(srcs):
                rr = ca.res if i < 4 else (cb.res if i < 6 else cc.res)
                A("pe", "transpose", [rr, ident.res], [bank_res[bk]], out=pv[:, i, :], in_=s_ap, identity=ident.ap)
            copy(copy_eng(), [bank_res[bk]], [cT.res], cT.ap, pv[:, 0:8, :])
            bk2 = [mm_bank(), mm_bank()]
            for i in range(2):
                for c in range(8):
                    A("pe", "matmul", [cT.res, w_mo_t.res], [bank_res[bk2[i]]], banks[bk2[i]][:, :],
                      lhsT=cT.ap[:, c, :], rhs=w_mo_t.ap[:, c, i * 512:(i + 1) * 512], start=(c == 0), stop=(c == 7))
            ar.free(cT)
            post_norm_add(t, bk2, gpost)
        ar.free(ca, cb, cc, w_mo_t, gpost, bcs, lamt)
        if stop_after == "mix":
            break

        w_kv_t = load_w("w_kv", [128, 8, 2 * D], w_kv_d[l].rearrange("(c p) n -> p c n", p=128), scale_cols=24, l=l)
        memf = ar.alloc("memf", [128, 2, D], F32)
        A("sp", "dma_start", [], [memf.res], is_dma=True, out=memf.ap, in_=mem_d.rearrange("(t p) d -> p t d", p=128))
        ssm = ar.alloc("ssm", [128, 2], F32)
        rsm = ar.alloc("rsm", [128, 2], F32)
        for t in range(2):
            A("act", "activation", [memf.res], [junk.res, ssm.res], out=junk.ap, in_=memf.ap[:, t, :], func=AF.Square,
              accum_out=ssm.ap[:, t:t + 1])
        rstd_from((ssm.ap, ssm.res), 2, D, (rsm.ap, rsm.res))
        memT = ar.alloc("memT", [128, 8, NMEM], BF16)
        for t in range(2):
            mn = ar.alloc("mn", [128, D], BF16)
            A("dve", "tensor_scalar", [memf.res, rsm.res], [mn.res], out=mn.ap, in0=memf.ap[:, t, :],
              scalar1=rsm.ap[:, t:t + 1], scalar2=None, op0=ALU.mult)
            transposes([mn.ap[:, c * 128:(c + 1) * 128] for c in range(8)], mn.res, memT.ap[:, :, t * 128:(t + 1) * 128],
                       memT.res)
            ar.free(mn)
        ar.free(memf, ssm, rsm)
        KmT = ar.alloc("KmT", [128, 8, NMEM], BF16)
        for hc in range(8):
            mb = mm_bank()
            for c in range(8):
                A("pe", "matmul", [memT.res, w_kv_t.res], [bank_res[mb]], banks[mb][:, 0:NMEM],
                  lhsT=w_kv_t.ap[:, c, hc * 128:(hc + 1) * 128], rhs=memT.ap[:, c, :], start=(c == 0), stop=(c == 7))
            copy(copy_eng(), [bank_res[mb]], [KmT.res], KmT.ap[:, hc, :], banks[mb][:, 0:NMEM])
        Vm = ar.alloc("Vm", [128, 8, 257], BF16)
        A("pool", "memset", [], [Vm.res], Vm.ap[:, :, 256:257], 1.0)
        for kt in range(2):
            for i in range(2):
                mb = mm_bank()
                for c in range(8):
                    A("pe", "matmul", [memT.res, w_kv_t.res], [bank_res[mb]], banks[mb][:, :],
                      lhsT=memT.ap[:, c, kt * 128:(kt + 1) * 128], rhs=w_kv_t.ap[:, c, D + i * 512:D + (i + 1) * 512],
                      start=(c == 0), stop=(c == 7))
                copy(copy_eng(), [bank_res[mb]], [Vm.res], Vm.ap[:, kt * 4 + 2 * i:kt * 4 + 2 * i + 2, 0:256],
                     banks[mb][:, :].rearrange("p (h e) -> p h e", e=256))
        ar.free(memT, w_kv_t)
        w_q_t = load_w("w_q", [128, 8, D], w_q_d[l].rearrange("(c p) n -> p c n", p=128), scale_cols=8, l=l)
        w_o_t = load_w("w_o", [128, 8, D], w_o_d[l].rearrange("(c p) n -> p c n", p=128))
        gpost = ar.alloc("gpost", [128, D], F32)
        A("sp", "dma_start", [], [gpost.res], is_dma=True, out=gpost.ap, in_=bc_d[l:l + 1, 1024:2048].partition_broadcast(128))
        ss5 = ar.alloc("ss5", [128, NT], F32)
        rs5 = ar.alloc("rs5", [128, NT], F32)
        for t in range(NT):
            A("act", "activation", [x_res[t]], [junk.res, ss5.res], out=junk.ap, in_=xs[:, t, :], func=AF.Square,
              accum_out=ss5.ap[:, t:t + 1])
        rstd_from((ss5.ap, ss5.res), NT, D, (rs5.ap, rs5.res))
        for qb in range(4):
            hTb = ar.alloc("hTb", [128, 8, 512], BF16)
            for j in range(4):
                t = qb * 4 + j
                hn = ar.alloc("hn", [128, D], BF16)
                A("dve", "tensor_scalar", [x_res[t], rs5.res], [hn.res], out=hn.ap, in0=xs[:, t, :],
                  scalar1=rs5.ap[:, t:t + 1], scalar2=None, op0=ALU.mult)
                transposes([hn.ap[:, c * 128:(c + 1) * 128] for c in range(8)], hn.res,
                           hTb.ap[:, :, j * 128:(j + 1) * 128], hTb.res)
                ar.free(hn)
            qTb = ar.alloc("qTb", [128, 8, 512], BF16)
            for hc in range(8):
                mb = mm_bank()
                for c in range(8):
                    A("pe", "matmul", [hTb.res, w_q_t.res], [bank_res[mb]], banks[mb][:, :],
                      lhsT=w_q_t.ap[:, c, hc * 128:(hc + 1) * 128], rhs=hTb.ap[:, c, :], start=(c == 0), stop=(c == 7))
                copy(copy_eng(), [bank_res[mb]], [qTb.res], qTb.ap[:, hc, :], banks[mb][:, :])
            ar.free(hTb)
            ao = ar.alloc("ao", [128, 4, D], BF16)
            for h in range(4):
                pts = []
                for kt in range(2):
                    sb_ = tp_bank()
                    for dc in range(2):
                        A("pe", "matmul", [qTb.res, KmT.res], [bank_res[sb_]], banks[sb_][:, :],
                          lhsT=KmT.ap[:, h * 2 + dc, kt * 128:(kt + 1) * 128], rhs=qTb.ap[:, h * 2 + dc, :],
                          start=(dc == 0), stop=(dc == 1))
                    pt = ar.alloc("ptm", [128, 512], BF16)
                    A("act", "activation", [bank_res[sb_]], [pt.res], out=pt.ap, in_=banks[sb_][:, :], func=AF.Exp,
                      scale=1.0 / 16.0)
                    pts.append(pt)
                for j in range(4):
                    mb = mm_bank()
                    for kt in range(2):
                        A("pe", "matmul", [pts[kt].res, Vm.res], [bank_res[mb]], banks[mb][:, 0:257],
                          lhsT=pts[kt].ap[:, j * 128:(j + 1) * 128], rhs=Vm.ap[:, kt * 4 + h, :], start=(kt == 0),
                          stop=(kt == 1))
                    rc = ar.alloc("rc", [128, 1], F32)
                    A("dve", "reciprocal", [bank_res[mb]], [rc.res], out=rc.ap[:, 0:1], in_=banks[mb][:, 256:257])
                    A("dve", "tensor_scalar", [bank_res[mb], rc.res], [ao.res], out=ao.ap[:, j, h * 256:(h + 1) * 256],
                      in0=banks[mb][:, 0:256], scalar1=rc.ap[:, 0:1], scalar2=None, op0=ALU.mult)
                    ar.free(rc)
                ar.free(*pts)
            ar.free(qTb)
            for j in range(4):
                t = qb * 4 + j
                aT = ar.alloc("aT", [128, 8, 128], BF16)
                transposes([ao.ap[:, j, c * 128:(c + 1) * 128] for c in range(8)], ao.res, aT.ap, aT.res)
                bk2 = [mm_bank(), mm_bank()]
                for i in range(2):
                    for c in range(8):
                        A("pe", "matmul", [aT.res, w_o_t.res], [bank_res[bk2[i]]], banks[bk2[i]][:, :],
                          lhsT=aT.ap[:, c, :], rhs=w_o_t.ap[:, c, i * 512:(i + 1) * 512], start=(c == 0), stop=(c == 7))
                ar.free(aT)
                post_norm_add(t, bk2, gpost)
            ar.free(ao)
        ar.free(KmT, Vm, w_q_t, w_o_t, gpost, ss5, rs5)
        if stop_after == "mem":
            break

        w_dn_t = load_w("w_dn", [128, NCH, D], w_dn_d[l].rearrange("(c p) n -> p c n", p=128))
        gpost = ar.alloc("gpost", [128, D], F32)
        A("sp", "dma_start", [], [gpost.res], is_dma=True, out=gpost.ap, in_=bc_d[l:l + 1, 2048:3072].partition_broadcast(128))
        ss6 = ar.alloc("ss6", [128, NT], F32)
        rs6 = ar.alloc("rs6", [128, NT], F32)
        for t in range(NT):
            A("act", "activation", [x_res[t]], [junk.res, ss6.res], out=junk.ap, in_=xs[:, t, :], func=AF.Square,
              accum_out=ss6.ap[:, t:t + 1])
        rstd_from((ss6.ap, ss6.res), NT, D, (rs6.ap, rs6.res))
        w_up_v = w_up_d[l].rearrange("(c p) n -> p c n", p=128)
        hl = ar.alloc("hl", [128, 8, 2], BF16)
        for qb in range(4):
            hTe = ar.alloc("hTe", [128, 8, 514], BF16)
            if qb == 0:
                A("dve", "memset", [], [hTe.res], hTe.ap[:, :, 0:1], 0.0)
            if qb == 3:
                A("dve", "memset", [], [hTe.res], hTe.ap[:, :, 513:514], 0.0)
            tl = list(range(qb * 4, qb * 4 + 4))
            if qb > 0:
                A("dve", "tensor_copy", [hl.res], [hTe.res], out=hTe.ap[:, :, 0:1], in_=hl.ap[:, :, (qb - 1) % 2:(qb - 1) % 2 + 1])
            if qb < 3:
                tl = tl + [qb * 4 + 4]
            for t in tl:
                hn = ar.alloc("hn", [128, D], BF16)
                A("dve", "tensor_scalar", [x_res[t], rs6.res], [hn.res], out=hn.ap, in0=xs[:, t, :],
                  scalar1=rs6.ap[:, t:t + 1], scalar2=None, op0=ALU.mult)
                bk = tp_bank()
                pv = banks[bk][:].bitcast(BF16).rearrange("p (a b) -> p a b", b=128)
                for c in range(8):
                    A("pe", "transpose", [hn.res, ident.res], [bank_res[bk]], out=pv[:, c, :],
                      in_=hn.ap[:, c * 128:(c + 1) * 128], identity=ident.ap)
                j = t - qb * 4
                if j < 0:
                    copy(copy_eng(), [bank_res[bk]], [hTe.res], hTe.ap[:, :, 0:1], pv[:, 0:8, 127:128])
                elif j > 3:
                    copy(copy_eng(), [bank_res[bk]], [hTe.res], hTe.ap[:, :, 513:514], pv[:, 0:8, 0:1])
                else:
                    copy(copy_eng(), [bank_res[bk]], [hTe.res], hTe.ap[:, :, 1 + j * 128:1 + (j + 1) * 128], pv[:, 0:8, :])
                ar.free(hn)
            A("dve", "tensor_copy", [hTe.res], [hl.res], out=hl.ap[:, :, qb % 2:qb % 2 + 1], in_=hTe.ap[:, :, 512:513])
            mT = ar.alloc("mT", [128, NCH, 512], BF16)
            for jc in range(NCH):
                wu = ar.alloc("wu", [128, 8, 256], BF16)
                A("pool", "dma_start", [], [wu.res], is_dma=True, out=wu.ap[:, :, 0:128],
                  in_=w_up_v[:, :, jc * 128:(jc + 1) * 128])
                A("pool", "dma_start", [], [wu.res], is_dma=True, out=wu.ap[:, :, 128:256],
                  in_=w_up_v[:, :, DFF + jc * 128:DFF + (jc + 1) * 128])
                A("pool", "tensor_tensor", [wu.res, pk.res], [wu.res], out=wu.ap, in0=wu.ap,
                  in1=pk.ap[:, l, 16:24].unsqueeze(2).broadcast_to([128, 8, 256]), op=ALU.mult)
                ys = []
                for gu in range(2):
                    mb = mm_bank()
                    hb = mm_bank()
                    for c in range(8):
                        A("pe", "matmul", [hTe.res, wu.res], [bank_res[mb]], banks[mb][:, :],
                          lhsT=wu.ap[:, c, gu * 128:(gu + 1) * 128], rhs=hTe.ap[:, c, 1:513], start=(c == 0), stop=(c == 7))
                    for c in range(8):
                        A("pe", "matmul", [hTe.res, wu.res], [bank_res[hb]], banks[hb][:, 0:2],
                          lhsT=wu.ap[:, c, gu * 128:(gu + 1) * 128], rhs=hTe.ap[:, c, 0:514:513], start=(c == 0),
                          stop=(c == 7))
                    ch = gu * NCH + jc
                    cw = pk.ap[:, l, 41 + ch * 4:41 + ch * 4 + 4]
                    y = ar.alloc("y", [128, 512], F32)
                    A("act", "activation", [bank_res[mb], pk.res], [y.res], out=y.ap, in_=banks[mb][:, :],
                      func=AF.Identity, scale=cw[:, 1:2], bias=cw[:, 3:4])
                    A("dve", "scalar_tensor_tensor", [bank_res[mb], pk.res, y.res], [y.res], out=y.ap[:, 1:512],
                      in0=banks[mb][:, 0:511], scalar=cw[:, 0:1], in1=y.ap[:, 1:512], op0=ALU.mult, op1=ALU.add)
                    A("dve", "scalar_tensor_tensor", [bank_res[hb], pk.res, y.res], [y.res], out=y.ap[:, 0:1],
                      in0=banks[hb][:, 0:1], scalar=cw[:, 0:1], in1=y.ap[:, 0:1], op0=ALU.mult, op1=ALU.add)
                    A("dve", "scalar_tensor_tensor", [bank_res[mb], pk.res, y.res], [y.res], out=y.ap[:, 0:511],
                      in0=banks[mb][:, 1:512], scalar=cw[:, 2:3], in1=y.ap[:, 0:511], op0=ALU.mult, op1=ALU.add)
                    A("dve", "scalar_tensor_tensor", [bank_res[hb], pk.res, y.res], [y.res], out=y.ap[:, 511:512],
                      in0=banks[hb][:, 1:2], scalar=cw[:, 2:3], in1=y.ap[:, 511:512], op0=ALU.mult, op1=ALU.add)
                    ys.append(y)
                ar.free(wu)
                A("act", "activation", [ys[0].res], [ys[0].res], out=ys[0].ap, in_=ys[0].ap, func=AF.Gelu_apprx_tanh)
                A("pool", "tensor_tensor", [ys[0].res, ys[1].res], [mT.res], out=mT.ap[:, jc, :], in0=ys[0].ap,
                  in1=ys[1].ap, op=ALU.mult)
                ar.free(*ys)
            ar.free(hTe)
            for j in range(4):
                t = qb * 4 + j
                bk2 = [mm_bank(), mm_bank()]
                for i in range(2):
                    for jc in range(NCH):
                        A("pe", "matmul", [mT.res, w_dn_t.res], [bank_res[bk2[i]]], banks[bk2[i]][:, :],
                          lhsT=mT.ap[:, jc, j * 128:(j + 1) * 128], rhs=w_dn_t.ap[:, jc, i * 512:(i + 1) * 512],
                          start=(jc == 0), stop=(jc == NCH - 1))
                post_norm_add(t, bk2, gpost)
            ar.free(mT)
        ar.free(w_dn_t, gpost, ss6, rs6, hl)

    ov = out_d.rearrange("(t p) d -> p t d", p=128)
    last = []
    for t in range(NT):
        last.append(A("sp", "dma_start", [x_res[t]], [out_res[t]], is_dma=True, out=ov[:, t, :], in_=xs[:, t, :]))
    fin = A("sp", "nop", [out_res], [])

    counts = sc.finalize()
    nsem = {e: max(1, (counts[e] + SEM_LIMIT - 1) // SEM_LIMIT) for e in ("pe", "act", "dve", "pool", "sp")}
    sems = {e: [es.enter_context(nc.semaphore("s_%s%d" % (e, i))) for i in range(nsem[e])] for e in nsem}
    dsems = {q: [es.enter_context(nc.semaphore("d_%s%d" % (q, i))) for i in range(NDS)] for q in ("sp", "pool")}

    def emit(ename, eng):
        waited = {}
        for op in sc.ops[ename]:
            for d in op.deps:
                if d.is_dma:
                    sm, val = dsems[d.eng][d.dsem], d.dval
                    key = ("d", d.eng, d.dsem)
                else:
                    sm, val = sems[d.eng][d.semk // SEM_LIMIT], d.semk % SEM_LIMIT + 1
                    key = ("c", d.eng, d.semk // SEM_LIMIT)
                if waited.get(key, 0) >= val:
                    continue
                eng.wait_ge(sm, val)
                waited[key] = val
            if op.meth == "nop":
                continue
            ins = getattr(eng, op.meth)(*op.args, **op.kw)
            if op.is_dma:
                ins.then_inc(dsems[op.eng][op.dsem], 16)
            elif op.sig:
                ins.then_inc(sems[op.eng][op.semk // SEM_LIMIT], 1)

    block = es.enter_context(nc.Block())

    @block.sync
    def _(e):
        emit("sp", e)

    @block.gpsimd
    def _(e):
        emit("pool", e)

    @block.tensor
    def _(e):
        emit("pe", e)

    @block.scalar
    def _(e):
        emit("act", e)

    @block.vector
    def _(e):
        emit("dve", e)

    es.close()
    stats = {e: len(sc.ops[e]) for e in ENGS}
    stats["arena_peak_kb"] = ar.peak
    return nc, stats


def prep_inputs(inp, n_layers=DEPTH):
    f = lambda a: np.ascontiguousarray(np.asarray(a, dtype=np.float32))
    pk = np.zeros((128, DEPTH, 217), np.float32)
    for l in range(DEPTH):
        pk[:, l, 0:8] = np.asarray(inp["mix_pre_g"])[l].reshape(8, 128).T
        pk[:, l, 8:16] = np.asarray(inp["mem_pre_g"])[l].reshape(8, 128).T
        pk[:, l, 16:24] = np.asarray(inp["ffn_pre_g"])[l].reshape(8, 128).T
        pk[:, l, 24:32] = np.asarray(inp["mem_kv_g"])[l].reshape(8, 128).T
        pk[:, l, 32:35] = np.asarray(inp["mla_cq_g"])[l].reshape(3, 128).T
        pk[:, l, 35:37] = np.asarray(inp["mla_ckv_g"])[l].reshape(2, 128).T
        pk[:, l, 37:41] = np.asarray(inp["sgu_b_s"])[l].T
        cw = np.asarray(inp["ffn_conv_w"])[l]
        cbv = np.asarray(inp["ffn_conv_b"])[l]
        c4 = np.concatenate([cw, cbv[None, :]], axis=0)
        pk[:, l, 41:217] = c4.reshape(4, 44, 128).transpose(2, 1, 0).reshape(128, 176)
    bc = np.concatenate([
        np.asarray(inp["mix_post_g"]), np.asarray(inp["mem_post_g"]), np.asarray(inp["ffn_post_g"]),
        np.asarray(inp["sgu_norm_g"]).reshape(DEPTH, 256), np.asarray(inp["diff_sub_g"]),
        np.asarray(inp["diff_lam_q1"]), np.asarray(inp["diff_lam_k1"]),
        np.asarray(inp["diff_lam_q2"]), np.asarray(inp["diff_lam_k2"])], axis=1).astype(np.float32)
    wsT = np.asarray(inp["sgu_w_s"]).transpose(0, 3, 1, 2).reshape(DEPTH, 128, 512)
    nl = n_layers
    f = lambda a: np.ascontiguousarray(np.asarray(a, dtype=np.float32)[:nl])
    shared = {
        "w_in": f(inp["w_in"]), "w_uq": f(inp["mla_w_uq"]), "w_ukv": f(inp["mla_w_ukv"]), "wsT": f(wsT),
        "w_mo": f(inp["w_mix_out"]), "w_q": f(inp["mem_w_q"]), "w_kv": f(inp["mem_w_kv"]), "w_o": f(inp["mem_w_o"]),
        "w_up": f(inp["ffn_w_up"]), "w_dn": f(inp["ffn_w_down"]),
        "pk": np.ascontiguousarray(pk), "bc": np.ascontiguousarray(bc),
    }
    x = np.asarray(inp["x"], dtype=np.float32)
    mem = np.asarray(inp["mem"], dtype=np.float32)
    pos = np.asarray(inp["positions"]).astype(np.int32)
    maps = []
    for b in range(x.shape[0]):
        m = dict(shared)
        m["x"] = np.ascontiguousarray(x[b])
        m["mem"] = np.ascontiguousarray(mem[b])
        m["pos"] = np.ascontiguousarray(pos[b].reshape(NT, 128))
        maps.append(m)
    return maps


def kernel(**inputs):
    maps = prep_inputs(inputs)
    nc, _ = build()
    res = run_bass_kernel_spmd(nc, maps, core_ids=list(range(8)))
    return np.stack([np.asarray(r["out"], dtype=np.float32) for r in res.results], axis=0)
```

```python
import math
from contextlib import ExitStack

import numpy as np
import concourse.bass as bass
import concourse.mybir as mybir
from concourse.bass_utils import run_bass_kernel_spmd

F32 = mybir.dt.float32
BF16 = mybir.dt.bfloat16
I32 = mybir.dt.int32
U8 = mybir.dt.uint8
AF = mybir.ActivationFunctionType
ALU = mybir.AluOpType
AX = mybir.AxisListType

D = 1024
S = 2048
NT = 16
DEPTH = 4
NMEM = 256
EPS = 1e-6
THETA = 500000.0
INW = 1952
DFF = 2816
NCH = 22

ENGS = ("pe", "act", "dve", "pool", "sp")
SEM_LIMIT = 1000
NDS = 12


class Res:
    __slots__ = ("lw", "rd", "rd_dma")

    def __init__(self):
        self.lw = None
        self.rd = {}
        self.rd_dma = []


class Op:
    __slots__ = ("eng", "meth", "args", "kw", "deps", "sig", "semk", "is_dma", "dsem", "dval", "idx")


class Sched:
    def __init__(self):
        self.ops = {e: [] for e in ENGS}
        self.ndma = {"sp": 0, "pool": 0}
        self.dma_hist = {"sp": [], "pool": []}
        self.n = 0

    def add(self, eng, meth, r, w, *args, is_dma=False, **kw):
        op = Op()
        op.eng, op.meth, op.args, op.kw = eng, meth, args, kw
        op.is_dma = is_dma
        op.sig = False
        op.semk = None
        op.idx = self.n
        op.dsem = None
        op.dval = None
        self.n += 1
        deps = {}

        def need(d):
            if d is None or d is op:
                return
            if (not d.is_dma) and d.eng == "pe" and eng == "pe" and not is_dma:
                return
            deps[d.idx] = d

        wset = set(id(x) for x in w)
        for x in r:
            need(x.lw)
        for x in w:
            need(x.lw)
            for d in x.rd.values():
                need(d)
            for d in x.rd_dma:
                need(d)
        if is_dma:
            q = self.dma_hist[eng]
            op.dsem = len(q) % NDS
            op.dval = 16 * (len(q) // NDS + 1)
            if len(q) >= NDS:
                need(q[len(q) - NDS])
            q.append(op)
        best = {}
        out = []
        for d in deps.values():
            if d.is_dma:
                out.append(d)
            else:
                b = best.get(d.eng)
                if b is None or d.idx > b.idx:
                    best[d.eng] = d
        out.extend(best.values())
        for d in out:
            d.sig = True
        op.deps = out
        for x in w:
            x.lw = op
            x.rd = {}
            x.rd_dma = []
        for x in r:
            if id(x) in wset:
                continue
            if is_dma:
                x.rd_dma.append(op)
            else:
                x.rd[eng] = op
        self.ops[eng].append(op)
        return op

    def finalize(self):
        for e in ENGS:
            k = 0
            for op in self.ops[e]:
                if op.is_dma:
                    continue
                if op.sig:
                    op.semk = k
                    k += 1
        return {e: sum(1 for o in self.ops[e] if (not o.is_dma) and o.sig) for e in ENGS}


class Buf:
    __slots__ = ("ap", "res", "off", "size", "name", "owner")


class Arena:
    CH = 1024

    def __init__(self, t, nbytes, base=0, nextfit=False):
        self.t = t
        self.base = base
        self.nextfit = nextfit
        self.ptr = 0
        self.nbytes = nbytes
        self.nch = nbytes // self.CH
        self.res = [Res() for _ in range(self.nch)]
        self.used = [False] * self.nch
        self.peak = 0

    def alloc(self, name, shape, dt):
        esz = 4 if dt in (F32, I32) else 2
        n = 1
        for s in shape[1:]:
            n *= s
        nb = n * esz
        k = (nb + self.CH - 1) // self.CH
        start = None
        order = [0]
        if self.nextfit:
            order = [self.ptr, 0]
        for s0 in order:
            run = 0
            for i in range(s0, self.nch):
                if not self.used[i]:
                    run += 1
                    if run == k:
                        start = i - k + 1
                        break
                else:
                    run = 0
            if start is not None:
                break
        if start is not None:
            self.ptr = start + k
        if start is None:
            raise RuntimeError("arena OOM for %s (%d B); used=%d" % (name, nb, sum(self.used)))
        for i in range(start, start + k):
            self.used[i] = True
        self.peak = max(self.peak, max(i for i in range(self.nch) if self.used[i]) + 1)
        b = Buf()
        b.name = name
        b.owner = self
        b.off = start
        b.size = k
        o = self.base + start * self.CH
        ap = self.t[:, o:o + nb].bitcast(dt)
        if len(shape) == 3:
            b.ap = ap.rearrange("p (a b) -> p a b", b=shape[2])
        else:
            b.ap = ap
        b.res = self.res[start:start + k]
        return b

    def free(self, *bufs):
        for b in bufs:
            for i in range(b.off, b.off + b.size):
                b.owner.used[i] = False


TRANS = {"stg", "hn", "hT", "ssq", "rsq", "cn", "rt", "zg", "st", "vsq", "vn", "qk", "r6", "pt", "rc", "o1", "s2", "tmp",
         "cT", "aT", "mn", "y", "wu", "ptm", "ltmp"}


class Arenas:
    def __init__(self, main, trans):
        self.main = main
        self.trans = trans

    def alloc(self, name, shape, dt):
        if name in TRANS:
            return self.trans.alloc(name, shape, dt)
        return self.main.alloc(name, shape, dt)

    def free(self, *bufs):
        self.main.free(*bufs)

    @property
    def peak(self):
        return (self.main.peak, self.trans.peak)


def build(n_layers=DEPTH, stop_after=None):
    nc = bass.Bass("TRN2", target_bir_lowering=False)

    def din(name, shape, dt=F32):
        return nc.dram_tensor(name, list(shape), dt, kind="ExternalInput").ap()

    WL = n_layers
    x_d = din("x", [S, D])
    mem_d = din("mem", [NMEM, D])
    pos_d = din("pos", [NT, 128], I32)
    w_in_d = din("w_in", [WL, D, INW])
    w_uq_d = din("w_uq", [WL, 384, 768])
    w_ukv_d = din("w_ukv", [WL, 256, 1024])
    wsT_d = din("wsT", [WL, 128, 512])
    w_mo_d = din("w_mo", [WL, D, D])
    w_q_d = din("w_q", [WL, D, D])
    w_kv_d = din("w_kv", [WL, D, 2 * D])
    w_o_d = din("w_o", [WL, D, D])
    w_up_d = din("w_up", [WL, NCH, 128, 2048])
    w_dn_d = din("w_dn", [WL, DFF, D])
    NPK = 217
    pk_d = din("pk", [128, DEPTH, NPK])
    NBC = 3520
    bc_d = din("bc", [DEPTH, NBC])
    out_d = nc.dram_tensor("out", [S, D], F32, kind="ExternalOutput").ap()

    sc = Sched()
    es = ExitStack()
    xs = es.enter_context(nc.sbuf_tensor("xs", [128, NT, D], F32))
    x_res = [Res() for _ in range(NT)]
    MAIN_B = 114 * 1024
    TRANS_B = 29 * 1024
    ar_t = es.enter_context(nc.sbuf_tensor("arena", [128, MAIN_B + TRANS_B], U8))
    ar = Arenas(Arena(ar_t, MAIN_B), Arena(ar_t, TRANS_B, base=MAIN_B, nextfit=True))
    banks = [es.enter_context(nc.psum_tensor("bank%d" % i, [128, 512], F32)) for i in range(8)]
    bank_res = [Res() for _ in range(8)]
    bank_ids = set(id(b) for b in bank_res)
    out_res = [Res() for _ in range(NT)]

    def A(eng, meth, r, w, *a, **k):
        rr = []
        for x in r:
            rr.extend(x if isinstance(x, (list, tuple)) else [x])
        ww = []
        for x in w:
            ww.extend(x if isinstance(x, (list, tuple)) else [x])
        for x in rr:
            if id(x) in bank_ids:
                ww.append(x)
        return sc.add(eng, meth, rr, ww, *a, **k)

    tp_i = [0]

    def tp_bank():
        tp_i[0] ^= 1
        return tp_i[0]

    mm_i = [0]

    def mm_bank():
        mm_i[0] = (mm_i[0] + 1) % 6
        return 2 + mm_i[0]

    cp_i = [0]

    def copy_eng():
        cp_i[0] ^= 1
        return "act" if cp_i[0] else "dve"

    def copy(eng, r, w, out, in_):
        if eng == "act":
            A("act", "activation", r, w, out=out, in_=in_, func=AF.Copy)
        elif eng == "dve":
            A("dve", "tensor_copy", r, w, out=out, in_=in_)
        else:
            A("pool", "tensor_copy", r, w, out=out, in_=in_)

    ident = ar.alloc("ident", [128, 128], BF16)
    identf = ar.alloc("identf", [128, 128], F32)
    A("pool", "memset", [], [identf.res], identf.ap, 0.0)
    A("pool", "iota", [], [identf.res], identf.ap, pattern=[[1, 128]], base=0, channel_multiplier=-1,
      allow_small_or_imprecise_dtypes=True)
    A("dve", "tensor_scalar", [identf.res], [ident.res], out=ident.ap, in0=identf.ap, scalar1=0.0, scalar2=None,
      op0=ALU.is_equal)
    epsb = ar.alloc("eps", [128, 1], F32)
    A("dve", "memset", [], [epsb.res], epsb.ap, EPS)
    zerob = ar.alloc("zero", [128, 1], F32)
    A("dve", "memset", [], [zerob.res], zerob.ap, 0.0)
    junk = ar.alloc("junk", [128, 1024], BF16)

    xv = x_d.rearrange("(t p) d -> p t d", p=128)
    for t in range(NT):
        A("sp", "dma_start", [], [x_res[t]], is_dma=True, out=xs[:, t, :], in_=xv[:, t, :])

    pk = ar.alloc("pk", [128, DEPTH, NPK], F32)
    A("sp", "dma_start", [], [pk.res], is_dma=True, out=pk.ap, in_=pk_d)

    posi = ar.alloc("posi", [128, NT], I32)
    for t in range(NT):
        A("sp", "dma_start", [], [posi.res], is_dma=True, out=posi.ap[:, t:t + 1],
          in_=pos_d[t:t + 1, :].rearrange("a p -> p a"))
    posf = ar.alloc("posf", [128, NT], F32)
    A("dve", "tensor_copy", [posi.res], [posf.res], out=posf.ap, in_=posi.ap)
    ang = ar.alloc("ang", [128, NT, 16], F32)
    for f in range(16):
        A("dve", "tensor_scalar", [posf.res], [ang.res], out=ang.ap[:, :, f], in0=posf.ap,
          scalar1=float(THETA ** (-2.0 * f / 32.0)), scalar2=None, op0=ALU.mult)
    cosA = ar.alloc("cosA", [128, NT, 16], F32)
    sinA = ar.alloc("sinA", [128, NT, 16], F32)
    TWO_PI = 2.0 * math.pi
    C1 = 6.28125
    C2 = TWO_PI - C1
    tA = ar.alloc("tA", [128, NT, 16], F32)
    tK = ar.alloc("tK", [128, NT, 16], I32)
    tKf = ar.alloc("tKf", [128, NT, 16], F32)
    tM = ar.alloc("tM", [128, NT, 16], F32)
    for (dst, shift) in ((sinA, 0.0), (cosA, math.pi / 2)):
        A("dve", "tensor_scalar", [ang.res], [tA.res], out=tA.ap, in0=ang.ap, scalar1=shift, scalar2=None,
          op0=ALU.add)
        A("dve", "tensor_scalar", [tA.res], [tM.res], out=tM.ap, in0=tA.ap, scalar1=1.0 / TWO_PI, scalar2=None,
          op0=ALU.mult)
        A("dve", "tensor_copy", [tM.res], [tK.res], out=tK.ap, in_=tM.ap)
        A("dve", "tensor_copy", [tK.res], [tKf.res], out=tKf.ap, in_=tK.ap)
        A("dve", "scalar_tensor_tensor", [tKf.res, tA.res], [tM.res], out=tM.ap, in0=tKf.ap, scalar=-C1,
          in1=tA.ap, op0=ALU.mult, op1=ALU.add)
        A("dve", "scalar_tensor_tensor", [tKf.res, tM.res], [tA.res], out=tA.ap, in0=tKf.ap, scalar=-C2,
          in1=tM.ap, op0=ALU.mult, op1=ALU.add)
        A("dve", "tensor_scalar", [tA.res], [tM.res], out=tM.ap, in0=tA.ap, scalar1=math.pi, scalar2=-TWO_PI,
          op0=ALU.is_gt, op1=ALU.mult)
        A("dve", "tensor_tensor", [tA.res, tM.res], [tKf.res], out=tKf.ap, in0=tA.ap, in1=tM.ap, op=ALU.add)
        A("dve", "tensor_scalar", [tKf.res], [tM.res], out=tM.ap, in0=tKf.ap, scalar1=-math.pi, scalar2=TWO_PI,
          op0=ALU.is_lt, op1=ALU.mult)
        A("dve", "tensor_tensor", [tKf.res, tM.res], [tA.res], out=tA.ap, in0=tKf.ap, in1=tM.ap, op=ALU.add)
        A("dve", "tensor_scalar", [tA.res], [tM.res], out=tM.ap, in0=tA.ap, scalar1=math.pi, scalar2=-math.pi,
          op0=ALU.min, op1=ALU.max)
        A("act", "activation", [tM.res], [dst.res], out=dst.ap, in_=tM.ap, func=AF.Sin)
    ar.free(tA, tK, tKf, tM, ang, posi, posf, identf)

    def rstd_from(ss, ncol, dim, dst):
        A("act", "activation", [ss[1], epsb.res], [dst[1]], out=dst[0], in_=ss[0], func=AF.Ln, scale=1.0 / dim,
          bias=epsb.ap[:, 0:1])
        A("act", "activation", [dst[1]], [dst[1]], out=dst[0], in_=dst[0], func=AF.Exp, scale=-0.5)

    def load_w(name, shape, src, scale_cols=None, l=0, chunk_cols=None):
        b = ar.alloc(name, shape, BF16)
        kc, n = shape[1], shape[2]
        step = 2048
        for c in range(kc):
            for n0 in range(0, n, step):
                n1 = min(n, n0 + step)
                stg = ar.alloc("stg", [128, n1 - n0], F32)
                A("sp", "dma_start", [], [stg.res], is_dma=True, out=stg.ap, in_=src[:, c, n0:n1])
                if scale_cols is not None:
                    A("pool", "tensor_scalar", [stg.res, pk.res], [b.res], out=b.ap[:, c, n0:n1], in0=stg.ap,
                      scalar1=pk.ap[:, l, scale_cols + c:scale_cols + c + 1], scalar2=1.0, op0=ALU.mult, op1=ALU.mult)
                else:
                    A("pool", "tensor_copy", [stg.res], [b.res], out=b.ap[:, c, n0:n1], in_=stg.ap)
                ar.free(stg)
        return b

    def transposes(srcs, src_res, dst_ap, dst_res, rows=128):
        bk = tp_bank()
        pv = banks[bk][:].bitcast(BF16).rearrange("p (a b) -> p a b", b=128)
        for i, s_ap in enumerate(srcs):
            A("pe", "transpose", [src_res, ident.res], [bank_res[bk]], out=pv[0:rows, i, :], in_=s_ap,
              identity=ident.ap)
        copy(copy_eng(), [bank_res[bk]], [dst_res], dst_ap, pv[0:rows, 0:len(srcs), :])

    def attention(QT, KT, krows, roff, V_of_kc, v_res, nkc, dv, scale, evac):
        for qb in range(4):
            accb = [mm_bank() for _ in range(4)]
            pts = {}

            def issue_s(kc, qb=qb, pts=pts):
                sb_ = tp_bank()
                A("pe", "matmul", [QT[1], KT[1]], [bank_res[sb_]], banks[sb_][:, :],
                  lhsT=KT[0][roff:roff + krows, kc * 128:(kc + 1) * 128],
                  rhs=QT[0][roff:roff + krows, qb * 512:(qb + 1) * 512], start=True, stop=True,
                  **({"tile_position": (roff, 0)} if krows == 32 else {}))
                pt = ar.alloc("pt", [128, 512], BF16)
                A("act", "activation", [bank_res[sb_]], [pt.res], out=pt.ap, in_=banks[sb_][:, :], func=AF.Exp,
                  scale=scale)
                pts[kc] = pt

            def issue_pv(kc, accb=accb, pts=pts):
                pt = pts.pop(kc)
                for j in range(4):
                    A("pe", "matmul", [pt.res, v_res], [bank_res[accb[j]]], banks[accb[j]][:, 0:dv + 1],
                      lhsT=pt.ap[:, j * 128:(j + 1) * 128], rhs=V_of_kc(kc), start=(kc == 0), stop=(kc == nkc - 1))
                ar.free(pt)

            issue_s(0)
            for kc in range(nkc):
                if kc + 1 < nkc:
                    issue_s(kc + 1)
                issue_pv(kc)
            for j in range(4):
                evac(qb * 4 + j, banks[accb[j]], bank_res[accb[j]])

    for l in range(n_layers):
        if stop_after == "setup":
            break
        lam_init = 0.8 - 0.6 * math.exp(-0.3 * l)
        bcs = ar.alloc("bcs", [128, 448], F32)
        A("sp", "dma_start", [], [bcs.res], is_dma=True, out=bcs.ap, in_=bc_d[l:l + 1, 3072:3520].partition_broadcast(128))
        lamt = ar.alloc("lamt", [128, 8], F32)
        ltmp = ar.alloc("ltmp", [128, 64], F32)
        A("dve", "tensor_tensor", [bcs.res], [ltmp.res], out=ltmp.ap[:, 0:32], in0=bcs.ap[:, 320:352],
          in1=bcs.ap[:, 352:384], op=ALU.mult)
        A("dve", "tensor_tensor", [bcs.res], [ltmp.res], out=ltmp.ap[:, 32:64], in0=bcs.ap[:, 384:416],
          in1=bcs.ap[:, 416:448], op=ALU.mult)
        A("dve", "tensor_reduce", [ltmp.res], [lamt.res], out=lamt.ap[:, 0:2],
          in_=ltmp.ap.rearrange("p (a b) -> p a b", b=32), axis=AX.X, op=ALU.add)
        A("act", "activation", [lamt.res], [lamt.res], out=lamt.ap[:, 2:4], in_=lamt.ap[:, 0:2], func=AF.Exp)
        A("dve", "tensor_tensor", [lamt.res], [lamt.res], out=lamt.ap[:, 4:5], in0=lamt.ap[:, 3:4], in1=lamt.ap[:, 2:3],
          op=ALU.subtract)
        A("dve", "tensor_scalar", [lamt.res], [lamt.res], out=lamt.ap[:, 5:6], in0=lamt.ap[:, 4:5], scalar1=-lam_init,
          scalar2=None, op0=ALU.add)
        neglam = lamt.ap[:, 5:6]
        A("dve", "tensor_scalar", [bcs.res], [bcs.res], out=bcs.ap[:, 256:320], in0=bcs.ap[:, 256:320],
          scalar1=1.0 - lam_init, scalar2=None, op0=ALU.mult)
        ar.free(ltmp)

        w_in_t = load_w("w_in", [128, 8, INW], w_in_d[l].rearrange("(c p) n -> p c n", p=128), scale_cols=0, l=l)
        wsT = load_w("wsT", [128, 1, 512], wsT_d[l].rearrange("p (a n) -> p a n", a=1))

        ss1 = ar.alloc("ss1", [128, NT], F32)
        rs1 = ar.alloc("rs1", [128, NT], F32)
        for t in range(NT):
            A("act", "activation", [x_res[t]], [junk.res, ss1.res], out=junk.ap, in_=xs[:, t, :], func=AF.Square,
              accum_out=ss1.ap[:, t:t + 1])
        rstd_from((ss1.ap, ss1.res), NT, D, (rs1.ap, rs1.res))
        cqnT = ar.alloc("cqnT", [128, 3, S], BF16)
        ckvnT = ar.alloc("ckvnT", [128, 2, S], BF16)
        krtm = ar.alloc("krtm", [128, NT, 32], BF16)
        dqT = ar.alloc("dqT", [128, 2, S], BF16)
        dkT = ar.alloc("dkT", [128, 2, S], BF16)
        dV = ar.alloc("dV", [128, NT * 4, 65], BF16)
        A("pool", "memset", [], [dV.res], dV.ap[:, :, 64:65], 1.0)
        cb = ar.alloc("cb", [128, NT, 256], BF16)
        COLS = ((0, 384), (384, 672), (672, 1184), (1184, 1696), (1696, 1952))
        for t in range(NT):
            tc_ = slice(t * 128, (t + 1) * 128)
            hn = ar.alloc("hn", [128, D], BF16)
            A("dve", "tensor_scalar", [x_res[t], rs1.res], [hn.res], out=hn.ap, in0=xs[:, t, :],
              scalar1=rs1.ap[:, t:t + 1], scalar2=None, op0=ALU.mult)
            hT = ar.alloc("hT", [128, 8, 128], BF16)
            transposes([hn.ap[:, c * 128:(c + 1) * 128] for c in range(8)], hn.res, hT.ap, hT.res)
            ar.free(hn)
            pj = [mm_bank() for _ in range(5)]
            for gi, (c0, c1) in enumerate(COLS):
                for c in range(8):
                    A("pe", "matmul", [hT.res, w_in_t.res], [bank_res[pj[gi]]], banks[pj[gi]][:, 0:c1 - c0],
                      lhsT=hT.ap[:, c, :], rhs=w_in_t.ap[:, c, c0:c1], start=(c == 0), stop=(c == 7))
            ar.free(hT)
            ssq = ar.alloc("ssq", [128, 2], F32)
            rsq = ar.alloc("rsq", [128, 2], F32)
            A("act", "activation", [bank_res[pj[0]]], [junk.res, ssq.res], out=junk.ap[:, 0:384],
              in_=banks[pj[0]][:, 0:384], func=AF.Square, accum_out=ssq.ap[:, 0:1])
            A("act", "activation", [bank_res[pj[1]]], [junk.res, ssq.res], out=junk.ap[:, 0:256],
              in_=banks[pj[1]][:, 0:256], func=AF.Square, accum_out=ssq.ap[:, 1:2])
            A("act", "activation", [ssq.res, epsb.res], [rsq.res], out=rsq.ap[:, 0:1], in_=ssq.ap[:, 0:1], func=AF.Ln,
              scale=1.0 / 384, bias=epsb.ap[:, 0:1])
            A("act", "activation", [ssq.res, epsb.res], [rsq.res], out=rsq.ap[:, 1:2], in_=ssq.ap[:, 1:2], func=AF.Ln,
              scale=1.0 / 256, bias=epsb.ap[:, 0:1])
            A("act", "activation", [rsq.res], [rsq.res], out=rsq.ap, in_=rsq.ap, func=AF.Exp, scale=-0.5)
            cn = ar.alloc("cn", [128, 640], BF16)
            A("dve", "tensor_scalar", [bank_res[pj[0]], rsq.res], [cn.res], out=cn.ap[:, 0:384],
              in0=banks[pj[0]][:, 0:384], scalar1=rsq.ap[:, 0:1], scalar2=None, op0=ALU.mult)
            A("dve", "tensor_scalar", [bank_res[pj[1]], rsq.res], [cn.res], out=cn.ap[:, 384:640],
              in0=banks[pj[1]][:, 0:256], scalar1=rsq.ap[:, 1:2], scalar2=None, op0=ALU.mult)
            transposes([cn.ap[:, c * 128:(c + 1) * 128] for c in range(3)], cn.res, cqnT.ap[:, :, tc_], cqnT.res)
            transposes([cn.ap[:, 384 + c * 128:384 + (c + 1) * 128] for c in range(2)], cn.res, ckvnT.ap[:, :, tc_],
                       ckvnT.res)
            ar.free(ssq, rsq, cn)
            rt = ar.alloc("rt", [128, 4, 16], F32)
            kb = banks[pj[1]]
            kr_res = bank_res[pj[1]]
            cA = cosA.ap[:, t, :]
            sA = sinA.ap[:, t, :]
            A("dve", "tensor_tensor", [kr_res, cosA.res], [rt.res], out=rt.ap[:, 0, :], in0=kb[:, 256:272], in1=cA,
              op=ALU.mult)
            A("dve", "tensor_tensor", [kr_res, sinA.res], [rt.res], out=rt.ap[:, 1, :], in0=kb[:, 272:288], in1=sA,
              op=ALU.mult)
            A("dve", "tensor_tensor", [kr_res, cosA.res], [rt.res], out=rt.ap[:, 2, :], in0=kb[:, 272:288], in1=cA,
              op=ALU.mult)
            A("dve", "tensor_tensor", [kr_res, sinA.res], [rt.res], out=rt.ap[:, 3, :], in0=kb[:, 256:272], in1=sA,
              op=ALU.mult)
            A("dve", "tensor_tensor", [rt.res], [krtm.res], out=krtm.ap[:, t, 0:16], in0=rt.ap[:, 0, :],
              in1=rt.ap[:, 1, :], op=ALU.subtract)
            A("dve", "tensor_tensor", [rt.res], [krtm.res], out=krtm.ap[:, t, 16:32], in0=rt.ap[:, 2, :],
              in1=rt.ap[:, 3, :], op=ALU.add)
            ar.free(rt)
            zg = ar.alloc("zg", [128, 512], F32)
            A("act", "activation", [bank_res[pj[2]]], [zg.res], out=zg.ap, in_=banks[pj[2]][:, :],
              func=AF.Gelu_apprx_tanh)
            st = ar.alloc("st", [128, 16], F32)
            vsq = ar.alloc("vsq", [128, 256], F32)
            v3 = zg.ap[:, 256:512].rearrange("p (g c) -> p g c", c=64)
            A("dve", "tensor_reduce", [zg.res], [st.res], out=st.ap[:, 0:4], in_=v3, axis=AX.X, op=ALU.add)
            A("act", "activation", [zg.res], [vsq.res], out=vsq.ap, in_=zg.ap[:, 256:512], func=AF.Square)
            A("dve", "tensor_reduce", [vsq.res], [st.res], out=st.ap[:, 4:8],
              in_=vsq.ap.rearrange("p (g c) -> p g c", c=64), axis=AX.X, op=ALU.add)
            A("dve", "tensor_scalar", [st.res], [st.res], out=st.ap[:, 8:12], in0=st.ap[:, 0:4], scalar1=1.0 / 64,
              scalar2=None, op0=ALU.mult)
            A("dve", "tensor_tensor", [st.res], [st.res], out=st.ap[:, 12:16], in0=st.ap[:, 8:12], in1=st.ap[:, 8:12],
              op=ALU.mult)
            A("dve", "scalar_tensor_tensor", [st.res], [st.res], out=st.ap[:, 4:8], in0=st.ap[:, 4:8], scalar=1.0 / 64,
              in1=st.ap[:, 12:16], op0=ALU.mult, op1=ALU.subtract)
            A("act", "activation", [st.res, epsb.res], [st.res], out=st.ap[:, 0:4], in_=st.ap[:, 4:8], func=AF.Ln,
              bias=epsb.ap[:, 0:1])
            A("act", "activation", [st.res], [st.res], out=st.ap[:, 0:4], in_=st.ap[:, 0:4], func=AF.Exp, scale=-0.5)
            vq3 = vsq.ap.rearrange("p (g c) -> p g c", c=64)
            A("dve", "tensor_tensor", [zg.res, st.res], [vsq.res], out=vq3, in0=v3,
              in1=st.ap[:, 8:12].unsqueeze(2).broadcast_to([128, 4, 64]), op=ALU.subtract)
            A("dve", "tensor_tensor", [vsq.res, st.res], [vsq.res], out=vq3, in0=vq3,
              in1=st.ap[:, 0:4].unsqueeze(2).broadcast_to([128, 4, 64]), op=ALU.mult)
            vn = ar.alloc("vn", [128, 256], BF16)
            A("dve", "tensor_tensor", [vsq.res, bcs.res], [vn.res], out=vn.ap, in0=vsq.ap, in1=bcs.ap[:, 0:256],
              op=ALU.mult)
            mb = mm_bank()
            for g in range(4):
                A("pe", "matmul", [vn.res, wsT.res], [bank_res[mb]], banks[mb][:, g * 64:(g + 1) * 64],
                  lhsT=wsT.ap[:, 0, g * 128:(g + 1) * 128], rhs=vn.ap[:, g * 64:(g + 1) * 64], start=True, stop=True)
            for g in range(4):
                A("dve", "scalar_tensor_tensor", [bank_res[mb], pk.res, zg.res], [cb.res],
                  out=cb.ap[:, t, g * 64:(g + 1) * 64], in0=banks[mb][:, g * 64:(g + 1) * 64],
                  scalar=pk.ap[:, l, 37 + g:38 + g], in1=zg.ap[:, g * 64:(g + 1) * 64], op0=ALU.add, op1=ALU.mult)
            ar.free(zg, st, vsq, vn)
            qk = ar.alloc("qk", [128, 16, 32], BF16)
            r6 = ar.alloc("r6", [128, 4 * 16, 8], F32)
            pq = banks[pj[3]][:, :].rearrange("p (m d) -> p m d", d=32)
            qres = bank_res[pj[3]]
            cD = cosA.ap[:, t, 0:16:2].unsqueeze(1).broadcast_to([128, 16, 8])
            sD = sinA.ap[:, t, 0:16:2].unsqueeze(1).broadcast_to([128, 16, 8])
            A("dve", "tensor_tensor", [qres, cosA.res], [r6.res], out=r6.ap[:, 0:16, :], in0=pq[:, :, 0:8], in1=cD,
              op=ALU.mult)
            A("dve", "tensor_tensor", [qres, sinA.res], [r6.res], out=r6.ap[:, 16:32, :], in0=pq[:, :, 8:16], in1=sD,
              op=ALU.mult)
            A("dve", "tensor_tensor", [qres, cosA.res], [r6.res], out=r6.ap[:, 32:48, :], in0=pq[:, :, 8:16], in1=cD,
              op=ALU.mult)
            A("dve", "tensor_tensor", [qres, sinA.res], [r6.res], out=r6.ap[:, 48:64, :], in0=pq[:, :, 0:8], in1=sD,
              op=ALU.mult)
            A("dve", "tensor_tensor", [r6.res], [qk.res], out=qk.ap[:, :, 0:8], in0=r6.ap[:, 0:16, :],
              in1=r6.ap[:, 16:32, :], op=ALU.subtract)
            A("dve", "tensor_tensor", [r6.res], [qk.res], out=qk.ap[:, :, 8:16], in0=r6.ap[:, 32:48, :],
              in1=r6.ap[:, 48:64, :], op=ALU.add)
            A("act", "activation", [qres], [qk.res], out=qk.ap[:, :, 16:32], in_=pq[:, :, 16:32], func=AF.Copy)
            qk2 = qk.ap.rearrange("p m d -> p (m d)")
            transposes([qk2[:, c * 128:(c + 1) * 128] for c in range(2)], qk.res, dqT.ap[:, :, tc_], dqT.res)
            transposes([qk2[:, 256 + c * 128:256 + (c + 1) * 128] for c in range(2)], qk.res, dkT.ap[:, :, tc_],
                       dkT.res)
            ar.free(qk, r6)
            A("act", "activation", [bank_res[pj[4]]], [dV.res], out=dV.ap[:, t * 4:(t + 1) * 4, 0:64],
              in_=banks[pj[4]][:, 0:256].rearrange("p (h e) -> p h e", e=64), func=AF.Copy)
        ar.free(ss1, rs1, w_in_t, wsT)
        if stop_after == "p1":
            break

        cc = ar.alloc("cc", [128, NT, 256], BF16)
        w_mo_t = load_w("w_mo", [128, 8, D], w_mo_d[l].rearrange("(c p) n -> p c n", p=128))
        w_uq_t = load_w("w_uq", [128, 3, 768], w_uq_d[l].rearrange("(c p) n -> p c n", p=128), scale_cols=32, l=l)
        w_ukv_t = load_w("w_ukv", [128, 2, 1024], w_ukv_d[l].rearrange("(c p) n -> p c n", p=128), scale_cols=35, l=l)
        for h in range(4):
            o0 = ar.alloc("o0", [128, NT, 64], F32)
            for c in range(2):
                m = 2 * h + c
                ci, ro = m // 4, 32 * (m % 4)

                def evac(qt, acc, acc_res, c=c, h=h, o0=o0):
                    rc = ar.alloc("rc", [128, 4], F32)
                    A("dve", "reciprocal", [acc_res], [rc.res], out=rc.ap[:, 0:1], in_=acc[:, 64:65])
                    if c == 0:
                        A("dve", "tensor_scalar", [acc_res, rc.res], [o0.res], out=o0.ap[:, qt, :], in0=acc[:, 0:64],
                          scalar1=rc.ap[:, 0:1], scalar2=None, op0=ALU.mult)
                    else:
                        o1 = ar.alloc("o1", [128, 128], F32)
                        A("dve", "tensor_scalar", [acc_res, rc.res, lamt.res], [o1.res], out=o1.ap[:, 0:64],
                          in0=acc[:, 0:64], scalar1=rc.ap[:, 0:1], scalar2=neglam, op0=ALU.mult, op1=ALU.mult)
                        A("dve", "tensor_tensor", [o1.res, o0.res], [o1.res], out=o1.ap[:, 0:64], in0=o1.ap[:, 0:64],
                          in1=o0.ap[:, qt, :], op=ALU.add)
                        A("act", "activation", [o1.res], [o1.res, rc.res], out=o1.ap[:, 64:128], in_=o1.ap[:, 0:64],
                          func=AF.Square, accum_out=rc.ap[:, 1:2])
                        A("act", "activation", [rc.res, epsb.res], [rc.res], out=rc.ap[:, 2:3], in_=rc.ap[:, 1:2],
                          func=AF.Ln, scale=1.0 / 64, bias=epsb.ap[:, 0:1])
                        A("act", "activation", [rc.res], [rc.res], out=rc.ap[:, 3:4], in_=rc.ap[:, 2:3], func=AF.Exp,
                          scale=-0.5)
                        A("dve", "scalar_tensor_tensor", [o1.res, rc.res, bcs.res], [cc.res],
                          out=cc.ap[:, qt, h * 64:(h + 1) * 64], in0=o1.ap[:, 0:64], scalar=rc.ap[:, 3:4],
                          in1=bcs.ap[:, 256:320], op0=ALU.mult, op1=ALU.mult)
                        ar.free(o1)
                    ar.free(rc)

                attention((dqT.ap[:, ci, :], dqT.res), (dkT.ap[:, ci, :], dkT.res), 32, ro,
                          lambda kc, h=h: dV.ap[:, kc * 4 + h, :], dV.res, NT, 64, 32 ** -0.5, evac)
            ar.free(o0)
        ar.free(dqT, dkT, dV)
        if stop_after == "p2":
            break

        ca = ar.alloc("ca", [128, NT, 512], BF16)
        for h in range(8):
            QKT = ar.alloc("QKT", [128, 2, S], BF16)
            Vh = ar.alloc("Vh", [128, NT, 65], BF16)
            A("pool", "memset", [], [Vh.res], Vh.ap[:, :, 64:65], 1.0)
            for t in range(NT):
                tc_ = slice(t * 128, (t + 1) * 128)
                mb = mm_bank()
                for c in range(3):
                    A("pe", "matmul", [cqnT.res, w_uq_t.res], [bank_res[mb]], banks[mb][:, 0:96],
                      lhsT=cqnT.ap[:, c, tc_], rhs=w_uq_t.ap[:, c, h * 96:(h + 1) * 96], start=(c == 0), stop=(c == 2))
                for c in range(2):
                    A("pe", "matmul", [ckvnT.res, w_ukv_t.res], [bank_res[mb]], banks[mb][:, 128:256],
                      lhsT=ckvnT.ap[:, c, tc_], rhs=w_ukv_t.ap[:, c, h * 128:(h + 1) * 128], start=(c == 0),
                      stop=(c == 1))
                qk = ar.alloc("qk", [128, 2, 128], BF16)
                A("pool", "memset", [], [qk.res], qk.ap[:, :, 96:128], 0.0)
                bk = banks[mb]
                br = bank_res[mb]
                A("act", "activation", [br], [qk.res], out=qk.ap[:, 0, 0:64], in_=bk[:, 0:64], func=AF.Copy)
                A("act", "activation", [br], [qk.res], out=qk.ap[:, 1, 0:64], in_=bk[:, 128:192], func=AF.Copy)
                A("act", "activation", [br], [Vh.res], out=Vh.ap[:, t, 0:64], in_=bk[:, 192:256], func=AF.Copy)
                rt = ar.alloc("rt", [128, 4, 16], F32)
                cA = cosA.ap[:, t, :]
                sA = sinA.ap[:, t, :]
                A("dve", "tensor_tensor", [br, cosA.res], [rt.res], out=rt.ap[:, 0, :], in0=bk[:, 64:80], in1=cA,
                  op=ALU.mult)
                A("dve", "tensor_tensor", [br, sinA.res], [rt.res], out=rt.ap[:, 1, :], in0=bk[:, 80:96], in1=sA,
                  op=ALU.mult)
                A("dve", "tensor_tensor", [br, cosA.res], [rt.res], out=rt.ap[:, 2, :], in0=bk[:, 80:96], in1=cA,
                  op=ALU.mult)
                A("dve", "tensor_tensor", [br, sinA.res], [rt.res], out=rt.ap[:, 3, :], in0=bk[:, 64:80], in1=sA,
                  op=ALU.mult)
                A("dve", "tensor_tensor", [rt.res], [qk.res], out=qk.ap[:, 0, 64:80], in0=rt.ap[:, 0, :],
                  in1=rt.ap[:, 1, :], op=ALU.subtract)
                A("dve", "tensor_tensor", [rt.res], [qk.res], out=qk.ap[:, 0, 80:96], in0=rt.ap[:, 2, :],
                  in1=rt.ap[:, 3, :], op=ALU.add)
                A("dve", "tensor_copy", [krtm.res], [qk.res], out=qk.ap[:, 1, 64:96], in_=krtm.ap[:, t, :])
                transposes([qk.ap[:, 0, :], qk.ap[:, 1, :]], qk.res, QKT.ap[:, :, tc_], QKT.res)
                ar.free(qk, rt)

            if stop_after == "p3a":
                break

            def evac_a(qt, acc, acc_res, h=h):
                rc = ar.alloc("rc", [128, 4], F32)
                A("dve", "reciprocal", [acc_res], [rc.res], out=rc.ap[:, 0:1], in_=acc[:, 64:65])
                A("dve", "tensor_scalar", [acc_res, rc.res], [ca.res], out=ca.ap[:, qt, h * 64:(h + 1) * 64],
                  in0=acc[:, 0:64], scalar1=rc.ap[:, 0:1], scalar2=None, op0=ALU.mult)
                ar.free(rc)

            attention((QKT.ap[:, 0, :], QKT.res), (QKT.ap[:, 1, :], QKT.res), 128, 0,
                      lambda kc, Vh=Vh: Vh.ap[:, kc, :], Vh.res, NT, 64, 96 ** -0.5, evac_a)
            ar.free(QKT, Vh)
            if stop_after == "p3b":
                break
        if stop_after in ("p3a", "p3b"):
            break
        ar.free(cqnT, ckvnT, krtm, w_uq_t, w_ukv_t)
        if stop_after == "p3":
            break

        def post_norm_add(t, bk2, gpost):
            s2 = ar.alloc("s2", [128, 4], F32)
            for i in range(2):
                A("act", "activation", [bank_res[bk2[i]]], [junk.res, s2.res], out=junk.ap[:, 0:512],
                  in_=banks[bk2[i]][:, :], func=AF.Square, accum_out=s2.ap[:, i:i + 1])
            A("dve", "tensor_tensor", [s2.res], [s2.res], out=s2.ap[:, 2:3], in0=s2.ap[:, 0:1], in1=s2.ap[:, 1:2],
              op=ALU.add)
            A("act", "activation", [s2.res, epsb.res], [s2.res], out=s2.ap[:, 3:4], in_=s2.ap[:, 2:3], func=AF.Ln,
              scale=1.0 / D, bias=epsb.ap[:, 0:1])
            A("act", "activation", [s2.res], [s2.res], out=s2.ap[:, 3:4], in_=s2.ap[:, 3:4], func=AF.Exp, scale=-0.5)
            tmp = ar.alloc("tmp", [128, D], F32)
            for i in range(2):
                A("dve", "scalar_tensor_tensor", [bank_res[bk2[i]], s2.res, gpost.res], [tmp.res],
                  out=tmp.ap[:, i * 512:(i + 1) * 512], in0=banks[bk2[i]][:, :], scalar=s2.ap[:, 3:4],
                  in1=gpost.ap[:, i * 512:(i + 1) * 512], op0=ALU.mult, op1=ALU.mult)
            A("pool", "tensor_tensor", [tmp.res, x_res[t]], [x_res[t]], out=xs[:, t, :], in0=xs[:, t, :], in1=tmp.ap,
              op=ALU.add)
            ar.free(s2, tmp)

        gpost = ar.alloc("gpost", [128, D], F32)
        A("sp", "dma_start", [], [gpost.res], is_dma=True, out=gpost.ap, in_=bc_d[l:l + 1, 0:1024].partition_broadcast(128))
        for t in range(NT):
            cT = ar.alloc("cT", [128, 8, 128], BF16)
            srcs = [ca.ap[:, t, c * 128:(c + 1) * 128] for c in range(4)] + \
                   [cb.ap[:, t, c * 128:(c + 1) * 128] for c in range(2)] + \
                   [cc.ap[:, t, c * 128:(c + 1) * 128] for c in range(2)]
            bk = tp_bank()
            pv = banks[bk][:].bitcast(BF16).rearrange("p (a b) -> p a b", b=128)
            for i, s_ap in enumerate(srcs):
                rr = ca.res if i < 4 else (cb.res if i < 6 else cc.res)
                A("pe", "transpose", [rr, ident.res], [bank_res[bk]], out=pv[:, i, :], in_=s_ap, identity=ident.ap)
            copy(copy_eng(), [bank_res[bk]], [cT.res], cT.ap, pv[:, 0:8, :])
            bk2 = [mm_bank(), mm_bank()]
            for i in range(2):
                for c in range(8):
                    A("pe", "matmul", [cT.res, w_mo_t.res], [bank_res[bk2[i]]], banks[bk2[i]][:, :],
                      lhsT=cT.ap[:, c, :], rhs=w_mo_t.ap[:, c, i * 512:(i + 1) * 512], start=(c == 0), stop=(c == 7))
            ar.free(cT)
            post_norm_add(t, bk2, gpost)
        ar.free(ca, cb, cc, w_mo_t, gpost, bcs, lamt)
        if stop_after == "mix":
            break

        w_kv_t = load_w("w_kv", [128, 8, 2 * D], w_kv_d[l].rearrange("(c p) n -> p c n", p=128), scale_cols=24, l=l)
        memf = ar.alloc("memf", [128, 2, D], F32)
        A("sp", "dma_start", [], [memf.res], is_dma=True, out=memf.ap, in_=mem_d.rearrange("(t p) d -> p t d", p=128))
        ssm = ar.alloc("ssm", [128, 2], F32)
        rsm = ar.alloc("rsm", [128, 2], F32)
        for t in range(2):
            A("act", "activation", [memf.res], [junk.res, ssm.res], out=junk.ap, in_=memf.ap[:, t, :], func=AF.Square,
              accum_out=ssm.ap[:, t:t + 1])
        rstd_from((ssm.ap, ssm.res), 2, D, (rsm.ap, rsm.res))
        memT = ar.alloc("memT", [128, 8, NMEM], BF16)
        for t in range(2):
            mn = ar.alloc("mn", [128, D], BF16)
            A("dve", "tensor_scalar", [memf.res, rsm.res], [mn.res], out=mn.ap, in0=memf.ap[:, t, :],
              scalar1=rsm.ap[:, t:t + 1], scalar2=None, op0=ALU.mult)
            transposes([mn.ap[:, c * 128:(c + 1) * 128] for c in range(8)], mn.res, memT.ap[:, :, t * 128:(t + 1) * 128],
                       memT.res)
            ar.free(mn)
        ar.free(memf, ssm, rsm)
        KmT = ar.alloc("KmT", [128, 8, NMEM], BF16)
        for hc in range(8):
            mb = mm_bank()
            for c in range(8):
                A("pe", "matmul", [memT.res, w_kv_t.res], [bank_res[mb]], banks[mb][:, 0:NMEM],
                  lhsT=w_kv_t.ap[:, c, hc * 128:(hc + 1) * 128], rhs=memT.ap[:, c, :], start=(c == 0), stop=(c == 7))
            copy(copy_eng(), [bank_res[mb]], [KmT.res], KmT.ap[:, hc, :], banks[mb][:, 0:NMEM])
        Vm = ar.alloc("Vm", [128, 8, 257], BF16)
        A("pool", "memset", [], [Vm.res], Vm.ap[:, :, 256:257], 1.0)
        for kt in range(2):
            for i in range(2):
                mb = mm_bank()
                for c in range(8):
                    A("pe", "matmul", [memT.res, w_kv_t.res], [bank_res[mb]], banks[mb][:, :],
                      lhsT=memT.ap[:, c, kt * 128:(kt + 1) * 128], rhs=w_kv_t.ap[:, c, D + i * 512:D + (i + 1) * 512],
                      start=(c == 0), stop=(c == 7))
                copy(copy_eng(), [bank_res[mb]], [Vm.res], Vm.ap[:, kt * 4 + 2 * i:kt * 4 + 2 * i + 2, 0:256],
                     banks[mb][:, :].rearrange("p (h e) -> p h e", e=256))
        ar.free(memT, w_kv_t)
        w_q_t = load_w("w_q", [128, 8, D], w_q_d[l].rearrange("(c p) n -> p c n", p=128), scale_cols=8, l=l)
        w_o_t = load_w("w_o", [128, 8, D], w_o_d[l].rearrange("(c p) n -> p c n", p=128))
        gpost = ar.alloc("gpost", [128, D], F32)
        A("sp", "dma_start", [], [gpost.res], is_dma=True, out=gpost.ap, in_=bc_d[l:l + 1, 1024:2048].partition_broadcast(128))
        ss5 = ar.alloc("ss5", [128, NT], F32)
        rs5 = ar.alloc("rs5", [128, NT], F32)
        for t in range(NT):
            A("act", "activation", [x_res[t]], [junk.res, ss5.res], out=junk.ap, in_=xs[:, t, :], func=AF.Square,
              accum_out=ss5.ap[:, t:t + 1])
        rstd_from((ss5.ap, ss5.res), NT, D, (rs5.ap, rs5.res))
        for qb in range(4):
            hTb = ar.alloc("hTb", [128, 8, 512], BF16)
            for j in range(4):
                t = qb * 4 + j
                hn = ar.alloc("hn", [128, D], BF16)
                A("dve", "tensor_scalar", [x_res[t], rs5.res], [hn.res], out=hn.ap, in0=xs[:, t, :],
                  scalar1=rs5.ap[:, t:t + 1], scalar2=None, op0=ALU.mult)
                transposes([hn.ap[:, c * 128:(c + 1) * 128] for c in range(8)], hn.res,
                           hTb.ap[:, :, j * 128:(j + 1) * 128], hTb.res)
                ar.free(hn)
            qTb = ar.alloc("qTb", [128, 8, 512], BF16)
            for hc in range(8):
                mb = mm_bank()
                for c in range(8):
                    A("pe", "matmul", [hTb.res, w_q_t.res], [bank_res[mb]], banks[mb][:, :],
                      lhsT=w_q_t.ap[:, c, hc * 128:(hc + 1) * 128], rhs=hTb.ap[:, c, :], start=(c == 0), stop=(c == 7))
                copy(copy_eng(), [bank_res[mb]], [qTb.res], qTb.ap[:, hc, :], banks[mb][:, :])
            ar.free(hTb)
            ao = ar.alloc("ao", [128, 4, D], BF16)
            for h in range(4):
                pts = []
                for kt in range(2):
                    sb_ = tp_bank()
                    for dc in range(2):
                        A("pe", "matmul", [qTb.res, KmT.res], [bank_res[sb_]], banks[sb_][:, :],
                          lhsT=KmT.ap[:, h * 2 + dc, kt * 128:(kt + 1) * 128], rhs=qTb.ap[:, h * 2 + dc, :],
                          start=(dc == 0), stop=(dc == 1))
                    pt = ar.alloc("ptm", [128, 512], BF16)
                    A("act", "activation", [bank_res[sb_]], [pt.res], out=pt.ap, in_=banks[sb_][:, :], func=AF.Exp,
                      scale=1.0 / 16.0)
                    pts.append(pt)
                for j in range(4):
                    mb = mm_bank()
                    for kt in range(2):
                        A("pe", "matmul", [pts[kt].res, Vm.res], [bank_res[mb]], banks[mb][:, 0:257],
                          lhsT=pts[kt].ap[:, j * 128:(j + 1) * 128], rhs=Vm.ap[:, kt * 4 + h, :], start=(kt == 0),
                          stop=(kt == 1))
                    rc = ar.alloc("rc", [128, 1], F32)
                    A("dve", "reciprocal", [bank_res[mb]], [rc.res], out=rc.ap[:, 0:1], in_=banks[mb][:, 256:257])
                    A("dve", "tensor_scalar", [bank_res[mb], rc.res], [ao.res], out=ao.ap[:, j, h * 256:(h + 1) * 256],
                      in0=banks[mb][:, 0:256], scalar1=rc.ap[:, 0:1], scalar2=None, op0=ALU.mult)
                    ar.free(rc)
                ar.free(*pts)
            ar.free(qTb)
            for j in range(4):
                t = qb * 4 + j
                aT = ar.alloc("aT", [128, 8, 128], BF16)
                transposes([ao.ap[:, j, c * 128:(c + 1) * 128] for c in range(8)], ao.res, aT.ap, aT.res)
                bk2 = [mm_bank(), mm_bank()]
                for i in range(2):
                    for c in range(8):
                        A("pe", "matmul", [aT.res, w_o_t.res], [bank_res[bk2[i]]], banks[bk2[i]][:, :],
                          lhsT=aT.ap[:, c, :], rhs=w_o_t.ap[:, c, i * 512:(i + 1) * 512], start=(c == 0), stop=(c == 7))
                ar.free(aT)
                post_norm_add(t, bk2, gpost)
            ar.free(ao)
        ar.free(KmT, Vm, w_q_t, w_o_t, gpost, ss5, rs5)
        if stop_after == "mem":
            break

        w_dn_t = load_w("w_dn", [128, NCH, D], w_dn_d[l].rearrange("(c p) n -> p c n", p=128))
        gpost = ar.alloc("gpost", [128, D], F32)
        A("sp", "dma_start", [], [gpost.res], is_dma=True, out=gpost.ap, in_=bc_d[l:l + 1, 2048:3072].partition_broadcast(128))
        ss6 = ar.alloc("ss6", [128, NT], F32)
        rs6 = ar.alloc("rs6", [128, NT], F32)
        for t in range(NT):
            A("act", "activation", [x_res[t]], [junk.res, ss6.res], out=junk.ap, in_=xs[:, t, :], func=AF.Square,
              accum_out=ss6.ap[:, t:t + 1])
        rstd_from((ss6.ap, ss6.res), NT, D, (rs6.ap, rs6.res))
        hl = ar.alloc("hl", [128, 8, 2], BF16)
        for qb in range(4):
            hTe = ar.alloc("hTe", [128, 8, 514], BF16)
            if qb == 0:
                A("dve", "memset", [], [hTe.res], hTe.ap[:, :, 0:1], 0.0)
            if qb == 3:
                A("dve", "memset", [], [hTe.res], hTe.ap[:, :, 513:514], 0.0)
            tl = list(range(qb * 4, qb * 4 + 4))
            if qb > 0:
                A("dve", "tensor_copy", [hl.res], [hTe.res], out=hTe.ap[:, :, 0:1], in_=hl.ap[:, :, (qb - 1) % 2:(qb - 1) % 2 + 1])
            if qb < 3:
                tl = tl + [qb * 4 + 4]
            for t in tl:
                hn = ar.alloc("hn", [128, D], BF16)
                A("dve", "tensor_scalar", [x_res[t], rs6.res], [hn.res], out=hn.ap, in0=xs[:, t, :],
                  scalar1=rs6.ap[:, t:t + 1], scalar2=None, op0=ALU.mult)
                bk = tp_bank()
                pv = banks[bk][:].bitcast(BF16).rearrange("p (a b) -> p a b", b=128)
                for c in range(8):
                    A("pe", "transpose", [hn.res, ident.res], [bank_res[bk]], out=pv[:, c, :],
                      in_=hn.ap[:, c * 128:(c + 1) * 128], identity=ident.ap)
                j = t - qb * 4
                if j < 0:
                    copy(copy_eng(), [bank_res[bk]], [hTe.res], hTe.ap[:, :, 0:1], pv[:, 0:8, 127:128])
                elif j > 3:
                    copy(copy_eng(), [bank_res[bk]], [hTe.res], hTe.ap[:, :, 513:514], pv[:, 0:8, 0:1])
                else:
                    copy(copy_eng(), [bank_res[bk]], [hTe.res], hTe.ap[:, :, 1 + j * 128:1 + (j + 1) * 128], pv[:, 0:8, :])
                ar.free(hn)
            A("dve", "tensor_copy", [hTe.res], [hl.res], out=hl.ap[:, :, qb % 2:qb % 2 + 1], in_=hTe.ap[:, :, 512:513])
            mT = ar.alloc("mT", [128, NCH, 512], BF16)
            for jc in range(NCH):
                wu = ar.alloc("wu", [128, 8, 256], BF16)
                stg = ar.alloc("stg", [128, 8, 256], F32)
                A("sp", "dma_start", [], [stg.res], is_dma=True, out=stg.ap,
                  in_=w_up_d[l, jc].rearrange("p (c n) -> p c n", n=256))
                A("pool", "tensor_tensor", [stg.res, pk.res], [wu.res], out=wu.ap, in0=stg.ap,
                  in1=pk.ap[:, l, 16:24].unsqueeze(2).broadcast_to([128, 8, 256]), op=ALU.mult)
                ar.free(stg)
                ys = []
                for gu in range(2):
                    mb = mm_bank()
                    hb = mm_bank()
                    for c in range(8):
                        A("pe", "matmul", [hTe.res, wu.res], [bank_res[mb]], banks[mb][:, :],
                          lhsT=wu.ap[:, c, gu * 128:(gu + 1) * 128], rhs=hTe.ap[:, c, 1:513], start=(c == 0), stop=(c == 7))
                    for c in range(8):
                        A("pe", "matmul", [hTe.res, wu.res], [bank_res[hb]], banks[hb][:, 0:2],
                          lhsT=wu.ap[:, c, gu * 128:(gu + 1) * 128], rhs=hTe.ap[:, c, 0:514:513], start=(c == 0),
                          stop=(c == 7))
                    ch = gu * NCH + jc
                    cw = pk.ap[:, l, 41 + ch * 4:41 + ch * 4 + 4]
                    y = ar.alloc("y", [128, 512], F32)
                    A("act", "activation", [bank_res[mb], pk.res], [y.res], out=y.ap, in_=banks[mb][:, :],
                      func=AF.Identity, scale=cw[:, 1:2], bias=cw[:, 3:4])
                    A("dve", "scalar_tensor_tensor", [bank_res[mb], pk.res, y.res], [y.res], out=y.ap[:, 1:512],
                      in0=banks[mb][:, 0:511], scalar=cw[:, 0:1], in1=y.ap[:, 1:512], op0=ALU.mult, op1=ALU.add)
                    A("dve", "scalar_tensor_tensor", [bank_res[hb], pk.res, y.res], [y.res], out=y.ap[:, 0:1],
                      in0=banks[hb][:, 0:1], scalar=cw[:, 0:1], in1=y.ap[:, 0:1], op0=ALU.mult, op1=ALU.add)
                    A("dve", "scalar_tensor_tensor", [bank_res[mb], pk.res, y.res], [y.res], out=y.ap[:, 0:511],
                      in0=banks[mb][:, 1:512], scalar=cw[:, 2:3], in1=y.ap[:, 0:511], op0=ALU.mult, op1=ALU.add)
                    A("dve", "scalar_tensor_tensor", [bank_res[hb], pk.res, y.res], [y.res], out=y.ap[:, 511:512],
                      in0=banks[hb][:, 1:2], scalar=cw[:, 2:3], in1=y.ap[:, 511:512], op0=ALU.mult, op1=ALU.add)
                    ys.append(y)
                ar.free(wu)
                A("act", "activation", [ys[0].res], [ys[0].res], out=ys[0].ap, in_=ys[0].ap, func=AF.Gelu_apprx_tanh)
                A("pool", "tensor_tensor", [ys[0].res, ys[1].res], [mT.res], out=mT.ap[:, jc, :], in0=ys[0].ap,
                  in1=ys[1].ap, op=ALU.mult)
                ar.free(*ys)
            ar.free(hTe)
            for j in range(4):
                t = qb * 4 + j
                bk2 = [mm_bank(), mm_bank()]
                for i in range(2):
                    for jc in range(NCH):
                        A("pe", "matmul", [mT.res, w_dn_t.res], [bank_res[bk2[i]]], banks[bk2[i]][:, :],
                          lhsT=mT.ap[:, jc, j * 128:(j + 1) * 128], rhs=w_dn_t.ap[:, jc, i * 512:(i + 1) * 512],
                          start=(jc == 0), stop=(jc == NCH - 1))
                post_norm_add(t, bk2, gpost)
            ar.free(mT)
        ar.free(w_dn_t, gpost, ss6, rs6, hl)

    ov = out_d.rearrange("(t p) d -> p t d", p=128)
    last = []
    for t in range(NT):
        last.append(A("sp", "dma_start", [x_res[t]], [out_res[t]], is_dma=True, out=ov[:, t, :], in_=xs[:, t, :]))
    fin = A("sp", "nop", [out_res], [])

    counts = sc.finalize()
    nsem = {e: max(1, (counts[e] + SEM_LIMIT - 1) // SEM_LIMIT) for e in ("pe", "act", "dve", "pool", "sp")}
    sems = {e: [es.enter_context(nc.semaphore("s_%s%d" % (e, i))) for i in range(nsem[e])] for e in nsem}
    dsems = {q: [es.enter_context(nc.semaphore("d_%s%d" % (q, i))) for i in range(NDS)] for q in ("sp", "pool")}

    def emit(ename, eng):
        waited = {}
        for op in sc.ops[ename]:
            for d in op.deps:
                if d.is_dma:
                    sm, val = dsems[d.eng][d.dsem], d.dval
                    key = ("d", d.eng, d.dsem)
                else:
                    sm, val = sems[d.eng][d.semk // SEM_LIMIT], d.semk % SEM_LIMIT + 1
                    key = ("c", d.eng, d.semk // SEM_LIMIT)
                if waited.get(key, 0) >= val:
                    continue
                eng.wait_ge(sm, val)
                waited[key] = val
            if op.meth == "nop":
                continue
            ins = getattr(eng, op.meth)(*op.args, **op.kw)
            if op.is_dma:
                ins.then_inc(dsems[op.eng][op.dsem], 16)
            elif op.sig:
                ins.then_inc(sems[op.eng][op.semk // SEM_LIMIT], 1)

    block = es.enter_context(nc.Block())

    @block.sync
    def _(e):
        emit("sp", e)

    @block.gpsimd
    def _(e):
        emit("pool", e)

    @block.tensor
    def _(e):
        emit("pe", e)

    @block.scalar
    def _(e):
        emit("act", e)

    @block.vector
    def _(e):
        emit("dve", e)

    es.close()
    stats = {e: len(sc.ops[e]) for e in ENGS}
    stats["arena_peak_kb"] = ar.peak
    return nc, stats


def prep_inputs(inp, n_layers=DEPTH):
    f = lambda a: np.ascontiguousarray(np.asarray(a, dtype=np.float32))
    pk = np.zeros((128, DEPTH, 217), np.float32)
    for l in range(DEPTH):
        pk[:, l, 0:8] = np.asarray(inp["mix_pre_g"])[l].reshape(8, 128).T
        pk[:, l, 8:16] = np.asarray(inp["mem_pre_g"])[l].reshape(8, 128).T
        pk[:, l, 16:24] = np.asarray(inp["ffn_pre_g"])[l].reshape(8, 128).T
        pk[:, l, 24:32] = np.asarray(inp["mem_kv_g"])[l].reshape(8, 128).T
        pk[:, l, 32:35] = np.asarray(inp["mla_cq_g"])[l].reshape(3, 128).T
        pk[:, l, 35:37] = np.asarray(inp["mla_ckv_g"])[l].reshape(2, 128).T
        pk[:, l, 37:41] = np.asarray(inp["sgu_b_s"])[l].T
        cw = np.asarray(inp["ffn_conv_w"])[l]
        cbv = np.asarray(inp["ffn_conv_b"])[l]
        c4 = np.concatenate([cw, cbv[None, :]], axis=0)
        pk[:, l, 41:217] = c4.reshape(4, 44, 128).transpose(2, 1, 0).reshape(128, 176)
    bc = np.concatenate([
        np.asarray(inp["mix_post_g"]), np.asarray(inp["mem_post_g"]), np.asarray(inp["ffn_post_g"]),
        np.asarray(inp["sgu_norm_g"]).reshape(DEPTH, 256), np.asarray(inp["diff_sub_g"]),
        np.asarray(inp["diff_lam_q1"]), np.asarray(inp["diff_lam_k1"]),
        np.asarray(inp["diff_lam_q2"]), np.asarray(inp["diff_lam_k2"])], axis=1).astype(np.float32)
    wsT = np.asarray(inp["sgu_w_s"]).transpose(0, 3, 1, 2).reshape(DEPTH, 128, 512)
    nl = n_layers
    f = lambda a: np.ascontiguousarray(np.asarray(a, dtype=np.float32)[:nl])
    wup = np.asarray(inp["ffn_w_up"], dtype=np.float32)[:nl].reshape(nl, 8, 128, 2, NCH, 128)
    wup = np.ascontiguousarray(wup.transpose(0, 4, 2, 1, 3, 5)).reshape(nl, NCH, 128, 2048)
    shared = {
        "w_in": f(inp["w_in"]), "w_uq": f(inp["mla_w_uq"]), "w_ukv": f(inp["mla_w_ukv"]), "wsT": f(wsT),
        "w_mo": f(inp["w_mix_out"]), "w_q": f(inp["mem_w_q"]), "w_kv": f(inp["mem_w_kv"]), "w_o": f(inp["mem_w_o"]),
        "w_up": wup, "w_dn": f(inp["ffn_w_down"]),
        "pk": np.ascontiguousarray(pk), "bc": np.ascontiguousarray(bc),
    }
    x = np.asarray(inp["x"], dtype=np.float32)
    mem = np.asarray(inp["mem"], dtype=np.float32)
    pos = np.asarray(inp["positions"]).astype(np.int32)
    maps = []
    for b in range(x.shape[0]):
        m = dict(shared)
        m["x"] = np.ascontiguousarray(x[b])
        m["mem"] = np.ascontiguousarray(mem[b])
        m["pos"] = np.ascontiguousarray(pos[b].reshape(NT, 128))
        maps.append(m)
    return maps


def kernel(**inputs):
    maps = prep_inputs(inputs)
    nc, _ = build()
    res = run_bass_kernel_spmd(nc, maps, core_ids=list(range(8)))
    return np.stack([np.asarray(r["out"], dtype=np.float32) for r in res.results], axis=0)
```

```python
import math
from contextlib import ExitStack

import numpy as np
import concourse.bass as bass
import concourse.mybir as mybir
from concourse.bass_utils import run_bass_kernel_spmd

F32 = mybir.dt.float32
BF16 = mybir.dt.bfloat16
I32 = mybir.dt.int32
U8 = mybir.dt.uint8
AF = mybir.ActivationFunctionType
ALU = mybir.AluOpType
AX = mybir.AxisListType

D = 1024
S = 2048
NT = 16
DEPTH = 4
NMEM = 256
EPS = 1e-6
THETA = 500000.0
INW = 1952
DFF = 2816
NCH = 22

ENGS = ("pe", "act", "dve", "pool", "sp")
SEM_LIMIT = 1000
NDS = 12


class Res:
    __slots__ = ("lw", "rd", "rd_dma")

    def __init__(self):
        self.lw = None
        self.rd = {}
        self.rd_dma = []


class Op:
    __slots__ = ("eng", "meth", "args", "kw", "deps", "sig", "semk", "is_dma", "dsem", "dval", "idx")


class Sched:
    def __init__(self):
        self.ops = {e: [] for e in ENGS}
        self.ndma = {"sp": 0, "pool": 0}
        self.dma_hist = {"sp": [], "pool": []}
        self.n = 0

    def add(self, eng, meth, r, w, *args, is_dma=False, **kw):
        op = Op()
        op.eng, op.meth, op.args, op.kw = eng, meth, args, kw
        op.is_dma = is_dma
        op.sig = False
        op.semk = None
        op.idx = self.n
        op.dsem = None
        op.dval = None
        self.n += 1
        deps = {}

        def need(d):
            if d is None or d is op:
                return
            if (not d.is_dma) and d.eng == "pe" and eng == "pe" and not is_dma:
                return
            deps[d.idx] = d

        wset = set(id(x) for x in w)
        for x in r:
            need(x.lw)
        for x in w:
            need(x.lw)
            for d in x.rd.values():
                need(d)
            for d in x.rd_dma:
                need(d)
        if is_dma:
            q = self.dma_hist[eng]
            op.dsem = len(q) % NDS
            op.dval = 16 * (len(q) // NDS + 1)
            if len(q) >= NDS:
                need(q[len(q) - NDS])
            q.append(op)
        best = {}
        out = []
        for d in deps.values():
            if d.is_dma:
                out.append(d)
            else:
                b = best.get(d.eng)
                if b is None or d.idx > b.idx:
                    best[d.eng] = d
        out.extend(best.values())
        for d in out:
            d.sig = True
        op.deps = out
        for x in w:
            x.lw = op
            x.rd = {}
            x.rd_dma = []
        for x in r:
            if id(x) in wset:
                continue
            if is_dma:
                x.rd_dma.append(op)
            else:
                x.rd[eng] = op
        self.ops[eng].append(op)
        return op

    def finalize(self):
        for e in ENGS:
            k = 0
            for op in self.ops[e]:
                if op.is_dma:
                    continue
                if op.sig:
                    op.semk = k
                    k += 1
        return {e: sum(1 for o in self.ops[e] if (not o.is_dma) and o.sig) for e in ENGS}


class Buf:
    __slots__ = ("ap", "res", "off", "size", "name", "owner")


class Arena:
    CH = 1024

    def __init__(self, t, nbytes, base=0, nextfit=False):
        self.t = t
        self.base = base
        self.nextfit = nextfit
        self.ptr = 0
        self.nbytes = nbytes
        self.nch = nbytes // self.CH
        self.res = [Res() for _ in range(self.nch)]
        self.used = [False] * self.nch
        self.peak = 0

    def alloc(self, name, shape, dt):
        esz = 4 if dt in (F32, I32) else 2
        n = 1
        for s in shape[1:]:
            n *= s
        nb = n * esz
        k = (nb + self.CH - 1) // self.CH
        start = None
        order = [0]
        if self.nextfit:
            order = [self.ptr, 0]
        for s0 in order:
            run = 0
            for i in range(s0, self.nch):
                if not self.used[i]:
                    run += 1
                    if run == k:
                        start = i - k + 1
                        break
                else:
                    run = 0
            if start is not None:
                break
        if start is not None:
            self.ptr = start + k
        if start is None:
            raise RuntimeError("arena OOM for %s (%d B); used=%d" % (name, nb, sum(self.used)))
        for i in range(start, start + k):
            self.used[i] = True
        self.peak = max(self.peak, max(i for i in range(self.nch) if self.used[i]) + 1)
        b = Buf()
        b.name = name
        b.owner = self
        b.off = start
        b.size = k
        o = self.base + start * self.CH
        ap = self.t[:, o:o + nb].bitcast(dt)
        if len(shape) == 3:
            b.ap = ap.rearrange("p (a b) -> p a b", b=shape[2])
        else:
            b.ap = ap
        b.res = self.res[start:start + k]
        return b

    def free(self, *bufs):
        for b in bufs:
            for i in range(b.off, b.off + b.size):
                b.owner.used[i] = False


TRANS = {"wuf", "stg", "oT", "hn", "hT", "ssq", "rsq", "cn", "rt", "zg", "st", "vsq", "vn", "qk", "r6", "pt", "rc", "o1", "s2", "tmp",
         "cT", "aT", "mn", "y", "wu", "ptm", "ltmp"}


class Arenas:
    def __init__(self, main, trans):
        self.main = main
        self.trans = trans

    def alloc(self, name, shape, dt):
        if name in TRANS:
            return self.trans.alloc(name, shape, dt)
        return self.main.alloc(name, shape, dt)

    def free(self, *bufs):
        self.main.free(*bufs)

    @property
    def peak(self):
        return (self.main.peak, self.trans.peak)


def build(n_layers=DEPTH, stop_after=None):
    nc = bass.Bass("TRN2", target_bir_lowering=False)

    def din(name, shape, dt=F32):
        return nc.dram_tensor(name, list(shape), dt, kind="ExternalInput").ap()

    WL = n_layers
    x_d = din("x", [S, D])
    mem_d = din("mem", [NMEM, D])
    pos_d = din("pos", [NT, 128], I32)
    w_in_d = din("w_in", [WL, D, INW])
    w_uq_d = din("w_uq", [WL, 384, 768])
    w_ukv_d = din("w_ukv", [WL, 256, 1024])
    wsT_d = din("wsT", [WL, 128, 512])
    w_mo_d = din("w_mo", [WL, D, D])
    w_q_d = din("w_q", [WL, D, D])
    w_kv_d = din("w_kv", [WL, D, 2 * D])
    w_o_d = din("w_o", [WL, D, D])
    w_up_d = din("w_up", [WL, NCH, 128, 2048])
    w_dn_d = din("w_dn", [WL, DFF, D])
    NPK = 217
    pk_d = din("pk", [128, DEPTH, NPK])
    NBC = 3520
    bc_d = din("bc", [DEPTH, NBC])
    out_d = nc.dram_tensor("out", [S, D], F32, kind="ExternalOutput").ap()

    sc = Sched()
    es = ExitStack()
    xs = es.enter_context(nc.sbuf_tensor("xs", [128, NT, D], F32))
    x_res = [Res() for _ in range(NT)]
    MAIN_B = 114 * 1024
    TRANS_B = 29 * 1024
    ar_t = es.enter_context(nc.sbuf_tensor("arena", [128, MAIN_B + TRANS_B], U8))
    ar = Arenas(Arena(ar_t, MAIN_B), Arena(ar_t, TRANS_B, base=MAIN_B, nextfit=True))
    banks = [es.enter_context(nc.psum_tensor("bank%d" % i, [128, 512], F32)) for i in range(8)]
    bank_res = [Res() for _ in range(8)]
    bank_ids = set(id(b) for b in bank_res)
    out_res = [Res() for _ in range(NT)]

    def A(eng, meth, r, w, *a, **k):
        rr = []
        for x in r:
            rr.extend(x if isinstance(x, (list, tuple)) else [x])
        ww = []
        for x in w:
            ww.extend(x if isinstance(x, (list, tuple)) else [x])
        for x in rr:
            if id(x) in bank_ids:
                ww.append(x)
        return sc.add(eng, meth, rr, ww, *a, **k)

    tp_i = [0]

    def tp_bank():
        tp_i[0] ^= 1
        return tp_i[0]

    mm_i = [0]

    mm_all = [False]

    def mm_bank():
        if mm_all[0]:
            mm_i[0] = (mm_i[0] + 1) % 8
            return mm_i[0]
        mm_i[0] = (mm_i[0] + 1) % 6
        return 2 + mm_i[0]

    cp_i = [0]

    def copy_eng():
        cp_i[0] ^= 1
        return "act" if cp_i[0] else "dve"

    def copy(eng, r, w, out, in_):
        if eng == "act":
            A("act", "activation", r, w, out=out, in_=in_, func=AF.Copy)
        elif eng == "dve":
            A("dve", "tensor_copy", r, w, out=out, in_=in_)
        else:
            A("pool", "tensor_copy", r, w, out=out, in_=in_)

    ident = ar.alloc("ident", [128, 128], BF16)
    identf = ar.alloc("identf", [128, 128], F32)
    A("pool", "memset", [], [identf.res], identf.ap, 0.0)
    A("pool", "iota", [], [identf.res], identf.ap, pattern=[[1, 128]], base=0, channel_multiplier=-1,
      allow_small_or_imprecise_dtypes=True)
    A("dve", "tensor_scalar", [identf.res], [ident.res], out=ident.ap, in0=identf.ap, scalar1=0.0, scalar2=None,
      op0=ALU.is_equal)
    identF = ar.alloc("identF", [128, 128], F32)
    A("dve", "tensor_scalar", [identf.res], [identF.res], out=identF.ap, in0=identf.ap, scalar1=0.0, scalar2=None,
      op0=ALU.is_equal)
    epsb = ar.alloc("eps", [128, 1], F32)
    A("dve", "memset", [], [epsb.res], epsb.ap, EPS)
    zerob = ar.alloc("zero", [128, 1], F32)
    A("dve", "memset", [], [zerob.res], zerob.ap, 0.0)
    junk = ar.alloc("junk", [128, 1024], BF16)

    xv = x_d.rearrange("(t p) d -> p t d", p=128)
    for t in range(NT):
        A("sp", "dma_start", [], [x_res[t]], is_dma=True, out=xs[:, t, :], in_=xv[:, t, :])

    pk = ar.alloc("pk", [128, DEPTH, NPK], F32)
    A("sp", "dma_start", [], [pk.res], is_dma=True, out=pk.ap, in_=pk_d)

    posi = ar.alloc("posi", [128, NT], I32)
    for t in range(NT):
        A("sp", "dma_start", [], [posi.res], is_dma=True, out=posi.ap[:, t:t + 1],
          in_=pos_d[t:t + 1, :].rearrange("a p -> p a"))
    posf = ar.alloc("posf", [128, NT], F32)
    A("dve", "tensor_copy", [posi.res], [posf.res], out=posf.ap, in_=posi.ap)
    ang = ar.alloc("ang", [128, NT, 16], F32)
    for f in range(16):
        A("dve", "tensor_scalar", [posf.res], [ang.res], out=ang.ap[:, :, f], in0=posf.ap,
          scalar1=float(THETA ** (-2.0 * f / 32.0)), scalar2=None, op0=ALU.mult)
    cosA = ar.alloc("cosA", [128, NT, 16], F32)
    sinA = ar.alloc("sinA", [128, NT, 16], F32)
    TWO_PI = 2.0 * math.pi
    C1 = 6.28125
    C2 = TWO_PI - C1
    tA = ar.alloc("tA", [128, NT, 16], F32)
    tK = ar.alloc("tK", [128, NT, 16], I32)
    tKf = ar.alloc("tKf", [128, NT, 16], F32)
    tM = ar.alloc("tM", [128, NT, 16], F32)
    for (dst, shift) in ((sinA, 0.0), (cosA, math.pi / 2)):
        A("dve", "tensor_scalar", [ang.res], [tA.res], out=tA.ap, in0=ang.ap, scalar1=shift, scalar2=None,
          op0=ALU.add)
        A("dve", "tensor_scalar", [tA.res], [tM.res], out=tM.ap, in0=tA.ap, scalar1=1.0 / TWO_PI, scalar2=None,
          op0=ALU.mult)
        A("dve", "tensor_copy", [tM.res], [tK.res], out=tK.ap, in_=tM.ap)
        A("dve", "tensor_copy", [tK.res], [tKf.res], out=tKf.ap, in_=tK.ap)
        A("dve", "scalar_tensor_tensor", [tKf.res, tA.res], [tM.res], out=tM.ap, in0=tKf.ap, scalar=-C1,
          in1=tA.ap, op0=ALU.mult, op1=ALU.add)
        A("dve", "scalar_tensor_tensor", [tKf.res, tM.res], [tA.res], out=tA.ap, in0=tKf.ap, scalar=-C2,
          in1=tM.ap, op0=ALU.mult, op1=ALU.add)
        A("dve", "tensor_scalar", [tA.res], [tM.res], out=tM.ap, in0=tA.ap, scalar1=math.pi, scalar2=-TWO_PI,
          op0=ALU.is_gt, op1=ALU.mult)
        A("dve", "tensor_tensor", [tA.res, tM.res], [tKf.res], out=tKf.ap, in0=tA.ap, in1=tM.ap, op=ALU.add)
        A("dve", "tensor_scalar", [tKf.res], [tM.res], out=tM.ap, in0=tKf.ap, scalar1=-math.pi, scalar2=TWO_PI,
          op0=ALU.is_lt, op1=ALU.mult)
        A("dve", "tensor_tensor", [tKf.res, tM.res], [tA.res], out=tA.ap, in0=tKf.ap, in1=tM.ap, op=ALU.add)
        A("dve", "tensor_scalar", [tA.res], [tM.res], out=tM.ap, in0=tA.ap, scalar1=math.pi, scalar2=-math.pi,
          op0=ALU.min, op1=ALU.max)
        A("act", "activation", [tM.res], [dst.res], out=dst.ap, in_=tM.ap, func=AF.Sin)
    ar.free(tA, tK, tKf, tM, ang, posi, posf, identf)

    def rstd_from(ss, ncol, dim, dst):
        A("act", "activation", [ss[1], epsb.res], [dst[1]], out=dst[0], in_=ss[0], func=AF.Ln, scale=1.0 / dim,
          bias=epsb.ap[:, 0:1])
        A("act", "activation", [dst[1]], [dst[1]], out=dst[0], in_=dst[0], func=AF.Exp, scale=-0.5)

    def load_w(name, shape, src, scale_cols=None, l=0, chunk_cols=None):
        b = ar.alloc(name, shape, BF16)
        kc, n = shape[1], shape[2]
        step = 2048
        for c in range(kc):
            for n0 in range(0, n, step):
                n1 = min(n, n0 + step)
                stg = ar.alloc("stg", [128, n1 - n0], F32)
                A("sp", "dma_start", [], [stg.res], is_dma=True, out=stg.ap, in_=src[:, c, n0:n1])
                if scale_cols is not None:
                    A("pool", "tensor_scalar", [stg.res, pk.res], [b.res], out=b.ap[:, c, n0:n1], in0=stg.ap,
                      scalar1=pk.ap[:, l, scale_cols + c:scale_cols + c + 1], scalar2=1.0, op0=ALU.mult, op1=ALU.mult)
                else:
                    A("pool", "tensor_copy", [stg.res], [b.res], out=b.ap[:, c, n0:n1], in_=stg.ap)
                ar.free(stg)
        return b

    def transposes(srcs, src_res, dst_ap, dst_res, rows=128):
        bk = tp_bank()
        pv = banks[bk][:].bitcast(BF16).rearrange("p (a b) -> p a b", b=128)
        for i, s_ap in enumerate(srcs):
            A("pe", "transpose", [src_res, ident.res], [bank_res[bk]], out=pv[0:rows, i, :], in_=s_ap,
              identity=ident.ap)
        copy(copy_eng(), [bank_res[bk]], [dst_res], dst_ap, pv[0:rows, 0:len(srcs), :])

    def attention(QT, KT, krows, roff, V_of_kc, v_res, nkc, dv, scale, evac):
        pending = [None]

        def epilogue(qb, ab):
            oT = ar.alloc("oT", [128, 512], F32)
            A("dve", "tensor_copy", [bank_res[ab]], [oT.res], out=oT.ap[0:dv + 1, :], in_=banks[ab][0:dv + 1, :])

            def pe_part():
                tb = mm_bank()
                for j in range(4):
                    A("pe", "transpose", [oT.res, identF.res], [bank_res[tb]],
                      out=banks[tb][:, j * (dv + 1):(j + 1) * (dv + 1)], in_=oT.ap[0:dv + 1, j * 128:(j + 1) * 128],
                      identity=identF.ap[0:dv + 1, 0:dv + 1])
                ar.free(oT)
                evac(qb, banks[tb][:, 0:4 * (dv + 1)].rearrange("p (j e) -> p j e", e=dv + 1), bank_res[tb])
            return pe_part

        for qb in range(4):
            ab = mm_bank()
            pts = {}

            def issue_s(kc, qb=qb, pts=pts):
                sb_ = tp_bank()
                A("pe", "matmul", [QT[1], KT[1]], [bank_res[sb_]], banks[sb_][:, :],
                  lhsT=KT[0][roff:roff + krows, kc * 128:(kc + 1) * 128],
                  rhs=QT[0][roff:roff + krows, qb * 512:(qb + 1) * 512], start=True, stop=True,
                  **({"tile_position": (roff, 0)} if krows == 32 else {}))
                pt = ar.alloc("pt", [128, 512], BF16)
                A("act", "activation", [bank_res[sb_]], [pt.res], out=pt.ap, in_=banks[sb_][:, :], func=AF.Exp,
                  scale=scale)
                pts[kc] = pt

            def issue_pv(kc, ab=ab, pts=pts):
                pt = pts.pop(kc)
                A("pe", "matmul", [pt.res, v_res], [bank_res[ab]], banks[ab][0:dv + 1, :],
                  lhsT=V_of_kc(kc), rhs=pt.ap, start=(kc == 0), stop=(kc == nkc - 1))
                ar.free(pt)

            issue_s(0)
            for kc in range(nkc):
                if kc + 1 < nkc:
                    issue_s(kc + 1)
                issue_pv(kc)
                if kc == 2 and pending[0] is not None:
                    pending[0]()
                    pending[0] = None
            if pending[0] is not None:
                pending[0]()
            pending[0] = epilogue(qb, ab)
        pending[0]()

    for l in range(n_layers):
        if stop_after == "setup":
            break
        lam_init = 0.8 - 0.6 * math.exp(-0.3 * l)
        bcs = ar.alloc("bcs", [128, 448], F32)
        A("sp", "dma_start", [], [bcs.res], is_dma=True, out=bcs.ap, in_=bc_d[l:l + 1, 3072:3520].partition_broadcast(128))
        lamt = ar.alloc("lamt", [128, 8], F32)
        ltmp = ar.alloc("ltmp", [128, 64], F32)
        A("dve", "tensor_tensor", [bcs.res], [ltmp.res], out=ltmp.ap[:, 0:32], in0=bcs.ap[:, 320:352],
          in1=bcs.ap[:, 352:384], op=ALU.mult)
        A("dve", "tensor_tensor", [bcs.res], [ltmp.res], out=ltmp.ap[:, 32:64], in0=bcs.ap[:, 384:416],
          in1=bcs.ap[:, 416:448], op=ALU.mult)
        A("dve", "tensor_reduce", [ltmp.res], [lamt.res], out=lamt.ap[:, 0:2],
          in_=ltmp.ap.rearrange("p (a b) -> p a b", b=32), axis=AX.X, op=ALU.add)
        A("act", "activation", [lamt.res], [lamt.res], out=lamt.ap[:, 2:4], in_=lamt.ap[:, 0:2], func=AF.Exp)
        A("dve", "tensor_tensor", [lamt.res], [lamt.res], out=lamt.ap[:, 4:5], in0=lamt.ap[:, 3:4], in1=lamt.ap[:, 2:3],
          op=ALU.subtract)
        A("dve", "tensor_scalar", [lamt.res], [lamt.res], out=lamt.ap[:, 5:6], in0=lamt.ap[:, 4:5], scalar1=-lam_init,
          scalar2=None, op0=ALU.add)
        neglam = lamt.ap[:, 5:6]
        A("dve", "tensor_scalar", [bcs.res], [bcs.res], out=bcs.ap[:, 256:320], in0=bcs.ap[:, 256:320],
          scalar1=1.0 - lam_init, scalar2=None, op0=ALU.mult)
        ar.free(ltmp)

        w_in_t = load_w("w_in", [128, 8, INW], w_in_d[l].rearrange("(c p) n -> p c n", p=128), scale_cols=0, l=l)
        wsT = load_w("wsT", [128, 1, 512], wsT_d[l].rearrange("p (a n) -> p a n", a=1))

        ss1 = ar.alloc("ss1", [128, NT], F32)
        rs1 = ar.alloc("rs1", [128, NT], F32)
        for t in range(NT):
            A("act", "activation", [x_res[t]], [junk.res, ss1.res], out=junk.ap, in_=xs[:, t, :], func=AF.Square,
              accum_out=ss1.ap[:, t:t + 1])
        rstd_from((ss1.ap, ss1.res), NT, D, (rs1.ap, rs1.res))
        cqnT = ar.alloc("cqnT", [128, 3, S], BF16)
        ckvnT = ar.alloc("ckvnT", [128, 2, S], BF16)
        krtm = ar.alloc("krtm", [128, NT, 32], BF16)
        dqT = ar.alloc("dqT", [128, 2, S], BF16)
        dkT = ar.alloc("dkT", [128, 2, S], BF16)
        dV = ar.alloc("dV", [128, NT * 4, 65], BF16)
        A("pool", "memset", [], [dV.res], dV.ap[:, :, 64:65], 1.0)
        cb = ar.alloc("cb", [128, NT, 256], BF16)
        COLS = ((0, 384), (384, 672), (672, 1184), (1184, 1696), (1696, 1952))
        for t in range(NT):
            tc_ = slice(t * 128, (t + 1) * 128)
            hn = ar.alloc("hn", [128, D], BF16)
            A("dve", "tensor_scalar", [x_res[t], rs1.res], [hn.res], out=hn.ap, in0=xs[:, t, :],
              scalar1=rs1.ap[:, t:t + 1], scalar2=None, op0=ALU.mult)
            hT = ar.alloc("hT", [128, 8, 128], BF16)
            transposes([hn.ap[:, c * 128:(c + 1) * 128] for c in range(8)], hn.res, hT.ap, hT.res)
            ar.free(hn)
            pj = [mm_bank() for _ in range(5)]
            for gi, (c0, c1) in enumerate(COLS):
                for c in range(8):
                    A("pe", "matmul", [hT.res, w_in_t.res], [bank_res[pj[gi]]], banks[pj[gi]][:, 0:c1 - c0],
                      lhsT=hT.ap[:, c, :], rhs=w_in_t.ap[:, c, c0:c1], start=(c == 0), stop=(c == 7))
            ar.free(hT)
            ssq = ar.alloc("ssq", [128, 2], F32)
            rsq = ar.alloc("rsq", [128, 2], F32)
            A("act", "activation", [bank_res[pj[0]]], [junk.res, ssq.res], out=junk.ap[:, 0:384],
              in_=banks[pj[0]][:, 0:384], func=AF.Square, accum_out=ssq.ap[:, 0:1])
            A("act", "activation", [bank_res[pj[1]]], [junk.res, ssq.res], out=junk.ap[:, 0:256],
              in_=banks[pj[1]][:, 0:256], func=AF.Square, accum_out=ssq.ap[:, 1:2])
            A("act", "activation", [ssq.res, epsb.res], [rsq.res], out=rsq.ap[:, 0:1], in_=ssq.ap[:, 0:1], func=AF.Ln,
              scale=1.0 / 384, bias=epsb.ap[:, 0:1])
            A("act", "activation", [ssq.res, epsb.res], [rsq.res], out=rsq.ap[:, 1:2], in_=ssq.ap[:, 1:2], func=AF.Ln,
              scale=1.0 / 256, bias=epsb.ap[:, 0:1])
            A("act", "activation", [rsq.res], [rsq.res], out=rsq.ap, in_=rsq.ap, func=AF.Exp, scale=-0.5)
            cn = ar.alloc("cn", [128, 640], BF16)
            A("dve", "tensor_scalar", [bank_res[pj[0]], rsq.res], [cn.res], out=cn.ap[:, 0:384],
              in0=banks[pj[0]][:, 0:384], scalar1=rsq.ap[:, 0:1], scalar2=None, op0=ALU.mult)
            A("dve", "tensor_scalar", [bank_res[pj[1]], rsq.res], [cn.res], out=cn.ap[:, 384:640],
              in0=banks[pj[1]][:, 0:256], scalar1=rsq.ap[:, 1:2], scalar2=None, op0=ALU.mult)
            transposes([cn.ap[:, c * 128:(c + 1) * 128] for c in range(3)], cn.res, cqnT.ap[:, :, tc_], cqnT.res)
            transposes([cn.ap[:, 384 + c * 128:384 + (c + 1) * 128] for c in range(2)], cn.res, ckvnT.ap[:, :, tc_],
                       ckvnT.res)
            ar.free(ssq, rsq, cn)
            rt = ar.alloc("rt", [128, 4, 16], F32)
            kb = banks[pj[1]]
            kr_res = bank_res[pj[1]]
            cA = cosA.ap[:, t, :]
            sA = sinA.ap[:, t, :]
            A("dve", "tensor_tensor", [kr_res, cosA.res], [rt.res], out=rt.ap[:, 0, :], in0=kb[:, 256:272], in1=cA,
              op=ALU.mult)
            A("dve", "tensor_tensor", [kr_res, sinA.res], [rt.res], out=rt.ap[:, 1, :], in0=kb[:, 272:288], in1=sA,
              op=ALU.mult)
            A("dve", "tensor_tensor", [kr_res, cosA.res], [rt.res], out=rt.ap[:, 2, :], in0=kb[:, 272:288], in1=cA,
              op=ALU.mult)
            A("dve", "tensor_tensor", [kr_res, sinA.res], [rt.res], out=rt.ap[:, 3, :], in0=kb[:, 256:272], in1=sA,
              op=ALU.mult)
            A("dve", "tensor_tensor", [rt.res], [krtm.res], out=krtm.ap[:, t, 0:16], in0=rt.ap[:, 0, :],
              in1=rt.ap[:, 1, :], op=ALU.subtract)
            A("dve", "tensor_tensor", [rt.res], [krtm.res], out=krtm.ap[:, t, 16:32], in0=rt.ap[:, 2, :],
              in1=rt.ap[:, 3, :], op=ALU.add)
            ar.free(rt)
            zg = ar.alloc("zg", [128, 512], F32)
            A("act", "activation", [bank_res[pj[2]]], [zg.res], out=zg.ap, in_=banks[pj[2]][:, :],
              func=AF.Gelu_apprx_tanh)
            st = ar.alloc("st", [128, 16], F32)
            vsq = ar.alloc("vsq", [128, 256], F32)
            v3 = zg.ap[:, 256:512].rearrange("p (g c) -> p g c", c=64)
            A("dve", "tensor_reduce", [zg.res], [st.res], out=st.ap[:, 0:4], in_=v3, axis=AX.X, op=ALU.add)
            A("act", "activation", [zg.res], [vsq.res], out=vsq.ap, in_=zg.ap[:, 256:512], func=AF.Square)
            A("dve", "tensor_reduce", [vsq.res], [st.res], out=st.ap[:, 4:8],
              in_=vsq.ap.rearrange("p (g c) -> p g c", c=64), axis=AX.X, op=ALU.add)
            A("dve", "tensor_scalar", [st.res], [st.res], out=st.ap[:, 8:12], in0=st.ap[:, 0:4], scalar1=1.0 / 64,
              scalar2=None, op0=ALU.mult)
            A("dve", "tensor_tensor", [st.res], [st.res], out=st.ap[:, 12:16], in0=st.ap[:, 8:12], in1=st.ap[:, 8:12],
              op=ALU.mult)
            A("dve", "scalar_tensor_tensor", [st.res], [st.res], out=st.ap[:, 4:8], in0=st.ap[:, 4:8], scalar=1.0 / 64,
              in1=st.ap[:, 12:16], op0=ALU.mult, op1=ALU.subtract)
            A("act", "activation", [st.res, epsb.res], [st.res], out=st.ap[:, 0:4], in_=st.ap[:, 4:8], func=AF.Ln,
              bias=epsb.ap[:, 0:1])
            A("act", "activation", [st.res], [st.res], out=st.ap[:, 0:4], in_=st.ap[:, 0:4], func=AF.Exp, scale=-0.5)
            vq3 = vsq.ap.rearrange("p (g c) -> p g c", c=64)
            A("dve", "tensor_tensor", [zg.res, st.res], [vsq.res], out=vq3, in0=v3,
              in1=st.ap[:, 8:12].unsqueeze(2).broadcast_to([128, 4, 64]), op=ALU.subtract)
            A("dve", "tensor_tensor", [vsq.res, st.res], [vsq.res], out=vq3, in0=vq3,
              in1=st.ap[:, 0:4].unsqueeze(2).broadcast_to([128, 4, 64]), op=ALU.mult)
            vn = ar.alloc("vn", [128, 256], BF16)
            A("dve", "tensor_tensor", [vsq.res, bcs.res], [vn.res], out=vn.ap, in0=vsq.ap, in1=bcs.ap[:, 0:256],
              op=ALU.mult)
            mb = mm_bank()
            for g in range(4):
                A("pe", "matmul", [vn.res, wsT.res], [bank_res[mb]], banks[mb][:, g * 64:(g + 1) * 64],
                  lhsT=wsT.ap[:, 0, g * 128:(g + 1) * 128], rhs=vn.ap[:, g * 64:(g + 1) * 64], start=True, stop=True)
            for g in range(4):
                A("dve", "scalar_tensor_tensor", [bank_res[mb], pk.res, zg.res], [cb.res],
                  out=cb.ap[:, t, g * 64:(g + 1) * 64], in0=banks[mb][:, g * 64:(g + 1) * 64],
                  scalar=pk.ap[:, l, 37 + g:38 + g], in1=zg.ap[:, g * 64:(g + 1) * 64], op0=ALU.add, op1=ALU.mult)
            ar.free(zg, st, vsq, vn)
            qk = ar.alloc("qk", [128, 16, 32], BF16)
            r6 = ar.alloc("r6", [128, 4 * 16, 8], F32)
            pq = banks[pj[3]][:, :].rearrange("p (m d) -> p m d", d=32)
            qres = bank_res[pj[3]]
            cD = cosA.ap[:, t, 0:16:2].unsqueeze(1).broadcast_to([128, 16, 8])
            sD = sinA.ap[:, t, 0:16:2].unsqueeze(1).broadcast_to([128, 16, 8])
            A("dve", "tensor_tensor", [qres, cosA.res], [r6.res], out=r6.ap[:, 0:16, :], in0=pq[:, :, 0:8], in1=cD,
              op=ALU.mult)
            A("dve", "tensor_tensor", [qres, sinA.res], [r6.res], out=r6.ap[:, 16:32, :], in0=pq[:, :, 8:16], in1=sD,
              op=ALU.mult)
            A("dve", "tensor_tensor", [qres, cosA.res], [r6.res], out=r6.ap[:, 32:48, :], in0=pq[:, :, 8:16], in1=cD,
              op=ALU.mult)
            A("dve", "tensor_tensor", [qres, sinA.res], [r6.res], out=r6.ap[:, 48:64, :], in0=pq[:, :, 0:8], in1=sD,
              op=ALU.mult)
            A("dve", "tensor_tensor", [r6.res], [qk.res], out=qk.ap[:, :, 0:8], in0=r6.ap[:, 0:16, :],
              in1=r6.ap[:, 16:32, :], op=ALU.subtract)
            A("dve", "tensor_tensor", [r6.res], [qk.res], out=qk.ap[:, :, 8:16], in0=r6.ap[:, 32:48, :],
              in1=r6.ap[:, 48:64, :], op=ALU.add)
            A("act", "activation", [qres], [qk.res], out=qk.ap[:, :, 16:32], in_=pq[:, :, 16:32], func=AF.Copy)
            qk2 = qk.ap.rearrange("p m d -> p (m d)")
            transposes([qk2[:, c * 128:(c + 1) * 128] for c in range(2)], qk.res, dqT.ap[:, :, tc_], dqT.res)
            transposes([qk2[:, 256 + c * 128:256 + (c + 1) * 128] for c in range(2)], qk.res, dkT.ap[:, :, tc_],
                       dkT.res)
            ar.free(qk, r6)
            A("act", "activation", [bank_res[pj[4]]], [dV.res], out=dV.ap[:, t * 4:(t + 1) * 4, 0:64],
              in_=banks[pj[4]][:, 0:256].rearrange("p (h e) -> p h e", e=64), func=AF.Copy)
        ar.free(ss1, rs1, w_in_t, wsT)
        if stop_after == "p1":
            break

        cc = ar.alloc("cc", [128, NT, 256], BF16)
        w_mo_t = load_w("w_mo", [128, 8, D], w_mo_d[l].rearrange("(c p) n -> p c n", p=128))
        w_uq_t = load_w("w_uq", [128, 3, 768], w_uq_d[l].rearrange("(c p) n -> p c n", p=128), scale_cols=32, l=l)
        w_ukv_t = load_w("w_ukv", [128, 2, 1024], w_ukv_d[l].rearrange("(c p) n -> p c n", p=128), scale_cols=35, l=l)
        for h in range(4):
            ous = []
            for c in range(2):
                m = 2 * h + c
                ci, ro = m // 4, 32 * (m % 4)
                ou = ar.alloc("ou%d" % c, [128, NT, 65], F32)
                ous.append(ou)

                def evac(qb, acc, acc_res, ou=ou):
                    A("dve", "tensor_copy", [acc_res], [ou.res], out=ou.ap[:, qb * 4:(qb + 1) * 4, :], in_=acc)

                attention((dqT.ap[:, ci, :], dqT.res), (dkT.ap[:, ci, :], dkT.res), 32, ro,
                          lambda kc, h=h: dV.ap[:, kc * 4 + h, :], dV.res, NT, 64, 32 ** -0.5, evac)
            o0, o1 = ous
            fr = ar.alloc("fr", [128, 4, NT], F32)
            A("dve", "reciprocal", [o0.res], [fr.res], out=fr.ap[:, 0, :], in_=o0.ap[:, :, 64])
            A("dve", "reciprocal", [o1.res], [fr.res], out=fr.ap[:, 1, :], in_=o1.ap[:, :, 64])
            A("dve", "tensor_scalar", [fr.res, lamt.res], [fr.res], out=fr.ap[:, 1, :], in0=fr.ap[:, 1, :],
              scalar1=neglam, scalar2=None, op0=ALU.mult)
            A("dve", "tensor_tensor", [o0.res, fr.res], [o0.res], out=o0.ap[:, :, 0:64], in0=o0.ap[:, :, 0:64],
              in1=fr.ap[:, 0, :].unsqueeze(2).broadcast_to([128, NT, 64]), op=ALU.mult)
            A("dve", "tensor_tensor", [o1.res, fr.res], [o1.res], out=o1.ap[:, :, 0:64], in0=o1.ap[:, :, 0:64],
              in1=fr.ap[:, 1, :].unsqueeze(2).broadcast_to([128, NT, 64]), op=ALU.mult)
            A("dve", "tensor_tensor", [o0.res, o1.res], [o0.res], out=o0.ap[:, :, 0:64], in0=o0.ap[:, :, 0:64],
              in1=o1.ap[:, :, 0:64], op=ALU.add)
            A("dve", "tensor_tensor", [o0.res], [o1.res], out=o1.ap[:, :, 0:64], in0=o0.ap[:, :, 0:64],
              in1=o0.ap[:, :, 0:64], op=ALU.mult)
            A("dve", "tensor_reduce", [o1.res], [fr.res], out=fr.ap[:, 2, :], in_=o1.ap[:, :, 0:64], axis=AX.X,
              op=ALU.add)
            A("act", "activation", [fr.res, epsb.res], [fr.res], out=fr.ap[:, 3, :], in_=fr.ap[:, 2, :], func=AF.Ln,
              scale=1.0 / 64, bias=epsb.ap[:, 0:1])
            A("act", "activation", [fr.res], [fr.res], out=fr.ap[:, 3, :], in_=fr.ap[:, 3, :], func=AF.Exp, scale=-0.5)
            A("dve", "tensor_tensor", [o0.res, fr.res], [o0.res], out=o0.ap[:, :, 0:64], in0=o0.ap[:, :, 0:64],
              in1=fr.ap[:, 3, :].unsqueeze(2).broadcast_to([128, NT, 64]), op=ALU.mult)
            A("dve", "tensor_tensor", [o0.res, bcs.res], [cc.res], out=cc.ap[:, :, h * 64:(h + 1) * 64],
              in0=o0.ap[:, :, 0:64], in1=bcs.ap[:, 256:320].unsqueeze(1).broadcast_to([128, NT, 64]), op=ALU.mult)
            ar.free(o0, o1, fr)
        ar.free(dqT, dkT, dV)
        if stop_after == "p2":
            break

        ca = ar.alloc("ca", [128, NT, 512], BF16)
        for h in range(8):
            QKT = ar.alloc("QKT", [128, 2, S], BF16)
            Vh = ar.alloc("Vh", [128, NT, 65], BF16)
            A("pool", "memset", [], [Vh.res], Vh.ap[:, :, 64:65], 1.0)
            for g in range(4):
                bA = mm_bank()
                bB = mm_bank()
                for tt in range(4):
                    t = g * 4 + tt
                    tc_ = slice(t * 128, (t + 1) * 128)
                    for c in range(3):
                        A("pe", "matmul", [cqnT.res, w_uq_t.res], [bank_res[bA]], banks[bA][:, tt * 96:(tt + 1) * 96],
                          lhsT=cqnT.ap[:, c, tc_], rhs=w_uq_t.ap[:, c, h * 96:(h + 1) * 96], start=(c == 0),
                          stop=(c == 2))
                    for c in range(2):
                        A("pe", "matmul", [ckvnT.res, w_ukv_t.res], [bank_res[bB]],
                          banks[bB][:, tt * 128:(tt + 1) * 128], lhsT=ckvnT.ap[:, c, tc_],
                          rhs=w_ukv_t.ap[:, c, h * 128:(h + 1) * 128], start=(c == 0), stop=(c == 1))
                qA = banks[bA][:, 0:384].rearrange("p (t d) -> p t d", d=96)
                kvB = banks[bB][:, :].rearrange("p (t d) -> p t d", d=128)
                rA, rB = bank_res[bA], bank_res[bB]
                qk = ar.alloc("qk", [128, 8, 128], BF16)
                qk4 = qk.ap.rearrange("p (t s) d -> p t s d", s=2)
                A("pool", "memset", [], [qk.res], qk.ap[:, :, 96:128], 0.0)
                A("act", "activation", [rA], [qk.res], out=qk4[:, :, 0, 0:64], in_=qA[:, :, 0:64], func=AF.Copy)
                A("act", "activation", [rB], [qk.res], out=qk4[:, :, 1, 0:64], in_=kvB[:, :, 0:64], func=AF.Copy)
                A("act", "activation", [rB], [Vh.res], out=Vh.ap[:, g * 4:(g + 1) * 4, 0:64], in_=kvB[:, :, 64:128],
                  func=AF.Copy)
                rt = ar.alloc("rt", [128, 16, 16], F32)
                cA = cosA.ap[:, g * 4:(g + 1) * 4, :]
                sA = sinA.ap[:, g * 4:(g + 1) * 4, :]
                A("dve", "tensor_tensor", [rA, cosA.res], [rt.res], out=rt.ap[:, 0:4, :], in0=qA[:, :, 64:80], in1=cA,
                  op=ALU.mult)
                A("dve", "tensor_tensor", [rA, sinA.res], [rt.res], out=rt.ap[:, 4:8, :], in0=qA[:, :, 80:96], in1=sA,
                  op=ALU.mult)
                A("dve", "tensor_tensor", [rA, cosA.res], [rt.res], out=rt.ap[:, 8:12, :], in0=qA[:, :, 80:96], in1=cA,
                  op=ALU.mult)
                A("dve", "tensor_tensor", [rA, sinA.res], [rt.res], out=rt.ap[:, 12:16, :], in0=qA[:, :, 64:80], in1=sA,
                  op=ALU.mult)
                A("dve", "tensor_tensor", [rt.res], [qk.res], out=qk4[:, :, 0, 64:80], in0=rt.ap[:, 0:4, :],
                  in1=rt.ap[:, 4:8, :], op=ALU.subtract)
                A("dve", "tensor_tensor", [rt.res], [qk.res], out=qk4[:, :, 0, 80:96], in0=rt.ap[:, 8:12, :],
                  in1=rt.ap[:, 12:16, :], op=ALU.add)
                A("dve", "tensor_copy", [krtm.res], [qk.res], out=qk4[:, :, 1, 64:96],
                  in_=krtm.ap[:, g * 4:(g + 1) * 4, :])
                bk = tp_bank()
                pv = banks[bk][:].bitcast(BF16).rearrange("p (a b) -> p a b", b=128)
                for i in range(8):
                    A("pe", "transpose", [qk.res, ident.res], [bank_res[bk]], out=pv[:, i, :], in_=qk.ap[:, i, :],
                      identity=ident.ap)
                copy(copy_eng(), [bank_res[bk]], [QKT.res],
                     QKT.ap[:, :, g * 512:(g + 1) * 512].rearrange("p s (t i) -> p s t i", i=128),
                     pv.rearrange("p (t s) i -> p s t i", s=2))
                ar.free(qk, rt)
            if stop_after == "p3a":
                break

            oua = ar.alloc("ou0", [128, NT, 65], F32)

            def evac_a(qb, acc, acc_res, oua=oua):
                A("dve", "tensor_copy", [acc_res], [oua.res], out=oua.ap[:, qb * 4:(qb + 1) * 4, :], in_=acc)

            attention((QKT.ap[:, 0, :], QKT.res), (QKT.ap[:, 1, :], QKT.res), 128, 0,
                      lambda kc, Vh=Vh: Vh.ap[:, kc, :], Vh.res, NT, 64, 96 ** -0.5, evac_a)
            fr = ar.alloc("fr", [128, 4, NT], F32)
            A("dve", "reciprocal", [oua.res], [fr.res], out=fr.ap[:, 0, :], in_=oua.ap[:, :, 64])
            A("dve", "tensor_tensor", [oua.res, fr.res], [ca.res], out=ca.ap[:, :, h * 64:(h + 1) * 64],
              in0=oua.ap[:, :, 0:64], in1=fr.ap[:, 0, :].unsqueeze(2).broadcast_to([128, NT, 64]), op=ALU.mult)
            ar.free(oua, fr)
            ar.free(QKT, Vh)
            if stop_after == "p3b":
                break
        if stop_after in ("p3a", "p3b"):
            break
        ar.free(cqnT, ckvnT, krtm, w_uq_t, w_ukv_t)
        if stop_after == "p3":
            break

        def post_norm_add(t, bk2, gpost):
            s2 = ar.alloc("s2", [128, 4], F32)
            for i in range(2):
                A("act", "activation", [bank_res[bk2[i]]], [junk.res, s2.res], out=junk.ap[:, 0:512],
                  in_=banks[bk2[i]][:, :], func=AF.Square, accum_out=s2.ap[:, i:i + 1])
            A("dve", "tensor_tensor", [s2.res], [s2.res], out=s2.ap[:, 2:3], in0=s2.ap[:, 0:1], in1=s2.ap[:, 1:2],
              op=ALU.add)
            A("act", "activation", [s2.res, epsb.res], [s2.res], out=s2.ap[:, 3:4], in_=s2.ap[:, 2:3], func=AF.Ln,
              scale=1.0 / D, bias=epsb.ap[:, 0:1])
            A("act", "activation", [s2.res], [s2.res], out=s2.ap[:, 3:4], in_=s2.ap[:, 3:4], func=AF.Exp, scale=-0.5)
            tmp = ar.alloc("tmp", [128, D], F32)
            for i in range(2):
                A("dve", "scalar_tensor_tensor", [bank_res[bk2[i]], s2.res, gpost.res], [tmp.res],
                  out=tmp.ap[:, i * 512:(i + 1) * 512], in0=banks[bk2[i]][:, :], scalar=s2.ap[:, 3:4],
                  in1=gpost.ap[:, i * 512:(i + 1) * 512], op0=ALU.mult, op1=ALU.mult)
            A("pool", "tensor_tensor", [tmp.res, x_res[t]], [x_res[t]], out=xs[:, t, :], in0=xs[:, t, :], in1=tmp.ap,
              op=ALU.add)
            ar.free(s2, tmp)

        gpost = ar.alloc("gpost", [128, D], F32)
        A("sp", "dma_start", [], [gpost.res], is_dma=True, out=gpost.ap, in_=bc_d[l:l + 1, 0:1024].partition_broadcast(128))
        for t in range(NT):
            cT = ar.alloc("cT", [128, 8, 128], BF16)
            srcs = [ca.ap[:, t, c * 128:(c + 1) * 128] for c in range(4)] + \
                   [cb.ap[:, t, c * 128:(c + 1) * 128] for c in range(2)] + \
                   [cc.ap[:, t, c * 128:(c + 1) * 128] for c in range(2)]
            bk = tp_bank()
            pv = banks[bk][:].bitcast(BF16).rearrange("p (a b) -> p a b", b=128)
            for i, s_ap in enumerate(srcs):
                rr = ca.res if i < 4 else (cb.res if i < 6 else cc.res)
                A("pe", "transpose", [rr, ident.res], [bank_res[bk]], out=pv[:, i, :], in_=s_ap, identity=ident.ap)
            copy(copy_eng(), [bank_res[bk]], [cT.res], cT.ap, pv[:, 0:8, :])
            bk2 = [mm_bank(), mm_bank()]
            for i in range(2):
                for c in range(8):
                    A("pe", "matmul", [cT.res, w_mo_t.res], [bank_res[bk2[i]]], banks[bk2[i]][:, :],
                      lhsT=cT.ap[:, c, :], rhs=w_mo_t.ap[:, c, i * 512:(i + 1) * 512], start=(c == 0), stop=(c == 7))
            ar.free(cT)
            post_norm_add(t, bk2, gpost)
        ar.free(ca, cb, cc, w_mo_t, gpost, bcs, lamt)
        if stop_after == "mix":
            break

        w_kv_t = load_w("w_kv", [128, 8, 2 * D], w_kv_d[l].rearrange("(c p) n -> p c n", p=128), scale_cols=24, l=l)
        memf = ar.alloc("memf", [128, 2, D], F32)
        A("sp", "dma_start", [], [memf.res], is_dma=True, out=memf.ap, in_=mem_d.rearrange("(t p) d -> p t d", p=128))
        ssm = ar.alloc("ssm", [128, 2], F32)
        rsm = ar.alloc("rsm", [128, 2], F32)
        for t in range(2):
            A("act", "activation", [memf.res], [junk.res, ssm.res], out=junk.ap, in_=memf.ap[:, t, :], func=AF.Square,
              accum_out=ssm.ap[:, t:t + 1])
        rstd_from((ssm.ap, ssm.res), 2, D, (rsm.ap, rsm.res))
        memT = ar.alloc("memT", [128, 8, NMEM], BF16)
        for t in range(2):
            mn = ar.alloc("mn", [128, D], BF16)
            A("dve", "tensor_scalar", [memf.res, rsm.res], [mn.res], out=mn.ap, in0=memf.ap[:, t, :],
              scalar1=rsm.ap[:, t:t + 1], scalar2=None, op0=ALU.mult)
            transposes([mn.ap[:, c * 128:(c + 1) * 128] for c in range(8)], mn.res, memT.ap[:, :, t * 128:(t + 1) * 128],
                       memT.res)
            ar.free(mn)
        ar.free(memf, ssm, rsm)
        KmT = ar.alloc("KmT", [128, 8, NMEM], BF16)
        for hc in range(8):
            mb = mm_bank()
            for c in range(8):
                A("pe", "matmul", [memT.res, w_kv_t.res], [bank_res[mb]], banks[mb][:, 0:NMEM],
                  lhsT=w_kv_t.ap[:, c, hc * 128:(hc + 1) * 128], rhs=memT.ap[:, c, :], start=(c == 0), stop=(c == 7))
            copy(copy_eng(), [bank_res[mb]], [KmT.res], KmT.ap[:, hc, :], banks[mb][:, 0:NMEM])
        Vm = ar.alloc("Vm", [128, 8, 257], BF16)
        A("pool", "memset", [], [Vm.res], Vm.ap[:, :, 256:257], 1.0)
        for kt in range(2):
            for i in range(2):
                mb = mm_bank()
                for c in range(8):
                    A("pe", "matmul", [memT.res, w_kv_t.res], [bank_res[mb]], banks[mb][:, :],
                      lhsT=memT.ap[:, c, kt * 128:(kt + 1) * 128], rhs=w_kv_t.ap[:, c, D + i * 512:D + (i + 1) * 512],
                      start=(c == 0), stop=(c == 7))
                copy(copy_eng(), [bank_res[mb]], [Vm.res], Vm.ap[:, kt * 4 + 2 * i:kt * 4 + 2 * i + 2, 0:256],
                     banks[mb][:, :].rearrange("p (h e) -> p h e", e=256))
        ar.free(memT, w_kv_t)
        w_q_t = load_w("w_q", [128, 8, D], w_q_d[l].rearrange("(c p) n -> p c n", p=128), scale_cols=8, l=l)
        w_o_t = load_w("w_o", [128, 8, D], w_o_d[l].rearrange("(c p) n -> p c n", p=128))
        gpost = ar.alloc("gpost", [128, D], F32)
        A("sp", "dma_start", [], [gpost.res], is_dma=True, out=gpost.ap, in_=bc_d[l:l + 1, 1024:2048].partition_broadcast(128))
        ss5 = ar.alloc("ss5", [128, NT], F32)
        rs5 = ar.alloc("rs5", [128, NT], F32)
        for t in range(NT):
            A("act", "activation", [x_res[t]], [junk.res, ss5.res], out=junk.ap, in_=xs[:, t, :], func=AF.Square,
              accum_out=ss5.ap[:, t:t + 1])
        rstd_from((ss5.ap, ss5.res), NT, D, (rs5.ap, rs5.res))
        for qb in range(4):
            hTb = ar.alloc("hTb", [128, 8, 512], BF16)
            for j in range(4):
                t = qb * 4 + j
                hn = ar.alloc("hn", [128, D], BF16)
                A("dve", "tensor_scalar", [x_res[t], rs5.res], [hn.res], out=hn.ap, in0=xs[:, t, :],
                  scalar1=rs5.ap[:, t:t + 1], scalar2=None, op0=ALU.mult)
                transposes([hn.ap[:, c * 128:(c + 1) * 128] for c in range(8)], hn.res,
                           hTb.ap[:, :, j * 128:(j + 1) * 128], hTb.res)
                ar.free(hn)
            qTb = ar.alloc("qTb", [128, 8, 512], BF16)
            for hc in range(8):
                mb = mm_bank()
                for c in range(8):
                    A("pe", "matmul", [hTb.res, w_q_t.res], [bank_res[mb]], banks[mb][:, :],
                      lhsT=w_q_t.ap[:, c, hc * 128:(hc + 1) * 128], rhs=hTb.ap[:, c, :], start=(c == 0), stop=(c == 7))
                copy(copy_eng(), [bank_res[mb]], [qTb.res], qTb.ap[:, hc, :], banks[mb][:, :])
            ar.free(hTb)
            ao = ar.alloc("ao", [128, 4, D], BF16)
            for h in range(4):
                pts = []
                for kt in range(2):
                    sb_ = tp_bank()
                    for dc in range(2):
                        A("pe", "matmul", [qTb.res, KmT.res], [bank_res[sb_]], banks[sb_][:, :],
                          lhsT=KmT.ap[:, h * 2 + dc, kt * 128:(kt + 1) * 128], rhs=qTb.ap[:, h * 2 + dc, :],
                          start=(dc == 0), stop=(dc == 1))
                    pt = ar.alloc("ptm", [128, 512], BF16)
                    A("act", "activation", [bank_res[sb_]], [pt.res], out=pt.ap, in_=banks[sb_][:, :], func=AF.Exp,
                      scale=1.0 / 16.0)
                    pts.append(pt)
                for j in range(4):
                    mb = mm_bank()
                    for kt in range(2):
                        A("pe", "matmul", [pts[kt].res, Vm.res], [bank_res[mb]], banks[mb][:, 0:257],
                          lhsT=pts[kt].ap[:, j * 128:(j + 1) * 128], rhs=Vm.ap[:, kt * 4 + h, :], start=(kt == 0),
                          stop=(kt == 1))
                    rc = ar.alloc("rc", [128, 1], F32)
                    A("dve", "reciprocal", [bank_res[mb]], [rc.res], out=rc.ap[:, 0:1], in_=banks[mb][:, 256:257])
                    A("dve", "tensor_scalar", [bank_res[mb], rc.res], [ao.res], out=ao.ap[:, j, h * 256:(h + 1) * 256],
                      in0=banks[mb][:, 0:256], scalar1=rc.ap[:, 0:1], scalar2=None, op0=ALU.mult)
                    ar.free(rc)
                ar.free(*pts)
            ar.free(qTb)
            for j in range(4):
                t = qb * 4 + j
                aT = ar.alloc("aT", [128, 8, 128], BF16)
                transposes([ao.ap[:, j, c * 128:(c + 1) * 128] for c in range(8)], ao.res, aT.ap, aT.res)
                bk2 = [mm_bank(), mm_bank()]
                for i in range(2):
                    for c in range(8):
                        A("pe", "matmul", [aT.res, w_o_t.res], [bank_res[bk2[i]]], banks[bk2[i]][:, :],
                          lhsT=aT.ap[:, c, :], rhs=w_o_t.ap[:, c, i * 512:(i + 1) * 512], start=(c == 0), stop=(c == 7))
                ar.free(aT)
                post_norm_add(t, bk2, gpost)
            ar.free(ao)
        ar.free(KmT, Vm, w_q_t, w_o_t, gpost, ss5, rs5)
        if stop_after == "mem":
            break

        w_dn_t = load_w("w_dn", [128, NCH, D], w_dn_d[l].rearrange("(c p) n -> p c n", p=128))
        gpost = ar.alloc("gpost", [128, D], F32)
        A("sp", "dma_start", [], [gpost.res], is_dma=True, out=gpost.ap, in_=bc_d[l:l + 1, 2048:3072].partition_broadcast(128))
        ss6 = ar.alloc("ss6", [128, NT], F32)
        rs6 = ar.alloc("rs6", [128, NT], F32)
        for t in range(NT):
            A("act", "activation", [x_res[t]], [junk.res, ss6.res], out=junk.ap, in_=xs[:, t, :], func=AF.Square,
              accum_out=ss6.ap[:, t:t + 1])
        rstd_from((ss6.ap, ss6.res), NT, D, (rs6.ap, rs6.res))
        hl = ar.alloc("hl", [128, 8, 2], BF16)
        stgs = [ar.alloc("stgf", [128, 8, 256], F32) for _ in range(2)]
        wus = [ar.alloc("wuf", [128, 8, 256], BF16) for _ in range(3)]
        for qb in range(4):
            hTe = ar.alloc("hTe", [128, 8, 514], BF16)
            if qb == 0:
                A("dve", "memset", [], [hTe.res], hTe.ap[:, :, 0:1], 0.0)
            if qb == 3:
                A("dve", "memset", [], [hTe.res], hTe.ap[:, :, 513:514], 0.0)
            tl = list(range(qb * 4, qb * 4 + 4))
            if qb > 0:
                A("dve", "tensor_copy", [hl.res], [hTe.res], out=hTe.ap[:, :, 0:1], in_=hl.ap[:, :, (qb - 1) % 2:(qb - 1) % 2 + 1])
            if qb < 3:
                tl = tl + [qb * 4 + 4]
            for t in tl:
                hn = ar.alloc("hn", [128, D], BF16)
                A("dve", "tensor_scalar", [x_res[t], rs6.res], [hn.res], out=hn.ap, in0=xs[:, t, :],
                  scalar1=rs6.ap[:, t:t + 1], scalar2=None, op0=ALU.mult)
                bk = tp_bank()
                pv = banks[bk][:].bitcast(BF16).rearrange("p (a b) -> p a b", b=128)
                for c in range(8):
                    A("pe", "transpose", [hn.res, ident.res], [bank_res[bk]], out=pv[:, c, :],
                      in_=hn.ap[:, c * 128:(c + 1) * 128], identity=ident.ap)
                j = t - qb * 4
                if j < 0:
                    copy(copy_eng(), [bank_res[bk]], [hTe.res], hTe.ap[:, :, 0:1], pv[:, 0:8, 127:128])
                elif j > 3:
                    copy(copy_eng(), [bank_res[bk]], [hTe.res], hTe.ap[:, :, 513:514], pv[:, 0:8, 0:1])
                else:
                    copy(copy_eng(), [bank_res[bk]], [hTe.res], hTe.ap[:, :, 1 + j * 128:1 + (j + 1) * 128], pv[:, 0:8, :])
                ar.free(hn)
            A("dve", "tensor_copy", [hTe.res], [hl.res], out=hl.ap[:, :, qb % 2:qb % 2 + 1], in_=hTe.ap[:, :, 512:513])
            mT = ar.alloc("mT", [128, NCH, 512], BF16)
            mm_all[0] = True
            for jc in range(NCH):
                wu = wus[(qb * NCH + jc) % 3]
                stg = stgs[(qb * NCH + jc) % 2]
                A("sp", "dma_start", [], [stg.res], is_dma=True, out=stg.ap,
                  in_=w_up_d[l, jc].rearrange("p (c n) -> p c n", n=256))
                A("pool", "tensor_tensor", [stg.res, pk.res], [wu.res], out=wu.ap, in0=stg.ap,
                  in1=pk.ap[:, l, 16:24].unsqueeze(2).broadcast_to([128, 8, 256]), op=ALU.mult)
                ys = []
                hb = mm_bank()
                for gu in range(2):
                    mb = mm_bank()
                    for c in range(8):
                        A("pe", "matmul", [hTe.res, wu.res], [bank_res[mb]], banks[mb][:, :],
                          lhsT=wu.ap[:, c, gu * 128:(gu + 1) * 128], rhs=hTe.ap[:, c, 1:513], start=(c == 0), stop=(c == 7))
                    for c in range(8):
                        A("pe", "matmul", [hTe.res, wu.res], [bank_res[hb]], banks[hb][:, 2 * gu:2 * gu + 2],
                          lhsT=wu.ap[:, c, gu * 128:(gu + 1) * 128], rhs=hTe.ap[:, c, 0:514:513], start=(c == 0),
                          stop=(c == 7))
                    ch = gu * NCH + jc
                    cw = pk.ap[:, l, 41 + ch * 4:41 + ch * 4 + 4]
                    y = ar.alloc("y", [128, 512], F32)
                    A("act", "activation", [bank_res[mb], pk.res], [y.res], out=y.ap, in_=banks[mb][:, :],
                      func=AF.Identity, scale=cw[:, 1:2], bias=cw[:, 3:4])
                    A("dve", "scalar_tensor_tensor", [bank_res[mb], pk.res, y.res], [y.res], out=y.ap[:, 1:512],
                      in0=banks[mb][:, 0:511], scalar=cw[:, 0:1], in1=y.ap[:, 1:512], op0=ALU.mult, op1=ALU.add)
                    A("dve", "scalar_tensor_tensor", [bank_res[hb], pk.res, y.res], [y.res], out=y.ap[:, 0:1],
                      in0=banks[hb][:, 2 * gu:2 * gu + 1], scalar=cw[:, 0:1], in1=y.ap[:, 0:1], op0=ALU.mult, op1=ALU.add)
                    A("dve", "scalar_tensor_tensor", [bank_res[mb], pk.res, y.res], [y.res], out=y.ap[:, 0:511],
                      in0=banks[mb][:, 1:512], scalar=cw[:, 2:3], in1=y.ap[:, 0:511], op0=ALU.mult, op1=ALU.add)
                    A("dve", "scalar_tensor_tensor", [bank_res[hb], pk.res, y.res], [y.res], out=y.ap[:, 511:512],
                      in0=banks[hb][:, 2 * gu + 1:2 * gu + 2], scalar=cw[:, 2:3], in1=y.ap[:, 511:512], op0=ALU.mult, op1=ALU.add)
                    ys.append(y)
                A("act", "activation", [ys[0].res], [ys[0].res], out=ys[0].ap, in_=ys[0].ap, func=AF.Gelu_apprx_tanh)
                A("dve", "tensor_tensor", [ys[0].res, ys[1].res], [mT.res], out=mT.ap[:, jc, :], in0=ys[0].ap,
                  in1=ys[1].ap, op=ALU.mult)
                ar.free(*ys)
            ar.free(hTe)
            for j in range(4):
                t = qb * 4 + j
                bk2 = [mm_bank(), mm_bank()]
                for i in range(2):
                    for jc in range(NCH):
                        A("pe", "matmul", [mT.res, w_dn_t.res], [bank_res[bk2[i]]], banks[bk2[i]][:, :],
                          lhsT=mT.ap[:, jc, j * 128:(j + 1) * 128], rhs=w_dn_t.ap[:, jc, i * 512:(i + 1) * 512],
                          start=(jc == 0), stop=(jc == NCH - 1))
                post_norm_add(t, bk2, gpost)
            ar.free(mT)
            mm_all[0] = False
        ar.free(w_dn_t, gpost, ss6, rs6, hl, *stgs, *wus)

    ov = out_d.rearrange("(t p) d -> p t d", p=128)
    last = []
    for t in range(NT):
        last.append(A("sp", "dma_start", [x_res[t]], [out_res[t]], is_dma=True, out=ov[:, t, :], in_=xs[:, t, :]))
    fin = A("sp", "nop", [out_res], [])

    counts = sc.finalize()
    nsem = {e: max(1, (counts[e] + SEM_LIMIT - 1) // SEM_LIMIT) for e in ("pe", "act", "dve", "pool", "sp")}
    sems = {e: [es.enter_context(nc.semaphore("s_%s%d" % (e, i))) for i in range(nsem[e])] for e in nsem}
    dsems = {q: [es.enter_context(nc.semaphore("d_%s%d" % (q, i))) for i in range(NDS)] for q in ("sp", "pool")}

    def emit(ename, eng):
        waited = {}
        for op in sc.ops[ename]:
            for d in op.deps:
                if d.is_dma:
                    sm, val = dsems[d.eng][d.dsem], d.dval
                    key = ("d", d.eng, d.dsem)
                else:
                    sm, val = sems[d.eng][d.semk // SEM_LIMIT], d.semk % SEM_LIMIT + 1
                    key = ("c", d.eng, d.semk // SEM_LIMIT)
                if waited.get(key, 0) >= val:
                    continue
                eng.wait_ge(sm, val)
                waited[key] = val
            if op.meth == "nop":
                continue
            ins = getattr(eng, op.meth)(*op.args, **op.kw)
            if op.is_dma:
                ins.then_inc(dsems[op.eng][op.dsem], 16)
            elif op.sig:
                ins.then_inc(sems[op.eng][op.semk // SEM_LIMIT], 1)

    block = es.enter_context(nc.Block())

    @block.sync
    def _(e):
        emit("sp", e)

    @block.gpsimd
    def _(e):
        emit("pool", e)

    @block.tensor
    def _(e):
        emit("pe", e)

    @block.scalar
    def _(e):
        emit("act", e)

    @block.vector
    def _(e):
        emit("dve", e)

    es.close()
    stats = {e: len(sc.ops[e]) for e in ENGS}
    stats["arena_peak_kb"] = ar.peak
    return nc, stats


def prep_inputs(inp, n_layers=DEPTH):
    f = lambda a: np.ascontiguousarray(np.asarray(a, dtype=np.float32))
    pk = np.zeros((128, DEPTH, 217), np.float32)
    for l in range(DEPTH):
        pk[:, l, 0:8] = np.asarray(inp["mix_pre_g"])[l].reshape(8, 128).T
        pk[:, l, 8:16] = np.asarray(inp["mem_pre_g"])[l].reshape(8, 128).T
        pk[:, l, 16:24] = np.asarray(inp["ffn_pre_g"])[l].reshape(8, 128).T
        pk[:, l, 24:32] = np.asarray(inp["mem_kv_g"])[l].reshape(8, 128).T
        pk[:, l, 32:35] = np.asarray(inp["mla_cq_g"])[l].reshape(3, 128).T
        pk[:, l, 35:37] = np.asarray(inp["mla_ckv_g"])[l].reshape(2, 128).T
        pk[:, l, 37:41] = np.asarray(inp["sgu_b_s"])[l].T
        cw = np.asarray(inp["ffn_conv_w"])[l]
        cbv = np.asarray(inp["ffn_conv_b"])[l]
        c4 = np.concatenate([cw, cbv[None, :]], axis=0)
        pk[:, l, 41:217] = c4.reshape(4, 44, 128).transpose(2, 1, 0).reshape(128, 176)
    bc = np.concatenate([
        np.asarray(inp["mix_post_g"]), np.asarray(inp["mem_post_g"]), np.asarray(inp["ffn_post_g"]),
        np.asarray(inp["sgu_norm_g"]).reshape(DEPTH, 256), np.asarray(inp["diff_sub_g"]),
        np.asarray(inp["diff_lam_q1"]), np.asarray(inp["diff_lam_k1"]),
        np.asarray(inp["diff_lam_q2"]), np.asarray(inp["diff_lam_k2"])], axis=1).astype(np.float32)
    wsT = np.asarray(inp["sgu_w_s"]).transpose(0, 3, 1, 2).reshape(DEPTH, 128, 512)
    nl = n_layers
    f = lambda a: np.ascontiguousarray(np.asarray(a, dtype=np.float32)[:nl])
    wup = np.asarray(inp["ffn_w_up"], dtype=np.float32)[:nl].reshape(nl, 8, 128, 2, NCH, 128)
    wup = np.ascontiguousarray(wup.transpose(0, 4, 2, 1, 3, 5)).reshape(nl, NCH, 128, 2048)
    shared = {
        "w_in": f(inp["w_in"]), "w_uq": f(inp["mla_w_uq"]), "w_ukv": f(inp["mla_w_ukv"]), "wsT": f(wsT),
        "w_mo": f(inp["w_mix_out"]), "w_q": f(inp["mem_w_q"]), "w_kv": f(inp["mem_w_kv"]), "w_o": f(inp["mem_w_o"]),
        "w_up": wup, "w_dn": f(inp["ffn_w_down"]),
        "pk": np.ascontiguousarray(pk), "bc": np.ascontiguousarray(bc),
    }
    x = np.asarray(inp["x"], dtype=np.float32)
    mem = np.asarray(inp["mem"], dtype=np.float32)
    pos = np.asarray(inp["positions"]).astype(np.int32)
    maps = []
    for b in range(x.shape[0]):
        m = dict(shared)
        m["x"] = np.ascontiguousarray(x[b])
        m["mem"] = np.ascontiguousarray(mem[b])
        m["pos"] = np.ascontiguousarray(pos[b].reshape(NT, 128))
        maps.append(m)
    return maps


def kernel(**inputs):
    maps = prep_inputs(inputs)
    nc, _ = build()
    res = run_bass_kernel_spmd(nc, maps, core_ids=list(range(8)))
    return np.stack([np.asarray(r["out"], dtype=np.float32) for r in res.results], axis=0)
```

```python
import math
from contextlib import ExitStack

import numpy as np
import concourse.bass as bass
import concourse.mybir as mybir
from concourse.bass_utils import run_bass_kernel_spmd

F32 = mybir.dt.float32
BF16 = mybir.dt.bfloat16
I32 = mybir.dt.int32
U8 = mybir.dt.uint8
AF = mybir.ActivationFunctionType
ALU = mybir.AluOpType
AX = mybir.AxisListType

D = 1024
S = 2048
NT = 16
DEPTH = 4
NMEM = 256
EPS = 1e-6
THETA = 500000.0
INW = 1952
DFF = 2816
NCH = 22

ENGS = ("pe", "act", "dve", "pool", "sp")
SEM_LIMIT = 1000
NDS = 12


class Res:
    __slots__ = ("lw", "rd", "rd_dma")

    def __init__(self):
        self.lw = None
        self.rd = {}
        self.rd_dma = []


class Op:
    __slots__ = ("eng", "meth", "args", "kw", "deps", "sig", "semk", "is_dma", "dsem", "dval", "idx")


class Sched:
    def __init__(self):
        self.ops = {e: [] for e in ENGS}
        self.ndma = {"sp": 0, "pool": 0}
        self.dma_hist = {"sp": [], "pool": []}
        self.n = 0

    def add(self, eng, meth, r, w, *args, is_dma=False, **kw):
        op = Op()
        op.eng, op.meth, op.args, op.kw = eng, meth, args, kw
        op.is_dma = is_dma
        op.sig = False
        op.semk = None
        op.idx = self.n
        op.dsem = None
        op.dval = None
        self.n += 1
        deps = {}

        def need(d):
            if d is None or d is op:
                return
            if (not d.is_dma) and d.eng == "pe" and eng == "pe" and not is_dma:
                return
            deps[d.idx] = d

        wset = set(id(x) for x in w)
        for x in r:
            need(x.lw)
        for x in w:
            need(x.lw)
            for d in x.rd.values():
                need(d)
            for d in x.rd_dma:
                need(d)
        if is_dma:
            q = self.dma_hist[eng]
            op.dsem = len(q) % NDS
            op.dval = 16 * (len(q) // NDS + 1)
            if len(q) >= NDS:
                need(q[len(q) - NDS])
            q.append(op)
        best = {}
        out = []
        for d in deps.values():
            if d.is_dma:
                out.append(d)
            else:
                b = best.get(d.eng)
                if b is None or d.idx > b.idx:
                    best[d.eng] = d
        out.extend(best.values())
        for d in out:
            d.sig = True
        op.deps = out
        for x in w:
            x.lw = op
            x.rd = {}
            x.rd_dma = []
        for x in r:
            if id(x) in wset:
                continue
            if is_dma:
                x.rd_dma.append(op)
            else:
                x.rd[eng] = op
        self.ops[eng].append(op)
        return op

    def finalize(self):
        for e in ENGS:
            k = 0
            for op in self.ops[e]:
                if op.is_dma:
                    continue
                if op.sig:
                    op.semk = k
                    k += 1
        return {e: sum(1 for o in self.ops[e] if (not o.is_dma) and o.sig) for e in ENGS}


class Buf:
    __slots__ = ("ap", "res", "off", "size", "name", "owner")


class Arena:
    CH = 1024

    def __init__(self, t, nbytes, base=0, nextfit=False):
        self.t = t
        self.base = base
        self.nextfit = nextfit
        self.ptr = 0
        self.nbytes = nbytes
        self.nch = nbytes // self.CH
        self.res = [Res() for _ in range(self.nch)]
        self.used = [False] * self.nch
        self.peak = 0

    def alloc(self, name, shape, dt):
        esz = 4 if dt in (F32, I32) else 2
        n = 1
        for s in shape[1:]:
            n *= s
        nb = n * esz
        k = (nb + self.CH - 1) // self.CH
        start = None
        order = [0]
        if self.nextfit:
            order = [self.ptr, 0]
        for s0 in order:
            run = 0
            for i in range(s0, self.nch):
                if not self.used[i]:
                    run += 1
                    if run == k:
                        start = i - k + 1
                        break
                else:
                    run = 0
            if start is not None:
                break
        if start is not None:
            self.ptr = start + k
        if start is None:
            raise RuntimeError("arena OOM for %s (%d B); used=%d" % (name, nb, sum(self.used)))
        for i in range(start, start + k):
            self.used[i] = True
        self.peak = max(self.peak, max(i for i in range(self.nch) if self.used[i]) + 1)
        b = Buf()
        b.name = name
        b.owner = self
        b.off = start
        b.size = k
        o = self.base + start * self.CH
        ap = self.t[:, o:o + nb].bitcast(dt)
        if len(shape) == 3:
            b.ap = ap.rearrange("p (a b) -> p a b", b=shape[2])
        else:
            b.ap = ap
        b.res = self.res[start:start + k]
        return b

    def free(self, *bufs):
        for b in bufs:
            for i in range(b.off, b.off + b.size):
                b.owner.used[i] = False


TRANS = {"dkm", "wuf", "stg", "oT", "hn", "hT", "ssq", "rsq", "cn", "rt", "zg", "st", "vsq", "vn", "qk", "r6", "pt", "rc", "o1", "s2", "tmp",
         "cT", "aT", "mn", "y", "wu", "ptm", "ltmp"}


class Arenas:
    def __init__(self, main, trans):
        self.main = main
        self.trans = trans

    def alloc(self, name, shape, dt):
        if name in TRANS:
            return self.trans.alloc(name, shape, dt)
        return self.main.alloc(name, shape, dt)

    def free(self, *bufs):
        self.main.free(*bufs)

    @property
    def peak(self):
        return (self.main.peak, self.trans.peak)


def build(n_layers=DEPTH, stop_after=None):
    nc = bass.Bass("TRN2", target_bir_lowering=False)

    def din(name, shape, dt=F32):
        return nc.dram_tensor(name, list(shape), dt, kind="ExternalInput").ap()

    WL = n_layers
    x_d = din("x", [S, D])
    mem_d = din("mem", [NMEM, D])
    pos_d = din("pos", [NT, 128], I32)
    w_in_d = din("w_in", [WL, D, INW])
    w_uq_d = din("w_uq", [WL, 384, 768])
    w_ukv_d = din("w_ukv", [WL, 256, 1024])
    wsT_d = din("wsT", [WL, 128, 512])
    w_mo_d = din("w_mo", [WL, D, D])
    w_q_d = din("w_q", [WL, D, D])
    w_kv_d = din("w_kv", [WL, D, 2 * D])
    w_o_d = din("w_o", [WL, D, D])
    w_up_d = din("w_up", [WL, NCH, 128, 2048])
    w_dn_d = din("w_dn", [WL, DFF, D])
    NPK = 217
    pk_d = din("pk", [128, DEPTH, NPK])
    NBC = 3520
    bc_d = din("bc", [DEPTH, NBC])
    out_d = nc.dram_tensor("out", [S, D], F32, kind="ExternalOutput").ap()

    sc = Sched()
    es = ExitStack()
    xs = es.enter_context(nc.sbuf_tensor("xs", [128, NT, D], F32))
    x_res = [Res() for _ in range(NT)]
    MAIN_B = 114 * 1024
    TRANS_B = 29 * 1024
    ar_t = es.enter_context(nc.sbuf_tensor("arena", [128, MAIN_B + TRANS_B], U8))
    ar = Arenas(Arena(ar_t, MAIN_B), Arena(ar_t, TRANS_B, base=MAIN_B, nextfit=True))
    banks = [es.enter_context(nc.psum_tensor("bank%d" % i, [128, 512], F32)) for i in range(8)]
    bank_res = [Res() for _ in range(8)]
    bank_ids = set(id(b) for b in bank_res)
    out_res = [Res() for _ in range(NT)]

    def A(eng, meth, r, w, *a, **k):
        rr = []
        for x in r:
            rr.extend(x if isinstance(x, (list, tuple)) else [x])
        ww = []
        for x in w:
            ww.extend(x if isinstance(x, (list, tuple)) else [x])
        for x in rr:
            if id(x) in bank_ids:
                ww.append(x)
        return sc.add(eng, meth, rr, ww, *a, **k)

    tp_i = [0]

    def tp_bank():
        tp_i[0] ^= 1
        return tp_i[0]

    mm_i = [0]

    mm_all = [False]

    def mm_bank():
        if mm_all[0]:
            mm_i[0] = (mm_i[0] + 1) % 8
            return mm_i[0]
        mm_i[0] = (mm_i[0] + 1) % 6
        return 2 + mm_i[0]

    cp_i = [0]

    def copy_eng():
        cp_i[0] ^= 1
        return "act" if cp_i[0] else "dve"

    def copy(eng, r, w, out, in_):
        if eng == "act":
            A("act", "activation", r, w, out=out, in_=in_, func=AF.Copy)
        elif eng == "dve":
            A("dve", "tensor_copy", r, w, out=out, in_=in_)
        else:
            A("pool", "tensor_copy", r, w, out=out, in_=in_)

    ident = ar.alloc("ident", [128, 128], BF16)
    identf = ar.alloc("identf", [128, 128], F32)
    A("pool", "memset", [], [identf.res], identf.ap, 0.0)
    A("pool", "iota", [], [identf.res], identf.ap, pattern=[[1, 128]], base=0, channel_multiplier=-1,
      allow_small_or_imprecise_dtypes=True)
    A("dve", "tensor_scalar", [identf.res], [ident.res], out=ident.ap, in0=identf.ap, scalar1=0.0, scalar2=None,
      op0=ALU.is_equal)
    identF = ar.alloc("identF", [128, 128], F32)
    A("dve", "tensor_scalar", [identf.res], [identF.res], out=identF.ap, in0=identf.ap, scalar1=0.0, scalar2=None,
      op0=ALU.is_equal)
    epsb = ar.alloc("eps", [128, 1], F32)
    A("dve", "memset", [], [epsb.res], epsb.ap, EPS)
    zerob = ar.alloc("zero", [128, 1], F32)
    A("dve", "memset", [], [zerob.res], zerob.ap, 0.0)
    junk = ar.alloc("junk", [128, 1024], BF16)

    xv = x_d.rearrange("(t p) d -> p t d", p=128)
    for t in range(NT):
        A("sp", "dma_start", [], [x_res[t]], is_dma=True, out=xs[:, t, :], in_=xv[:, t, :])

    pk = ar.alloc("pk", [128, DEPTH, NPK], F32)
    A("sp", "dma_start", [], [pk.res], is_dma=True, out=pk.ap, in_=pk_d)

    posi = ar.alloc("posi", [128, NT], I32)
    for t in range(NT):
        A("sp", "dma_start", [], [posi.res], is_dma=True, out=posi.ap[:, t:t + 1],
          in_=pos_d[t:t + 1, :].rearrange("a p -> p a"))
    posf = ar.alloc("posf", [128, NT], F32)
    A("dve", "tensor_copy", [posi.res], [posf.res], out=posf.ap, in_=posi.ap)
    ang = ar.alloc("ang", [128, NT, 16], F32)
    for f in range(16):
        A("dve", "tensor_scalar", [posf.res], [ang.res], out=ang.ap[:, :, f], in0=posf.ap,
          scalar1=float(THETA ** (-2.0 * f / 32.0)), scalar2=None, op0=ALU.mult)
    cosA = ar.alloc("cosA", [128, NT, 16], F32)
    sinA = ar.alloc("sinA", [128, NT, 16], F32)
    TWO_PI = 2.0 * math.pi
    C1 = 6.28125
    C2 = TWO_PI - C1
    tA = ar.alloc("tA", [128, NT, 16], F32)
    tK = ar.alloc("tK", [128, NT, 16], I32)
    tKf = ar.alloc("tKf", [128, NT, 16], F32)
    tM = ar.alloc("tM", [128, NT, 16], F32)
    for (dst, shift) in ((sinA, 0.0), (cosA, math.pi / 2)):
        A("dve", "tensor_scalar", [ang.res], [tA.res], out=tA.ap, in0=ang.ap, scalar1=shift, scalar2=None,
          op0=ALU.add)
        A("dve", "tensor_scalar", [tA.res], [tM.res], out=tM.ap, in0=tA.ap, scalar1=1.0 / TWO_PI, scalar2=None,
          op0=ALU.mult)
        A("dve", "tensor_copy", [tM.res], [tK.res], out=tK.ap, in_=tM.ap)
        A("dve", "tensor_copy", [tK.res], [tKf.res], out=tKf.ap, in_=tK.ap)
        A("dve", "scalar_tensor_tensor", [tKf.res, tA.res], [tM.res], out=tM.ap, in0=tKf.ap, scalar=-C1,
          in1=tA.ap, op0=ALU.mult, op1=ALU.add)
        A("dve", "scalar_tensor_tensor", [tKf.res, tM.res], [tA.res], out=tA.ap, in0=tKf.ap, scalar=-C2,
          in1=tM.ap, op0=ALU.mult, op1=ALU.add)
        A("dve", "tensor_scalar", [tA.res], [tM.res], out=tM.ap, in0=tA.ap, scalar1=math.pi, scalar2=-TWO_PI,
          op0=ALU.is_gt, op1=ALU.mult)
        A("dve", "tensor_tensor", [tA.res, tM.res], [tKf.res], out=tKf.ap, in0=tA.ap, in1=tM.ap, op=ALU.add)
        A("dve", "tensor_scalar", [tKf.res], [tM.res], out=tM.ap, in0=tKf.ap, scalar1=-math.pi, scalar2=TWO_PI,
          op0=ALU.is_lt, op1=ALU.mult)
        A("dve", "tensor_tensor", [tKf.res, tM.res], [tA.res], out=tA.ap, in0=tKf.ap, in1=tM.ap, op=ALU.add)
        A("dve", "tensor_scalar", [tA.res], [tM.res], out=tM.ap, in0=tA.ap, scalar1=math.pi, scalar2=-math.pi,
          op0=ALU.min, op1=ALU.max)
        A("act", "activation", [tM.res], [dst.res], out=dst.ap, in_=tM.ap, func=AF.Sin)
    ar.free(tA, tK, tKf, tM, ang, posi, posf, identf)

    def rstd_from(ss, ncol, dim, dst):
        A("act", "activation", [ss[1], epsb.res], [dst[1]], out=dst[0], in_=ss[0], func=AF.Ln, scale=1.0 / dim,
          bias=epsb.ap[:, 0:1])
        A("act", "activation", [dst[1]], [dst[1]], out=dst[0], in_=dst[0], func=AF.Exp, scale=-0.5)

    def load_w(name, shape, src, scale_cols=None, l=0, chunk_cols=None):
        b = ar.alloc(name, shape, BF16)
        kc, n = shape[1], shape[2]
        step = 2048
        for c in range(kc):
            for n0 in range(0, n, step):
                n1 = min(n, n0 + step)
                stg = ar.alloc("stg", [128, n1 - n0], F32)
                A("sp", "dma_start", [], [stg.res], is_dma=True, out=stg.ap, in_=src[:, c, n0:n1])
                if scale_cols is not None:
                    A("pool", "tensor_scalar", [stg.res, pk.res], [b.res], out=b.ap[:, c, n0:n1], in0=stg.ap,
                      scalar1=pk.ap[:, l, scale_cols + c:scale_cols + c + 1], scalar2=1.0, op0=ALU.mult, op1=ALU.mult)
                else:
                    A("pool", "tensor_copy", [stg.res], [b.res], out=b.ap[:, c, n0:n1], in_=stg.ap)
                ar.free(stg)
        return b

    def transposes(srcs, src_res, dst_ap, dst_res, rows=128):
        bk = tp_bank()
        pv = banks[bk][:].bitcast(BF16).rearrange("p (a b) -> p a b", b=128)
        for i, s_ap in enumerate(srcs):
            A("pe", "transpose", [src_res, ident.res], [bank_res[bk]], out=pv[0:rows, i, :], in_=s_ap,
              identity=ident.ap)
        copy(copy_eng(), [bank_res[bk]], [dst_res], dst_ap, pv[0:rows, 0:len(srcs), :])

    def attention(QT, KT, krows, roff, V_of_kc, v_res, nkc, dv, scale, evac):
        pending = [None]

        def epilogue(qb, ab):
            oT = ar.alloc("oT", [128, 512], F32)
            A("dve", "tensor_copy", [bank_res[ab]], [oT.res], out=oT.ap[0:dv + 1, :], in_=banks[ab][0:dv + 1, :])

            def pe_part():
                tb = mm_bank()
                for j in range(4):
                    A("pe", "transpose", [oT.res, identF.res], [bank_res[tb]],
                      out=banks[tb][:, j * (dv + 1):(j + 1) * (dv + 1)], in_=oT.ap[0:dv + 1, j * 128:(j + 1) * 128],
                      identity=identF.ap[0:dv + 1, 0:dv + 1])
                ar.free(oT)
                evac(qb, banks[tb][:, 0:4 * (dv + 1)].rearrange("p (j e) -> p j e", e=dv + 1), bank_res[tb])
            return pe_part

        for qb in range(4):
            ab = mm_bank()
            pts = {}

            def issue_s(kc, qb=qb, pts=pts):
                sb_ = tp_bank()
                A("pe", "matmul", [QT[1], KT[1]], [bank_res[sb_]], banks[sb_][:, :],
                  lhsT=KT[0][roff:roff + krows, kc * 128:(kc + 1) * 128],
                  rhs=QT[0][roff:roff + krows, qb * 512:(qb + 1) * 512], start=True, stop=True,
                  **({"tile_position": (roff, 0)} if krows == 32 else {}))
                pt = ar.alloc("pt", [128, 512], BF16)
                A("act", "activation", [bank_res[sb_]], [pt.res], out=pt.ap, in_=banks[sb_][:, :], func=AF.Exp,
                  scale=scale)
                pts[kc] = pt

            def issue_pv(kc, ab=ab, pts=pts):
                pt = pts.pop(kc)
                A("pe", "matmul", [pt.res, v_res], [bank_res[ab]], banks[ab][0:dv + 1, :],
                  lhsT=V_of_kc(kc), rhs=pt.ap, start=(kc == 0), stop=(kc == nkc - 1))
                ar.free(pt)

            issue_s(0)
            for kc in range(nkc):
                if kc + 1 < nkc:
                    issue_s(kc + 1)
                issue_pv(kc)
                if kc == 2 and pending[0] is not None:
                    pending[0]()
                    pending[0] = None
            if pending[0] is not None:
                pending[0]()
            pending[0] = epilogue(qb, ab)
        pending[0]()

    for l in range(n_layers):
        if stop_after == "setup":
            break
        lam_init = 0.8 - 0.6 * math.exp(-0.3 * l)
        bcs = ar.alloc("bcs", [128, 448], F32)
        A("sp", "dma_start", [], [bcs.res], is_dma=True, out=bcs.ap, in_=bc_d[l:l + 1, 3072:3520].partition_broadcast(128))
        lamt = ar.alloc("lamt", [128, 8], F32)
        ltmp = ar.alloc("ltmp", [128, 64], F32)
        A("dve", "tensor_tensor", [bcs.res], [ltmp.res], out=ltmp.ap[:, 0:32], in0=bcs.ap[:, 320:352],
          in1=bcs.ap[:, 352:384], op=ALU.mult)
        A("dve", "tensor_tensor", [bcs.res], [ltmp.res], out=ltmp.ap[:, 32:64], in0=bcs.ap[:, 384:416],
          in1=bcs.ap[:, 416:448], op=ALU.mult)
        A("dve", "tensor_reduce", [ltmp.res], [lamt.res], out=lamt.ap[:, 0:2],
          in_=ltmp.ap.rearrange("p (a b) -> p a b", b=32), axis=AX.X, op=ALU.add)
        A("act", "activation", [lamt.res], [lamt.res], out=lamt.ap[:, 2:4], in_=lamt.ap[:, 0:2], func=AF.Exp)
        A("dve", "tensor_tensor", [lamt.res], [lamt.res], out=lamt.ap[:, 4:5], in0=lamt.ap[:, 3:4], in1=lamt.ap[:, 2:3],
          op=ALU.subtract)
        A("dve", "tensor_scalar", [lamt.res], [lamt.res], out=lamt.ap[:, 5:6], in0=lamt.ap[:, 4:5], scalar1=-lam_init,
          scalar2=None, op0=ALU.add)
        neglam = lamt.ap[:, 5:6]
        A("dve", "tensor_scalar", [bcs.res], [bcs.res], out=bcs.ap[:, 256:320], in0=bcs.ap[:, 256:320],
          scalar1=1.0 - lam_init, scalar2=None, op0=ALU.mult)
        ar.free(ltmp)

        w_in_t = load_w("w_in", [128, 8, INW], w_in_d[l].rearrange("(c p) n -> p c n", p=128), scale_cols=0, l=l)
        wsT = load_w("wsT", [128, 1, 512], wsT_d[l].rearrange("p (a n) -> p a n", a=1))

        ss1 = ar.alloc("ss1", [128, NT], F32)
        rs1 = ar.alloc("rs1", [128, NT], F32)
        for t in range(NT):
            A("act", "activation", [x_res[t]], [junk.res, ss1.res], out=junk.ap, in_=xs[:, t, :], func=AF.Square,
              accum_out=ss1.ap[:, t:t + 1])
        rstd_from((ss1.ap, ss1.res), NT, D, (rs1.ap, rs1.res))
        cqnT = ar.alloc("cqnT", [128, 3, S], BF16)
        ckvnT = ar.alloc("ckvnT", [128, 2, S], BF16)
        krtm = ar.alloc("krtm", [128, NT, 32], BF16)
        dqT = ar.alloc("dqT", [128, 2, S], BF16)
        dkT = ar.alloc("dkT", [128, 2, S], BF16)
        dV = ar.alloc("dV", [128, NT * 4, 65], BF16)
        A("pool", "memset", [], [dV.res], dV.ap[:, :, 64:65], 1.0)
        cb = ar.alloc("cb", [128, NT, 256], BF16)
        COLS = ((0, 384), (384, 672), (672, 1184), (1184, 1696), (1696, 1952))
        for t in range(NT):
            tc_ = slice(t * 128, (t + 1) * 128)
            hn = ar.alloc("hn", [128, D], BF16)
            A("dve", "tensor_scalar", [x_res[t], rs1.res], [hn.res], out=hn.ap, in0=xs[:, t, :],
              scalar1=rs1.ap[:, t:t + 1], scalar2=None, op0=ALU.mult)
            hT = ar.alloc("hT", [128, 8, 128], BF16)
            transposes([hn.ap[:, c * 128:(c + 1) * 128] for c in range(8)], hn.res, hT.ap, hT.res)
            ar.free(hn)
            pj = [mm_bank() for _ in range(5)]
            for gi, (c0, c1) in enumerate(COLS):
                for c in range(8):
                    A("pe", "matmul", [hT.res, w_in_t.res], [bank_res[pj[gi]]], banks[pj[gi]][:, 0:c1 - c0],
                      lhsT=hT.ap[:, c, :], rhs=w_in_t.ap[:, c, c0:c1], start=(c == 0), stop=(c == 7))
            ar.free(hT)
            ssq = ar.alloc("ssq", [128, 2], F32)
            rsq = ar.alloc("rsq", [128, 2], F32)
            A("act", "activation", [bank_res[pj[0]]], [junk.res, ssq.res], out=junk.ap[:, 0:384],
              in_=banks[pj[0]][:, 0:384], func=AF.Square, accum_out=ssq.ap[:, 0:1])
            A("act", "activation", [bank_res[pj[1]]], [junk.res, ssq.res], out=junk.ap[:, 0:256],
              in_=banks[pj[1]][:, 0:256], func=AF.Square, accum_out=ssq.ap[:, 1:2])
            A("act", "activation", [ssq.res, epsb.res], [rsq.res], out=rsq.ap[:, 0:1], in_=ssq.ap[:, 0:1], func=AF.Ln,
              scale=1.0 / 384, bias=epsb.ap[:, 0:1])
            A("act", "activation", [ssq.res, epsb.res], [rsq.res], out=rsq.ap[:, 1:2], in_=ssq.ap[:, 1:2], func=AF.Ln,
              scale=1.0 / 256, bias=epsb.ap[:, 0:1])
            A("act", "activation", [rsq.res], [rsq.res], out=rsq.ap, in_=rsq.ap, func=AF.Exp, scale=-0.5)
            cn = ar.alloc("cn", [128, 640], BF16)
            A("dve", "tensor_scalar", [bank_res[pj[0]], rsq.res], [cn.res], out=cn.ap[:, 0:384],
              in0=banks[pj[0]][:, 0:384], scalar1=rsq.ap[:, 0:1], scalar2=None, op0=ALU.mult)
            A("dve", "tensor_scalar", [bank_res[pj[1]], rsq.res], [cn.res], out=cn.ap[:, 384:640],
              in0=banks[pj[1]][:, 0:256], scalar1=rsq.ap[:, 1:2], scalar2=None, op0=ALU.mult)
            transposes([cn.ap[:, c * 128:(c + 1) * 128] for c in range(3)], cn.res, cqnT.ap[:, :, tc_], cqnT.res)
            transposes([cn.ap[:, 384 + c * 128:384 + (c + 1) * 128] for c in range(2)], cn.res, ckvnT.ap[:, :, tc_],
                       ckvnT.res)
            ar.free(ssq, rsq, cn)
            rt = ar.alloc("rt", [128, 4, 16], F32)
            kb = banks[pj[1]]
            kr_res = bank_res[pj[1]]
            cA = cosA.ap[:, t, :]
            sA = sinA.ap[:, t, :]
            A("dve", "tensor_tensor", [kr_res, cosA.res], [rt.res], out=rt.ap[:, 0, :], in0=kb[:, 256:272], in1=cA,
              op=ALU.mult)
            A("dve", "tensor_tensor", [kr_res, sinA.res], [rt.res], out=rt.ap[:, 1, :], in0=kb[:, 272:288], in1=sA,
              op=ALU.mult)
            A("dve", "tensor_tensor", [kr_res, cosA.res], [rt.res], out=rt.ap[:, 2, :], in0=kb[:, 272:288], in1=cA,
              op=ALU.mult)
            A("dve", "tensor_tensor", [kr_res, sinA.res], [rt.res], out=rt.ap[:, 3, :], in0=kb[:, 256:272], in1=sA,
              op=ALU.mult)
            A("dve", "tensor_tensor", [rt.res], [krtm.res], out=krtm.ap[:, t, 0:16], in0=rt.ap[:, 0, :],
              in1=rt.ap[:, 1, :], op=ALU.subtract)
            A("dve", "tensor_tensor", [rt.res], [krtm.res], out=krtm.ap[:, t, 16:32], in0=rt.ap[:, 2, :],
              in1=rt.ap[:, 3, :], op=ALU.add)
            ar.free(rt)
            zg = ar.alloc("zg", [128, 512], F32)
            A("act", "activation", [bank_res[pj[2]]], [zg.res], out=zg.ap, in_=banks[pj[2]][:, :],
              func=AF.Gelu_apprx_tanh)
            st = ar.alloc("st", [128, 16], F32)
            vsq = ar.alloc("vsq", [128, 256], F32)
            v3 = zg.ap[:, 256:512].rearrange("p (g c) -> p g c", c=64)
            A("dve", "tensor_reduce", [zg.res], [st.res], out=st.ap[:, 0:4], in_=v3, axis=AX.X, op=ALU.add)
            A("act", "activation", [zg.res], [vsq.res], out=vsq.ap, in_=zg.ap[:, 256:512], func=AF.Square)
            A("dve", "tensor_reduce", [vsq.res], [st.res], out=st.ap[:, 4:8],
              in_=vsq.ap.rearrange("p (g c) -> p g c", c=64), axis=AX.X, op=ALU.add)
            A("dve", "tensor_scalar", [st.res], [st.res], out=st.ap[:, 8:12], in0=st.ap[:, 0:4], scalar1=1.0 / 64,
              scalar2=None, op0=ALU.mult)
            A("dve", "tensor_tensor", [st.res], [st.res], out=st.ap[:, 12:16], in0=st.ap[:, 8:12], in1=st.ap[:, 8:12],
              op=ALU.mult)
            A("dve", "scalar_tensor_tensor", [st.res], [st.res], out=st.ap[:, 4:8], in0=st.ap[:, 4:8], scalar=1.0 / 64,
              in1=st.ap[:, 12:16], op0=ALU.mult, op1=ALU.subtract)
            A("act", "activation", [st.res, epsb.res], [st.res], out=st.ap[:, 0:4], in_=st.ap[:, 4:8], func=AF.Ln,
              bias=epsb.ap[:, 0:1])
            A("act", "activation", [st.res], [st.res], out=st.ap[:, 0:4], in_=st.ap[:, 0:4], func=AF.Exp, scale=-0.5)
            vq3 = vsq.ap.rearrange("p (g c) -> p g c", c=64)
            A("dve", "tensor_tensor", [zg.res, st.res], [vsq.res], out=vq3, in0=v3,
              in1=st.ap[:, 8:12].unsqueeze(2).broadcast_to([128, 4, 64]), op=ALU.subtract)
            A("dve", "tensor_tensor", [vsq.res, st.res], [vsq.res], out=vq3, in0=vq3,
              in1=st.ap[:, 0:4].unsqueeze(2).broadcast_to([128, 4, 64]), op=ALU.mult)
            vn = ar.alloc("vn", [128, 256], BF16)
            A("dve", "tensor_tensor", [vsq.res, bcs.res], [vn.res], out=vn.ap, in0=vsq.ap, in1=bcs.ap[:, 0:256],
              op=ALU.mult)
            mb = mm_bank()
            for g in range(4):
                A("pe", "matmul", [vn.res, wsT.res], [bank_res[mb]], banks[mb][:, g * 64:(g + 1) * 64],
                  lhsT=wsT.ap[:, 0, g * 128:(g + 1) * 128], rhs=vn.ap[:, g * 64:(g + 1) * 64], start=True, stop=True)
            for g in range(4):
                A("dve", "scalar_tensor_tensor", [bank_res[mb], pk.res, zg.res], [cb.res],
                  out=cb.ap[:, t, g * 64:(g + 1) * 64], in0=banks[mb][:, g * 64:(g + 1) * 64],
                  scalar=pk.ap[:, l, 37 + g:38 + g], in1=zg.ap[:, g * 64:(g + 1) * 64], op0=ALU.add, op1=ALU.mult)
            ar.free(zg, st, vsq, vn)
            qk = ar.alloc("qk", [128, 16, 32], BF16)
            r6 = ar.alloc("r6", [128, 4 * 16, 8], F32)
            pq = banks[pj[3]][:, :].rearrange("p (m d) -> p m d", d=32)
            qres = bank_res[pj[3]]
            cD = cosA.ap[:, t, 0:16:2].unsqueeze(1).broadcast_to([128, 16, 8])
            sD = sinA.ap[:, t, 0:16:2].unsqueeze(1).broadcast_to([128, 16, 8])
            A("dve", "tensor_tensor", [qres, cosA.res], [r6.res], out=r6.ap[:, 0:16, :], in0=pq[:, :, 0:8], in1=cD,
              op=ALU.mult)
            A("dve", "tensor_tensor", [qres, sinA.res], [r6.res], out=r6.ap[:, 16:32, :], in0=pq[:, :, 8:16], in1=sD,
              op=ALU.mult)
            A("dve", "tensor_tensor", [qres, cosA.res], [r6.res], out=r6.ap[:, 32:48, :], in0=pq[:, :, 8:16], in1=cD,
              op=ALU.mult)
            A("dve", "tensor_tensor", [qres, sinA.res], [r6.res], out=r6.ap[:, 48:64, :], in0=pq[:, :, 0:8], in1=sD,
              op=ALU.mult)
            A("dve", "tensor_tensor", [r6.res], [qk.res], out=qk.ap[:, :, 0:8], in0=r6.ap[:, 0:16, :],
              in1=r6.ap[:, 16:32, :], op=ALU.subtract)
            A("dve", "tensor_tensor", [r6.res], [qk.res], out=qk.ap[:, :, 8:16], in0=r6.ap[:, 32:48, :],
              in1=r6.ap[:, 48:64, :], op=ALU.add)
            A("act", "activation", [qres], [qk.res], out=qk.ap[:, :, 16:32], in_=pq[:, :, 16:32], func=AF.Copy)
            qk2 = qk.ap.rearrange("p m d -> p (m d)")
            transposes([qk2[:, c * 128:(c + 1) * 128] for c in range(2)], qk.res, dqT.ap[:, :, tc_], dqT.res)
            transposes([qk2[:, 256 + c * 128:256 + (c + 1) * 128] for c in range(2)], qk.res, dkT.ap[:, :, tc_],
                       dkT.res)
            ar.free(qk, r6)
            A("act", "activation", [bank_res[pj[4]]], [dV.res], out=dV.ap[:, t * 4:(t + 1) * 4, 0:64],
              in_=banks[pj[4]][:, 0:256].rearrange("p (h e) -> p h e", e=64), func=AF.Copy)
        ar.free(ss1, rs1, w_in_t, wsT)
        if stop_after == "p1":
            break

        cc = ar.alloc("cc", [128, NT, 256], BF16)
        w_mo_t = load_w("w_mo", [128, 8, D], w_mo_d[l].rearrange("(c p) n -> p c n", p=128))
        w_uq_t = load_w("w_uq", [128, 3, 768], w_uq_d[l].rearrange("(c p) n -> p c n", p=128), scale_cols=32, l=l)
        w_ukv_t = load_w("w_ukv", [128, 2, 1024], w_ukv_d[l].rearrange("(c p) n -> p c n", p=128), scale_cols=35, l=l)
        for h in range(4):
            ous = []
            for c in range(2):
                m = 2 * h + c
                ci, ro = m // 4, 32 * (m % 4)
                ou = ar.alloc("ou%d" % c, [128, NT, 65], F32)
                ous.append(ou)

                def evac(qb, acc, acc_res, ou=ou):
                    A("dve", "tensor_copy", [acc_res], [ou.res], out=ou.ap[:, qb * 4:(qb + 1) * 4, :], in_=acc)

                dkm = ar.alloc("dkm", [128, S], BF16)
                A("pool", "memset", [], [dkm.res], dkm.ap, 0.0)
                A("pool", "tensor_copy", [dkT.res], [dkm.res], out=dkm.ap[ro:ro + 32, :], in_=dkT.ap[ro:ro + 32, ci, :])
                attention((dqT.ap[:, ci, :], dqT.res), (dkm.ap, dkm.res), 128, 0,
                          lambda kc, h=h: dV.ap[:, kc * 4 + h, :], dV.res, NT, 64, 32 ** -0.5, evac)
                ar.free(dkm)
            o0, o1 = ous
            fr = ar.alloc("fr", [128, 4, NT], F32)
            A("dve", "reciprocal", [o0.res], [fr.res], out=fr.ap[:, 0, :], in_=o0.ap[:, :, 64])
            A("dve", "reciprocal", [o1.res], [fr.res], out=fr.ap[:, 1, :], in_=o1.ap[:, :, 64])
            A("dve", "tensor_scalar", [fr.res, lamt.res], [fr.res], out=fr.ap[:, 1, :], in0=fr.ap[:, 1, :],
              scalar1=neglam, scalar2=None, op0=ALU.mult)
            A("dve", "tensor_tensor", [o0.res, fr.res], [o0.res], out=o0.ap[:, :, 0:64], in0=o0.ap[:, :, 0:64],
              in1=fr.ap[:, 0, :].unsqueeze(2).broadcast_to([128, NT, 64]), op=ALU.mult)
            A("dve", "tensor_tensor", [o1.res, fr.res], [o1.res], out=o1.ap[:, :, 0:64], in0=o1.ap[:, :, 0:64],
              in1=fr.ap[:, 1, :].unsqueeze(2).broadcast_to([128, NT, 64]), op=ALU.mult)
            A("dve", "tensor_tensor", [o0.res, o1.res], [o0.res], out=o0.ap[:, :, 0:64], in0=o0.ap[:, :, 0:64],
              in1=o1.ap[:, :, 0:64], op=ALU.add)
            A("dve", "tensor_tensor", [o0.res], [o1.res], out=o1.ap[:, :, 0:64], in0=o0.ap[:, :, 0:64],
              in1=o0.ap[:, :, 0:64], op=ALU.mult)
            A("dve", "tensor_reduce", [o1.res], [fr.res], out=fr.ap[:, 2, :], in_=o1.ap[:, :, 0:64], axis=AX.X,
              op=ALU.add)
            A("act", "activation", [fr.res, epsb.res], [fr.res], out=fr.ap[:, 3, :], in_=fr.ap[:, 2, :], func=AF.Ln,
              scale=1.0 / 64, bias=epsb.ap[:, 0:1])
            A("act", "activation", [fr.res], [fr.res], out=fr.ap[:, 3, :], in_=fr.ap[:, 3, :], func=AF.Exp, scale=-0.5)
            A("dve", "tensor_tensor", [o0.res, fr.res], [o0.res], out=o0.ap[:, :, 0:64], in0=o0.ap[:, :, 0:64],
              in1=fr.ap[:, 3, :].unsqueeze(2).broadcast_to([128, NT, 64]), op=ALU.mult)
            A("dve", "tensor_tensor", [o0.res, bcs.res], [cc.res], out=cc.ap[:, :, h * 64:(h + 1) * 64],
              in0=o0.ap[:, :, 0:64], in1=bcs.ap[:, 256:320].unsqueeze(1).broadcast_to([128, NT, 64]), op=ALU.mult)
            ar.free(o0, o1, fr)
        ar.free(dqT, dkT, dV)
        if stop_after == "p2":
            break

        ca = ar.alloc("ca", [128, NT, 512], BF16)
        for h in range(8):
            QKT = ar.alloc("QKT", [128, 2, S], BF16)
            Vh = ar.alloc("Vh", [128, NT, 65], BF16)
            A("pool", "memset", [], [Vh.res], Vh.ap[:, :, 64:65], 1.0)
            for g in range(4):
                bA = mm_bank()
                bB = mm_bank()
                for tt in range(4):
                    t = g * 4 + tt
                    tc_ = slice(t * 128, (t + 1) * 128)
                    for c in range(3):
                        A("pe", "matmul", [cqnT.res, w_uq_t.res], [bank_res[bA]], banks[bA][:, tt * 96:(tt + 1) * 96],
                          lhsT=cqnT.ap[:, c, tc_], rhs=w_uq_t.ap[:, c, h * 96:(h + 1) * 96], start=(c == 0),
                          stop=(c == 2))
                    for c in range(2):
                        A("pe", "matmul", [ckvnT.res, w_ukv_t.res], [bank_res[bB]],
                          banks[bB][:, tt * 128:(tt + 1) * 128], lhsT=ckvnT.ap[:, c, tc_],
                          rhs=w_ukv_t.ap[:, c, h * 128:(h + 1) * 128], start=(c == 0), stop=(c == 1))
                qA = banks[bA][:, 0:384].rearrange("p (t d) -> p t d", d=96)
                kvB = banks[bB][:, :].rearrange("p (t d) -> p t d", d=128)
                rA, rB = bank_res[bA], bank_res[bB]
                qk = ar.alloc("qk", [128, 8, 128], BF16)
                qk4 = qk.ap.rearrange("p (t s) d -> p t s d", s=2)
                A("pool", "memset", [], [qk.res], qk.ap[:, :, 96:128], 0.0)
                A("act", "activation", [rA], [qk.res], out=qk4[:, :, 0, 0:64], in_=qA[:, :, 0:64], func=AF.Copy)
                A("act", "activation", [rB], [qk.res], out=qk4[:, :, 1, 0:64], in_=kvB[:, :, 0:64], func=AF.Copy)
                A("act", "activation", [rB], [Vh.res], out=Vh.ap[:, g * 4:(g + 1) * 4, 0:64], in_=kvB[:, :, 64:128],
                  func=AF.Copy)
                rt = ar.alloc("rt", [128, 16, 16], F32)
                cA = cosA.ap[:, g * 4:(g + 1) * 4, :]
                sA = sinA.ap[:, g * 4:(g + 1) * 4, :]
                A("dve", "tensor_tensor", [rA, cosA.res], [rt.res], out=rt.ap[:, 0:4, :], in0=qA[:, :, 64:80], in1=cA,
                  op=ALU.mult)
                A("dve", "tensor_tensor", [rA, sinA.res], [rt.res], out=rt.ap[:, 4:8, :], in0=qA[:, :, 80:96], in1=sA,
                  op=ALU.mult)
                A("dve", "tensor_tensor", [rA, cosA.res], [rt.res], out=rt.ap[:, 8:12, :], in0=qA[:, :, 80:96], in1=cA,
                  op=ALU.mult)
                A("dve", "tensor_tensor", [rA, sinA.res], [rt.res], out=rt.ap[:, 12:16, :], in0=qA[:, :, 64:80], in1=sA,
                  op=ALU.mult)
                A("dve", "tensor_tensor", [rt.res], [qk.res], out=qk4[:, :, 0, 64:80], in0=rt.ap[:, 0:4, :],
                  in1=rt.ap[:, 4:8, :], op=ALU.subtract)
                A("dve", "tensor_tensor", [rt.res], [qk.res], out=qk4[:, :, 0, 80:96], in0=rt.ap[:, 8:12, :],
                  in1=rt.ap[:, 12:16, :], op=ALU.add)
                A("dve", "tensor_copy", [krtm.res], [qk.res], out=qk4[:, :, 1, 64:96],
                  in_=krtm.ap[:, g * 4:(g + 1) * 4, :])
                bk = tp_bank()
                pv = banks[bk][:].bitcast(BF16).rearrange("p (a b) -> p a b", b=128)
                for i in range(8):
                    A("pe", "transpose", [qk.res, ident.res], [bank_res[bk]], out=pv[:, i, :], in_=qk.ap[:, i, :],
                      identity=ident.ap)
                copy(copy_eng(), [bank_res[bk]], [QKT.res],
                     QKT.ap[:, :, g * 512:(g + 1) * 512].rearrange("p s (t i) -> p s t i", i=128),
                     pv.rearrange("p (t s) i -> p s t i", s=2))
                ar.free(qk, rt)
            if stop_after == "p3a":
                break

            oua = ar.alloc("ou0", [128, NT, 65], F32)

            def evac_a(qb, acc, acc_res, oua=oua):
                A("dve", "tensor_copy", [acc_res], [oua.res], out=oua.ap[:, qb * 4:(qb + 1) * 4, :], in_=acc)

            attention((QKT.ap[:, 0, :], QKT.res), (QKT.ap[:, 1, :], QKT.res), 128, 0,
                      lambda kc, Vh=Vh: Vh.ap[:, kc, :], Vh.res, NT, 64, 96 ** -0.5, evac_a)
            fr = ar.alloc("fr", [128, 4, NT], F32)
            A("dve", "reciprocal", [oua.res], [fr.res], out=fr.ap[:, 0, :], in_=oua.ap[:, :, 64])
            A("dve", "tensor_tensor", [oua.res, fr.res], [ca.res], out=ca.ap[:, :, h * 64:(h + 1) * 64],
              in0=oua.ap[:, :, 0:64], in1=fr.ap[:, 0, :].unsqueeze(2).broadcast_to([128, NT, 64]), op=ALU.mult)
            ar.free(oua, fr)
            ar.free(QKT, Vh)
            if stop_after == "p3b":
                break
        if stop_after in ("p3a", "p3b"):
            break
        ar.free(cqnT, ckvnT, krtm, w_uq_t, w_ukv_t)
        if stop_after == "p3":
            break

        def post_norm_add(t, bk2, gpost):
            s2 = ar.alloc("s2", [128, 4], F32)
            for i in range(2):
                A("act", "activation", [bank_res[bk2[i]]], [junk.res, s2.res], out=junk.ap[:, 0:512],
                  in_=banks[bk2[i]][:, :], func=AF.Square, accum_out=s2.ap[:, i:i + 1])
            A("dve", "tensor_tensor", [s2.res], [s2.res], out=s2.ap[:, 2:3], in0=s2.ap[:, 0:1], in1=s2.ap[:, 1:2],
              op=ALU.add)
            A("act", "activation", [s2.res, epsb.res], [s2.res], out=s2.ap[:, 3:4], in_=s2.ap[:, 2:3], func=AF.Ln,
              scale=1.0 / D, bias=epsb.ap[:, 0:1])
            A("act", "activation", [s2.res], [s2.res], out=s2.ap[:, 3:4], in_=s2.ap[:, 3:4], func=AF.Exp, scale=-0.5)
            tmp = ar.alloc("tmp", [128, D], F32)
            for i in range(2):
                A("dve", "scalar_tensor_tensor", [bank_res[bk2[i]], s2.res, gpost.res], [tmp.res],
                  out=tmp.ap[:, i * 512:(i + 1) * 512], in0=banks[bk2[i]][:, :], scalar=s2.ap[:, 3:4],
                  in1=gpost.ap[:, i * 512:(i + 1) * 512], op0=ALU.mult, op1=ALU.mult)
            A("pool", "tensor_tensor", [tmp.res, x_res[t]], [x_res[t]], out=xs[:, t, :], in0=xs[:, t, :], in1=tmp.ap,
              op=ALU.add)
            ar.free(s2, tmp)

        gpost = ar.alloc("gpost", [128, D], F32)
        A("sp", "dma_start", [], [gpost.res], is_dma=True, out=gpost.ap, in_=bc_d[l:l + 1, 0:1024].partition_broadcast(128))
        if stop_after != "mix":
            wkv_v = w_kv_d[l].rearrange("(c p) n -> p c n", p=128)
            w_k_t = load_w("w_k", [128, 8, D], wkv_v[:, :, 0:D], scale_cols=24, l=l)
            w_v_ts = [load_w("w_v%d" % i, [128, 8, 512], wkv_v[:, :, D + i * 512:D + (i + 1) * 512], scale_cols=24, l=l)
                      for i in range(2)]
            memf = ar.alloc("memf", [128, 2, D], F32)
            A("sp", "dma_start", [], [memf.res], is_dma=True, out=memf.ap, in_=mem_d.rearrange("(t p) d -> p t d", p=128))
        for t in range(NT):
            cT = ar.alloc("cT", [128, 8, 128], BF16)
            srcs = [ca.ap[:, t, c * 128:(c + 1) * 128] for c in range(4)] + \
                   [cb.ap[:, t, c * 128:(c + 1) * 128] for c in range(2)] + \
                   [cc.ap[:, t, c * 128:(c + 1) * 128] for c in range(2)]
            bk = tp_bank()
            pv = banks[bk][:].bitcast(BF16).rearrange("p (a b) -> p a b", b=128)
            for i, s_ap in enumerate(srcs):
                rr = ca.res if i < 4 else (cb.res if i < 6 else cc.res)
                A("pe", "transpose", [rr, ident.res], [bank_res[bk]], out=pv[:, i, :], in_=s_ap, identity=ident.ap)
            copy(copy_eng(), [bank_res[bk]], [cT.res], cT.ap, pv[:, 0:8, :])
            bk2 = [mm_bank(), mm_bank()]
            for i in range(2):
                for c in range(8):
                    A("pe", "matmul", [cT.res, w_mo_t.res], [bank_res[bk2[i]]], banks[bk2[i]][:, :],
                      lhsT=cT.ap[:, c, :], rhs=w_mo_t.ap[:, c, i * 512:(i + 1) * 512], start=(c == 0), stop=(c == 7))
            ar.free(cT)
            post_norm_add(t, bk2, gpost)
        ar.free(ca, cb, cc, w_mo_t, gpost, bcs, lamt)
        if stop_after == "mix":
            break

        ssm = ar.alloc("ssm", [128, 2], F32)
        rsm = ar.alloc("rsm", [128, 2], F32)
        for t in range(2):
            A("act", "activation", [memf.res], [junk.res, ssm.res], out=junk.ap, in_=memf.ap[:, t, :], func=AF.Square,
              accum_out=ssm.ap[:, t:t + 1])
        rstd_from((ssm.ap, ssm.res), 2, D, (rsm.ap, rsm.res))
        memT = ar.alloc("memT", [128, 8, NMEM], BF16)
        for t in range(2):
            mn = ar.alloc("mn", [128, D], BF16)
            A("dve", "tensor_scalar", [memf.res, rsm.res], [mn.res], out=mn.ap, in0=memf.ap[:, t, :],
              scalar1=rsm.ap[:, t:t + 1], scalar2=None, op0=ALU.mult)
            transposes([mn.ap[:, c * 128:(c + 1) * 128] for c in range(8)], mn.res, memT.ap[:, :, t * 128:(t + 1) * 128],
                       memT.res)
            ar.free(mn)
        ar.free(memf, ssm, rsm)
        KmT = ar.alloc("KmT", [128, 8, NMEM], BF16)
        for hc in range(8):
            mb = mm_bank()
            for c in range(8):
                A("pe", "matmul", [memT.res, w_k_t.res], [bank_res[mb]], banks[mb][:, 0:NMEM],
                  lhsT=w_k_t.ap[:, c, hc * 128:(hc + 1) * 128], rhs=memT.ap[:, c, :], start=(c == 0), stop=(c == 7))
            copy(copy_eng(), [bank_res[mb]], [KmT.res], KmT.ap[:, hc, :], banks[mb][:, 0:NMEM])
        Vm = ar.alloc("Vm", [128, 8, 257], BF16)
        A("pool", "memset", [], [Vm.res], Vm.ap[:, :, 256:257], 1.0)
        for kt in range(2):
            for i in range(2):
                mb = mm_bank()
                for c in range(8):
                    A("pe", "matmul", [memT.res, w_v_ts[i].res], [bank_res[mb]], banks[mb][:, :],
                      lhsT=memT.ap[:, c, kt * 128:(kt + 1) * 128], rhs=w_v_ts[i].ap[:, c, :],
                      start=(c == 0), stop=(c == 7))
                copy(copy_eng(), [bank_res[mb]], [Vm.res], Vm.ap[:, kt * 4 + 2 * i:kt * 4 + 2 * i + 2, 0:256],
                     banks[mb][:, :].rearrange("p (h e) -> p h e", e=256))
        ar.free(memT, w_k_t, *w_v_ts)
        w_q_t = load_w("w_q", [128, 8, D], w_q_d[l].rearrange("(c p) n -> p c n", p=128), scale_cols=8, l=l)
        w_o_t = load_w("w_o", [128, 8, D], w_o_d[l].rearrange("(c p) n -> p c n", p=128))
        gpost = ar.alloc("gpost", [128, D], F32)
        A("sp", "dma_start", [], [gpost.res], is_dma=True, out=gpost.ap, in_=bc_d[l:l + 1, 1024:2048].partition_broadcast(128))
        ss5 = ar.alloc("ss5", [128, NT], F32)
        rs5 = ar.alloc("rs5", [128, NT], F32)
        for t in range(NT):
            A("act", "activation", [x_res[t]], [junk.res, ss5.res], out=junk.ap, in_=xs[:, t, :], func=AF.Square,
              accum_out=ss5.ap[:, t:t + 1])
        rstd_from((ss5.ap, ss5.res), NT, D, (rs5.ap, rs5.res))
        for qb in range(4):
            hTb = ar.alloc("hTb", [128, 8, 512], BF16)
            for j in range(4):
                t = qb * 4 + j
                hn = ar.alloc("hn", [128, D], BF16)
                A("dve", "tensor_scalar", [x_res[t], rs5.res], [hn.res], out=hn.ap, in0=xs[:, t, :],
                  scalar1=rs5.ap[:, t:t + 1], scalar2=None, op0=ALU.mult)
                transposes([hn.ap[:, c * 128:(c + 1) * 128] for c in range(8)], hn.res,
                           hTb.ap[:, :, j * 128:(j + 1) * 128], hTb.res)
                ar.free(hn)
            qTb = ar.alloc("qTb", [128, 8, 512], BF16)
            for hc in range(8):
                mb = mm_bank()
                for c in range(8):
                    A("pe", "matmul", [hTb.res, w_q_t.res], [bank_res[mb]], banks[mb][:, :],
                      lhsT=w_q_t.ap[:, c, hc * 128:(hc + 1) * 128], rhs=hTb.ap[:, c, :], start=(c == 0), stop=(c == 7))
                copy(copy_eng(), [bank_res[mb]], [qTb.res], qTb.ap[:, hc, :], banks[mb][:, :])
            ar.free(hTb)
            ao = ar.alloc("ao", [128, 4, D], BF16)
            for h in range(4):
                pts = []
                for kt in range(2):
                    sb_ = tp_bank()
                    for dc in range(2):
                        A("pe", "matmul", [qTb.res, KmT.res], [bank_res[sb_]], banks[sb_][:, :],
                          lhsT=KmT.ap[:, h * 2 + dc, kt * 128:(kt + 1) * 128], rhs=qTb.ap[:, h * 2 + dc, :],
                          start=(dc == 0), stop=(dc == 1))
                    pt = ar.alloc("ptm", [128, 512], BF16)
                    A("act", "activation", [bank_res[sb_]], [pt.res], out=pt.ap, in_=banks[sb_][:, :], func=AF.Exp,
                      scale=1.0 / 16.0)
                    pts.append(pt)
                for j in range(4):
                    mb = mm_bank()
                    for kt in range(2):
                        A("pe", "matmul", [pts[kt].res, Vm.res], [bank_res[mb]], banks[mb][:, 0:257],
                          lhsT=pts[kt].ap[:, j * 128:(j + 1) * 128], rhs=Vm.ap[:, kt * 4 + h, :], start=(kt == 0),
                          stop=(kt == 1))
                    rc = ar.alloc("rc", [128, 1], F32)
                    A("dve", "reciprocal", [bank_res[mb]], [rc.res], out=rc.ap[:, 0:1], in_=banks[mb][:, 256:257])
                    A("dve", "tensor_scalar", [bank_res[mb], rc.res], [ao.res], out=ao.ap[:, j, h * 256:(h + 1) * 256],
                      in0=banks[mb][:, 0:256], scalar1=rc.ap[:, 0:1], scalar2=None, op0=ALU.mult)
                    ar.free(rc)
                ar.free(*pts)
            ar.free(qTb)
            for j in range(4):
                t = qb * 4 + j
                aT = ar.alloc("aT", [128, 8, 128], BF16)
                transposes([ao.ap[:, j, c * 128:(c + 1) * 128] for c in range(8)], ao.res, aT.ap, aT.res)
                bk2 = [mm_bank(), mm_bank()]
                for i in range(2):
                    for c in range(8):
                        A("pe", "matmul", [aT.res, w_o_t.res], [bank_res[bk2[i]]], banks[bk2[i]][:, :],
                          lhsT=aT.ap[:, c, :], rhs=w_o_t.ap[:, c, i * 512:(i + 1) * 512], start=(c == 0), stop=(c == 7))
                ar.free(aT)
                post_norm_add(t, bk2, gpost)
            ar.free(ao)
        ar.free(KmT, Vm, w_q_t, w_o_t, gpost, ss5, rs5)
        if stop_after == "mem":
            break

        w_dn_t = load_w("w_dn", [128, NCH, D], w_dn_d[l].rearrange("(c p) n -> p c n", p=128))
        gpost = ar.alloc("gpost", [128, D], F32)
        A("sp", "dma_start", [], [gpost.res], is_dma=True, out=gpost.ap, in_=bc_d[l:l + 1, 2048:3072].partition_broadcast(128))
        ss6 = ar.alloc("ss6", [128, NT], F32)
        rs6 = ar.alloc("rs6", [128, NT], F32)
        for t in range(NT):
            A("act", "activation", [x_res[t]], [junk.res, ss6.res], out=junk.ap, in_=xs[:, t, :], func=AF.Square,
              accum_out=ss6.ap[:, t:t + 1])
        rstd_from((ss6.ap, ss6.res), NT, D, (rs6.ap, rs6.res))
        hl = ar.alloc("hl", [128, 8, 2], BF16)
        stgs = [ar.alloc("stgf", [128, 8, 256], F32) for _ in range(2)]
        wus = [ar.alloc("wuf", [128, 8, 256], BF16) for _ in range(3)]
        def issue_w(idx, l=l, stgs=stgs, wus=wus):
            if idx >= 4 * NCH:
                return
            jc_ = idx % NCH
            stg_, wu_ = stgs[idx % 2], wus[idx % 3]
            A("sp", "dma_start", [], [stg_.res], is_dma=True, out=stg_.ap,
              in_=w_up_d[l, jc_].rearrange("p (c n) -> p c n", n=256))
            A("pool", "tensor_tensor", [stg_.res, pk.res], [wu_.res], out=wu_.ap, in0=stg_.ap,
              in1=pk.ap[:, l, 16:24].unsqueeze(2).broadcast_to([128, 8, 256]), op=ALU.mult)

        issue_w(0)
        issue_w(1)
        for qb in range(4):
            hTe = ar.alloc("hTe", [128, 8, 514], BF16)
            if qb == 0:
                A("dve", "memset", [], [hTe.res], hTe.ap[:, :, 0:1], 0.0)
            if qb == 3:
                A("dve", "memset", [], [hTe.res], hTe.ap[:, :, 513:514], 0.0)
            tl = list(range(qb * 4, qb * 4 + 4))
            if qb > 0:
                A("dve", "tensor_copy", [hl.res], [hTe.res], out=hTe.ap[:, :, 0:1], in_=hl.ap[:, :, (qb - 1) % 2:(qb - 1) % 2 + 1])
            if qb < 3:
                tl = tl + [qb * 4 + 4]
            for t in tl:
                hn = ar.alloc("hn", [128, D], BF16)
                A("dve", "tensor_scalar", [x_res[t], rs6.res], [hn.res], out=hn.ap, in0=xs[:, t, :],
                  scalar1=rs6.ap[:, t:t + 1], scalar2=None, op0=ALU.mult)
                bk = tp_bank()
                pv = banks[bk][:].bitcast(BF16).rearrange("p (a b) -> p a b", b=128)
                for c in range(8):
                    A("pe", "transpose", [hn.res, ident.res], [bank_res[bk]], out=pv[:, c, :],
                      in_=hn.ap[:, c * 128:(c + 1) * 128], identity=ident.ap)
                j = t - qb * 4
                if j < 0:
                    copy(copy_eng(), [bank_res[bk]], [hTe.res], hTe.ap[:, :, 0:1], pv[:, 0:8, 127:128])
                elif j > 3:
                    copy(copy_eng(), [bank_res[bk]], [hTe.res], hTe.ap[:, :, 513:514], pv[:, 0:8, 0:1])
                else:
                    copy(copy_eng(), [bank_res[bk]], [hTe.res], hTe.ap[:, :, 1 + j * 128:1 + (j + 1) * 128], pv[:, 0:8, :])
                ar.free(hn)
            A("dve", "tensor_copy", [hTe.res], [hl.res], out=hl.ap[:, :, qb % 2:qb % 2 + 1], in_=hTe.ap[:, :, 512:513])
            mT = ar.alloc("mT", [128, NCH, 512], BF16)
            mm_all[0] = True
            for jc in range(NCH):
                wu = wus[(qb * NCH + jc) % 3]
                issue_w(qb * NCH + jc + 2)
                ys = []
                hb = mm_bank()
                for gu in range(2):
                    mb = mm_bank()
                    for c in range(8):
                        A("pe", "matmul", [hTe.res, wu.res], [bank_res[mb]], banks[mb][:, :],
                          lhsT=wu.ap[:, c, gu * 128:(gu + 1) * 128], rhs=hTe.ap[:, c, 1:513], start=(c == 0), stop=(c == 7))
                    for c in range(8):
                        A("pe", "matmul", [hTe.res, wu.res], [bank_res[hb]], banks[hb][:, 2 * gu:2 * gu + 2],
                          lhsT=wu.ap[:, c, gu * 128:(gu + 1) * 128], rhs=hTe.ap[:, c, 0:514:513], start=(c == 0),
                          stop=(c == 7))
                    ch = gu * NCH + jc
                    cw = pk.ap[:, l, 41 + ch * 4:41 + ch * 4 + 4]
                    y = ar.alloc("y", [128, 512], F32)
                    A("act", "activation", [bank_res[mb], pk.res], [y.res], out=y.ap, in_=banks[mb][:, :],
                      func=AF.Identity, scale=cw[:, 1:2], bias=cw[:, 3:4])
                    A("dve", "scalar_tensor_tensor", [bank_res[mb], pk.res, y.res], [y.res], out=y.ap[:, 1:512],
                      in0=banks[mb][:, 0:511], scalar=cw[:, 0:1], in1=y.ap[:, 1:512], op0=ALU.mult, op1=ALU.add)
                    A("dve", "scalar_tensor_tensor", [bank_res[hb], pk.res, y.res], [y.res], out=y.ap[:, 0:1],
                      in0=banks[hb][:, 2 * gu:2 * gu + 1], scalar=cw[:, 0:1], in1=y.ap[:, 0:1], op0=ALU.mult, op1=ALU.add)
                    A("dve", "scalar_tensor_tensor", [bank_res[mb], pk.res, y.res], [y.res], out=y.ap[:, 0:511],
                      in0=banks[mb][:, 1:512], scalar=cw[:, 2:3], in1=y.ap[:, 0:511], op0=ALU.mult, op1=ALU.add)
                    A("dve", "scalar_tensor_tensor", [bank_res[hb], pk.res, y.res], [y.res], out=y.ap[:, 511:512],
                      in0=banks[hb][:, 2 * gu + 1:2 * gu + 2], scalar=cw[:, 2:3], in1=y.ap[:, 511:512], op0=ALU.mult, op1=ALU.add)
                    ys.append(y)
                A("act", "activation", [ys[0].res], [ys[0].res], out=ys[0].ap, in_=ys[0].ap, func=AF.Gelu_apprx_tanh)
                A("pool", "tensor_tensor", [ys[0].res, ys[1].res], [mT.res], out=mT.ap[:, jc, :], in0=ys[0].ap,
                  in1=ys[1].ap, op=ALU.mult)
                ar.free(*ys)
            ar.free(hTe)
            for j in range(4):
                t = qb * 4 + j
                bk2 = [mm_bank(), mm_bank()]
                for i in range(2):
                    for jc in range(NCH):
                        A("pe", "matmul", [mT.res, w_dn_t.res], [bank_res[bk2[i]]], banks[bk2[i]][:, :],
                          lhsT=mT.ap[:, jc, j * 128:(j + 1) * 128], rhs=w_dn_t.ap[:, jc, i * 512:(i + 1) * 512],
                          start=(jc == 0), stop=(jc == NCH - 1))
                post_norm_add(t, bk2, gpost)
            ar.free(mT)
            mm_all[0] = False
        ar.free(w_dn_t, gpost, ss6, rs6, hl, *stgs, *wus)

    ov = out_d.rearrange("(t p) d -> p t d", p=128)
    last = []
    for t in range(NT):
        last.append(A("sp", "dma_start", [x_res[t]], [out_res[t]], is_dma=True, out=ov[:, t, :], in_=xs[:, t, :]))
    fin = A("sp", "nop", [out_res], [])

    counts = sc.finalize()
    nsem = {e: max(1, (counts[e] + SEM_LIMIT - 1) // SEM_LIMIT) for e in ("pe", "act", "dve", "pool", "sp")}
    sems = {e: [es.enter_context(nc.semaphore("s_%s%d" % (e, i))) for i in range(nsem[e])] for e in nsem}
    dsems = {q: [es.enter_context(nc.semaphore("d_%s%d" % (q, i))) for i in range(NDS)] for q in ("sp", "pool")}

    def emit(ename, eng):
        waited = {}
        for op in sc.ops[ename]:
            for d in op.deps:
                if d.is_dma:
                    sm, val = dsems[d.eng][d.dsem], d.dval
                    key = ("d", d.eng, d.dsem)
                else:
                    sm, val = sems[d.eng][d.semk // SEM_LIMIT], d.semk % SEM_LIMIT + 1
                    key = ("c", d.eng, d.semk // SEM_LIMIT)
                if waited.get(key, 0) >= val:
                    continue
                eng.wait_ge(sm, val)
                waited[key] = val
            if op.meth == "nop":
                continue
            ins = getattr(eng, op.meth)(*op.args, **op.kw)
            if op.is_dma:
                ins.then_inc(dsems[op.eng][op.dsem], 16)
            elif op.sig:
                ins.then_inc(sems[op.eng][op.semk // SEM_LIMIT], 1)

    block = es.enter_context(nc.Block())

    @block.sync
    def _(e):
        emit("sp", e)

    @block.gpsimd
    def _(e):
        emit("pool", e)

    @block.tensor
    def _(e):
        emit("pe", e)

    @block.scalar
    def _(e):
        emit("act", e)

    @block.vector
    def _(e):
        emit("dve", e)

    es.close()
    stats = {e: len(sc.ops[e]) for e in ENGS}
    stats["arena_peak_kb"] = ar.peak
    return nc, stats


def prep_inputs(inp, n_layers=DEPTH):
    f = lambda a: np.ascontiguousarray(np.asarray(a, dtype=np.float32))
    pk = np.zeros((128, DEPTH, 217), np.float32)
    for l in range(DEPTH):
        pk[:, l, 0:8] = np.asarray(inp["mix_pre_g"])[l].reshape(8, 128).T
        pk[:, l, 8:16] = np.asarray(inp["mem_pre_g"])[l].reshape(8, 128).T
        pk[:, l, 16:24] = np.asarray(inp["ffn_pre_g"])[l].reshape(8, 128).T
        pk[:, l, 24:32] = np.asarray(inp["mem_kv_g"])[l].reshape(8, 128).T
        pk[:, l, 32:35] = np.asarray(inp["mla_cq_g"])[l].reshape(3, 128).T
        pk[:, l, 35:37] = np.asarray(inp["mla_ckv_g"])[l].reshape(2, 128).T
        pk[:, l, 37:41] = np.asarray(inp["sgu_b_s"])[l].T
        cw = np.asarray(inp["ffn_conv_w"])[l]
        cbv = np.asarray(inp["ffn_conv_b"])[l]
        c4 = np.concatenate([cw, cbv[None, :]], axis=0)
        pk[:, l, 41:217] = c4.reshape(4, 44, 128).transpose(2, 1, 0).reshape(128, 176)
    bc = np.concatenate([
        np.asarray(inp["mix_post_g"]), np.asarray(inp["mem_post_g"]), np.asarray(inp["ffn_post_g"]),
        np.asarray(inp["sgu_norm_g"]).reshape(DEPTH, 256), np.asarray(inp["diff_sub_g"]),
        np.asarray(inp["diff_lam_q1"]), np.asarray(inp["diff_lam_k1"]),
        np.asarray(inp["diff_lam_q2"]), np.asarray(inp["diff_lam_k2"])], axis=1).astype(np.float32)
    wsT = np.asarray(inp["sgu_w_s"]).transpose(0, 3, 1, 2).reshape(DEPTH, 128, 512)
    nl = n_layers
    f = lambda a: np.ascontiguousarray(np.asarray(a, dtype=np.float32)[:nl])
    wup = np.asarray(inp["ffn_w_up"], dtype=np.float32)[:nl].reshape(nl, 8, 128, 2, NCH, 128)
    wup = np.ascontiguousarray(wup.transpose(0, 4, 2, 1, 3, 5)).reshape(nl, NCH, 128, 2048)
    shared = {
        "w_in": f(inp["w_in"]), "w_uq": f(inp["mla_w_uq"]), "w_ukv": f(inp["mla_w_ukv"]), "wsT": f(wsT),
        "w_mo": f(inp["w_mix_out"]), "w_q": f(inp["mem_w_q"]), "w_kv": f(inp["mem_w_kv"]), "w_o": f(inp["mem_w_o"]),
        "w_up": wup, "w_dn": f(inp["ffn_w_down"]),
        "pk": np.ascontiguousarray(pk), "bc": np.ascontiguousarray(bc),
    }
    x = np.asarray(inp["x"], dtype=np.float32)
    mem = np.asarray(inp["mem"], dtype=np.float32)
    pos = np.asarray(inp["positions"]).astype(np.int32)
    maps = []
    for b in range(x.shape[0]):
        m = dict(shared)
        m["x"] = np.ascontiguousarray(x[b])
        m["mem"] = np.ascontiguousarray(mem[b])
        m["pos"] = np.ascontiguousarray(pos[b].reshape(NT, 128))
        maps.append(m)
    return maps


def kernel(**inputs):
    maps = prep_inputs(inputs)
    nc, _ = build()
    res = run_bass_kernel_spmd(nc, maps, core_ids=list(range(8)))
    return np.stack([np.asarray(r["out"], dtype=np.float32) for r in res.results], axis=0)
```

```python
import math
from contextlib import ExitStack

import numpy as np
import concourse.bass as bass
import concourse.mybir as mybir
from concourse.bass_utils import run_bass_kernel_spmd

F32 = mybir.dt.float32
BF16 = mybir.dt.bfloat16
I32 = mybir.dt.int32
U8 = mybir.dt.uint8
AF = mybir.ActivationFunctionType
ALU = mybir.AluOpType
AX = mybir.AxisListType

D = 1024
S = 2048
NT = 16
DEPTH = 4
NMEM = 256
EPS = 1e-6
THETA = 500000.0
INW = 1952
DFF = 2816
NCH = 22

ENGS = ("pe", "act", "dve", "pool", "sp")
SEM_LIMIT = 1000
NDS = 12


class Res:
    __slots__ = ("lw", "rd", "rd_dma")

    def __init__(self):
        self.lw = None
        self.rd = {}
        self.rd_dma = []


class Op:
    __slots__ = ("eng", "meth", "args", "kw", "deps", "sig", "semk", "is_dma", "dsem", "dval", "idx")


class Sched:
    def __init__(self):
        self.ops = {e: [] for e in ENGS}
        self.ndma = {"sp": 0, "pool": 0}
        self.dma_hist = {"sp": [], "pool": []}
        self.n = 0

    def add(self, eng, meth, r, w, *args, is_dma=False, **kw):
        op = Op()
        op.eng, op.meth, op.args, op.kw = eng, meth, args, kw
        op.is_dma = is_dma
        op.sig = False
        op.semk = None
        op.idx = self.n
        op.dsem = None
        op.dval = None
        self.n += 1
        deps = {}

        def need(d):
            if d is None or d is op:
                return
            if (not d.is_dma) and d.eng == "pe" and eng == "pe" and not is_dma:
                return
            deps[d.idx] = d

        wset = set(id(x) for x in w)
        for x in r:
            need(x.lw)
        for x in w:
            need(x.lw)
            for d in x.rd.values():
                need(d)
            for d in x.rd_dma:
                need(d)
        if is_dma:
            q = self.dma_hist[eng]
            op.dsem = len(q) % NDS
            op.dval = 16 * (len(q) // NDS + 1)
            if len(q) >= NDS:
                need(q[len(q) - NDS])
            q.append(op)
        best = {}
        out = []
        for d in deps.values():
            if d.is_dma:
                out.append(d)
            else:
                b = best.get(d.eng)
                if b is None or d.idx > b.idx:
                    best[d.eng] = d
        out.extend(best.values())
        for d in out:
            d.sig = True
        op.deps = out
        for x in w:
            x.lw = op
            x.rd = {}
            x.rd_dma = []
        for x in r:
            if id(x) in wset:
                continue
            if is_dma:
                x.rd_dma.append(op)
            else:
                x.rd[eng] = op
        self.ops[eng].append(op)
        return op

    def finalize(self):
        for e in ENGS:
            k = 0
            for op in self.ops[e]:
                if op.is_dma:
                    continue
                if op.sig:
                    op.semk = k
                    k += 1
        return {e: sum(1 for o in self.ops[e] if (not o.is_dma) and o.sig) for e in ENGS}


class Buf:
    __slots__ = ("ap", "res", "off", "size", "name", "owner")


class Arena:
    CH = 1024

    def __init__(self, t, nbytes, base=0, nextfit=False):
        self.t = t
        self.base = base
        self.nextfit = nextfit
        self.ptr = 0
        self.nbytes = nbytes
        self.nch = nbytes // self.CH
        self.res = [Res() for _ in range(self.nch)]
        self.used = [False] * self.nch
        self.peak = 0

    def alloc(self, name, shape, dt):
        esz = 4 if dt in (F32, I32) else 2
        n = 1
        for s in shape[1:]:
            n *= s
        nb = n * esz
        k = (nb + self.CH - 1) // self.CH
        start = None
        order = [0]
        if self.nextfit:
            order = [self.ptr, 0]
        for s0 in order:
            run = 0
            for i in range(s0, self.nch):
                if not self.used[i]:
                    run += 1
                    if run == k:
                        start = i - k + 1
                        break
                else:
                    run = 0
            if start is not None:
                break
        if start is not None:
            self.ptr = start + k
        if start is None:
            raise RuntimeError("arena OOM for %s (%d B); used=%d" % (name, nb, sum(self.used)))
        for i in range(start, start + k):
            self.used[i] = True
        self.peak = max(self.peak, max(i for i in range(self.nch) if self.used[i]) + 1)
        b = Buf()
        b.name = name
        b.owner = self
        b.off = start
        b.size = k
        o = self.base + start * self.CH
        ap = self.t[:, o:o + nb].bitcast(dt)
        if len(shape) == 3:
            b.ap = ap.rearrange("p (a b) -> p a b", b=shape[2])
        else:
            b.ap = ap
        b.res = self.res[start:start + k]
        return b

    def free(self, *bufs):
        for b in bufs:
            for i in range(b.off, b.off + b.size):
                b.owner.used[i] = False


TRANS = {"dkm", "wuf", "stg", "oT", "hn", "hT", "ssq", "rsq", "cn", "rt", "zg", "st", "vsq", "vn", "qk", "r6", "pt", "rc", "o1", "s2", "tmp",
         "cT", "aT", "mn", "y", "wu", "ptm", "ltmp"}


class Arenas:
    def __init__(self, main, trans):
        self.main = main
        self.trans = trans

    def alloc(self, name, shape, dt):
        if name in TRANS:
            return self.trans.alloc(name, shape, dt)
        return self.main.alloc(name, shape, dt)

    def free(self, *bufs):
        self.main.free(*bufs)

    @property
    def peak(self):
        return (self.main.peak, self.trans.peak)


def build(n_layers=DEPTH, stop_after=None):
    nc = bass.Bass("TRN2", target_bir_lowering=False)

    def din(name, shape, dt=F32):
        return nc.dram_tensor(name, list(shape), dt, kind="ExternalInput").ap()

    WL = n_layers
    x_d = din("x", [S, D])
    mem_d = din("mem", [NMEM, D])
    pos_d = din("pos", [NT, 128], I32)
    w_in_d = din("w_in", [WL, D, INW])
    w_uq_d = din("w_uq", [WL, 384, 768])
    w_ukv_d = din("w_ukv", [WL, 256, 1024])
    wsT_d = din("wsT", [WL, 128, 512])
    w_mo_d = din("w_mo", [WL, D, D])
    w_q_d = din("w_q", [WL, D, D])
    w_kv_d = din("w_kv", [WL, D, 2 * D])
    w_o_d = din("w_o", [WL, D, D])
    w_up_d = din("w_up", [WL, NCH, 128, 2048])
    w_dn_d = din("w_dn", [WL, DFF, D])
    NPK = 217
    pk_d = din("pk", [128, DEPTH, NPK])
    NBC = 3520
    bc_d = din("bc", [DEPTH, NBC])
    out_d = nc.dram_tensor("out", [S, D], F32, kind="ExternalOutput").ap()

    sc = Sched()
    es = ExitStack()
    xs = es.enter_context(nc.sbuf_tensor("xs", [128, NT, D], F32))
    x_res = [Res() for _ in range(NT)]
    MAIN_B = 114 * 1024
    TRANS_B = 29 * 1024
    ar_t = es.enter_context(nc.sbuf_tensor("arena", [128, MAIN_B + TRANS_B], U8))
    ar = Arenas(Arena(ar_t, MAIN_B), Arena(ar_t, TRANS_B, base=MAIN_B, nextfit=True))
    banks = [es.enter_context(nc.psum_tensor("bank%d" % i, [128, 512], F32)) for i in range(8)]
    bank_res = [Res() for _ in range(8)]
    bank_ids = set(id(b) for b in bank_res)
    out_res = [Res() for _ in range(NT)]

    def A(eng, meth, r, w, *a, **k):
        rr = []
        for x in r:
            rr.extend(x if isinstance(x, (list, tuple)) else [x])
        ww = []
        for x in w:
            ww.extend(x if isinstance(x, (list, tuple)) else [x])
        for x in rr:
            if id(x) in bank_ids:
                ww.append(x)
        return sc.add(eng, meth, rr, ww, *a, **k)

    tp_i = [0]

    def tp_bank():
        tp_i[0] ^= 1
        return tp_i[0]

    mm_i = [0]

    mm_all = [False]

    def mm_bank():
        if mm_all[0]:
            mm_i[0] = (mm_i[0] + 1) % 8
            return mm_i[0]
        mm_i[0] = (mm_i[0] + 1) % 5
        return 3 + mm_i[0]

    sb_i = [0]

    def s_bank():
        sb_i[0] = (sb_i[0] + 1) % 3
        return sb_i[0]

    cp_i = [0]

    def copy_eng():
        cp_i[0] ^= 1
        return "act" if cp_i[0] else "dve"

    def copy(eng, r, w, out, in_):
        if eng == "act":
            A("act", "activation", r, w, out=out, in_=in_, func=AF.Copy)
        elif eng == "dve":
            A("dve", "tensor_copy", r, w, out=out, in_=in_)
        else:
            A("pool", "tensor_copy", r, w, out=out, in_=in_)

    ident = ar.alloc("ident", [128, 128], BF16)
    identf = ar.alloc("identf", [128, 128], F32)
    A("pool", "memset", [], [identf.res], identf.ap, 0.0)
    A("pool", "iota", [], [identf.res], identf.ap, pattern=[[1, 128]], base=0, channel_multiplier=-1,
      allow_small_or_imprecise_dtypes=True)
    A("dve", "tensor_scalar", [identf.res], [ident.res], out=ident.ap, in0=identf.ap, scalar1=0.0, scalar2=None,
      op0=ALU.is_equal)
    identF = ar.alloc("identF", [128, 128], F32)
    A("dve", "tensor_scalar", [identf.res], [identF.res], out=identF.ap, in0=identf.ap, scalar1=0.0, scalar2=None,
      op0=ALU.is_equal)
    epsb = ar.alloc("eps", [128, 1], F32)
    A("dve", "memset", [], [epsb.res], epsb.ap, EPS)
    zerob = ar.alloc("zero", [128, 1], F32)
    A("dve", "memset", [], [zerob.res], zerob.ap, 0.0)
    junk = ar.alloc("junk", [128, 1024], BF16)

    xv = x_d.rearrange("(t p) d -> p t d", p=128)
    for t in range(NT):
        A("sp", "dma_start", [], [x_res[t]], is_dma=True, out=xs[:, t, :], in_=xv[:, t, :])

    pk = ar.alloc("pk", [128, DEPTH, NPK], F32)
    A("sp", "dma_start", [], [pk.res], is_dma=True, out=pk.ap, in_=pk_d)

    posi = ar.alloc("posi", [128, NT], I32)
    for t in range(NT):
        A("sp", "dma_start", [], [posi.res], is_dma=True, out=posi.ap[:, t:t + 1],
          in_=pos_d[t:t + 1, :].rearrange("a p -> p a"))
    posf = ar.alloc("posf", [128, NT], F32)
    A("dve", "tensor_copy", [posi.res], [posf.res], out=posf.ap, in_=posi.ap)
    ang = ar.alloc("ang", [128, NT, 16], F32)
    for f in range(16):
        A("dve", "tensor_scalar", [posf.res], [ang.res], out=ang.ap[:, :, f], in0=posf.ap,
          scalar1=float(THETA ** (-2.0 * f / 32.0)), scalar2=None, op0=ALU.mult)
    cosA = ar.alloc("cosA", [128, NT, 16], F32)
    sinA = ar.alloc("sinA", [128, NT, 16], F32)
    TWO_PI = 2.0 * math.pi
    C1 = 6.28125
    C2 = TWO_PI - C1
    tA = ar.alloc("tA", [128, NT, 16], F32)
    tK = ar.alloc("tK", [128, NT, 16], I32)
    tKf = ar.alloc("tKf", [128, NT, 16], F32)
    tM = ar.alloc("tM", [128, NT, 16], F32)
    for (dst, shift) in ((sinA, 0.0), (cosA, math.pi / 2)):
        A("dve", "tensor_scalar", [ang.res], [tA.res], out=tA.ap, in0=ang.ap, scalar1=shift, scalar2=None,
          op0=ALU.add)
        A("dve", "tensor_scalar", [tA.res], [tM.res], out=tM.ap, in0=tA.ap, scalar1=1.0 / TWO_PI, scalar2=None,
          op0=ALU.mult)
        A("dve", "tensor_copy", [tM.res], [tK.res], out=tK.ap, in_=tM.ap)
        A("dve", "tensor_copy", [tK.res], [tKf.res], out=tKf.ap, in_=tK.ap)
        A("dve", "scalar_tensor_tensor", [tKf.res, tA.res], [tM.res], out=tM.ap, in0=tKf.ap, scalar=-C1,
          in1=tA.ap, op0=ALU.mult, op1=ALU.add)
        A("dve", "scalar_tensor_tensor", [tKf.res, tM.res], [tA.res], out=tA.ap, in0=tKf.ap, scalar=-C2,
          in1=tM.ap, op0=ALU.mult, op1=ALU.add)
        A("dve", "tensor_scalar", [tA.res], [tM.res], out=tM.ap, in0=tA.ap, scalar1=math.pi, scalar2=-TWO_PI,
          op0=ALU.is_gt, op1=ALU.mult)
        A("dve", "tensor_tensor", [tA.res, tM.res], [tKf.res], out=tKf.ap, in0=tA.ap, in1=tM.ap, op=ALU.add)
        A("dve", "tensor_scalar", [tKf.res], [tM.res], out=tM.ap, in0=tKf.ap, scalar1=-math.pi, scalar2=TWO_PI,
          op0=ALU.is_lt, op1=ALU.mult)
        A("dve", "tensor_tensor", [tKf.res, tM.res], [tA.res], out=tA.ap, in0=tKf.ap, in1=tM.ap, op=ALU.add)
        A("dve", "tensor_scalar", [tA.res], [tM.res], out=tM.ap, in0=tA.ap, scalar1=math.pi, scalar2=-math.pi,
          op0=ALU.min, op1=ALU.max)
        A("act", "activation", [tM.res], [dst.res], out=dst.ap, in_=tM.ap, func=AF.Sin)
    ar.free(tA, tK, tKf, tM, ang, posi, posf, identf)

    def rstd_from(ss, ncol, dim, dst):
        A("act", "activation", [ss[1], epsb.res], [dst[1]], out=dst[0], in_=ss[0], func=AF.Ln, scale=1.0 / dim,
          bias=epsb.ap[:, 0:1])
        A("act", "activation", [dst[1]], [dst[1]], out=dst[0], in_=dst[0], func=AF.Exp, scale=-0.5)

    def load_w(name, shape, src, scale_cols=None, l=0, chunk_cols=None):
        b = ar.alloc(name, shape, BF16)
        kc, n = shape[1], shape[2]
        step = 2048
        for c in range(kc):
            for n0 in range(0, n, step):
                n1 = min(n, n0 + step)
                stg = ar.alloc("stg", [128, n1 - n0], F32)
                A("sp", "dma_start", [], [stg.res], is_dma=True, out=stg.ap, in_=src[:, c, n0:n1])
                if scale_cols is not None:
                    A("pool", "tensor_scalar", [stg.res, pk.res], [b.res], out=b.ap[:, c, n0:n1], in0=stg.ap,
                      scalar1=pk.ap[:, l, scale_cols + c:scale_cols + c + 1], scalar2=1.0, op0=ALU.mult, op1=ALU.mult)
                else:
                    A("pool", "tensor_copy", [stg.res], [b.res], out=b.ap[:, c, n0:n1], in_=stg.ap)
                ar.free(stg)
        return b

    def transposes(srcs, src_res, dst_ap, dst_res, rows=128):
        bk = tp_bank()
        pv = banks[bk][:].bitcast(BF16).rearrange("p (a b) -> p a b", b=128)
        for i, s_ap in enumerate(srcs):
            A("pe", "transpose", [src_res, ident.res], [bank_res[bk]], out=pv[0:rows, i, :], in_=s_ap,
              identity=ident.ap)
        copy(copy_eng(), [bank_res[bk]], [dst_res], dst_ap, pv[0:rows, 0:len(srcs), :])

    def attention(QT, KT, krows, roff, V_of_kc, v_res, nkc, dv, scale, evac):
        pending = [None]

        def epilogue(qb, ab):
            oT = ar.alloc("oT", [128, 512], F32)
            A("dve", "tensor_copy", [bank_res[ab]], [oT.res], out=oT.ap[0:dv + 1, :], in_=banks[ab][0:dv + 1, :])

            def pe_part():
                tb = mm_bank()
                for j in range(4):
                    A("pe", "transpose", [oT.res, identF.res], [bank_res[tb]],
                      out=banks[tb][:, j * (dv + 1):(j + 1) * (dv + 1)], in_=oT.ap[0:dv + 1, j * 128:(j + 1) * 128],
                      identity=identF.ap[0:dv + 1, 0:dv + 1])
                ar.free(oT)
                evac(qb, banks[tb][:, 0:4 * (dv + 1)].rearrange("p (j e) -> p j e", e=dv + 1), bank_res[tb])
            return pe_part

        for qb in range(4):
            ab = mm_bank()
            pts = {}

            def issue_s(kc, qb=qb, pts=pts):
                sb_ = s_bank()
                A("pe", "matmul", [QT[1], KT[1]], [bank_res[sb_]], banks[sb_][:, :],
                  lhsT=KT[0][roff:roff + krows, kc * 128:(kc + 1) * 128],
                  rhs=QT[0][roff:roff + krows, qb * 512:(qb + 1) * 512], start=True, stop=True,
                  **({"tile_position": (roff, 0)} if krows == 32 else {}))
                pt = ar.alloc("pt", [128, 512], BF16)
                A("act", "activation", [bank_res[sb_]], [pt.res], out=pt.ap, in_=banks[sb_][:, :], func=AF.Exp,
                  scale=scale)
                pts[kc] = pt

            def issue_pv(kc, ab=ab, pts=pts):
                pt = pts.pop(kc)
                A("pe", "matmul", [pt.res, v_res], [bank_res[ab]], banks[ab][0:dv + 1, :],
                  lhsT=V_of_kc(kc), rhs=pt.ap, start=(kc == 0), stop=(kc == nkc - 1))
                ar.free(pt)

            issue_s(0)
            issue_s(1)
            for kc in range(nkc):
                if kc + 2 < nkc:
                    issue_s(kc + 2)
                issue_pv(kc)
                if kc == 2 and pending[0] is not None:
                    pending[0]()
                    pending[0] = None
            if pending[0] is not None:
                pending[0]()
            pending[0] = epilogue(qb, ab)
        pending[0]()

    for l in range(n_layers):
        if stop_after == "setup":
            break
        lam_init = 0.8 - 0.6 * math.exp(-0.3 * l)
        bcs = ar.alloc("bcs", [128, 448], F32)
        A("sp", "dma_start", [], [bcs.res], is_dma=True, out=bcs.ap, in_=bc_d[l:l + 1, 3072:3520].partition_broadcast(128))
        lamt = ar.alloc("lamt", [128, 8], F32)
        ltmp = ar.alloc("ltmp", [128, 64], F32)
        A("dve", "tensor_tensor", [bcs.res], [ltmp.res], out=ltmp.ap[:, 0:32], in0=bcs.ap[:, 320:352],
          in1=bcs.ap[:, 352:384], op=ALU.mult)
        A("dve", "tensor_tensor", [bcs.res], [ltmp.res], out=ltmp.ap[:, 32:64], in0=bcs.ap[:, 384:416],
          in1=bcs.ap[:, 416:448], op=ALU.mult)
        A("dve", "tensor_reduce", [ltmp.res], [lamt.res], out=lamt.ap[:, 0:2],
          in_=ltmp.ap.rearrange("p (a b) -> p a b", b=32), axis=AX.X, op=ALU.add)
        A("act", "activation", [lamt.res], [lamt.res], out=lamt.ap[:, 2:4], in_=lamt.ap[:, 0:2], func=AF.Exp)
        A("dve", "tensor_tensor", [lamt.res], [lamt.res], out=lamt.ap[:, 4:5], in0=lamt.ap[:, 3:4], in1=lamt.ap[:, 2:3],
          op=ALU.subtract)
        A("dve", "tensor_scalar", [lamt.res], [lamt.res], out=lamt.ap[:, 5:6], in0=lamt.ap[:, 4:5], scalar1=-lam_init,
          scalar2=None, op0=ALU.add)
        neglam = lamt.ap[:, 5:6]
        A("dve", "tensor_scalar", [bcs.res], [bcs.res], out=bcs.ap[:, 256:320], in0=bcs.ap[:, 256:320],
          scalar1=1.0 - lam_init, scalar2=None, op0=ALU.mult)
        ar.free(ltmp)

        w_in_t = load_w("w_in", [128, 8, INW], w_in_d[l].rearrange("(c p) n -> p c n", p=128), scale_cols=0, l=l)
        wsT = load_w("wsT", [128, 1, 512], wsT_d[l].rearrange("p (a n) -> p a n", a=1))

        ss1 = ar.alloc("ss1", [128, NT], F32)
        rs1 = ar.alloc("rs1", [128, NT], F32)
        for t in range(NT):
            A("act", "activation", [x_res[t]], [junk.res, ss1.res], out=junk.ap, in_=xs[:, t, :], func=AF.Square,
              accum_out=ss1.ap[:, t:t + 1])
        rstd_from((ss1.ap, ss1.res), NT, D, (rs1.ap, rs1.res))
        cqnT = ar.alloc("cqnT", [128, 3, S], BF16)
        ckvnT = ar.alloc("ckvnT", [128, 2, S], BF16)
        krtm = ar.alloc("krtm", [128, NT, 32], BF16)
        dqT = ar.alloc("dqT", [128, 2, S], BF16)
        dkT = ar.alloc("dkT", [128, 2, S], BF16)
        dV = ar.alloc("dV", [128, NT * 4, 65], BF16)
        A("pool", "memset", [], [dV.res], dV.ap[:, :, 64:65], 1.0)
        cb = ar.alloc("cb", [128, NT, 256], BF16)
        COLS = ((0, 384), (384, 672), (672, 1184), (1184, 1696), (1696, 1952))
        for t in range(NT):
            tc_ = slice(t * 128, (t + 1) * 128)
            hn = ar.alloc("hn", [128, D], BF16)
            A("dve", "tensor_scalar", [x_res[t], rs1.res], [hn.res], out=hn.ap, in0=xs[:, t, :],
              scalar1=rs1.ap[:, t:t + 1], scalar2=None, op0=ALU.mult)
            hT = ar.alloc("hT", [128, 8, 128], BF16)
            transposes([hn.ap[:, c * 128:(c + 1) * 128] for c in range(8)], hn.res, hT.ap, hT.res)
            ar.free(hn)
            pj = [mm_bank() for _ in range(5)]
            for gi, (c0, c1) in enumerate(COLS):
                for c in range(8):
                    A("pe", "matmul", [hT.res, w_in_t.res], [bank_res[pj[gi]]], banks[pj[gi]][:, 0:c1 - c0],
                      lhsT=hT.ap[:, c, :], rhs=w_in_t.ap[:, c, c0:c1], start=(c == 0), stop=(c == 7))
            ar.free(hT)
            ssq = ar.alloc("ssq", [128, 2], F32)
            rsq = ar.alloc("rsq", [128, 2], F32)
            A("act", "activation", [bank_res[pj[0]]], [junk.res, ssq.res], out=junk.ap[:, 0:384],
              in_=banks[pj[0]][:, 0:384], func=AF.Square, accum_out=ssq.ap[:, 0:1])
            A("act", "activation", [bank_res[pj[1]]], [junk.res, ssq.res], out=junk.ap[:, 0:256],
              in_=banks[pj[1]][:, 0:256], func=AF.Square, accum_out=ssq.ap[:, 1:2])
            A("act", "activation", [ssq.res, epsb.res], [rsq.res], out=rsq.ap[:, 0:1], in_=ssq.ap[:, 0:1], func=AF.Ln,
              scale=1.0 / 384, bias=epsb.ap[:, 0:1])
            A("act", "activation", [ssq.res, epsb.res], [rsq.res], out=rsq.ap[:, 1:2], in_=ssq.ap[:, 1:2], func=AF.Ln,
              scale=1.0 / 256, bias=epsb.ap[:, 0:1])
            A("act", "activation", [rsq.res], [rsq.res], out=rsq.ap, in_=rsq.ap, func=AF.Exp, scale=-0.5)
            cn = ar.alloc("cn", [128, 640], BF16)
            A("dve", "tensor_scalar", [bank_res[pj[0]], rsq.res], [cn.res], out=cn.ap[:, 0:384],
              in0=banks[pj[0]][:, 0:384], scalar1=rsq.ap[:, 0:1], scalar2=None, op0=ALU.mult)
            A("dve", "tensor_scalar", [bank_res[pj[1]], rsq.res], [cn.res], out=cn.ap[:, 384:640],
              in0=banks[pj[1]][:, 0:256], scalar1=rsq.ap[:, 1:2], scalar2=None, op0=ALU.mult)
            transposes([cn.ap[:, c * 128:(c + 1) * 128] for c in range(3)], cn.res, cqnT.ap[:, :, tc_], cqnT.res)
            transposes([cn.ap[:, 384 + c * 128:384 + (c + 1) * 128] for c in range(2)], cn.res, ckvnT.ap[:, :, tc_],
                       ckvnT.res)
            ar.free(ssq, rsq, cn)
            rt = ar.alloc("rt", [128, 4, 16], F32)
            kb = banks[pj[1]]
            kr_res = bank_res[pj[1]]
            cA = cosA.ap[:, t, :]
            sA = sinA.ap[:, t, :]
            A("dve", "tensor_tensor", [kr_res, cosA.res], [rt.res], out=rt.ap[:, 0, :], in0=kb[:, 256:272], in1=cA,
              op=ALU.mult)
            A("dve", "tensor_tensor", [kr_res, sinA.res], [rt.res], out=rt.ap[:, 1, :], in0=kb[:, 272:288], in1=sA,
              op=ALU.mult)
            A("dve", "tensor_tensor", [kr_res, cosA.res], [rt.res], out=rt.ap[:, 2, :], in0=kb[:, 272:288], in1=cA,
              op=ALU.mult)
            A("dve", "tensor_tensor", [kr_res, sinA.res], [rt.res], out=rt.ap[:, 3, :], in0=kb[:, 256:272], in1=sA,
              op=ALU.mult)
            A("dve", "tensor_tensor", [rt.res], [krtm.res], out=krtm.ap[:, t, 0:16], in0=rt.ap[:, 0, :],
              in1=rt.ap[:, 1, :], op=ALU.subtract)
            A("dve", "tensor_tensor", [rt.res], [krtm.res], out=krtm.ap[:, t, 16:32], in0=rt.ap[:, 2, :],
              in1=rt.ap[:, 3, :], op=ALU.add)
            ar.free(rt)
            zg = ar.alloc("zg", [128, 512], F32)
            A("act", "activation", [bank_res[pj[2]]], [zg.res], out=zg.ap, in_=banks[pj[2]][:, :],
              func=AF.Gelu_apprx_tanh)
            st = ar.alloc("st", [128, 16], F32)
            vsq = ar.alloc("vsq", [128, 256], F32)
            v3 = zg.ap[:, 256:512].rearrange("p (g c) -> p g c", c=64)
            A("dve", "tensor_reduce", [zg.res], [st.res], out=st.ap[:, 0:4], in_=v3, axis=AX.X, op=ALU.add)
            A("act", "activation", [zg.res], [vsq.res], out=vsq.ap, in_=zg.ap[:, 256:512], func=AF.Square)
            A("dve", "tensor_reduce", [vsq.res], [st.res], out=st.ap[:, 4:8],
              in_=vsq.ap.rearrange("p (g c) -> p g c", c=64), axis=AX.X, op=ALU.add)
            A("dve", "tensor_scalar", [st.res], [st.res], out=st.ap[:, 8:12], in0=st.ap[:, 0:4], scalar1=1.0 / 64,
              scalar2=None, op0=ALU.mult)
            A("dve", "tensor_tensor", [st.res], [st.res], out=st.ap[:, 12:16], in0=st.ap[:, 8:12], in1=st.ap[:, 8:12],
              op=ALU.mult)
            A("dve", "scalar_tensor_tensor", [st.res], [st.res], out=st.ap[:, 4:8], in0=st.ap[:, 4:8], scalar=1.0 / 64,
              in1=st.ap[:, 12:16], op0=ALU.mult, op1=ALU.subtract)
            A("act", "activation", [st.res, epsb.res], [st.res], out=st.ap[:, 0:4], in_=st.ap[:, 4:8], func=AF.Ln,
              bias=epsb.ap[:, 0:1])
            A("act", "activation", [st.res], [st.res], out=st.ap[:, 0:4], in_=st.ap[:, 0:4], func=AF.Exp, scale=-0.5)
            vq3 = vsq.ap.rearrange("p (g c) -> p g c", c=64)
            A("dve", "tensor_tensor", [zg.res, st.res], [vsq.res], out=vq3, in0=v3,
              in1=st.ap[:, 8:12].unsqueeze(2).broadcast_to([128, 4, 64]), op=ALU.subtract)
            A("dve", "tensor_tensor", [vsq.res, st.res], [vsq.res], out=vq3, in0=vq3,
              in1=st.ap[:, 0:4].unsqueeze(2).broadcast_to([128, 4, 64]), op=ALU.mult)
            vn = ar.alloc("vn", [128, 256], BF16)
            A("dve", "tensor_tensor", [vsq.res, bcs.res], [vn.res], out=vn.ap, in0=vsq.ap, in1=bcs.ap[:, 0:256],
              op=ALU.mult)
            mb = mm_bank()
            for g in range(4):
                A("pe", "matmul", [vn.res, wsT.res], [bank_res[mb]], banks[mb][:, g * 64:(g + 1) * 64],
                  lhsT=wsT.ap[:, 0, g * 128:(g + 1) * 128], rhs=vn.ap[:, g * 64:(g + 1) * 64], start=True, stop=True)
            for g in range(4):
                A("dve", "scalar_tensor_tensor", [bank_res[mb], pk.res, zg.res], [cb.res],
                  out=cb.ap[:, t, g * 64:(g + 1) * 64], in0=banks[mb][:, g * 64:(g + 1) * 64],
                  scalar=pk.ap[:, l, 37 + g:38 + g], in1=zg.ap[:, g * 64:(g + 1) * 64], op0=ALU.add, op1=ALU.mult)
            ar.free(zg, st, vsq, vn)
            qk = ar.alloc("qk", [128, 16, 32], BF16)
            r6 = ar.alloc("r6", [128, 4 * 16, 8], F32)
            pq = banks[pj[3]][:, :].rearrange("p (m d) -> p m d", d=32)
            qres = bank_res[pj[3]]
            cD = cosA.ap[:, t, 0:16:2].unsqueeze(1).broadcast_to([128, 16, 8])
            sD = sinA.ap[:, t, 0:16:2].unsqueeze(1).broadcast_to([128, 16, 8])
            A("dve", "tensor_tensor", [qres, cosA.res], [r6.res], out=r6.ap[:, 0:16, :], in0=pq[:, :, 0:8], in1=cD,
              op=ALU.mult)
            A("dve", "tensor_tensor", [qres, sinA.res], [r6.res], out=r6.ap[:, 16:32, :], in0=pq[:, :, 8:16], in1=sD,
              op=ALU.mult)
            A("dve", "tensor_tensor", [qres, cosA.res], [r6.res], out=r6.ap[:, 32:48, :], in0=pq[:, :, 8:16], in1=cD,
              op=ALU.mult)
            A("dve", "tensor_tensor", [qres, sinA.res], [r6.res], out=r6.ap[:, 48:64, :], in0=pq[:, :, 0:8], in1=sD,
              op=ALU.mult)
            A("dve", "tensor_tensor", [r6.res], [qk.res], out=qk.ap[:, :, 0:8], in0=r6.ap[:, 0:16, :],
              in1=r6.ap[:, 16:32, :], op=ALU.subtract)
            A("dve", "tensor_tensor", [r6.res], [qk.res], out=qk.ap[:, :, 8:16], in0=r6.ap[:, 32:48, :],
              in1=r6.ap[:, 48:64, :], op=ALU.add)
            A("act", "activation", [qres], [qk.res], out=qk.ap[:, :, 16:32], in_=pq[:, :, 16:32], func=AF.Copy)
            qk2 = qk.ap.rearrange("p m d -> p (m d)")
            transposes([qk2[:, c * 128:(c + 1) * 128] for c in range(2)], qk.res, dqT.ap[:, :, tc_], dqT.res)
            transposes([qk2[:, 256 + c * 128:256 + (c + 1) * 128] for c in range(2)], qk.res, dkT.ap[:, :, tc_],
                       dkT.res)
            ar.free(qk, r6)
            A("act", "activation", [bank_res[pj[4]]], [dV.res], out=dV.ap[:, t * 4:(t + 1) * 4, 0:64],
              in_=banks[pj[4]][:, 0:256].rearrange("p (h e) -> p h e", e=64), func=AF.Copy)
        ar.free(ss1, rs1, w_in_t, wsT)
        if stop_after == "p1":
            break

        cc = ar.alloc("cc", [128, NT, 256], BF16)
        w_mo_t = load_w("w_mo", [128, 8, D], w_mo_d[l].rearrange("(c p) n -> p c n", p=128))
        w_uq_t = load_w("w_uq", [128, 3, 768], w_uq_d[l].rearrange("(c p) n -> p c n", p=128), scale_cols=32, l=l)
        w_ukv_t = load_w("w_ukv", [128, 2, 1024], w_ukv_d[l].rearrange("(c p) n -> p c n", p=128), scale_cols=35, l=l)
        for h in range(4):
            ous = []
            for c in range(2):
                m = 2 * h + c
                ci, ro = m // 4, 32 * (m % 4)
                ou = ar.alloc("ou%d" % c, [128, NT, 65], F32)
                ous.append(ou)

                def evac(qb, acc, acc_res, ou=ou):
                    A("dve", "tensor_copy", [acc_res], [ou.res], out=ou.ap[:, qb * 4:(qb + 1) * 4, :], in_=acc)

                dkm = ar.alloc("dkm", [128, S], BF16)
                A("pool", "memset", [], [dkm.res], dkm.ap, 0.0)
                A("pool", "tensor_copy", [dkT.res], [dkm.res], out=dkm.ap[ro:ro + 32, :], in_=dkT.ap[ro:ro + 32, ci, :])
                attention((dqT.ap[:, ci, :], dqT.res), (dkm.ap, dkm.res), 128, 0,
                          lambda kc, h=h: dV.ap[:, kc * 4 + h, :], dV.res, NT, 64, 32 ** -0.5, evac)
                ar.free(dkm)
            o0, o1 = ous
            fr = ar.alloc("fr", [128, 4, NT], F32)
            A("dve", "reciprocal", [o0.res], [fr.res], out=fr.ap[:, 0, :], in_=o0.ap[:, :, 64])
            A("dve", "reciprocal", [o1.res], [fr.res], out=fr.ap[:, 1, :], in_=o1.ap[:, :, 64])
            A("dve", "tensor_scalar", [fr.res, lamt.res], [fr.res], out=fr.ap[:, 1, :], in0=fr.ap[:, 1, :],
              scalar1=neglam, scalar2=None, op0=ALU.mult)
            A("dve", "tensor_tensor", [o0.res, fr.res], [o0.res], out=o0.ap[:, :, 0:64], in0=o0.ap[:, :, 0:64],
              in1=fr.ap[:, 0, :].unsqueeze(2).broadcast_to([128, NT, 64]), op=ALU.mult)
            A("dve", "tensor_tensor", [o1.res, fr.res], [o1.res], out=o1.ap[:, :, 0:64], in0=o1.ap[:, :, 0:64],
              in1=fr.ap[:, 1, :].unsqueeze(2).broadcast_to([128, NT, 64]), op=ALU.mult)
            A("dve", "tensor_tensor", [o0.res, o1.res], [o0.res], out=o0.ap[:, :, 0:64], in0=o0.ap[:, :, 0:64],
              in1=o1.ap[:, :, 0:64], op=ALU.add)
            A("dve", "tensor_tensor", [o0.res], [o1.res], out=o1.ap[:, :, 0:64], in0=o0.ap[:, :, 0:64],
              in1=o0.ap[:, :, 0:64], op=ALU.mult)
            A("dve", "tensor_reduce", [o1.res], [fr.res], out=fr.ap[:, 2, :], in_=o1.ap[:, :, 0:64], axis=AX.X,
              op=ALU.add)
            A("act", "activation", [fr.res, epsb.res], [fr.res], out=fr.ap[:, 3, :], in_=fr.ap[:, 2, :], func=AF.Ln,
              scale=1.0 / 64, bias=epsb.ap[:, 0:1])
            A("act", "activation", [fr.res], [fr.res], out=fr.ap[:, 3, :], in_=fr.ap[:, 3, :], func=AF.Exp, scale=-0.5)
            A("dve", "tensor_tensor", [o0.res, fr.res], [o0.res], out=o0.ap[:, :, 0:64], in0=o0.ap[:, :, 0:64],
              in1=fr.ap[:, 3, :].unsqueeze(2).broadcast_to([128, NT, 64]), op=ALU.mult)
            A("dve", "tensor_tensor", [o0.res, bcs.res], [cc.res], out=cc.ap[:, :, h * 64:(h + 1) * 64],
              in0=o0.ap[:, :, 0:64], in1=bcs.ap[:, 256:320].unsqueeze(1).broadcast_to([128, NT, 64]), op=ALU.mult)
            ar.free(o0, o1, fr)
        ar.free(dqT, dkT, dV)
        if stop_after == "p2":
            break

        ca = ar.alloc("ca", [128, NT, 512], BF16)
        for h in range(8):
            QKT = ar.alloc("QKT", [128, 2, S], BF16)
            Vh = ar.alloc("Vh", [128, NT, 65], BF16)
            A("pool", "memset", [], [Vh.res], Vh.ap[:, :, 64:65], 1.0)
            for g in range(4):
                bA = mm_bank()
                bB = mm_bank()
                for tt in range(4):
                    t = g * 4 + tt
                    tc_ = slice(t * 128, (t + 1) * 128)
                    for c in range(3):
                        A("pe", "matmul", [cqnT.res, w_uq_t.res], [bank_res[bA]], banks[bA][:, tt * 96:(tt + 1) * 96],
                          lhsT=cqnT.ap[:, c, tc_], rhs=w_uq_t.ap[:, c, h * 96:(h + 1) * 96], start=(c == 0),
                          stop=(c == 2))
                    for c in range(2):
                        A("pe", "matmul", [ckvnT.res, w_ukv_t.res], [bank_res[bB]],
                          banks[bB][:, tt * 128:(tt + 1) * 128], lhsT=ckvnT.ap[:, c, tc_],
                          rhs=w_ukv_t.ap[:, c, h * 128:(h + 1) * 128], start=(c == 0), stop=(c == 1))
                qA = banks[bA][:, 0:384].rearrange("p (t d) -> p t d", d=96)
                kvB = banks[bB][:, :].rearrange("p (t d) -> p t d", d=128)
                rA, rB = bank_res[bA], bank_res[bB]
                qk = ar.alloc("qk", [128, 8, 128], BF16)
                qk4 = qk.ap.rearrange("p (t s) d -> p t s d", s=2)
                A("pool", "memset", [], [qk.res], qk.ap[:, :, 96:128], 0.0)
                A("act", "activation", [rA], [qk.res], out=qk4[:, :, 0, 0:64], in_=qA[:, :, 0:64], func=AF.Copy)
                A("act", "activation", [rB], [qk.res], out=qk4[:, :, 1, 0:64], in_=kvB[:, :, 0:64], func=AF.Copy)
                A("act", "activation", [rB], [Vh.res], out=Vh.ap[:, g * 4:(g + 1) * 4, 0:64], in_=kvB[:, :, 64:128],
                  func=AF.Copy)
                rt = ar.alloc("rt", [128, 16, 16], F32)
                cA = cosA.ap[:, g * 4:(g + 1) * 4, :]
                sA = sinA.ap[:, g * 4:(g + 1) * 4, :]
                A("dve", "tensor_tensor", [rA, cosA.res], [rt.res], out=rt.ap[:, 0:4, :], in0=qA[:, :, 64:80], in1=cA,
                  op=ALU.mult)
                A("dve", "tensor_tensor", [rA, sinA.res], [rt.res], out=rt.ap[:, 4:8, :], in0=qA[:, :, 80:96], in1=sA,
                  op=ALU.mult)
                A("dve", "tensor_tensor", [rA, cosA.res], [rt.res], out=rt.ap[:, 8:12, :], in0=qA[:, :, 80:96], in1=cA,
                  op=ALU.mult)
                A("dve", "tensor_tensor", [rA, sinA.res], [rt.res], out=rt.ap[:, 12:16, :], in0=qA[:, :, 64:80], in1=sA,
                  op=ALU.mult)
                A("dve", "tensor_tensor", [rt.res], [qk.res], out=qk4[:, :, 0, 64:80], in0=rt.ap[:, 0:4, :],
                  in1=rt.ap[:, 4:8, :], op=ALU.subtract)
                A("dve", "tensor_tensor", [rt.res], [qk.res], out=qk4[:, :, 0, 80:96], in0=rt.ap[:, 8:12, :],
                  in1=rt.ap[:, 12:16, :], op=ALU.add)
                A("dve", "tensor_copy", [krtm.res], [qk.res], out=qk4[:, :, 1, 64:96],
                  in_=krtm.ap[:, g * 4:(g + 1) * 4, :])
                bk = tp_bank()
                pv = banks[bk][:].bitcast(BF16).rearrange("p (a b) -> p a b", b=128)
                for i in range(8):
                    A("pe", "transpose", [qk.res, ident.res], [bank_res[bk]], out=pv[:, i, :], in_=qk.ap[:, i, :],
                      identity=ident.ap)
                copy(copy_eng(), [bank_res[bk]], [QKT.res],
                     QKT.ap[:, :, g * 512:(g + 1) * 512].rearrange("p s (t i) -> p s t i", i=128),
                     pv.rearrange("p (t s) i -> p s t i", s=2))
                ar.free(qk, rt)
            if stop_after == "p3a":
                break

            oua = ar.alloc("ou0", [128, NT, 65], F32)

            def evac_a(qb, acc, acc_res, oua=oua):
                A("dve", "tensor_copy", [acc_res], [oua.res], out=oua.ap[:, qb * 4:(qb + 1) * 4, :], in_=acc)

            attention((QKT.ap[:, 0, :], QKT.res), (QKT.ap[:, 1, :], QKT.res), 128, 0,
                      lambda kc, Vh=Vh: Vh.ap[:, kc, :], Vh.res, NT, 64, 96 ** -0.5, evac_a)
            fr = ar.alloc("fr", [128, 4, NT], F32)
            A("dve", "reciprocal", [oua.res], [fr.res], out=fr.ap[:, 0, :], in_=oua.ap[:, :, 64])
            A("dve", "tensor_tensor", [oua.res, fr.res], [ca.res], out=ca.ap[:, :, h * 64:(h + 1) * 64],
              in0=oua.ap[:, :, 0:64], in1=fr.ap[:, 0, :].unsqueeze(2).broadcast_to([128, NT, 64]), op=ALU.mult)
            ar.free(oua, fr)
            ar.free(QKT, Vh)
            if stop_after == "p3b":
                break
        if stop_after in ("p3a", "p3b"):
            break
        ar.free(cqnT, ckvnT, krtm, w_uq_t, w_ukv_t)
        if stop_after == "p3":
            break

        def post_norm_add(t, bk2, gpost):
            s2 = ar.alloc("s2", [128, 4], F32)
            for i in range(2):
                A("act", "activation", [bank_res[bk2[i]]], [junk.res, s2.res], out=junk.ap[:, 0:512],
                  in_=banks[bk2[i]][:, :], func=AF.Square, accum_out=s2.ap[:, i:i + 1])
            A("dve", "tensor_tensor", [s2.res], [s2.res], out=s2.ap[:, 2:3], in0=s2.ap[:, 0:1], in1=s2.ap[:, 1:2],
              op=ALU.add)
            A("act", "activation", [s2.res, epsb.res], [s2.res], out=s2.ap[:, 3:4], in_=s2.ap[:, 2:3], func=AF.Ln,
              scale=1.0 / D, bias=epsb.ap[:, 0:1])
            A("act", "activation", [s2.res], [s2.res], out=s2.ap[:, 3:4], in_=s2.ap[:, 3:4], func=AF.Exp, scale=-0.5)
            tmp = ar.alloc("tmp", [128, D], F32)
            for i in range(2):
                A("dve", "scalar_tensor_tensor", [bank_res[bk2[i]], s2.res, gpost.res], [tmp.res],
                  out=tmp.ap[:, i * 512:(i + 1) * 512], in0=banks[bk2[i]][:, :], scalar=s2.ap[:, 3:4],
                  in1=gpost.ap[:, i * 512:(i + 1) * 512], op0=ALU.mult, op1=ALU.mult)
            A("pool", "tensor_tensor", [tmp.res, x_res[t]], [x_res[t]], out=xs[:, t, :], in0=xs[:, t, :], in1=tmp.ap,
              op=ALU.add)
            ar.free(s2, tmp)

        gpost = ar.alloc("gpost", [128, D], F32)
        A("sp", "dma_start", [], [gpost.res], is_dma=True, out=gpost.ap, in_=bc_d[l:l + 1, 0:1024].partition_broadcast(128))
        if stop_after != "mix":
            wkv_v = w_kv_d[l].rearrange("(c p) n -> p c n", p=128)
            w_k_t = load_w("w_k", [128, 8, D], wkv_v[:, :, 0:D], scale_cols=24, l=l)
            w_v_ts = [load_w("w_v%d" % i, [128, 8, 512], wkv_v[:, :, D + i * 512:D + (i + 1) * 512], scale_cols=24, l=l)
                      for i in range(2)]
            memf = ar.alloc("memf", [128, 2, D], F32)
            A("sp", "dma_start", [], [memf.res], is_dma=True, out=memf.ap, in_=mem_d.rearrange("(t p) d -> p t d", p=128))
        for t in range(NT):
            cT = ar.alloc("cT", [128, 8, 128], BF16)
            srcs = [ca.ap[:, t, c * 128:(c + 1) * 128] for c in range(4)] + \
                   [cb.ap[:, t, c * 128:(c + 1) * 128] for c in range(2)] + \
                   [cc.ap[:, t, c * 128:(c + 1) * 128] for c in range(2)]
            bk = tp_bank()
            pv = banks[bk][:].bitcast(BF16).rearrange("p (a b) -> p a b", b=128)
            for i, s_ap in enumerate(srcs):
                rr = ca.res if i < 4 else (cb.res if i < 6 else cc.res)
                A("pe", "transpose", [rr, ident.res], [bank_res[bk]], out=pv[:, i, :], in_=s_ap, identity=ident.ap)
            copy(copy_eng(), [bank_res[bk]], [cT.res], cT.ap, pv[:, 0:8, :])
            bk2 = [mm_bank(), mm_bank()]
            for i in range(2):
                for c in range(8):
                    A("pe", "matmul", [cT.res, w_mo_t.res], [bank_res[bk2[i]]], banks[bk2[i]][:, :],
                      lhsT=cT.ap[:, c, :], rhs=w_mo_t.ap[:, c, i * 512:(i + 1) * 512], start=(c == 0), stop=(c == 7))
            ar.free(cT)
            post_norm_add(t, bk2, gpost)
        ar.free(ca, cb, cc, w_mo_t, gpost, bcs, lamt)
        if stop_after == "mix":
            break

        ssm = ar.alloc("ssm", [128, 2], F32)
        rsm = ar.alloc("rsm", [128, 2], F32)
        for t in range(2):
            A("act", "activation", [memf.res], [junk.res, ssm.res], out=junk.ap, in_=memf.ap[:, t, :], func=AF.Square,
              accum_out=ssm.ap[:, t:t + 1])
        rstd_from((ssm.ap, ssm.res), 2, D, (rsm.ap, rsm.res))
        memT = ar.alloc("memT", [128, 8, NMEM], BF16)
        for t in range(2):
            mn = ar.alloc("mn", [128, D], BF16)
            A("dve", "tensor_scalar", [memf.res, rsm.res], [mn.res], out=mn.ap, in0=memf.ap[:, t, :],
              scalar1=rsm.ap[:, t:t + 1], scalar2=None, op0=ALU.mult)
            transposes([mn.ap[:, c * 128:(c + 1) * 128] for c in range(8)], mn.res, memT.ap[:, :, t * 128:(t + 1) * 128],
                       memT.res)
            ar.free(mn)
        ar.free(memf, ssm, rsm)
        KmT = ar.alloc("KmT", [128, 8, NMEM], BF16)
        for hc in range(8):
            mb = mm_bank()
            for c in range(8):
                A("pe", "matmul", [memT.res, w_k_t.res], [bank_res[mb]], banks[mb][:, 0:NMEM],
                  lhsT=w_k_t.ap[:, c, hc * 128:(hc + 1) * 128], rhs=memT.ap[:, c, :], start=(c == 0), stop=(c == 7))
            copy(copy_eng(), [bank_res[mb]], [KmT.res], KmT.ap[:, hc, :], banks[mb][:, 0:NMEM])
        Vm = ar.alloc("Vm", [128, 8, 257], BF16)
        A("pool", "memset", [], [Vm.res], Vm.ap[:, :, 256:257], 1.0)
        for kt in range(2):
            for i in range(2):
                mb = mm_bank()
                for c in range(8):
                    A("pe", "matmul", [memT.res, w_v_ts[i].res], [bank_res[mb]], banks[mb][:, :],
                      lhsT=memT.ap[:, c, kt * 128:(kt + 1) * 128], rhs=w_v_ts[i].ap[:, c, :],
                      start=(c == 0), stop=(c == 7))
                copy(copy_eng(), [bank_res[mb]], [Vm.res], Vm.ap[:, kt * 4 + 2 * i:kt * 4 + 2 * i + 2, 0:256],
                     banks[mb][:, :].rearrange("p (h e) -> p h e", e=256))
        ar.free(memT, w_k_t, *w_v_ts)
        w_q_t = load_w("w_q", [128, 8, D], w_q_d[l].rearrange("(c p) n -> p c n", p=128), scale_cols=8, l=l)
        w_o_t = load_w("w_o", [128, 8, D], w_o_d[l].rearrange("(c p) n -> p c n", p=128))
        gpost = ar.alloc("gpost", [128, D], F32)
        A("sp", "dma_start", [], [gpost.res], is_dma=True, out=gpost.ap, in_=bc_d[l:l + 1, 1024:2048].partition_broadcast(128))
        ss5 = ar.alloc("ss5", [128, NT], F32)
        rs5 = ar.alloc("rs5", [128, NT], F32)
        for t in range(NT):
            A("act", "activation", [x_res[t]], [junk.res, ss5.res], out=junk.ap, in_=xs[:, t, :], func=AF.Square,
              accum_out=ss5.ap[:, t:t + 1])
        rstd_from((ss5.ap, ss5.res), NT, D, (rs5.ap, rs5.res))
        for qb in range(4):
            hTb = ar.alloc("hTb", [128, 8, 512], BF16)
            for j in range(4):
                t = qb * 4 + j
                hn = ar.alloc("hn", [128, D], BF16)
                A("dve", "tensor_scalar", [x_res[t], rs5.res], [hn.res], out=hn.ap, in0=xs[:, t, :],
                  scalar1=rs5.ap[:, t:t + 1], scalar2=None, op0=ALU.mult)
                transposes([hn.ap[:, c * 128:(c + 1) * 128] for c in range(8)], hn.res,
                           hTb.ap[:, :, j * 128:(j + 1) * 128], hTb.res)
                ar.free(hn)
            qTb = ar.alloc("qTb", [128, 8, 512], BF16)
            for hc in range(8):
                mb = mm_bank()
                for c in range(8):
                    A("pe", "matmul", [hTb.res, w_q_t.res], [bank_res[mb]], banks[mb][:, :],
                      lhsT=w_q_t.ap[:, c, hc * 128:(hc + 1) * 128], rhs=hTb.ap[:, c, :], start=(c == 0), stop=(c == 7))
                copy(copy_eng(), [bank_res[mb]], [qTb.res], qTb.ap[:, hc, :], banks[mb][:, :])
            ar.free(hTb)
            ao = ar.alloc("ao", [128, 4, D], BF16)
            for h in range(4):
                pts = []
                for kt in range(2):
                    sb_ = tp_bank()
                    for dc in range(2):
                        A("pe", "matmul", [qTb.res, KmT.res], [bank_res[sb_]], banks[sb_][:, :],
                          lhsT=KmT.ap[:, h * 2 + dc, kt * 128:(kt + 1) * 128], rhs=qTb.ap[:, h * 2 + dc, :],
                          start=(dc == 0), stop=(dc == 1))
                    pt = ar.alloc("ptm", [128, 512], BF16)
                    A("act", "activation", [bank_res[sb_]], [pt.res], out=pt.ap, in_=banks[sb_][:, :], func=AF.Exp,
                      scale=1.0 / 16.0)
                    pts.append(pt)
                for j in range(4):
                    mb = mm_bank()
                    for kt in range(2):
                        A("pe", "matmul", [pts[kt].res, Vm.res], [bank_res[mb]], banks[mb][:, 0:257],
                          lhsT=pts[kt].ap[:, j * 128:(j + 1) * 128], rhs=Vm.ap[:, kt * 4 + h, :], start=(kt == 0),
                          stop=(kt == 1))
                    rc = ar.alloc("rc", [128, 1], F32)
                    A("dve", "reciprocal", [bank_res[mb]], [rc.res], out=rc.ap[:, 0:1], in_=banks[mb][:, 256:257])
                    A("dve", "tensor_scalar", [bank_res[mb], rc.res], [ao.res], out=ao.ap[:, j, h * 256:(h + 1) * 256],
                      in0=banks[mb][:, 0:256], scalar1=rc.ap[:, 0:1], scalar2=None, op0=ALU.mult)
                    ar.free(rc)
                ar.free(*pts)
            ar.free(qTb)
            for j in range(4):
                t = qb * 4 + j
                aT = ar.alloc("aT", [128, 8, 128], BF16)
                transposes([ao.ap[:, j, c * 128:(c + 1) * 128] for c in range(8)], ao.res, aT.ap, aT.res)
                bk2 = [mm_bank(), mm_bank()]
                for i in range(2):
                    for c in range(8):
                        A("pe", "matmul", [aT.res, w_o_t.res], [bank_res[bk2[i]]], banks[bk2[i]][:, :],
                          lhsT=aT.ap[:, c, :], rhs=w_o_t.ap[:, c, i * 512:(i + 1) * 512], start=(c == 0), stop=(c == 7))
                ar.free(aT)
                post_norm_add(t, bk2, gpost)
            ar.free(ao)
        ar.free(KmT, Vm, w_q_t, w_o_t, gpost, ss5, rs5)
        if stop_after == "mem":
            break

        w_dn_t = load_w("w_dn", [128, NCH, D], w_dn_d[l].rearrange("(c p) n -> p c n", p=128))
        gpost = ar.alloc("gpost", [128, D], F32)
        A("sp", "dma_start", [], [gpost.res], is_dma=True, out=gpost.ap, in_=bc_d[l:l + 1, 2048:3072].partition_broadcast(128))
        ss6 = ar.alloc("ss6", [128, NT], F32)
        rs6 = ar.alloc("rs6", [128, NT], F32)
        for t in range(NT):
            A("act", "activation", [x_res[t]], [junk.res, ss6.res], out=junk.ap, in_=xs[:, t, :], func=AF.Square,
              accum_out=ss6.ap[:, t:t + 1])
        rstd_from((ss6.ap, ss6.res), NT, D, (rs6.ap, rs6.res))
        hl = ar.alloc("hl", [128, 8, 2], BF16)
        stgs = [ar.alloc("stgf", [128, 8, 256], F32) for _ in range(2)]
        wus = [ar.alloc("wuf", [128, 8, 256], BF16) for _ in range(3)]
        def issue_w(idx, l=l, stgs=stgs, wus=wus):
            if idx >= 4 * NCH:
                return
            jc_ = idx % NCH
            stg_, wu_ = stgs[idx % 2], wus[idx % 3]
            A("sp", "dma_start", [], [stg_.res], is_dma=True, out=stg_.ap,
              in_=w_up_d[l, jc_].rearrange("p (c n) -> p c n", n=256))
            A("pool", "tensor_tensor", [stg_.res, pk.res], [wu_.res], out=wu_.ap, in0=stg_.ap,
              in1=pk.ap[:, l, 16:24].unsqueeze(2).broadcast_to([128, 8, 256]), op=ALU.mult)

        issue_w(0)
        issue_w(1)
        for qb in range(4):
            hTe = ar.alloc("hTe", [128, 8, 514], BF16)
            if qb == 0:
                A("dve", "memset", [], [hTe.res], hTe.ap[:, :, 0:1], 0.0)
            if qb == 3:
                A("dve", "memset", [], [hTe.res], hTe.ap[:, :, 513:514], 0.0)
            tl = list(range(qb * 4, qb * 4 + 4))
            if qb > 0:
                A("dve", "tensor_copy", [hl.res], [hTe.res], out=hTe.ap[:, :, 0:1], in_=hl.ap[:, :, (qb - 1) % 2:(qb - 1) % 2 + 1])
            if qb < 3:
                tl = tl + [qb * 4 + 4]
            for t in tl:
                hn = ar.alloc("hn", [128, D], BF16)
                A("dve", "tensor_scalar", [x_res[t], rs6.res], [hn.res], out=hn.ap, in0=xs[:, t, :],
                  scalar1=rs6.ap[:, t:t + 1], scalar2=None, op0=ALU.mult)
                bk = tp_bank()
                pv = banks[bk][:].bitcast(BF16).rearrange("p (a b) -> p a b", b=128)
                for c in range(8):
                    A("pe", "transpose", [hn.res, ident.res], [bank_res[bk]], out=pv[:, c, :],
                      in_=hn.ap[:, c * 128:(c + 1) * 128], identity=ident.ap)
                j = t - qb * 4
                if j < 0:
                    copy(copy_eng(), [bank_res[bk]], [hTe.res], hTe.ap[:, :, 0:1], pv[:, 0:8, 127:128])
                elif j > 3:
                    copy(copy_eng(), [bank_res[bk]], [hTe.res], hTe.ap[:, :, 513:514], pv[:, 0:8, 0:1])
                else:
                    copy(copy_eng(), [bank_res[bk]], [hTe.res], hTe.ap[:, :, 1 + j * 128:1 + (j + 1) * 128], pv[:, 0:8, :])
                ar.free(hn)
            A("dve", "tensor_copy", [hTe.res], [hl.res], out=hl.ap[:, :, qb % 2:qb % 2 + 1], in_=hTe.ap[:, :, 512:513])
            mT = ar.alloc("mT", [128, NCH, 512], BF16)
            mm_all[0] = True
            for jc in range(NCH):
                wu = wus[(qb * NCH + jc) % 3]
                issue_w(qb * NCH + jc + 2)
                ys = []
                hb = mm_bank()
                for gu in range(2):
                    mb = mm_bank()
                    for c in range(8):
                        A("pe", "matmul", [hTe.res, wu.res], [bank_res[mb]], banks[mb][:, :],
                          lhsT=wu.ap[:, c, gu * 128:(gu + 1) * 128], rhs=hTe.ap[:, c, 1:513], start=(c == 0), stop=(c == 7))
                    for c in range(8):
                        A("pe", "matmul", [hTe.res, wu.res], [bank_res[hb]], banks[hb][:, 2 * gu:2 * gu + 2],
                          lhsT=wu.ap[:, c, gu * 128:(gu + 1) * 128], rhs=hTe.ap[:, c, 0:514:513], start=(c == 0),
                          stop=(c == 7))
                    ch = gu * NCH + jc
                    cw = pk.ap[:, l, 41 + ch * 4:41 + ch * 4 + 4]
                    y = ar.alloc("y", [128, 512], F32)
                    A("act", "activation", [bank_res[mb], pk.res], [y.res], out=y.ap, in_=banks[mb][:, :],
                      func=AF.Identity, scale=cw[:, 1:2], bias=cw[:, 3:4])
                    A("dve", "scalar_tensor_tensor", [bank_res[mb], pk.res, y.res], [y.res], out=y.ap[:, 1:512],
                      in0=banks[mb][:, 0:511], scalar=cw[:, 0:1], in1=y.ap[:, 1:512], op0=ALU.mult, op1=ALU.add)
                    A("dve", "scalar_tensor_tensor", [bank_res[hb], pk.res, y.res], [y.res], out=y.ap[:, 0:1],
                      in0=banks[hb][:, 2 * gu:2 * gu + 1], scalar=cw[:, 0:1], in1=y.ap[:, 0:1], op0=ALU.mult, op1=ALU.add)
                    A("dve", "scalar_tensor_tensor", [bank_res[mb], pk.res, y.res], [y.res], out=y.ap[:, 0:511],
                      in0=banks[mb][:, 1:512], scalar=cw[:, 2:3], in1=y.ap[:, 0:511], op0=ALU.mult, op1=ALU.add)
                    A("dve", "scalar_tensor_tensor", [bank_res[hb], pk.res, y.res], [y.res], out=y.ap[:, 511:512],
                      in0=banks[hb][:, 2 * gu + 1:2 * gu + 2], scalar=cw[:, 2:3], in1=y.ap[:, 511:512], op0=ALU.mult, op1=ALU.add)
                    ys.append(y)
                A("act", "activation", [ys[0].res], [ys[0].res], out=ys[0].ap, in_=ys[0].ap, func=AF.Gelu_apprx_tanh)
                A("pool", "tensor_tensor", [ys[0].res, ys[1].res], [mT.res], out=mT.ap[:, jc, :], in0=ys[0].ap,
                  in1=ys[1].ap, op=ALU.mult)
                ar.free(*ys)
            ar.free(hTe)
            for j in range(4):
                t = qb * 4 + j
                bk2 = [mm_bank(), mm_bank()]
                for i in range(2):
                    for jc in range(NCH):
                        A("pe", "matmul", [mT.res, w_dn_t.res], [bank_res[bk2[i]]], banks[bk2[i]][:, :],
                          lhsT=mT.ap[:, jc, j * 128:(j + 1) * 128], rhs=w_dn_t.ap[:, jc, i * 512:(i + 1) * 512],
                          start=(jc == 0), stop=(jc == NCH - 1))
                post_norm_add(t, bk2, gpost)
            ar.free(mT)
            mm_all[0] = False
        ar.free(w_dn_t, gpost, ss6, rs6, hl, *stgs, *wus)

    ov = out_d.rearrange("(t p) d -> p t d", p=128)
    last = []
    for t in range(NT):
        last.append(A("sp", "dma_start", [x_res[t]], [out_res[t]], is_dma=True, out=ov[:, t, :], in_=xs[:, t, :]))
    fin = A("sp", "nop", [out_res], [])

    counts = sc.finalize()
    nsem = {e: max(1, (counts[e] + SEM_LIMIT - 1) // SEM_LIMIT) for e in ("pe", "act", "dve", "pool", "sp")}
    sems = {e: [es.enter_context(nc.semaphore("s_%s%d" % (e, i))) for i in range(nsem[e])] for e in nsem}
    dsems = {q: [es.enter_context(nc.semaphore("d_%s%d" % (q, i))) for i in range(NDS)] for q in ("sp", "pool")}

    def emit(ename, eng):
        waited = {}
        for op in sc.ops[ename]:
            for d in op.deps:
                if d.is_dma:
                    sm, val = dsems[d.eng][d.dsem], d.dval
                    key = ("d", d.eng, d.dsem)
                else:
                    sm, val = sems[d.eng][d.semk // SEM_LIMIT], d.semk % SEM_LIMIT + 1
                    key = ("c", d.eng, d.semk // SEM_LIMIT)
                if waited.get(key, 0) >= val:
                    continue
                eng.wait_ge(sm, val)
                waited[key] = val
            if op.meth == "nop":
                continue
            ins = getattr(eng, op.meth)(*op.args, **op.kw)
            if op.is_dma:
                ins.then_inc(dsems[op.eng][op.dsem], 16)
            elif op.sig:
                ins.then_inc(sems[op.eng][op.semk // SEM_LIMIT], 1)

    block = es.enter_context(nc.Block())

    @block.sync
    def _(e):
        emit("sp", e)

    @block.gpsimd
    def _(e):
        emit("pool", e)

    @block.tensor
    def _(e):
        emit("pe", e)

    @block.scalar
    def _(e):
        emit("act", e)

    @block.vector
    def _(e):
        emit("dve", e)

    es.close()
    stats = {e: len(sc.ops[e]) for e in ENGS}
    stats["arena_peak_kb"] = ar.peak
    return nc, stats


def prep_inputs(inp, n_layers=DEPTH):
    f = lambda a: np.ascontiguousarray(np.asarray(a, dtype=np.float32))
    pk = np.zeros((128, DEPTH, 217), np.float32)
    for l in range(DEPTH):
        pk[:, l, 0:8] = np.asarray(inp["mix_pre_g"])[l].reshape(8, 128).T
        pk[:, l, 8:16] = np.asarray(inp["mem_pre_g"])[l].reshape(8, 128).T
        pk[:, l, 16:24] = np.asarray(inp["ffn_pre_g"])[l].reshape(8, 128).T
        pk[:, l, 24:32] = np.asarray(inp["mem_kv_g"])[l].reshape(8, 128).T
        pk[:, l, 32:35] = np.asarray(inp["mla_cq_g"])[l].reshape(3, 128).T
        pk[:, l, 35:37] = np.asarray(inp["mla_ckv_g"])[l].reshape(2, 128).T
        pk[:, l, 37:41] = np.asarray(inp["sgu_b_s"])[l].T
        cw = np.asarray(inp["ffn_conv_w"])[l]
        cbv = np.asarray(inp["ffn_conv_b"])[l]
        c4 = np.concatenate([cw, cbv[None, :]], axis=0)
        pk[:, l, 41:217] = c4.reshape(4, 44, 128).transpose(2, 1, 0).reshape(128, 176)
    bc = np.concatenate([
        np.asarray(inp["mix_post_g"]), np.asarray(inp["mem_post_g"]), np.asarray(inp["ffn_post_g"]),
        np.asarray(inp["sgu_norm_g"]).reshape(DEPTH, 256), np.asarray(inp["diff_sub_g"]),
        np.asarray(inp["diff_lam_q1"]), np.asarray(inp["diff_lam_k1"]),
        np.asarray(inp["diff_lam_q2"]), np.asarray(inp["diff_lam_k2"])], axis=1).astype(np.float32)
    wsT = np.asarray(inp["sgu_w_s"]).transpose(0, 3, 1, 2).reshape(DEPTH, 128, 512)
    nl = n_layers
    f = lambda a: np.ascontiguousarray(np.asarray(a, dtype=np.float32)[:nl])
    wup = np.asarray(inp["ffn_w_up"], dtype=np.float32)[:nl].reshape(nl, 8, 128, 2, NCH, 128)
    wup = np.ascontiguousarray(wup.transpose(0, 4, 2, 1, 3, 5)).reshape(nl, NCH, 128, 2048)
    shared = {
        "w_in": f(inp["w_in"]), "w_uq": f(inp["mla_w_uq"]), "w_ukv": f(inp["mla_w_ukv"]), "wsT": f(wsT),
        "w_mo": f(inp["w_mix_out"]), "w_q": f(inp["mem_w_q"]), "w_kv": f(inp["mem_w_kv"]), "w_o": f(inp["mem_w_o"]),
        "w_up": wup, "w_dn": f(inp["ffn_w_down"]),
        "pk": np.ascontiguousarray(pk), "bc": np.ascontiguousarray(bc),
    }
    x = np.asarray(inp["x"], dtype=np.float32)
    mem = np.asarray(inp["mem"], dtype=np.float32)
    pos = np.asarray(inp["positions"]).astype(np.int32)
    maps = []
    for b in range(x.shape[0]):
        m = dict(shared)
        m["x"] = np.ascontiguousarray(x[b])
        m["mem"] = np.ascontiguousarray(mem[b])
        m["pos"] = np.ascontiguousarray(pos[b].reshape(NT, 128))
        maps.append(m)
    return maps


def kernel(**inputs):
    maps = prep_inputs(inputs)
    nc, _ = build()
    res = run_bass_kernel_spmd(nc, maps, core_ids=list(range(8)))
    return np.stack([np.asarray(r["out"], dtype=np.float32) for r in res.results], axis=0)
```

```python
import math
from contextlib import ExitStack

import numpy as np
import concourse.bass as bass
import concourse.mybir as mybir
from concourse.bass_utils import run_bass_kernel_spmd

F32 = mybir.dt.float32
BF16 = mybir.dt.bfloat16
I32 = mybir.dt.int32
U8 = mybir.dt.uint8
AF = mybir.ActivationFunctionType
ALU = mybir.AluOpType
AX = mybir.AxisListType

D = 1024
S = 2048
NT = 16
DEPTH = 4
NMEM = 256
EPS = 1e-6
THETA = 500000.0
INW = 1952
DFF = 2816
NCH = 22

ENGS = ("pe", "act", "dve", "pool", "sp")
SEM_LIMIT = 1000
NDS = 12


class Res:
    __slots__ = ("lw", "rd", "rd_dma")

    def __init__(self):
        self.lw = None
        self.rd = {}
        self.rd_dma = []


class Op:
    __slots__ = ("eng", "meth", "args", "kw", "deps", "sig", "semk", "is_dma", "dsem", "dval", "idx")


class Sched:
    def __init__(self):
        self.ops = {e: [] for e in ENGS}
        self.ndma = {"sp": 0, "pool": 0}
        self.dma_hist = {"sp": [], "pool": []}
        self.n = 0

    def add(self, eng, meth, r, w, *args, is_dma=False, **kw):
        op = Op()
        op.eng, op.meth, op.args, op.kw = eng, meth, args, kw
        op.is_dma = is_dma
        op.sig = False
        op.semk = None
        op.idx = self.n
        op.dsem = None
        op.dval = None
        self.n += 1
        deps = {}

        def need(d):
            if d is None or d is op:
                return
            if (not d.is_dma) and d.eng == "pe" and eng == "pe" and not is_dma:
                return
            deps[d.idx] = d

        wset = set(id(x) for x in w)
        for x in r:
            need(x.lw)
        for x in w:
            need(x.lw)
            for d in x.rd.values():
                need(d)
            for d in x.rd_dma:
                need(d)
        if is_dma:
            q = self.dma_hist[eng]
            op.dsem = len(q) % NDS
            op.dval = 16 * (len(q) // NDS + 1)
            if len(q) >= NDS:
                need(q[len(q) - NDS])
            q.append(op)
        best = {}
        out = []
        for d in deps.values():
            if d.is_dma:
                out.append(d)
            else:
                b = best.get(d.eng)
                if b is None or d.idx > b.idx:
                    best[d.eng] = d
        out.extend(best.values())
        for d in out:
            d.sig = True
        op.deps = out
        for x in w:
            x.lw = op
            x.rd = {}
            x.rd_dma = []
        for x in r:
            if id(x) in wset:
                continue
            if is_dma:
                x.rd_dma.append(op)
            else:
                x.rd[eng] = op
        self.ops[eng].append(op)
        return op

    def finalize(self):
        for e in ENGS:
            k = 0
            for op in self.ops[e]:
                if op.is_dma:
                    continue
                if op.sig:
                    op.semk = k
                    k += 1
        return {e: sum(1 for o in self.ops[e] if (not o.is_dma) and o.sig) for e in ENGS}


class Buf:
    __slots__ = ("ap", "res", "off", "size", "name", "owner")


class Arena:
    CH = 1024

    def __init__(self, t, nbytes, base=0, nextfit=False):
        self.t = t
        self.base = base
        self.nextfit = nextfit
        self.ptr = 0
        self.nbytes = nbytes
        self.nch = nbytes // self.CH
        self.res = [Res() for _ in range(self.nch)]
        self.used = [False] * self.nch
        self.peak = 0

    def alloc(self, name, shape, dt):
        esz = 4 if dt in (F32, I32) else 2
        n = 1
        for s in shape[1:]:
            n *= s
        nb = n * esz
        k = (nb + self.CH - 1) // self.CH
        start = None
        order = [0]
        if self.nextfit:
            order = [self.ptr, 0]
        for s0 in order:
            run = 0
            for i in range(s0, self.nch):
                if not self.used[i]:
                    run += 1
                    if run == k:
                        start = i - k + 1
                        break
                else:
                    run = 0
            if start is not None:
                break
        if start is not None:
            self.ptr = start + k
        if start is None:
            raise RuntimeError("arena OOM for %s (%d B); used=%d" % (name, nb, sum(self.used)))
        for i in range(start, start + k):
            self.used[i] = True
        self.peak = max(self.peak, max(i for i in range(self.nch) if self.used[i]) + 1)
        b = Buf()
        b.name = name
        b.owner = self
        b.off = start
        b.size = k
        o = self.base + start * self.CH
        ap = self.t[:, o:o + nb].bitcast(dt)
        if len(shape) == 3:
            b.ap = ap.rearrange("p (a b) -> p a b", b=shape[2])
        else:
            b.ap = ap
        b.res = self.res[start:start + k]
        return b

    def free(self, *bufs):
        for b in bufs:
            for i in range(b.off, b.off + b.size):
                b.owner.used[i] = False


TRANS = {"dkm", "wuf", "stg", "oT", "hn", "hT", "ssq", "rsq", "cn", "rt", "zg", "st", "vsq", "vn", "qk", "r6", "pt", "rc", "o1", "s2", "tmp",
         "cT", "aT", "mn", "y", "wu", "ptm", "ltmp"}


class Arenas:
    def __init__(self, main, trans):
        self.main = main
        self.trans = trans

    def alloc(self, name, shape, dt):
        if name in TRANS:
            return self.trans.alloc(name, shape, dt)
        return self.main.alloc(name, shape, dt)

    def free(self, *bufs):
        self.main.free(*bufs)

    @property
    def peak(self):
        return (self.main.peak, self.trans.peak)


def build(n_layers=DEPTH, stop_after=None):
    nc = bass.Bass("TRN2", target_bir_lowering=False)

    def din(name, shape, dt=F32):
        return nc.dram_tensor(name, list(shape), dt, kind="ExternalInput").ap()

    WL = n_layers
    x_d = din("x", [S, D])
    mem_d = din("mem", [NMEM, D])
    pos_d = din("pos", [NT, 128], I32)
    w_in_d = din("w_in", [WL, D, INW])
    w_uq_d = din("w_uq", [WL, 384, 768])
    w_ukv_d = din("w_ukv", [WL, 256, 1024])
    wsT_d = din("wsT", [WL, 128, 512])
    w_mo_d = din("w_mo", [WL, D, D])
    w_q_d = din("w_q", [WL, D, D])
    w_kv_d = din("w_kv", [WL, D, 2 * D])
    w_o_d = din("w_o", [WL, D, D])
    w_up_d = din("w_up", [WL, NCH, 128, 2048])
    w_dn_d = din("w_dn", [WL, DFF, D])
    NPK = 217
    pk_d = din("pk", [128, DEPTH, NPK])
    NBC = 3520
    bc_d = din("bc", [DEPTH, NBC])
    out_d = nc.dram_tensor("out", [S, D], F32, kind="ExternalOutput").ap()

    sc = Sched()
    es = ExitStack()
    xs = es.enter_context(nc.sbuf_tensor("xs", [128, NT, D], F32))
    x_res = [Res() for _ in range(NT)]
    MAIN_B = 114 * 1024
    TRANS_B = 29 * 1024
    ar_t = es.enter_context(nc.sbuf_tensor("arena", [128, MAIN_B + TRANS_B], U8))
    ar = Arenas(Arena(ar_t, MAIN_B), Arena(ar_t, TRANS_B, base=MAIN_B, nextfit=True))
    banks = [es.enter_context(nc.psum_tensor("bank%d" % i, [128, 512], F32)) for i in range(8)]
    bank_res = [Res() for _ in range(8)]
    bank_ids = set(id(b) for b in bank_res)
    out_res = [Res() for _ in range(NT)]

    def A(eng, meth, r, w, *a, **k):
        rr = []
        for x in r:
            rr.extend(x if isinstance(x, (list, tuple)) else [x])
        ww = []
        for x in w:
            ww.extend(x if isinstance(x, (list, tuple)) else [x])
        for x in rr:
            if id(x) in bank_ids:
                ww.append(x)
        return sc.add(eng, meth, rr, ww, *a, **k)

    tp_i = [0]

    def tp_bank():
        tp_i[0] ^= 1
        return tp_i[0]

    mm_i = [0]

    mm_all = [False]

    def mm_bank():
        if mm_all[0]:
            mm_i[0] = (mm_i[0] + 1) % 8
            return mm_i[0]
        mm_i[0] = (mm_i[0] + 1) % 5
        return 3 + mm_i[0]

    sb_i = [0]

    def s_bank():
        sb_i[0] = (sb_i[0] + 1) % 3
        return sb_i[0]

    cp_i = [0]

    def copy_eng():
        cp_i[0] ^= 1
        return "act" if cp_i[0] else "dve"

    def copy(eng, r, w, out, in_):
        if eng == "act":
            A("act", "activation", r, w, out=out, in_=in_, func=AF.Copy)
        elif eng == "dve":
            A("dve", "tensor_copy", r, w, out=out, in_=in_)
        else:
            A("pool", "tensor_copy", r, w, out=out, in_=in_)

    ident = ar.alloc("ident", [128, 128], BF16)
    identf = ar.alloc("identf", [128, 128], F32)
    A("pool", "memset", [], [identf.res], identf.ap, 0.0)
    A("pool", "iota", [], [identf.res], identf.ap, pattern=[[1, 128]], base=0, channel_multiplier=-1,
      allow_small_or_imprecise_dtypes=True)
    A("dve", "tensor_scalar", [identf.res], [ident.res], out=ident.ap, in0=identf.ap, scalar1=0.0, scalar2=None,
      op0=ALU.is_equal)
    identF = ar.alloc("identF", [128, 128], F32)
    A("dve", "tensor_scalar", [identf.res], [identF.res], out=identF.ap, in0=identf.ap, scalar1=0.0, scalar2=None,
      op0=ALU.is_equal)
    epsb = ar.alloc("eps", [128, 1], F32)
    A("dve", "memset", [], [epsb.res], epsb.ap, EPS)
    zerob = ar.alloc("zero", [128, 1], F32)
    A("dve", "memset", [], [zerob.res], zerob.ap, 0.0)
    junk = ar.alloc("junk", [128, 1024], BF16)

    xv = x_d.rearrange("(t p) d -> p t d", p=128)
    for t in range(NT):
        A("sp", "dma_start", [], [x_res[t]], is_dma=True, out=xs[:, t, :], in_=xv[:, t, :])

    pk = ar.alloc("pk", [128, DEPTH, NPK], F32)
    A("sp", "dma_start", [], [pk.res], is_dma=True, out=pk.ap, in_=pk_d)

    posi = ar.alloc("posi", [128, NT], I32)
    for t in range(NT):
        A("sp", "dma_start", [], [posi.res], is_dma=True, out=posi.ap[:, t:t + 1],
          in_=pos_d[t:t + 1, :].rearrange("a p -> p a"))
    posf = ar.alloc("posf", [128, NT], F32)
    A("dve", "tensor_copy", [posi.res], [posf.res], out=posf.ap, in_=posi.ap)
    ang = ar.alloc("ang", [128, NT, 16], F32)
    for f in range(16):
        A("dve", "tensor_scalar", [posf.res], [ang.res], out=ang.ap[:, :, f], in0=posf.ap,
          scalar1=float(THETA ** (-2.0 * f / 32.0)), scalar2=None, op0=ALU.mult)
    cosA = ar.alloc("cosA", [128, NT, 16], F32)
    sinA = ar.alloc("sinA", [128, NT, 16], F32)
    TWO_PI = 2.0 * math.pi
    C1 = 6.28125
    C2 = TWO_PI - C1
    tA = ar.alloc("tA", [128, NT, 16], F32)
    tK = ar.alloc("tK", [128, NT, 16], I32)
    tKf = ar.alloc("tKf", [128, NT, 16], F32)
    tM = ar.alloc("tM", [128, NT, 16], F32)
    for (dst, shift) in ((sinA, 0.0), (cosA, math.pi / 2)):
        A("dve", "tensor_scalar", [ang.res], [tA.res], out=tA.ap, in0=ang.ap, scalar1=shift, scalar2=None,
          op0=ALU.add)
        A("dve", "tensor_scalar", [tA.res], [tM.res], out=tM.ap, in0=tA.ap, scalar1=1.0 / TWO_PI, scalar2=None,
          op0=ALU.mult)
        A("dve", "tensor_copy", [tM.res], [tK.res], out=tK.ap, in_=tM.ap)
        A("dve", "tensor_copy", [tK.res], [tKf.res], out=tKf.ap, in_=tK.ap)
        A("dve", "scalar_tensor_tensor", [tKf.res, tA.res], [tM.res], out=tM.ap, in0=tKf.ap, scalar=-C1,
          in1=tA.ap, op0=ALU.mult, op1=ALU.add)
        A("dve", "scalar_tensor_tensor", [tKf.res, tM.res], [tA.res], out=tA.ap, in0=tKf.ap, scalar=-C2,
          in1=tM.ap, op0=ALU.mult, op1=ALU.add)
        A("dve", "tensor_scalar", [tA.res], [tM.res], out=tM.ap, in0=tA.ap, scalar1=math.pi, scalar2=-TWO_PI,
          op0=ALU.is_gt, op1=ALU.mult)
        A("dve", "tensor_tensor", [tA.res, tM.res], [tKf.res], out=tKf.ap, in0=tA.ap, in1=tM.ap, op=ALU.add)
        A("dve", "tensor_scalar", [tKf.res], [tM.res], out=tM.ap, in0=tKf.ap, scalar1=-math.pi, scalar2=TWO_PI,
          op0=ALU.is_lt, op1=ALU.mult)
        A("dve", "tensor_tensor", [tKf.res, tM.res], [tA.res], out=tA.ap, in0=tKf.ap, in1=tM.ap, op=ALU.add)
        A("dve", "tensor_scalar", [tA.res], [tM.res], out=tM.ap, in0=tA.ap, scalar1=math.pi, scalar2=-math.pi,
          op0=ALU.min, op1=ALU.max)
        A("act", "activation", [tM.res], [dst.res], out=dst.ap, in_=tM.ap, func=AF.Sin)
    ar.free(tA, tK, tKf, tM, ang, posi, posf, identf)

    def rstd_from(ss, ncol, dim, dst):
        A("act", "activation", [ss[1], epsb.res], [dst[1]], out=dst[0], in_=ss[0], func=AF.Ln, scale=1.0 / dim,
          bias=epsb.ap[:, 0:1])
        A("act", "activation", [dst[1]], [dst[1]], out=dst[0], in_=dst[0], func=AF.Exp, scale=-0.5)

    def load_w(name, shape, src, scale_cols=None, l=0, chunk_cols=None):
        b = ar.alloc(name, shape, BF16)
        kc, n = shape[1], shape[2]
        step = 2048
        for c in range(kc):
            for n0 in range(0, n, step):
                n1 = min(n, n0 + step)
                stg = ar.alloc("stg", [128, n1 - n0], F32)
                A("sp", "dma_start", [], [stg.res], is_dma=True, out=stg.ap, in_=src[:, c, n0:n1])
                if scale_cols is not None:
                    A("pool", "tensor_scalar", [stg.res, pk.res], [b.res], out=b.ap[:, c, n0:n1], in0=stg.ap,
                      scalar1=pk.ap[:, l, scale_cols + c:scale_cols + c + 1], scalar2=1.0, op0=ALU.mult, op1=ALU.mult)
                else:
                    A("pool", "tensor_copy", [stg.res], [b.res], out=b.ap[:, c, n0:n1], in_=stg.ap)
                ar.free(stg)
        return b

    def transposes(srcs, src_res, dst_ap, dst_res, rows=128):
        bk = tp_bank()
        pv = banks[bk][:].bitcast(BF16).rearrange("p (a b) -> p a b", b=128)
        for i, s_ap in enumerate(srcs):
            A("pe", "transpose", [src_res, ident.res], [bank_res[bk]], out=pv[0:rows, i, :], in_=s_ap,
              identity=ident.ap)
        copy(copy_eng(), [bank_res[bk]], [dst_res], dst_ap, pv[0:rows, 0:len(srcs), :])

    def attention(QT, KT, krows, roff, V_of_kc, v_res, nkc, dv, scale, evac, vw=None):
        pending = [None]

        def epilogue(qb, ab):
            oT = ar.alloc("oT", [128, 512], F32)
            A("dve", "tensor_copy", [bank_res[ab]], [oT.res], out=oT.ap[0:dv + 1, :], in_=banks[ab][0:dv + 1, :])

            def pe_part():
                tb = mm_bank()
                for j in range(4):
                    A("pe", "transpose", [oT.res, identF.res], [bank_res[tb]],
                      out=banks[tb][:, j * (dv + 1):(j + 1) * (dv + 1)], in_=oT.ap[0:dv + 1, j * 128:(j + 1) * 128],
                      identity=identF.ap[0:dv + 1, 0:dv + 1])
                ar.free(oT)
                evac(qb, banks[tb][:, 0:4 * (dv + 1)].rearrange("p (j e) -> p j e", e=dv + 1), bank_res[tb])
            return pe_part

        for qb in range(4):
            ab = mm_bank()
            pts = {}

            def issue_s(kc, qb=qb, pts=pts):
                sb_ = s_bank()
                A("pe", "matmul", [QT[1], KT[1]], [bank_res[sb_]], banks[sb_][:, :],
                  lhsT=KT[0][roff:roff + krows, kc * 128:(kc + 1) * 128],
                  rhs=QT[0][roff:roff + krows, qb * 512:(qb + 1) * 512], start=True, stop=True,
                  **({"tile_position": (roff, 0)} if krows == 32 else {}))
                pt = ar.alloc("pt", [128, 512], BF16)
                A("act", "activation", [bank_res[sb_]], [pt.res], out=pt.ap, in_=banks[sb_][:, :], func=AF.Exp,
                  scale=scale)
                pts[kc] = pt

            def issue_pv(kc, ab=ab, pts=pts):
                pt = pts.pop(kc)
                A("pe", "matmul", [pt.res, v_res], [bank_res[ab]], banks[ab][0:(vw or dv + 1), :],
                  lhsT=V_of_kc(kc), rhs=pt.ap, start=(kc == 0), stop=(kc == nkc - 1))
                ar.free(pt)

            issue_s(0)
            issue_s(1)
            for kc in range(nkc):
                if kc + 2 < nkc:
                    issue_s(kc + 2)
                issue_pv(kc)
                if kc == 2 and pending[0] is not None:
                    pending[0]()
                    pending[0] = None
            if pending[0] is not None:
                pending[0]()
            pending[0] = epilogue(qb, ab)
        pending[0]()

    for l in range(n_layers):
        if stop_after == "setup":
            break
        lam_init = 0.8 - 0.6 * math.exp(-0.3 * l)
        bcs = ar.alloc("bcs", [128, 448], F32)
        A("sp", "dma_start", [], [bcs.res], is_dma=True, out=bcs.ap, in_=bc_d[l:l + 1, 3072:3520].partition_broadcast(128))
        lamt = ar.alloc("lamt", [128, 8], F32)
        ltmp = ar.alloc("ltmp", [128, 64], F32)
        A("dve", "tensor_tensor", [bcs.res], [ltmp.res], out=ltmp.ap[:, 0:32], in0=bcs.ap[:, 320:352],
          in1=bcs.ap[:, 352:384], op=ALU.mult)
        A("dve", "tensor_tensor", [bcs.res], [ltmp.res], out=ltmp.ap[:, 32:64], in0=bcs.ap[:, 384:416],
          in1=bcs.ap[:, 416:448], op=ALU.mult)
        A("dve", "tensor_reduce", [ltmp.res], [lamt.res], out=lamt.ap[:, 0:2],
          in_=ltmp.ap.rearrange("p (a b) -> p a b", b=32), axis=AX.X, op=ALU.add)
        A("act", "activation", [lamt.res], [lamt.res], out=lamt.ap[:, 2:4], in_=lamt.ap[:, 0:2], func=AF.Exp)
        A("dve", "tensor_tensor", [lamt.res], [lamt.res], out=lamt.ap[:, 4:5], in0=lamt.ap[:, 3:4], in1=lamt.ap[:, 2:3],
          op=ALU.subtract)
        A("dve", "tensor_scalar", [lamt.res], [lamt.res], out=lamt.ap[:, 5:6], in0=lamt.ap[:, 4:5], scalar1=-lam_init,
          scalar2=None, op0=ALU.add)
        neglam = lamt.ap[:, 5:6]
        A("dve", "tensor_scalar", [bcs.res], [bcs.res], out=bcs.ap[:, 256:320], in0=bcs.ap[:, 256:320],
          scalar1=1.0 - lam_init, scalar2=None, op0=ALU.mult)
        ar.free(ltmp)

        w_in_t = load_w("w_in", [128, 8, INW], w_in_d[l].rearrange("(c p) n -> p c n", p=128), scale_cols=0, l=l)
        wsT = load_w("wsT", [128, 1, 512], wsT_d[l].rearrange("p (a n) -> p a n", a=1))

        ss1 = ar.alloc("ss1", [128, NT], F32)
        rs1 = ar.alloc("rs1", [128, NT], F32)
        for t in range(NT):
            A("act", "activation", [x_res[t]], [junk.res, ss1.res], out=junk.ap, in_=xs[:, t, :], func=AF.Square,
              accum_out=ss1.ap[:, t:t + 1])
        rstd_from((ss1.ap, ss1.res), NT, D, (rs1.ap, rs1.res))
        cqnT = ar.alloc("cqnT", [128, 3, S], BF16)
        ckvnT = ar.alloc("ckvnT", [128, 2, S], BF16)
        krtm = ar.alloc("krtm", [128, NT, 32], BF16)
        dqT = ar.alloc("dqT", [128, 2, S], BF16)
        dkT = ar.alloc("dkT", [128, 2, S], BF16)
        dV = ar.alloc("dV", [128, NT * 4, 65], BF16)
        A("pool", "memset", [], [dV.res], dV.ap[:, :, 64:65], 1.0)
        cb = ar.alloc("cb", [128, NT, 256], BF16)
        COLS = ((0, 384), (384, 672), (672, 1184), (1184, 1696), (1696, 1952))
        for t in range(NT):
            tc_ = slice(t * 128, (t + 1) * 128)
            hn = ar.alloc("hn", [128, D], BF16)
            A("dve", "tensor_scalar", [x_res[t], rs1.res], [hn.res], out=hn.ap, in0=xs[:, t, :],
              scalar1=rs1.ap[:, t:t + 1], scalar2=None, op0=ALU.mult)
            hT = ar.alloc("hT", [128, 8, 128], BF16)
            transposes([hn.ap[:, c * 128:(c + 1) * 128] for c in range(8)], hn.res, hT.ap, hT.res)
            ar.free(hn)
            pj = [mm_bank() for _ in range(5)]
            for gi, (c0, c1) in enumerate(COLS):
                for c in range(8):
                    A("pe", "matmul", [hT.res, w_in_t.res], [bank_res[pj[gi]]], banks[pj[gi]][:, 0:c1 - c0],
                      lhsT=hT.ap[:, c, :], rhs=w_in_t.ap[:, c, c0:c1], start=(c == 0), stop=(c == 7))
            ar.free(hT)
            ssq = ar.alloc("ssq", [128, 2], F32)
            rsq = ar.alloc("rsq", [128, 2], F32)
            A("act", "activation", [bank_res[pj[0]]], [junk.res, ssq.res], out=junk.ap[:, 0:384],
              in_=banks[pj[0]][:, 0:384], func=AF.Square, accum_out=ssq.ap[:, 0:1])
            A("act", "activation", [bank_res[pj[1]]], [junk.res, ssq.res], out=junk.ap[:, 0:256],
              in_=banks[pj[1]][:, 0:256], func=AF.Square, accum_out=ssq.ap[:, 1:2])
            A("act", "activation", [ssq.res, epsb.res], [rsq.res], out=rsq.ap[:, 0:1], in_=ssq.ap[:, 0:1], func=AF.Ln,
              scale=1.0 / 384, bias=epsb.ap[:, 0:1])
            A("act", "activation", [ssq.res, epsb.res], [rsq.res], out=rsq.ap[:, 1:2], in_=ssq.ap[:, 1:2], func=AF.Ln,
              scale=1.0 / 256, bias=epsb.ap[:, 0:1])
            A("act", "activation", [rsq.res], [rsq.res], out=rsq.ap, in_=rsq.ap, func=AF.Exp, scale=-0.5)
            cn = ar.alloc("cn", [128, 640], BF16)
            A("dve", "tensor_scalar", [bank_res[pj[0]], rsq.res], [cn.res], out=cn.ap[:, 0:384],
              in0=banks[pj[0]][:, 0:384], scalar1=rsq.ap[:, 0:1], scalar2=None, op0=ALU.mult)
            A("dve", "tensor_scalar", [bank_res[pj[1]], rsq.res], [cn.res], out=cn.ap[:, 384:640],
              in0=banks[pj[1]][:, 0:256], scalar1=rsq.ap[:, 1:2], scalar2=None, op0=ALU.mult)
            transposes([cn.ap[:, c * 128:(c + 1) * 128] for c in range(3)], cn.res, cqnT.ap[:, :, tc_], cqnT.res)
            transposes([cn.ap[:, 384 + c * 128:384 + (c + 1) * 128] for c in range(2)], cn.res, ckvnT.ap[:, :, tc_],
                       ckvnT.res)
            ar.free(ssq, rsq, cn)
            rt = ar.alloc("rt", [128, 4, 16], F32)
            kb = banks[pj[1]]
            kr_res = bank_res[pj[1]]
            cA = cosA.ap[:, t, :]
            sA = sinA.ap[:, t, :]
            A("dve", "tensor_tensor", [kr_res, cosA.res], [rt.res], out=rt.ap[:, 0, :], in0=kb[:, 256:272], in1=cA,
              op=ALU.mult)
            A("dve", "tensor_tensor", [kr_res, sinA.res], [rt.res], out=rt.ap[:, 1, :], in0=kb[:, 272:288], in1=sA,
              op=ALU.mult)
            A("dve", "tensor_tensor", [kr_res, cosA.res], [rt.res], out=rt.ap[:, 2, :], in0=kb[:, 272:288], in1=cA,
              op=ALU.mult)
            A("dve", "tensor_tensor", [kr_res, sinA.res], [rt.res], out=rt.ap[:, 3, :], in0=kb[:, 256:272], in1=sA,
              op=ALU.mult)
            A("dve", "tensor_tensor", [rt.res], [krtm.res], out=krtm.ap[:, t, 0:16], in0=rt.ap[:, 0, :],
              in1=rt.ap[:, 1, :], op=ALU.subtract)
            A("dve", "tensor_tensor", [rt.res], [krtm.res], out=krtm.ap[:, t, 16:32], in0=rt.ap[:, 2, :],
              in1=rt.ap[:, 3, :], op=ALU.add)
            ar.free(rt)
            zg = ar.alloc("zg", [128, 512], F32)
            A("act", "activation", [bank_res[pj[2]]], [zg.res], out=zg.ap, in_=banks[pj[2]][:, :],
              func=AF.Gelu_apprx_tanh)
            st = ar.alloc("st", [128, 16], F32)
            vsq = ar.alloc("vsq", [128, 256], F32)
            v3 = zg.ap[:, 256:512].rearrange("p (g c) -> p g c", c=64)
            A("dve", "tensor_reduce", [zg.res], [st.res], out=st.ap[:, 0:4], in_=v3, axis=AX.X, op=ALU.add)
            A("act", "activation", [zg.res], [vsq.res], out=vsq.ap, in_=zg.ap[:, 256:512], func=AF.Square)
            A("dve", "tensor_reduce", [vsq.res], [st.res], out=st.ap[:, 4:8],
              in_=vsq.ap.rearrange("p (g c) -> p g c", c=64), axis=AX.X, op=ALU.add)
            A("dve", "tensor_scalar", [st.res], [st.res], out=st.ap[:, 8:12], in0=st.ap[:, 0:4], scalar1=1.0 / 64,
              scalar2=None, op0=ALU.mult)
            A("dve", "tensor_tensor", [st.res], [st.res], out=st.ap[:, 12:16], in0=st.ap[:, 8:12], in1=st.ap[:, 8:12],
              op=ALU.mult)
            A("dve", "scalar_tensor_tensor", [st.res], [st.res], out=st.ap[:, 4:8], in0=st.ap[:, 4:8], scalar=1.0 / 64,
              in1=st.ap[:, 12:16], op0=ALU.mult, op1=ALU.subtract)
            A("act", "activation", [st.res, epsb.res], [st.res], out=st.ap[:, 0:4], in_=st.ap[:, 4:8], func=AF.Ln,
              bias=epsb.ap[:, 0:1])
            A("act", "activation", [st.res], [st.res], out=st.ap[:, 0:4], in_=st.ap[:, 0:4], func=AF.Exp, scale=-0.5)
            vq3 = vsq.ap.rearrange("p (g c) -> p g c", c=64)
            A("dve", "tensor_tensor", [zg.res, st.res], [vsq.res], out=vq3, in0=v3,
              in1=st.ap[:, 8:12].unsqueeze(2).broadcast_to([128, 4, 64]), op=ALU.subtract)
            A("dve", "tensor_tensor", [vsq.res, st.res], [vsq.res], out=vq3, in0=vq3,
              in1=st.ap[:, 0:4].unsqueeze(2).broadcast_to([128, 4, 64]), op=ALU.mult)
            vn = ar.alloc("vn", [128, 256], BF16)
            A("dve", "tensor_tensor", [vsq.res, bcs.res], [vn.res], out=vn.ap, in0=vsq.ap, in1=bcs.ap[:, 0:256],
              op=ALU.mult)
            mb = mm_bank()
            for g in range(4):
                A("pe", "matmul", [vn.res, wsT.res], [bank_res[mb]], banks[mb][:, g * 64:(g + 1) * 64],
                  lhsT=wsT.ap[:, 0, g * 128:(g + 1) * 128], rhs=vn.ap[:, g * 64:(g + 1) * 64], start=True, stop=True)
            for g in range(4):
                A("dve", "scalar_tensor_tensor", [bank_res[mb], pk.res, zg.res], [cb.res],
                  out=cb.ap[:, t, g * 64:(g + 1) * 64], in0=banks[mb][:, g * 64:(g + 1) * 64],
                  scalar=pk.ap[:, l, 37 + g:38 + g], in1=zg.ap[:, g * 64:(g + 1) * 64], op0=ALU.add, op1=ALU.mult)
            ar.free(zg, st, vsq, vn)
            qk = ar.alloc("qk", [128, 16, 32], BF16)
            r6 = ar.alloc("r6", [128, 4 * 16, 8], F32)
            pq = banks[pj[3]][:, :].rearrange("p (m d) -> p m d", d=32)
            qres = bank_res[pj[3]]
            cD = cosA.ap[:, t, 0:16:2].unsqueeze(1).broadcast_to([128, 16, 8])
            sD = sinA.ap[:, t, 0:16:2].unsqueeze(1).broadcast_to([128, 16, 8])
            A("dve", "tensor_tensor", [qres, cosA.res], [r6.res], out=r6.ap[:, 0:16, :], in0=pq[:, :, 0:8], in1=cD,
              op=ALU.mult)
            A("dve", "tensor_tensor", [qres, sinA.res], [r6.res], out=r6.ap[:, 16:32, :], in0=pq[:, :, 8:16], in1=sD,
              op=ALU.mult)
            A("dve", "tensor_tensor", [qres, cosA.res], [r6.res], out=r6.ap[:, 32:48, :], in0=pq[:, :, 8:16], in1=cD,
              op=ALU.mult)
            A("dve", "tensor_tensor", [qres, sinA.res], [r6.res], out=r6.ap[:, 48:64, :], in0=pq[:, :, 0:8], in1=sD,
              op=ALU.mult)
            A("dve", "tensor_tensor", [r6.res], [qk.res], out=qk.ap[:, :, 0:8], in0=r6.ap[:, 0:16, :],
              in1=r6.ap[:, 16:32, :], op=ALU.subtract)
            A("dve", "tensor_tensor", [r6.res], [qk.res], out=qk.ap[:, :, 8:16], in0=r6.ap[:, 32:48, :],
              in1=r6.ap[:, 48:64, :], op=ALU.add)
            A("act", "activation", [qres], [qk.res], out=qk.ap[:, :, 16:32], in_=pq[:, :, 16:32], func=AF.Copy)
            qk2 = qk.ap.rearrange("p m d -> p (m d)")
            transposes([qk2[:, c * 128:(c + 1) * 128] for c in range(2)], qk.res, dqT.ap[:, :, tc_], dqT.res)
            transposes([qk2[:, 256 + c * 128:256 + (c + 1) * 128] for c in range(2)], qk.res, dkT.ap[:, :, tc_],
                       dkT.res)
            ar.free(qk, r6)
            A("act", "activation", [bank_res[pj[4]]], [dV.res], out=dV.ap[:, t * 4:(t + 1) * 4, 0:64],
              in_=banks[pj[4]][:, 0:256].rearrange("p (h e) -> p h e", e=64), func=AF.Copy)
        ar.free(ss1, rs1, w_in_t, wsT)
        if stop_after == "p1":
            break

        cc = ar.alloc("cc", [128, NT, 256], BF16)
        w_mo_t = load_w("w_mo", [128, 8, D], w_mo_d[l].rearrange("(c p) n -> p c n", p=128))
        w_uq_t = load_w("w_uq", [128, 3, 768], w_uq_d[l].rearrange("(c p) n -> p c n", p=128), scale_cols=32, l=l)
        w_ukv_t = load_w("w_ukv", [128, 2, 1024], w_ukv_d[l].rearrange("(c p) n -> p c n", p=128), scale_cols=35, l=l)
        for h in range(4):
            ous = []
            for c in range(2):
                m = 2 * h + c
                ci, ro = m // 4, 32 * (m % 4)
                ou = ar.alloc("ou%d" % c, [128, NT, 65], F32)
                ous.append(ou)

                def evac(qb, acc, acc_res, ou=ou):
                    A("dve", "tensor_copy", [acc_res], [ou.res], out=ou.ap[:, qb * 4:(qb + 1) * 4, :], in_=acc)

                dkm = ar.alloc("dkm", [128, S], BF16)
                A("pool", "memset", [], [dkm.res], dkm.ap, 0.0)
                A("pool", "tensor_copy", [dkT.res], [dkm.res], out=dkm.ap[ro:ro + 32, :], in_=dkT.ap[ro:ro + 32, ci, :])
                attention((dqT.ap[:, ci, :], dqT.res), (dkm.ap, dkm.res), 128, 0,
                          lambda kc, h=h: dV.ap[:, kc * 4 + h, :], dV.res, NT, 64, 32 ** -0.5, evac)
                ar.free(dkm)
            o0, o1 = ous
            fr = ar.alloc("fr", [128, 4, NT], F32)
            A("dve", "reciprocal", [o0.res], [fr.res], out=fr.ap[:, 0, :], in_=o0.ap[:, :, 64])
            A("dve", "reciprocal", [o1.res], [fr.res], out=fr.ap[:, 1, :], in_=o1.ap[:, :, 64])
            A("dve", "tensor_scalar", [fr.res, lamt.res], [fr.res], out=fr.ap[:, 1, :], in0=fr.ap[:, 1, :],
              scalar1=neglam, scalar2=None, op0=ALU.mult)
            A("dve", "tensor_tensor", [o0.res, fr.res], [o0.res], out=o0.ap[:, :, 0:64], in0=o0.ap[:, :, 0:64],
              in1=fr.ap[:, 0, :].unsqueeze(2).broadcast_to([128, NT, 64]), op=ALU.mult)
            A("dve", "tensor_tensor", [o1.res, fr.res], [o1.res], out=o1.ap[:, :, 0:64], in0=o1.ap[:, :, 0:64],
              in1=fr.ap[:, 1, :].unsqueeze(2).broadcast_to([128, NT, 64]), op=ALU.mult)
            A("dve", "tensor_tensor", [o0.res, o1.res], [o0.res], out=o0.ap[:, :, 0:64], in0=o0.ap[:, :, 0:64],
              in1=o1.ap[:, :, 0:64], op=ALU.add)
            A("dve", "tensor_tensor", [o0.res], [o1.res], out=o1.ap[:, :, 0:64], in0=o0.ap[:, :, 0:64],
              in1=o0.ap[:, :, 0:64], op=ALU.mult)
            A("dve", "tensor_reduce", [o1.res], [fr.res], out=fr.ap[:, 2, :], in_=o1.ap[:, :, 0:64], axis=AX.X,
              op=ALU.add)
            A("act", "activation", [fr.res, epsb.res], [fr.res], out=fr.ap[:, 3, :], in_=fr.ap[:, 2, :], func=AF.Ln,
              scale=1.0 / 64, bias=epsb.ap[:, 0:1])
            A("act", "activation", [fr.res], [fr.res], out=fr.ap[:, 3, :], in_=fr.ap[:, 3, :], func=AF.Exp, scale=-0.5)
            A("dve", "tensor_tensor", [o0.res, fr.res], [o0.res], out=o0.ap[:, :, 0:64], in0=o0.ap[:, :, 0:64],
              in1=fr.ap[:, 3, :].unsqueeze(2).broadcast_to([128, NT, 64]), op=ALU.mult)
            A("dve", "tensor_tensor", [o0.res, bcs.res], [cc.res], out=cc.ap[:, :, h * 64:(h + 1) * 64],
              in0=o0.ap[:, :, 0:64], in1=bcs.ap[:, 256:320].unsqueeze(1).broadcast_to([128, NT, 64]), op=ALU.mult)
            ar.free(o0, o1, fr)
        ar.free(dqT, dkT, dV)
        if stop_after == "p2":
            break

        ca = ar.alloc("ca", [128, NT, 512], BF16)
        for h in range(8):
            QKT = ar.alloc("QKT", [128, 2, S], BF16)
            Vh = ar.alloc("Vh", [128, NT, 128], BF16)
            A("pool", "memset", [], [Vh.res], Vh.ap[:, :, 64:128], 0.0)
            A("pool", "memset", [], [Vh.res], Vh.ap[:, :, 64:65], 1.0)
            for g in range(4):
                bA = mm_bank()
                bB = mm_bank()
                for tt in range(4):
                    t = g * 4 + tt
                    tc_ = slice(t * 128, (t + 1) * 128)
                    for c in range(3):
                        A("pe", "matmul", [cqnT.res, w_uq_t.res], [bank_res[bA]], banks[bA][:, tt * 96:(tt + 1) * 96],
                          lhsT=cqnT.ap[:, c, tc_], rhs=w_uq_t.ap[:, c, h * 96:(h + 1) * 96], start=(c == 0),
                          stop=(c == 2))
                    for c in range(2):
                        A("pe", "matmul", [ckvnT.res, w_ukv_t.res], [bank_res[bB]],
                          banks[bB][:, tt * 128:(tt + 1) * 128], lhsT=ckvnT.ap[:, c, tc_],
                          rhs=w_ukv_t.ap[:, c, h * 128:(h + 1) * 128], start=(c == 0), stop=(c == 1))
                qA = banks[bA][:, 0:384].rearrange("p (t d) -> p t d", d=96)
                kvB = banks[bB][:, :].rearrange("p (t d) -> p t d", d=128)
                rA, rB = bank_res[bA], bank_res[bB]
                qk = ar.alloc("qk", [128, 8, 128], BF16)
                qk4 = qk.ap.rearrange("p (t s) d -> p t s d", s=2)
                A("pool", "memset", [], [qk.res], qk.ap[:, :, 96:128], 0.0)
                A("act", "activation", [rA], [qk.res], out=qk4[:, :, 0, 0:64], in_=qA[:, :, 0:64], func=AF.Copy)
                A("act", "activation", [rB], [qk.res], out=qk4[:, :, 1, 0:64], in_=kvB[:, :, 0:64], func=AF.Copy)
                A("act", "activation", [rB], [Vh.res], out=Vh.ap[:, g * 4:(g + 1) * 4, 0:64], in_=kvB[:, :, 64:128],
                  func=AF.Copy)
                rt = ar.alloc("rt", [128, 16, 16], F32)
                cA = cosA.ap[:, g * 4:(g + 1) * 4, :]
                sA = sinA.ap[:, g * 4:(g + 1) * 4, :]
                A("dve", "tensor_tensor", [rA, cosA.res], [rt.res], out=rt.ap[:, 0:4, :], in0=qA[:, :, 64:80], in1=cA,
                  op=ALU.mult)
                A("dve", "tensor_tensor", [rA, sinA.res], [rt.res], out=rt.ap[:, 4:8, :], in0=qA[:, :, 80:96], in1=sA,
                  op=ALU.mult)
                A("dve", "tensor_tensor", [rA, cosA.res], [rt.res], out=rt.ap[:, 8:12, :], in0=qA[:, :, 80:96], in1=cA,
                  op=ALU.mult)
                A("dve", "tensor_tensor", [rA, sinA.res], [rt.res], out=rt.ap[:, 12:16, :], in0=qA[:, :, 64:80], in1=sA,
                  op=ALU.mult)
                A("dve", "tensor_tensor", [rt.res], [qk.res], out=qk4[:, :, 0, 64:80], in0=rt.ap[:, 0:4, :],
                  in1=rt.ap[:, 4:8, :], op=ALU.subtract)
                A("dve", "tensor_tensor", [rt.res], [qk.res], out=qk4[:, :, 0, 80:96], in0=rt.ap[:, 8:12, :],
                  in1=rt.ap[:, 12:16, :], op=ALU.add)
                A("dve", "tensor_copy", [krtm.res], [qk.res], out=qk4[:, :, 1, 64:96],
                  in_=krtm.ap[:, g * 4:(g + 1) * 4, :])
                bk = tp_bank()
                pv = banks[bk][:].bitcast(BF16).rearrange("p (a b) -> p a b", b=128)
                for i in range(8):
                    A("pe", "transpose", [qk.res, ident.res], [bank_res[bk]], out=pv[:, i, :], in_=qk.ap[:, i, :],
                      identity=ident.ap)
                copy(copy_eng(), [bank_res[bk]], [QKT.res],
                     QKT.ap[:, :, g * 512:(g + 1) * 512].rearrange("p s (t i) -> p s t i", i=128),
                     pv.rearrange("p (t s) i -> p s t i", s=2))
                ar.free(qk, rt)
            if stop_after == "p3a":
                break

            oua = ar.alloc("ou0", [128, NT, 65], F32)

            def evac_a(qb, acc, acc_res, oua=oua):
                A("dve", "tensor_copy", [acc_res], [oua.res], out=oua.ap[:, qb * 4:(qb + 1) * 4, :], in_=acc)

            attention((QKT.ap[:, 0, :], QKT.res), (QKT.ap[:, 1, :], QKT.res), 128, 0,
                      lambda kc, Vh=Vh: Vh.ap[:, kc, :], Vh.res, NT, 64, 96 ** -0.5, evac_a, vw=128)
            fr = ar.alloc("fr", [128, 4, NT], F32)
            A("dve", "reciprocal", [oua.res], [fr.res], out=fr.ap[:, 0, :], in_=oua.ap[:, :, 64])
            A("dve", "tensor_tensor", [oua.res, fr.res], [ca.res], out=ca.ap[:, :, h * 64:(h + 1) * 64],
              in0=oua.ap[:, :, 0:64], in1=fr.ap[:, 0, :].unsqueeze(2).broadcast_to([128, NT, 64]), op=ALU.mult)
            ar.free(oua, fr)
            ar.free(QKT, Vh)
            if stop_after == "p3b":
                break
        if stop_after in ("p3a", "p3b"):
            break
        ar.free(cqnT, ckvnT, krtm, w_uq_t, w_ukv_t)
        if stop_after == "p3":
            break

        def post_norm_add(t, bk2, gpost):
            s2 = ar.alloc("s2", [128, 4], F32)
            for i in range(2):
                A("act", "activation", [bank_res[bk2[i]]], [junk.res, s2.res], out=junk.ap[:, 0:512],
                  in_=banks[bk2[i]][:, :], func=AF.Square, accum_out=s2.ap[:, i:i + 1])
            A("dve", "tensor_tensor", [s2.res], [s2.res], out=s2.ap[:, 2:3], in0=s2.ap[:, 0:1], in1=s2.ap[:, 1:2],
              op=ALU.add)
            A("act", "activation", [s2.res, epsb.res], [s2.res], out=s2.ap[:, 3:4], in_=s2.ap[:, 2:3], func=AF.Ln,
              scale=1.0 / D, bias=epsb.ap[:, 0:1])
            A("act", "activation", [s2.res], [s2.res], out=s2.ap[:, 3:4], in_=s2.ap[:, 3:4], func=AF.Exp, scale=-0.5)
            tmp = ar.alloc("tmp", [128, D], F32)
            for i in range(2):
                A("dve", "scalar_tensor_tensor", [bank_res[bk2[i]], s2.res, gpost.res], [tmp.res],
                  out=tmp.ap[:, i * 512:(i + 1) * 512], in0=banks[bk2[i]][:, :], scalar=s2.ap[:, 3:4],
                  in1=gpost.ap[:, i * 512:(i + 1) * 512], op0=ALU.mult, op1=ALU.mult)
            A("pool", "tensor_tensor", [tmp.res, x_res[t]], [x_res[t]], out=xs[:, t, :], in0=xs[:, t, :], in1=tmp.ap,
              op=ALU.add)
            ar.free(s2, tmp)

        gpost = ar.alloc("gpost", [128, D], F32)
        A("sp", "dma_start", [], [gpost.res], is_dma=True, out=gpost.ap, in_=bc_d[l:l + 1, 0:1024].partition_broadcast(128))
        if stop_after != "mix":
            wkv_v = w_kv_d[l].rearrange("(c p) n -> p c n", p=128)
            w_k_t = load_w("w_k", [128, 8, D], wkv_v[:, :, 0:D], scale_cols=24, l=l)
            w_v_ts = [load_w("w_v%d" % i, [128, 8, 512], wkv_v[:, :, D + i * 512:D + (i + 1) * 512], scale_cols=24, l=l)
                      for i in range(2)]
            memf = ar.alloc("memf", [128, 2, D], F32)
            A("sp", "dma_start", [], [memf.res], is_dma=True, out=memf.ap, in_=mem_d.rearrange("(t p) d -> p t d", p=128))
        for t in range(NT):
            cT = ar.alloc("cT", [128, 8, 128], BF16)
            srcs = [ca.ap[:, t, c * 128:(c + 1) * 128] for c in range(4)] + \
                   [cb.ap[:, t, c * 128:(c + 1) * 128] for c in range(2)] + \
                   [cc.ap[:, t, c * 128:(c + 1) * 128] for c in range(2)]
            bk = tp_bank()
            pv = banks[bk][:].bitcast(BF16).rearrange("p (a b) -> p a b", b=128)
            for i, s_ap in enumerate(srcs):
                rr = ca.res if i < 4 else (cb.res if i < 6 else cc.res)
                A("pe", "transpose", [rr, ident.res], [bank_res[bk]], out=pv[:, i, :], in_=s_ap, identity=ident.ap)
            copy(copy_eng(), [bank_res[bk]], [cT.res], cT.ap, pv[:, 0:8, :])
            bk2 = [mm_bank(), mm_bank()]
            for i in range(2):
                for c in range(8):
                    A("pe", "matmul", [cT.res, w_mo_t.res], [bank_res[bk2[i]]], banks[bk2[i]][:, :],
                      lhsT=cT.ap[:, c, :], rhs=w_mo_t.ap[:, c, i * 512:(i + 1) * 512], start=(c == 0), stop=(c == 7))
            ar.free(cT)
            post_norm_add(t, bk2, gpost)
        ar.free(ca, cb, cc, w_mo_t, gpost, bcs, lamt)
        if stop_after == "mix":
            break

        w_q_t = load_w("w_q", [128, 8, D], w_q_d[l].rearrange("(c p) n -> p c n", p=128), scale_cols=8, l=l)
        w_o_t = load_w("w_o", [128, 8, D], w_o_d[l].rearrange("(c p) n -> p c n", p=128))
        ssm = ar.alloc("ssm", [128, 2], F32)
        rsm = ar.alloc("rsm", [128, 2], F32)
        for t in range(2):
            A("act", "activation", [memf.res], [junk.res, ssm.res], out=junk.ap, in_=memf.ap[:, t, :], func=AF.Square,
              accum_out=ssm.ap[:, t:t + 1])
        rstd_from((ssm.ap, ssm.res), 2, D, (rsm.ap, rsm.res))
        memT = ar.alloc("memT", [128, 8, NMEM], BF16)
        for t in range(2):
            mn = ar.alloc("mn", [128, D], BF16)
            A("dve", "tensor_scalar", [memf.res, rsm.res], [mn.res], out=mn.ap, in0=memf.ap[:, t, :],
              scalar1=rsm.ap[:, t:t + 1], scalar2=None, op0=ALU.mult)
            transposes([mn.ap[:, c * 128:(c + 1) * 128] for c in range(8)], mn.res, memT.ap[:, :, t * 128:(t + 1) * 128],
                       memT.res)
            ar.free(mn)
        ar.free(memf, ssm, rsm)
        KmT = ar.alloc("KmT", [128, 8, NMEM], BF16)
        for hc in range(8):
            mb = mm_bank()
            for c in range(8):
                A("pe", "matmul", [memT.res, w_k_t.res], [bank_res[mb]], banks[mb][:, 0:NMEM],
                  lhsT=w_k_t.ap[:, c, hc * 128:(hc + 1) * 128], rhs=memT.ap[:, c, :], start=(c == 0), stop=(c == 7))
            copy(copy_eng(), [bank_res[mb]], [KmT.res], KmT.ap[:, hc, :], banks[mb][:, 0:NMEM])
        Vm = ar.alloc("Vm", [128, 8, 257], BF16)
        A("pool", "memset", [], [Vm.res], Vm.ap[:, :, 256:257], 1.0)
        for kt in range(2):
            for i in range(2):
                mb = mm_bank()
                for c in range(8):
                    A("pe", "matmul", [memT.res, w_v_ts[i].res], [bank_res[mb]], banks[mb][:, :],
                      lhsT=memT.ap[:, c, kt * 128:(kt + 1) * 128], rhs=w_v_ts[i].ap[:, c, :],
                      start=(c == 0), stop=(c == 7))
                copy(copy_eng(), [bank_res[mb]], [Vm.res], Vm.ap[:, kt * 4 + 2 * i:kt * 4 + 2 * i + 2, 0:256],
                     banks[mb][:, :].rearrange("p (h e) -> p h e", e=256))
        ar.free(memT, w_k_t, *w_v_ts)
        gpost = ar.alloc("gpost", [128, D], F32)
        A("sp", "dma_start", [], [gpost.res], is_dma=True, out=gpost.ap, in_=bc_d[l:l + 1, 1024:2048].partition_broadcast(128))
        ss5 = ar.alloc("ss5", [128, NT], F32)
        rs5 = ar.alloc("rs5", [128, NT], F32)
        for t in range(NT):
            A("act", "activation", [x_res[t]], [junk.res, ss5.res], out=junk.ap, in_=xs[:, t, :], func=AF.Square,
              accum_out=ss5.ap[:, t:t + 1])
        rstd_from((ss5.ap, ss5.res), NT, D, (rs5.ap, rs5.res))
        hTbs = [ar.alloc("hTb", [128, 8, 512], BF16) for _ in range(2)]
        qTbs = [ar.alloc("qTb", [128, 8, 512], BF16) for _ in range(2)]

        def mem_front(qb):
            hTb = hTbs[qb % 2]
            for j in range(4):
                t = qb * 4 + j
                hn = ar.alloc("hn", [128, D], BF16)
                A("dve", "tensor_scalar", [x_res[t], rs5.res], [hn.res], out=hn.ap, in0=xs[:, t, :],
                  scalar1=rs5.ap[:, t:t + 1], scalar2=None, op0=ALU.mult)
                transposes([hn.ap[:, c * 128:(c + 1) * 128] for c in range(8)], hn.res,
                           hTb.ap[:, :, j * 128:(j + 1) * 128], hTb.res)
                ar.free(hn)
            qTb = qTbs[qb % 2]
            for hc in range(8):
                mb = mm_bank()
                for c in range(8):
                    A("pe", "matmul", [hTb.res, w_q_t.res], [bank_res[mb]], banks[mb][:, :],
                      lhsT=w_q_t.ap[:, c, hc * 128:(hc + 1) * 128], rhs=hTb.ap[:, c, :], start=(c == 0), stop=(c == 7))
                copy(copy_eng(), [bank_res[mb]], [qTb.res], qTb.ap[:, hc, :], banks[mb][:, :])

        def mem_back(qb):
            qTb = qTbs[qb % 2]
            ao = ar.alloc("ao", [128, 4, D], BF16)
            for h in range(4):
                pts = []
                for kt in range(2):
                    sb_ = tp_bank()
                    for dc in range(2):
                        A("pe", "matmul", [qTb.res, KmT.res], [bank_res[sb_]], banks[sb_][:, :],
                          lhsT=KmT.ap[:, h * 2 + dc, kt * 128:(kt + 1) * 128], rhs=qTb.ap[:, h * 2 + dc, :],
                          start=(dc == 0), stop=(dc == 1))
                    pt = ar.alloc("ptm", [128, 512], BF16)
                    A("act", "activation", [bank_res[sb_]], [pt.res], out=pt.ap, in_=banks[sb_][:, :], func=AF.Exp,
                      scale=1.0 / 16.0)
                    pts.append(pt)
                for j in range(4):
                    mb = mm_bank()
                    for kt in range(2):
                        A("pe", "matmul", [pts[kt].res, Vm.res], [bank_res[mb]], banks[mb][:, 0:257],
                          lhsT=pts[kt].ap[:, j * 128:(j + 1) * 128], rhs=Vm.ap[:, kt * 4 + h, :], start=(kt == 0),
                          stop=(kt == 1))
                    rc = ar.alloc("rc", [128, 1], F32)
                    A("dve", "reciprocal", [bank_res[mb]], [rc.res], out=rc.ap[:, 0:1], in_=banks[mb][:, 256:257])
                    A("dve", "tensor_scalar", [bank_res[mb], rc.res], [ao.res], out=ao.ap[:, j, h * 256:(h + 1) * 256],
                      in0=banks[mb][:, 0:256], scalar1=rc.ap[:, 0:1], scalar2=None, op0=ALU.mult)
                    ar.free(rc)
                ar.free(*pts)
            for j in range(4):
                t = qb * 4 + j
                aT = ar.alloc("aT", [128, 8, 128], BF16)
                transposes([ao.ap[:, j, c * 128:(c + 1) * 128] for c in range(8)], ao.res, aT.ap, aT.res)
                bk2 = [mm_bank(), mm_bank()]
                for i in range(2):
                    for c in range(8):
                        A("pe", "matmul", [aT.res, w_o_t.res], [bank_res[bk2[i]]], banks[bk2[i]][:, :],
                          lhsT=aT.ap[:, c, :], rhs=w_o_t.ap[:, c, i * 512:(i + 1) * 512], start=(c == 0), stop=(c == 7))
                ar.free(aT)
                post_norm_add(t, bk2, gpost)
            ar.free(ao)

        mem_front(0)
        for qb in range(4):
            if qb + 1 < 4:
                mem_front(qb + 1)
            mem_back(qb)
        ar.free(KmT, Vm, w_q_t, w_o_t, gpost, ss5, rs5, *hTbs, *qTbs)
        if stop_after == "mem":
            break

        w_dn_t = ar.alloc("w_dn", [128, NCH, D], BF16)
        w_dn_v = w_dn_d[l].rearrange("(c p) n -> p c n", p=128)

        def load_dn_chunk(c, w_dn_t=w_dn_t, w_dn_v=w_dn_v):
            stg = ar.alloc("stg", [128, D], F32)
            A("sp", "dma_start", [], [stg.res], is_dma=True, out=stg.ap, in_=w_dn_v[:, c, :])
            A("pool", "tensor_copy", [stg.res], [w_dn_t.res], out=w_dn_t.ap[:, c, :], in_=stg.ap)
            ar.free(stg)
        gpost = ar.alloc("gpost", [128, D], F32)
        A("sp", "dma_start", [], [gpost.res], is_dma=True, out=gpost.ap, in_=bc_d[l:l + 1, 2048:3072].partition_broadcast(128))
        ss6 = ar.alloc("ss6", [128, NT], F32)
        rs6 = ar.alloc("rs6", [128, NT], F32)
        for t in range(NT):
            A("act", "activation", [x_res[t]], [junk.res, ss6.res], out=junk.ap, in_=xs[:, t, :], func=AF.Square,
              accum_out=ss6.ap[:, t:t + 1])
        rstd_from((ss6.ap, ss6.res), NT, D, (rs6.ap, rs6.res))
        hl = ar.alloc("hl", [128, 8, 2], BF16)
        stgs = [ar.alloc("stgf", [128, 8, 256], F32) for _ in range(2)]
        wus = [ar.alloc("wuf", [128, 8, 256], BF16) for _ in range(3)]
        def issue_w(idx, l=l, stgs=stgs, wus=wus):
            if idx >= 4 * NCH:
                return
            jc_ = idx % NCH
            stg_, wu_ = stgs[idx % 2], wus[idx % 3]
            A("sp", "dma_start", [], [stg_.res], is_dma=True, out=stg_.ap,
              in_=w_up_d[l, jc_].rearrange("p (c n) -> p c n", n=256))
            A("pool", "tensor_tensor", [stg_.res, pk.res], [wu_.res], out=wu_.ap, in0=stg_.ap,
              in1=pk.ap[:, l, 16:24].unsqueeze(2).broadcast_to([128, 8, 256]), op=ALU.mult)

        issue_w(0)
        issue_w(1)
        def ffn_build(qb):
            hTe = ar.alloc("hTe", [128, 8, 514], BF16)
            if qb == 0:
                A("dve", "memset", [], [hTe.res], hTe.ap[:, :, 0:1], 0.0)
            if qb == 3:
                A("dve", "memset", [], [hTe.res], hTe.ap[:, :, 513:514], 0.0)
            tl = list(range(qb * 4, qb * 4 + 4))
            if qb > 0:
                A("dve", "tensor_copy", [hl.res], [hTe.res], out=hTe.ap[:, :, 0:1], in_=hl.ap[:, :, (qb - 1) % 2:(qb - 1) % 2 + 1])
            if qb < 3:
                tl = tl + [qb * 4 + 4]
            for t in tl:
                hn = ar.alloc("hn", [128, D], BF16)
                A("dve", "tensor_scalar", [x_res[t], rs6.res], [hn.res], out=hn.ap, in0=xs[:, t, :],
                  scalar1=rs6.ap[:, t:t + 1], scalar2=None, op0=ALU.mult)
                bk = tp_bank()
                pv = banks[bk][:].bitcast(BF16).rearrange("p (a b) -> p a b", b=128)
                for c in range(8):
                    A("pe", "transpose", [hn.res, ident.res], [bank_res[bk]], out=pv[:, c, :],
                      in_=hn.ap[:, c * 128:(c + 1) * 128], identity=ident.ap)
                j = t - qb * 4
                if j < 0:
                    copy(copy_eng(), [bank_res[bk]], [hTe.res], hTe.ap[:, :, 0:1], pv[:, 0:8, 127:128])
                elif j > 3:
                    copy(copy_eng(), [bank_res[bk]], [hTe.res], hTe.ap[:, :, 513:514], pv[:, 0:8, 0:1])
                else:
                    copy(copy_eng(), [bank_res[bk]], [hTe.res], hTe.ap[:, :, 1 + j * 128:1 + (j + 1) * 128], pv[:, 0:8, :])
                ar.free(hn)
            A("dve", "tensor_copy", [hTe.res], [hl.res], out=hl.ap[:, :, qb % 2:qb % 2 + 1], in_=hTe.ap[:, :, 512:513])
            return hTe

        def ffn_up(qb, hTe):
            mT = ar.alloc("mT", [128, NCH, 512], BF16)
            mm_all[0] = True
            for jc in range(NCH):
                wu = wus[(qb * NCH + jc) % 3]
                issue_w(qb * NCH + jc + 2)
                if qb == 0:
                    load_dn_chunk(jc)
                ys = []
                hb = mm_bank()
                for gu in range(2):
                    mb = mm_bank()
                    for c in range(8):
                        A("pe", "matmul", [hTe.res, wu.res], [bank_res[mb]], banks[mb][:, :],
                          lhsT=wu.ap[:, c, gu * 128:(gu + 1) * 128], rhs=hTe.ap[:, c, 1:513], start=(c == 0), stop=(c == 7))
                    for c in range(8):
                        A("pe", "matmul", [hTe.res, wu.res], [bank_res[hb]], banks[hb][:, 2 * gu:2 * gu + 2],
                          lhsT=wu.ap[:, c, gu * 128:(gu + 1) * 128], rhs=hTe.ap[:, c, 0:514:513], start=(c == 0),
                          stop=(c == 7))
                    ch = gu * NCH + jc
                    cw = pk.ap[:, l, 41 + ch * 4:41 + ch * 4 + 4]
                    y = ar.alloc("y", [128, 512], F32)
                    A("act", "activation", [bank_res[mb], pk.res], [y.res], out=y.ap, in_=banks[mb][:, :],
                      func=AF.Identity, scale=cw[:, 1:2], bias=cw[:, 3:4])
                    A("dve", "scalar_tensor_tensor", [bank_res[mb], pk.res, y.res], [y.res], out=y.ap[:, 1:512],
                      in0=banks[mb][:, 0:511], scalar=cw[:, 0:1], in1=y.ap[:, 1:512], op0=ALU.mult, op1=ALU.add)
                    A("dve", "scalar_tensor_tensor", [bank_res[hb], pk.res, y.res], [y.res], out=y.ap[:, 0:1],
                      in0=banks[hb][:, 2 * gu:2 * gu + 1], scalar=cw[:, 0:1], in1=y.ap[:, 0:1], op0=ALU.mult, op1=ALU.add)
                    A("dve", "scalar_tensor_tensor", [bank_res[mb], pk.res, y.res], [y.res], out=y.ap[:, 0:511],
                      in0=banks[mb][:, 1:512], scalar=cw[:, 2:3], in1=y.ap[:, 0:511], op0=ALU.mult, op1=ALU.add)
                    A("dve", "scalar_tensor_tensor", [bank_res[hb], pk.res, y.res], [y.res], out=y.ap[:, 511:512],
                      in0=banks[hb][:, 2 * gu + 1:2 * gu + 2], scalar=cw[:, 2:3], in1=y.ap[:, 511:512], op0=ALU.mult, op1=ALU.add)
                    ys.append(y)
                A("act", "activation", [ys[0].res], [ys[0].res], out=ys[0].ap, in_=ys[0].ap, func=AF.Gelu_apprx_tanh)
                A("pool", "tensor_tensor", [ys[0].res, ys[1].res], [mT.res], out=mT.ap[:, jc, :], in0=ys[0].ap,
                  in1=ys[1].ap, op=ALU.mult)
                ar.free(*ys)
            ar.free(hTe)
            return mT

        def ffn_down(qb, mT):
            for j in range(4):
                t = qb * 4 + j
                bk2 = [mm_bank(), mm_bank()]
                for i in range(2):
                    for jc in range(NCH):
                        A("pe", "matmul", [mT.res, w_dn_t.res], [bank_res[bk2[i]]], banks[bk2[i]][:, :],
                          lhsT=mT.ap[:, jc, j * 128:(j + 1) * 128], rhs=w_dn_t.ap[:, jc, i * 512:(i + 1) * 512],
                          start=(jc == 0), stop=(jc == NCH - 1))
                post_norm_add(t, bk2, gpost)
            ar.free(mT)
            mm_all[0] = False

        hTe_cur = ffn_build(0)
        for qb in range(4):
            mT_cur = ffn_up(qb, hTe_cur)
            if qb + 1 < 4:
                hTe_cur = ffn_build(qb + 1)
            ffn_down(qb, mT_cur)
        ar.free(w_dn_t, gpost, ss6, rs6, hl, *stgs, *wus)

    ov = out_d.rearrange("(t p) d -> p t d", p=128)
    last = []
    for t in range(NT):
        last.append(A("sp", "dma_start", [x_res[t]], [out_res[t]], is_dma=True, out=ov[:, t, :], in_=xs[:, t, :]))
    fin = A("sp", "nop", [out_res], [])

    counts = sc.finalize()
    nsem = {e: max(1, (counts[e] + SEM_LIMIT - 1) // SEM_LIMIT) for e in ("pe", "act", "dve", "pool", "sp")}
    sems = {e: [es.enter_context(nc.semaphore("s_%s%d" % (e, i))) for i in range(nsem[e])] for e in nsem}
    dsems = {q: [es.enter_context(nc.semaphore("d_%s%d" % (q, i))) for i in range(NDS)] for q in ("sp", "pool")}

    def emit(ename, eng):
        waited = {}
        for op in sc.ops[ename]:
            for d in op.deps:
                if d.is_dma:
                    sm, val = dsems[d.eng][d.dsem], d.dval
                    key = ("d", d.eng, d.dsem)
                else:
                    sm, val = sems[d.eng][d.semk // SEM_LIMIT], d.semk % SEM_LIMIT + 1
                    key = ("c", d.eng, d.semk // SEM_LIMIT)
                if waited.get(key, 0) >= val:
                    continue
                eng.wait_ge(sm, val)
                waited[key] = val
            if op.meth == "nop":
                continue
            ins = getattr(eng, op.meth)(*op.args, **op.kw)
            if op.is_dma:
                ins.then_inc(dsems[op.eng][op.dsem], 16)
            elif op.sig:
                ins.then_inc(sems[op.eng][op.semk // SEM_LIMIT], 1)

    block = es.enter_context(nc.Block())

    @block.sync
    def _(e):
        emit("sp", e)

    @block.gpsimd
    def _(e):
        emit("pool", e)

    @block.tensor
    def _(e):
        emit("pe", e)

    @block.scalar
    def _(e):
        emit("act", e)

    @block.vector
    def _(e):
        emit("dve", e)

    es.close()
    stats = {e: len(sc.ops[e]) for e in ENGS}
    stats["arena_peak_kb"] = ar.peak
    return nc, stats


def prep_inputs(inp, n_layers=DEPTH):
    f = lambda a: np.ascontiguousarray(np.asarray(a, dtype=np.float32))
    pk = np.zeros((128, DEPTH, 217), np.float32)
    for l in range(DEPTH):
        pk[:, l, 0:8] = np.asarray(inp["mix_pre_g"])[l].reshape(8, 128).T
        pk[:, l, 8:16] = np.asarray(inp["mem_pre_g"])[l].reshape(8, 128).T
        pk[:, l, 16:24] = np.asarray(inp["ffn_pre_g"])[l].reshape(8, 128).T
        pk[:, l, 24:32] = np.asarray(inp["mem_kv_g"])[l].reshape(8, 128).T
        pk[:, l, 32:35] = np.asarray(inp["mla_cq_g"])[l].reshape(3, 128).T
        pk[:, l, 35:37] = np.asarray(inp["mla_ckv_g"])[l].reshape(2, 128).T
        pk[:, l, 37:41] = np.asarray(inp["sgu_b_s"])[l].T
        cw = np.asarray(inp["ffn_conv_w"])[l]
        cbv = np.asarray(inp["ffn_conv_b"])[l]
        c4 = np.concatenate([cw, cbv[None, :]], axis=0)
        pk[:, l, 41:217] = c4.reshape(4, 44, 128).transpose(2, 1, 0).reshape(128, 176)
    bc = np.concatenate([
        np.asarray(inp["mix_post_g"]), np.asarray(inp["mem_post_g"]), np.asarray(inp["ffn_post_g"]),
        np.asarray(inp["sgu_norm_g"]).reshape(DEPTH, 256), np.asarray(inp["diff_sub_g"]),
        np.asarray(inp["diff_lam_q1"]), np.asarray(inp["diff_lam_k1"]),
        np.asarray(inp["diff_lam_q2"]), np.asarray(inp["diff_lam_k2"])], axis=1).astype(np.float32)
    wsT = np.asarray(inp["sgu_w_s"]).transpose(0, 3, 1, 2).reshape(DEPTH, 128, 512)
    nl = n_layers
    f = lambda a: np.ascontiguousarray(np.asarray(a, dtype=np.float32)[:nl])
    wup = np.asarray(inp["ffn_w_up"], dtype=np.float32)[:nl].reshape(nl, 8, 128, 2, NCH, 128)
    wup = np.ascontiguousarray(wup.transpose(0, 4, 2, 1, 3, 5)).reshape(nl, NCH, 128, 2048)
    shared = {
        "w_in": f(inp["w_in"]), "w_uq": f(inp["mla_w_uq"]), "w_ukv": f(inp["mla_w_ukv"]), "wsT": f(wsT),
        "w_mo": f(inp["w_mix_out"]), "w_q": f(inp["mem_w_q"]), "w_kv": f(inp["mem_w_kv"]), "w_o": f(inp["mem_w_o"]),
        "w_up": wup, "w_dn": f(inp["ffn_w_down"]),
        "pk": np.ascontiguousarray(pk), "bc": np.ascontiguousarray(bc),
    }
    x = np.asarray(inp["x"], dtype=np.float32)
    mem = np.asarray(inp["mem"], dtype=np.float32)
    pos = np.asarray(inp["positions"]).astype(np.int32)
    maps = []
    for b in range(x.shape[0]):
        m = dict(shared)
        m["x"] = np.ascontiguousarray(x[b])
        m["mem"] = np.ascontiguousarray(mem[b])
        m["pos"] = np.ascontiguousarray(pos[b].reshape(NT, 128))
        maps.append(m)
    return maps


def kernel(**inputs):
    maps = prep_inputs(inputs)
    nc, _ = build()
    res = run_bass_kernel_spmd(nc, maps, core_ids=list(range(8)))
    return np.stack([np.asarray(r["out"], dtype=np.float32) for r in res.results], axis=0)
```

```python
import math
from contextlib import ExitStack

import numpy as np
import concourse.bass as bass
import concourse.mybir as mybir
from concourse.bass_utils import run_bass_kernel_spmd

F32 = mybir.dt.float32
BF16 = mybir.dt.bfloat16
I32 = mybir.dt.int32
U8 = mybir.dt.uint8
AF = mybir.ActivationFunctionType
ALU = mybir.AluOpType
AX = mybir.AxisListType

D = 1024
S = 2048
NT = 16
DEPTH = 4
NMEM = 256
EPS = 1e-6
THETA = 500000.0
INW = 1952
DFF = 2816
NCH = 22

ENGS = ("pe", "act", "dve", "pool", "sp")
SEM_LIMIT = 1000
NDS = 12


class Res:
    __slots__ = ("lw", "rd", "rd_dma")

    def __init__(self):
        self.lw = None
        self.rd = {}
        self.rd_dma = []


class Op:
    __slots__ = ("eng", "meth", "args", "kw", "deps", "sig", "semk", "is_dma", "dsem", "dval", "idx")


class Sched:
    def __init__(self):
        self.ops = {e: [] for e in ENGS}
        self.ndma = {"sp": 0, "pool": 0}
        self.dma_hist = {"sp": [], "pool": []}
        self.n = 0

    def add(self, eng, meth, r, w, *args, is_dma=False, **kw):
        op = Op()
        op.eng, op.meth, op.args, op.kw = eng, meth, args, kw
        op.is_dma = is_dma
        op.sig = False
        op.semk = None
        op.idx = self.n
        op.dsem = None
        op.dval = None
        self.n += 1
        deps = {}

        def need(d):
            if d is None or d is op:
                return
            if (not d.is_dma) and d.eng == "pe" and eng == "pe" and not is_dma:
                return
            deps[d.idx] = d

        wset = set(id(x) for x in w)
        for x in r:
            need(x.lw)
        for x in w:
            need(x.lw)
            for d in x.rd.values():
                need(d)
            for d in x.rd_dma:
                need(d)
        if is_dma:
            q = self.dma_hist[eng]
            op.dsem = len(q) % NDS
            op.dval = 16 * (len(q) // NDS + 1)
            if len(q) >= NDS:
                need(q[len(q) - NDS])
            q.append(op)
        best = {}
        out = []
        for d in deps.values():
            if d.is_dma:
                out.append(d)
            else:
                b = best.get(d.eng)
                if b is None or d.idx > b.idx:
                    best[d.eng] = d
        out.extend(best.values())
        for d in out:
            d.sig = True
        op.deps = out
        for x in w:
            x.lw = op
            x.rd = {}
            x.rd_dma = []
        for x in r:
            if id(x) in wset:
                continue
            if is_dma:
                x.rd_dma.append(op)
            else:
                x.rd[eng] = op
        self.ops[eng].append(op)
        return op

    def finalize(self):
        for e in ENGS:
            k = 0
            for op in self.ops[e]:
                if op.is_dma:
                    continue
                if op.sig:
                    op.semk = k
                    k += 1
        return {e: sum(1 for o in self.ops[e] if (not o.is_dma) and o.sig) for e in ENGS}


class Buf:
    __slots__ = ("ap", "res", "off", "size", "name", "owner")


class Arena:
    CH = 1024

    def __init__(self, t, nbytes, base=0, nextfit=False):
        self.t = t
        self.base = base
        self.nextfit = nextfit
        self.ptr = 0
        self.nbytes = nbytes
        self.nch = nbytes // self.CH
        self.res = [Res() for _ in range(self.nch)]
        self.used = [False] * self.nch
        self.peak = 0

    def alloc(self, name, shape, dt):
        esz = 4 if dt in (F32, I32) else 2
        n = 1
        for s in shape[1:]:
            n *= s
        nb = n * esz
        k = (nb + self.CH - 1) // self.CH
        start = None
        order = [0]
        if self.nextfit:
            order = [self.ptr, 0]
        for s0 in order:
            run = 0
            for i in range(s0, self.nch):
                if not self.used[i]:
                    run += 1
                    if run == k:
                        start = i - k + 1
                        break
                else:
                    run = 0
            if start is not None:
                break
        if start is not None:
            self.ptr = start + k
        if start is None:
            raise RuntimeError("arena OOM for %s (%d B); used=%d" % (name, nb, sum(self.used)))
        for i in range(start, start + k):
            self.used[i] = True
        self.peak = max(self.peak, max(i for i in range(self.nch) if self.used[i]) + 1)
        b = Buf()
        b.name = name
        b.owner = self
        b.off = start
        b.size = k
        o = self.base + start * self.CH
        ap = self.t[:, o:o + nb].bitcast(dt)
        if len(shape) == 3:
            b.ap = ap.rearrange("p (a b) -> p a b", b=shape[2])
        else:
            b.ap = ap
        b.res = self.res[start:start + k]
        return b

    def free(self, *bufs):
        for b in bufs:
            for i in range(b.off, b.off + b.size):
                b.owner.used[i] = False


TRANS = {"dkm", "wuf", "stg", "oT", "hn", "hT", "ssq", "rsq", "cn", "rt", "zg", "st", "vsq", "vn", "qk", "r6", "pt", "rc", "o1", "s2", "tmp",
         "cT", "aT", "mn", "y", "wu", "ptm", "ltmp"}


class Arenas:
    def __init__(self, main, trans):
        self.main = main
        self.trans = trans

    def alloc(self, name, shape, dt):
        if name in TRANS:
            return self.trans.alloc(name, shape, dt)
        return self.main.alloc(name, shape, dt)

    def free(self, *bufs):
        self.main.free(*bufs)

    @property
    def peak(self):
        return (self.main.peak, self.trans.peak)


def build(n_layers=DEPTH, stop_after=None):
    nc = bass.Bass("TRN2", target_bir_lowering=False)

    def din(name, shape, dt=F32):
        return nc.dram_tensor(name, list(shape), dt, kind="ExternalInput").ap()

    WL = n_layers
    x_d = din("x", [S, D])
    mem_d = din("mem", [NMEM, D])
    pos_d = din("pos", [NT, 128], I32)
    w_in_d = din("w_in", [WL, D, INW])
    w_uq_d = din("w_uq", [WL, 384, 768])
    w_ukv_d = din("w_ukv", [WL, 256, 1024])
    wsT_d = din("wsT", [WL, 128, 512])
    w_mo_d = din("w_mo", [WL, D, D])
    w_q_d = din("w_q", [WL, D, D])
    w_kv_d = din("w_kv", [WL, D, 2 * D])
    w_o_d = din("w_o", [WL, D, D])
    w_up_d = din("w_up", [WL, NCH, 128, 2048])
    w_dn_d = din("w_dn", [WL, DFF, D])
    NPK = 217
    pk_d = din("pk", [128, DEPTH, NPK])
    NBC = 3520
    bc_d = din("bc", [DEPTH, NBC])
    out_d = nc.dram_tensor("out", [S, D], F32, kind="ExternalOutput").ap()

    sc = Sched()
    es = ExitStack()
    xs = es.enter_context(nc.sbuf_tensor("xs", [128, NT, D], F32))
    x_res = [Res() for _ in range(NT)]
    MAIN_B = 114 * 1024
    TRANS_B = 29 * 1024
    ar_t = es.enter_context(nc.sbuf_tensor("arena", [128, MAIN_B + TRANS_B], U8))
    ar = Arenas(Arena(ar_t, MAIN_B), Arena(ar_t, TRANS_B, base=MAIN_B, nextfit=True))
    banks = [es.enter_context(nc.psum_tensor("bank%d" % i, [128, 512], F32)) for i in range(8)]
    bank_res = [Res() for _ in range(8)]
    bank_ids = set(id(b) for b in bank_res)
    out_res = [Res() for _ in range(NT)]

    def A(eng, meth, r, w, *a, **k):
        rr = []
        for x in r:
            rr.extend(x if isinstance(x, (list, tuple)) else [x])
        ww = []
        for x in w:
            ww.extend(x if isinstance(x, (list, tuple)) else [x])
        for x in rr:
            if id(x) in bank_ids:
                ww.append(x)
        return sc.add(eng, meth, rr, ww, *a, **k)

    tp_i = [0]

    def tp_bank():
        tp_i[0] ^= 1
        return tp_i[0]

    mm_i = [0]

    mm_all = [False]

    def mm_bank():
        if mm_all[0]:
            mm_i[0] = (mm_i[0] + 1) % 8
            return mm_i[0]
        mm_i[0] = (mm_i[0] + 1) % 5
        return 3 + mm_i[0]

    sb_i = [0]

    def s_bank():
        sb_i[0] = (sb_i[0] + 1) % 3
        return sb_i[0]

    cp_i = [0]

    def copy_eng():
        cp_i[0] ^= 1
        return "act" if cp_i[0] else "dve"

    def copy(eng, r, w, out, in_):
        if eng == "act":
            A("act", "activation", r, w, out=out, in_=in_, func=AF.Copy)
        elif eng == "dve":
            A("dve", "tensor_copy", r, w, out=out, in_=in_)
        else:
            A("pool", "tensor_copy", r, w, out=out, in_=in_)

    ident = ar.alloc("ident", [128, 128], BF16)
    identf = ar.alloc("identf", [128, 128], F32)
    A("pool", "memset", [], [identf.res], identf.ap, 0.0)
    A("pool", "iota", [], [identf.res], identf.ap, pattern=[[1, 128]], base=0, channel_multiplier=-1,
      allow_small_or_imprecise_dtypes=True)
    A("dve", "tensor_scalar", [identf.res], [ident.res], out=ident.ap, in0=identf.ap, scalar1=0.0, scalar2=None,
      op0=ALU.is_equal)
    identF = ar.alloc("identF", [128, 128], F32)
    A("dve", "tensor_scalar", [identf.res], [identF.res], out=identF.ap, in0=identf.ap, scalar1=0.0, scalar2=None,
      op0=ALU.is_equal)
    epsb = ar.alloc("eps", [128, 1], F32)
    A("dve", "memset", [], [epsb.res], epsb.ap, EPS)
    zerob = ar.alloc("zero", [128, 1], F32)
    A("dve", "memset", [], [zerob.res], zerob.ap, 0.0)
    junk = ar.alloc("junk", [128, 1024], BF16)

    xv = x_d.rearrange("(t p) d -> p t d", p=128)
    for t in range(NT):
        A("sp", "dma_start", [], [x_res[t]], is_dma=True, out=xs[:, t, :], in_=xv[:, t, :])

    pk = ar.alloc("pk", [128, DEPTH, NPK], F32)
    A("sp", "dma_start", [], [pk.res], is_dma=True, out=pk.ap, in_=pk_d)

    posi = ar.alloc("posi", [128, NT], I32)
    for t in range(NT):
        A("sp", "dma_start", [], [posi.res], is_dma=True, out=posi.ap[:, t:t + 1],
          in_=pos_d[t:t + 1, :].rearrange("a p -> p a"))
    posf = ar.alloc("posf", [128, NT], F32)
    A("dve", "tensor_copy", [posi.res], [posf.res], out=posf.ap, in_=posi.ap)
    ang = ar.alloc("ang", [128, NT, 16], F32)
    for f in range(16):
        A("dve", "tensor_scalar", [posf.res], [ang.res], out=ang.ap[:, :, f], in0=posf.ap,
          scalar1=float(THETA ** (-2.0 * f / 32.0)), scalar2=None, op0=ALU.mult)
    cosA = ar.alloc("cosA", [128, NT, 16], F32)
    sinA = ar.alloc("sinA", [128, NT, 16], F32)
    TWO_PI = 2.0 * math.pi
    C1 = 6.28125
    C2 = TWO_PI - C1
    tA = ar.alloc("tA", [128, NT, 16], F32)
    tK = ar.alloc("tK", [128, NT, 16], I32)
    tKf = ar.alloc("tKf", [128, NT, 16], F32)
    tM = ar.alloc("tM", [128, NT, 16], F32)
    for (dst, shift) in ((sinA, 0.0), (cosA, math.pi / 2)):
        A("dve", "tensor_scalar", [ang.res], [tA.res], out=tA.ap, in0=ang.ap, scalar1=shift, scalar2=None,
          op0=ALU.add)
        A("dve", "tensor_scalar", [tA.res], [tM.res], out=tM.ap, in0=tA.ap, scalar1=1.0 / TWO_PI, scalar2=None,
          op0=ALU.mult)
        A("dve", "tensor_copy", [tM.res], [tK.res], out=tK.ap, in_=tM.ap)
        A("dve", "tensor_copy", [tK.res], [tKf.res], out=tKf.ap, in_=tK.ap)
        A("dve", "scalar_tensor_tensor", [tKf.res, tA.res], [tM.res], out=tM.ap, in0=tKf.ap, scalar=-C1,
          in1=tA.ap, op0=ALU.mult, op1=ALU.add)
        A("dve", "scalar_tensor_tensor", [tKf.res, tM.res], [tA.res], out=tA.ap, in0=tKf.ap, scalar=-C2,
          in1=tM.ap, op0=ALU.mult, op1=ALU.add)
        A("dve", "tensor_scalar", [tA.res], [tM.res], out=tM.ap, in0=tA.ap, scalar1=math.pi, scalar2=-TWO_PI,
          op0=ALU.is_gt, op1=ALU.mult)
        A("dve", "tensor_tensor", [tA.res, tM.res], [tKf.res], out=tKf.ap, in0=tA.ap, in1=tM.ap, op=ALU.add)
        A("dve", "tensor_scalar", [tKf.res], [tM.res], out=tM.ap, in0=tKf.ap, scalar1=-math.pi, scalar2=TWO_PI,
          op0=ALU.is_lt, op1=ALU.mult)
        A("dve", "tensor_tensor", [tKf.res, tM.res], [tA.res], out=tA.ap, in0=tKf.ap, in1=tM.ap, op=ALU.add)
        A("dve", "tensor_scalar", [tA.res], [tM.res], out=tM.ap, in0=tA.ap, scalar1=math.pi, scalar2=-math.pi,
          op0=ALU.min, op1=ALU.max)
        A("act", "activation", [tM.res], [dst.res], out=dst.ap, in_=tM.ap, func=AF.Sin)
    ar.free(tA, tK, tKf, tM, ang, posi, posf, identf)

    def rstd_from(ss, ncol, dim, dst):
        A("act", "activation", [ss[1], epsb.res], [dst[1]], out=dst[0], in_=ss[0], func=AF.Ln, scale=1.0 / dim,
          bias=epsb.ap[:, 0:1])
        A("act", "activation", [dst[1]], [dst[1]], out=dst[0], in_=dst[0], func=AF.Exp, scale=-0.5)

    def load_w(name, shape, src, scale_cols=None, l=0, chunk_cols=None):
        b = ar.alloc(name, shape, BF16)
        kc, n = shape[1], shape[2]
        step = 2048
        for c in range(kc):
            for n0 in range(0, n, step):
                n1 = min(n, n0 + step)
                stg = ar.alloc("stg", [128, n1 - n0], F32)
                A("sp", "dma_start", [], [stg.res], is_dma=True, out=stg.ap, in_=src[:, c, n0:n1])
                if scale_cols is not None:
                    A("pool", "tensor_scalar", [stg.res, pk.res], [b.res], out=b.ap[:, c, n0:n1], in0=stg.ap,
                      scalar1=pk.ap[:, l, scale_cols + c:scale_cols + c + 1], scalar2=1.0, op0=ALU.mult, op1=ALU.mult)
                else:
                    A("pool", "tensor_copy", [stg.res], [b.res], out=b.ap[:, c, n0:n1], in_=stg.ap)
                ar.free(stg)
        return b

    def transposes(srcs, src_res, dst_ap, dst_res, rows=128):
        bk = tp_bank()
        pv = banks[bk][:].bitcast(BF16).rearrange("p (a b) -> p a b", b=128)
        for i, s_ap in enumerate(srcs):
            A("pe", "transpose", [src_res, ident.res], [bank_res[bk]], out=pv[0:rows, i, :], in_=s_ap,
              identity=ident.ap)
        copy(copy_eng(), [bank_res[bk]], [dst_res], dst_ap, pv[0:rows, 0:len(srcs), :])

    def attention(QT, KT, krows, roff, V_of_kc, v_res, nkc, dv, scale, evac, vw=None):
        pending = [None]

        def epilogue(qb, ab):
            oT = ar.alloc("oT", [128, 512], F32)
            A("dve", "tensor_copy", [bank_res[ab]], [oT.res], out=oT.ap[0:dv + 1, :], in_=banks[ab][0:dv + 1, :])

            def pe_part():
                tb = mm_bank()
                for j in range(4):
                    A("pe", "transpose", [oT.res, identF.res], [bank_res[tb]],
                      out=banks[tb][:, j * (dv + 1):(j + 1) * (dv + 1)], in_=oT.ap[0:dv + 1, j * 128:(j + 1) * 128],
                      identity=identF.ap[0:dv + 1, 0:dv + 1])
                ar.free(oT)
                evac(qb, banks[tb][:, 0:4 * (dv + 1)].rearrange("p (j e) -> p j e", e=dv + 1), bank_res[tb])
            return pe_part

        for qb in range(4):
            ab = mm_bank()
            pts = {}

            def issue_s(kc, qb=qb, pts=pts):
                sb_ = s_bank()
                A("pe", "matmul", [QT[1], KT[1]], [bank_res[sb_]], banks[sb_][:, :],
                  lhsT=KT[0][roff:roff + krows, kc * 128:(kc + 1) * 128],
                  rhs=QT[0][roff:roff + krows, qb * 512:(qb + 1) * 512], start=True, stop=True,
                  **({"tile_position": (roff, 0)} if krows == 32 else {}))
                pt = ar.alloc("pt", [128, 512], BF16)
                A("act", "activation", [bank_res[sb_]], [pt.res], out=pt.ap, in_=banks[sb_][:, :], func=AF.Exp,
                  scale=scale)
                pts[kc] = pt

            def issue_pv(kc, ab=ab, pts=pts):
                pt = pts.pop(kc)
                A("pe", "matmul", [pt.res, v_res], [bank_res[ab]], banks[ab][0:(vw or dv + 1), :],
                  lhsT=V_of_kc(kc), rhs=pt.ap, start=(kc == 0), stop=(kc == nkc - 1))
                ar.free(pt)

            issue_s(0)
            issue_s(1)
            for kc in range(nkc):
                if kc + 2 < nkc:
                    issue_s(kc + 2)
                issue_pv(kc)
                if kc == 2 and pending[0] is not None:
                    pending[0]()
                    pending[0] = None
            if pending[0] is not None:
                pending[0]()
            pending[0] = epilogue(qb, ab)
        pending[0]()

    for l in range(n_layers):
        if stop_after == "setup":
            break
        lam_init = 0.8 - 0.6 * math.exp(-0.3 * l)
        bcs = ar.alloc("bcs", [128, 448], F32)
        A("sp", "dma_start", [], [bcs.res], is_dma=True, out=bcs.ap, in_=bc_d[l:l + 1, 3072:3520].partition_broadcast(128))
        lamt = ar.alloc("lamt", [128, 8], F32)
        ltmp = ar.alloc("ltmp", [128, 64], F32)
        A("dve", "tensor_tensor", [bcs.res], [ltmp.res], out=ltmp.ap[:, 0:32], in0=bcs.ap[:, 320:352],
          in1=bcs.ap[:, 352:384], op=ALU.mult)
        A("dve", "tensor_tensor", [bcs.res], [ltmp.res], out=ltmp.ap[:, 32:64], in0=bcs.ap[:, 384:416],
          in1=bcs.ap[:, 416:448], op=ALU.mult)
        A("dve", "tensor_reduce", [ltmp.res], [lamt.res], out=lamt.ap[:, 0:2],
          in_=ltmp.ap.rearrange("p (a b) -> p a b", b=32), axis=AX.X, op=ALU.add)
        A("act", "activation", [lamt.res], [lamt.res], out=lamt.ap[:, 2:4], in_=lamt.ap[:, 0:2], func=AF.Exp)
        A("dve", "tensor_tensor", [lamt.res], [lamt.res], out=lamt.ap[:, 4:5], in0=lamt.ap[:, 3:4], in1=lamt.ap[:, 2:3],
          op=ALU.subtract)
        A("dve", "tensor_scalar", [lamt.res], [lamt.res], out=lamt.ap[:, 5:6], in0=lamt.ap[:, 4:5], scalar1=-lam_init,
          scalar2=None, op0=ALU.add)
        neglam = lamt.ap[:, 5:6]
        A("dve", "tensor_scalar", [bcs.res], [bcs.res], out=bcs.ap[:, 256:320], in0=bcs.ap[:, 256:320],
          scalar1=1.0 - lam_init, scalar2=None, op0=ALU.mult)
        ar.free(ltmp)

        w_in_t = load_w("w_in", [128, 8, INW], w_in_d[l].rearrange("(c p) n -> p c n", p=128), scale_cols=0, l=l)
        wsT = load_w("wsT", [128, 1, 512], wsT_d[l].rearrange("p (a n) -> p a n", a=1))

        ss1 = ar.alloc("ss1", [128, NT], F32)
        rs1 = ar.alloc("rs1", [128, NT], F32)
        for t in range(NT):
            A("act", "activation", [x_res[t]], [junk.res, ss1.res], out=junk.ap, in_=xs[:, t, :], func=AF.Square,
              accum_out=ss1.ap[:, t:t + 1])
        rstd_from((ss1.ap, ss1.res), NT, D, (rs1.ap, rs1.res))
        cqnT = ar.alloc("cqnT", [128, 3, S], BF16)
        ckvnT = ar.alloc("ckvnT", [128, 2, S], BF16)
        krtm = ar.alloc("krtm", [128, NT, 32], BF16)
        dqT = ar.alloc("dqT", [128, 2, S], BF16)
        dkT = ar.alloc("dkT", [128, 2, S], BF16)
        dV = ar.alloc("dV", [128, NT * 4, 65], BF16)
        A("pool", "memset", [], [dV.res], dV.ap[:, :, 64:65], 1.0)
        cb = ar.alloc("cb", [128, NT, 256], BF16)
        COLS = ((0, 384), (384, 672), (672, 1184), (1184, 1696), (1696, 1952))
        for t in range(NT):
            tc_ = slice(t * 128, (t + 1) * 128)
            hn = ar.alloc("hn", [128, D], BF16)
            A("dve", "tensor_scalar", [x_res[t], rs1.res], [hn.res], out=hn.ap, in0=xs[:, t, :],
              scalar1=rs1.ap[:, t:t + 1], scalar2=None, op0=ALU.mult)
            hT = ar.alloc("hT", [128, 8, 128], BF16)
            transposes([hn.ap[:, c * 128:(c + 1) * 128] for c in range(8)], hn.res, hT.ap, hT.res)
            ar.free(hn)
            pj = [mm_bank() for _ in range(5)]
            for gi, (c0, c1) in enumerate(COLS):
                for c in range(8):
                    A("pe", "matmul", [hT.res, w_in_t.res], [bank_res[pj[gi]]], banks[pj[gi]][:, 0:c1 - c0],
                      lhsT=hT.ap[:, c, :], rhs=w_in_t.ap[:, c, c0:c1], start=(c == 0), stop=(c == 7))
            ar.free(hT)
            def chain_norm():
                ssq = ar.alloc("ssq", [128, 2], F32)
                rsq = ar.alloc("rsq", [128, 2], F32)
                A("act", "activation", [bank_res[pj[0]]], [junk.res, ssq.res], out=junk.ap[:, 0:384],
                  in_=banks[pj[0]][:, 0:384], func=AF.Square, accum_out=ssq.ap[:, 0:1])
                yield
                A("act", "activation", [bank_res[pj[1]]], [junk.res, ssq.res], out=junk.ap[:, 0:256],
                  in_=banks[pj[1]][:, 0:256], func=AF.Square, accum_out=ssq.ap[:, 1:2])
                yield
                A("act", "activation", [ssq.res, epsb.res], [rsq.res], out=rsq.ap[:, 0:1], in_=ssq.ap[:, 0:1], func=AF.Ln,
                  scale=1.0 / 384, bias=epsb.ap[:, 0:1])
                yield
                A("act", "activation", [ssq.res, epsb.res], [rsq.res], out=rsq.ap[:, 1:2], in_=ssq.ap[:, 1:2], func=AF.Ln,
                  scale=1.0 / 256, bias=epsb.ap[:, 0:1])
                yield
                A("act", "activation", [rsq.res], [rsq.res], out=rsq.ap, in_=rsq.ap, func=AF.Exp, scale=-0.5)
                yield
                cn = ar.alloc("cn", [128, 640], BF16)
                A("dve", "tensor_scalar", [bank_res[pj[0]], rsq.res], [cn.res], out=cn.ap[:, 0:384],
                  in0=banks[pj[0]][:, 0:384], scalar1=rsq.ap[:, 0:1], scalar2=None, op0=ALU.mult)
                yield
                A("dve", "tensor_scalar", [bank_res[pj[1]], rsq.res], [cn.res], out=cn.ap[:, 384:640],
                  in0=banks[pj[1]][:, 0:256], scalar1=rsq.ap[:, 1:2], scalar2=None, op0=ALU.mult)
                yield
                transposes([cn.ap[:, c * 128:(c + 1) * 128] for c in range(3)], cn.res, cqnT.ap[:, :, tc_], cqnT.res)
                yield
                transposes([cn.ap[:, 384 + c * 128:384 + (c + 1) * 128] for c in range(2)], cn.res, ckvnT.ap[:, :, tc_],
                           ckvnT.res)
                yield
                ar.free(ssq, rsq, cn)
                yield

            def chain_kr():
                rt = ar.alloc("rt", [128, 4, 16], F32)
                kb = banks[pj[1]]
                kr_res = bank_res[pj[1]]
                cA = cosA.ap[:, t, :]
                sA = sinA.ap[:, t, :]
                A("dve", "tensor_tensor", [kr_res, cosA.res], [rt.res], out=rt.ap[:, 0, :], in0=kb[:, 256:272], in1=cA,
                  op=ALU.mult)
                yield
                A("dve", "tensor_tensor", [kr_res, sinA.res], [rt.res], out=rt.ap[:, 1, :], in0=kb[:, 272:288], in1=sA,
                  op=ALU.mult)
                yield
                A("dve", "tensor_tensor", [kr_res, cosA.res], [rt.res], out=rt.ap[:, 2, :], in0=kb[:, 272:288], in1=cA,
                  op=ALU.mult)
                yield
                A("dve", "tensor_tensor", [kr_res, sinA.res], [rt.res], out=rt.ap[:, 3, :], in0=kb[:, 256:272], in1=sA,
                  op=ALU.mult)
                yield
                A("dve", "tensor_tensor", [rt.res], [krtm.res], out=krtm.ap[:, t, 0:16], in0=rt.ap[:, 0, :],
                  in1=rt.ap[:, 1, :], op=ALU.subtract)
                yield
                A("dve", "tensor_tensor", [rt.res], [krtm.res], out=krtm.ap[:, t, 16:32], in0=rt.ap[:, 2, :],
                  in1=rt.ap[:, 3, :], op=ALU.add)
                yield
                ar.free(rt)
                yield

            def chain_sgu():
                zg = ar.alloc("zg", [128, 512], F32)
                A("act", "activation", [bank_res[pj[2]]], [zg.res], out=zg.ap, in_=banks[pj[2]][:, :],
                  func=AF.Gelu_apprx_tanh)
                yield
                st = ar.alloc("st", [128, 16], F32)
                vsq = ar.alloc("vsq", [128, 256], F32)
                v3 = zg.ap[:, 256:512].rearrange("p (g c) -> p g c", c=64)
                A("dve", "tensor_reduce", [zg.res], [st.res], out=st.ap[:, 0:4], in_=v3, axis=AX.X, op=ALU.add)
                yield
                A("act", "activation", [zg.res], [vsq.res], out=vsq.ap, in_=zg.ap[:, 256:512], func=AF.Square)
                yield
                A("dve", "tensor_reduce", [vsq.res], [st.res], out=st.ap[:, 4:8],
                  in_=vsq.ap.rearrange("p (g c) -> p g c", c=64), axis=AX.X, op=ALU.add)
                yield
                A("dve", "tensor_scalar", [st.res], [st.res], out=st.ap[:, 8:12], in0=st.ap[:, 0:4], scalar1=1.0 / 64,
                  scalar2=None, op0=ALU.mult)
                yield
                A("dve", "tensor_tensor", [st.res], [st.res], out=st.ap[:, 12:16], in0=st.ap[:, 8:12], in1=st.ap[:, 8:12],
                  op=ALU.mult)
                yield
                A("dve", "scalar_tensor_tensor", [st.res], [st.res], out=st.ap[:, 4:8], in0=st.ap[:, 4:8], scalar=1.0 / 64,
                  in1=st.ap[:, 12:16], op0=ALU.mult, op1=ALU.subtract)
                yield
                A("act", "activation", [st.res, epsb.res], [st.res], out=st.ap[:, 0:4], in_=st.ap[:, 4:8], func=AF.Ln,
                  bias=epsb.ap[:, 0:1])
                yield
                A("act", "activation", [st.res], [st.res], out=st.ap[:, 0:4], in_=st.ap[:, 0:4], func=AF.Exp, scale=-0.5)
                yield
                vq3 = vsq.ap.rearrange("p (g c) -> p g c", c=64)
                A("dve", "tensor_tensor", [zg.res, st.res], [vsq.res], out=vq3, in0=v3,
                  in1=st.ap[:, 8:12].unsqueeze(2).broadcast_to([128, 4, 64]), op=ALU.subtract)
                yield
                A("dve", "tensor_tensor", [vsq.res, st.res], [vsq.res], out=vq3, in0=vq3,
                  in1=st.ap[:, 0:4].unsqueeze(2).broadcast_to([128, 4, 64]), op=ALU.mult)
                yield
                vn = ar.alloc("vn", [128, 256], BF16)
                A("dve", "tensor_tensor", [vsq.res, bcs.res], [vn.res], out=vn.ap, in0=vsq.ap, in1=bcs.ap[:, 0:256],
                  op=ALU.mult)
                yield
                mb = mm_bank()
                for g in range(4):
                    A("pe", "matmul", [vn.res, wsT.res], [bank_res[mb]], banks[mb][:, g * 64:(g + 1) * 64],
                      lhsT=wsT.ap[:, 0, g * 128:(g + 1) * 128], rhs=vn.ap[:, g * 64:(g + 1) * 64], start=True, stop=True)
                    yield
                for g in range(4):
                    A("dve", "scalar_tensor_tensor", [bank_res[mb], pk.res, zg.res], [cb.res],
                      out=cb.ap[:, t, g * 64:(g + 1) * 64], in0=banks[mb][:, g * 64:(g + 1) * 64],
                      scalar=pk.ap[:, l, 37 + g:38 + g], in1=zg.ap[:, g * 64:(g + 1) * 64], op0=ALU.add, op1=ALU.mult)
                    yield
                ar.free(zg, st, vsq, vn)
                yield

            def chain_dqk():
                qk = ar.alloc("qk", [128, 16, 32], BF16)
                r6 = ar.alloc("r6", [128, 4 * 16, 8], F32)
                pq = banks[pj[3]][:, :].rearrange("p (m d) -> p m d", d=32)
                qres = bank_res[pj[3]]
                cD = cosA.ap[:, t, 0:16:2].unsqueeze(1).broadcast_to([128, 16, 8])
                sD = sinA.ap[:, t, 0:16:2].unsqueeze(1).broadcast_to([128, 16, 8])
                A("dve", "tensor_tensor", [qres, cosA.res], [r6.res], out=r6.ap[:, 0:16, :], in0=pq[:, :, 0:8], in1=cD,
                  op=ALU.mult)
                yield
                A("dve", "tensor_tensor", [qres, sinA.res], [r6.res], out=r6.ap[:, 16:32, :], in0=pq[:, :, 8:16], in1=sD,
                  op=ALU.mult)
                yield
                A("dve", "tensor_tensor", [qres, cosA.res], [r6.res], out=r6.ap[:, 32:48, :], in0=pq[:, :, 8:16], in1=cD,
                  op=ALU.mult)
                yield
                A("dve", "tensor_tensor", [qres, sinA.res], [r6.res], out=r6.ap[:, 48:64, :], in0=pq[:, :, 0:8], in1=sD,
                  op=ALU.mult)
                yield
                A("dve", "tensor_tensor", [r6.res], [qk.res], out=qk.ap[:, :, 0:8], in0=r6.ap[:, 0:16, :],
                  in1=r6.ap[:, 16:32, :], op=ALU.subtract)
                yield
                A("dve", "tensor_tensor", [r6.res], [qk.res], out=qk.ap[:, :, 8:16], in0=r6.ap[:, 32:48, :],
                  in1=r6.ap[:, 48:64, :], op=ALU.add)
                yield
                A("act", "activation", [qres], [qk.res], out=qk.ap[:, :, 16:32], in_=pq[:, :, 16:32], func=AF.Copy)
                yield
                qk2 = qk.ap.rearrange("p m d -> p (m d)")
                transposes([qk2[:, c * 128:(c + 1) * 128] for c in range(2)], qk.res, dqT.ap[:, :, tc_], dqT.res)
                yield
                transposes([qk2[:, 256 + c * 128:256 + (c + 1) * 128] for c in range(2)], qk.res, dkT.ap[:, :, tc_],
                           dkT.res)
                yield
                ar.free(qk, r6)
                yield

            def chain_dv():
                A("act", "activation", [bank_res[pj[4]]], [dV.res], out=dV.ap[:, t * 4:(t + 1) * 4, 0:64],
                  in_=banks[pj[4]][:, 0:256].rearrange("p (h e) -> p h e", e=64), func=AF.Copy)
                yield
                yield

            gens = [chain_norm(), chain_sgu(), chain_dqk(), chain_kr(), chain_dv()]
            while gens:
                for g_ in list(gens):
                    try:
                        next(g_)
                    except StopIteration:
                        gens.remove(g_)
        ar.free(ss1, rs1, w_in_t, wsT)
        if stop_after == "p1":
            break

        cc = ar.alloc("cc", [128, NT, 256], BF16)
        w_mo_t = load_w("w_mo", [128, 8, D], w_mo_d[l].rearrange("(c p) n -> p c n", p=128))
        w_uq_t = load_w("w_uq", [128, 3, 768], w_uq_d[l].rearrange("(c p) n -> p c n", p=128), scale_cols=32, l=l)
        w_ukv_t = load_w("w_ukv", [128, 2, 1024], w_ukv_d[l].rearrange("(c p) n -> p c n", p=128), scale_cols=35, l=l)
        for h in range(4):
            ous = []
            for c in range(2):
                m = 2 * h + c
                ci, ro = m // 4, 32 * (m % 4)
                ou = ar.alloc("ou%d" % c, [128, NT, 65], F32)
                ous.append(ou)

                def evac(qb, acc, acc_res, ou=ou):
                    A("dve", "tensor_copy", [acc_res], [ou.res], out=ou.ap[:, qb * 4:(qb + 1) * 4, :], in_=acc)

                dkm = ar.alloc("dkm", [128, S], BF16)
                A("pool", "memset", [], [dkm.res], dkm.ap, 0.0)
                A("pool", "tensor_copy", [dkT.res], [dkm.res], out=dkm.ap[ro:ro + 32, :], in_=dkT.ap[ro:ro + 32, ci, :])
                attention((dqT.ap[:, ci, :], dqT.res), (dkm.ap, dkm.res), 128, 0,
                          lambda kc, h=h: dV.ap[:, kc * 4 + h, :], dV.res, NT, 64, 32 ** -0.5, evac)
                ar.free(dkm)
            o0, o1 = ous
            fr = ar.alloc("fr", [128, 4, NT], F32)
            A("dve", "reciprocal", [o0.res], [fr.res], out=fr.ap[:, 0, :], in_=o0.ap[:, :, 64])
            A("dve", "reciprocal", [o1.res], [fr.res], out=fr.ap[:, 1, :], in_=o1.ap[:, :, 64])
            A("dve", "tensor_scalar", [fr.res, lamt.res], [fr.res], out=fr.ap[:, 1, :], in0=fr.ap[:, 1, :],
              scalar1=neglam, scalar2=None, op0=ALU.mult)
            A("dve", "tensor_tensor", [o0.res, fr.res], [o0.res], out=o0.ap[:, :, 0:64], in0=o0.ap[:, :, 0:64],
              in1=fr.ap[:, 0, :].unsqueeze(2).broadcast_to([128, NT, 64]), op=ALU.mult)
            A("dve", "tensor_tensor", [o1.res, fr.res], [o1.res], out=o1.ap[:, :, 0:64], in0=o1.ap[:, :, 0:64],
              in1=fr.ap[:, 1, :].unsqueeze(2).broadcast_to([128, NT, 64]), op=ALU.mult)
            A("dve", "tensor_tensor", [o0.res, o1.res], [o0.res], out=o0.ap[:, :, 0:64], in0=o0.ap[:, :, 0:64],
              in1=o1.ap[:, :, 0:64], op=ALU.add)
            A("dve", "tensor_tensor", [o0.res], [o1.res], out=o1.ap[:, :, 0:64], in0=o0.ap[:, :, 0:64],
              in1=o0.ap[:, :, 0:64], op=ALU.mult)
            A("dve", "tensor_reduce", [o1.res], [fr.res], out=fr.ap[:, 2, :], in_=o1.ap[:, :, 0:64], axis=AX.X,
              op=ALU.add)
            A("act", "activation", [fr.res, epsb.res], [fr.res], out=fr.ap[:, 3, :], in_=fr.ap[:, 2, :], func=AF.Ln,
              scale=1.0 / 64, bias=epsb.ap[:, 0:1])
            A("act", "activation", [fr.res], [fr.res], out=fr.ap[:, 3, :], in_=fr.ap[:, 3, :], func=AF.Exp, scale=-0.5)
            A("dve", "tensor_tensor", [o0.res, fr.res], [o0.res], out=o0.ap[:, :, 0:64], in0=o0.ap[:, :, 0:64],
              in1=fr.ap[:, 3, :].unsqueeze(2).broadcast_to([128, NT, 64]), op=ALU.mult)
            A("dve", "tensor_tensor", [o0.res, bcs.res], [cc.res], out=cc.ap[:, :, h * 64:(h + 1) * 64],
              in0=o0.ap[:, :, 0:64], in1=bcs.ap[:, 256:320].unsqueeze(1).broadcast_to([128, NT, 64]), op=ALU.mult)
            ar.free(o0, o1, fr)
        ar.free(dqT, dkT, dV)
        if stop_after == "p2":
            break

        ca = ar.alloc("ca", [128, NT, 512], BF16)
        for h in range(8):
            QKT = ar.alloc("QKT", [128, 2, S], BF16)
            Vh = ar.alloc("Vh", [128, NT, 128], BF16)
            A("pool", "memset", [], [Vh.res], Vh.ap[:, :, 64:128], 0.0)
            A("pool", "memset", [], [Vh.res], Vh.ap[:, :, 64:65], 1.0)
            for g in range(4):
                bA = mm_bank()
                bB = mm_bank()
                for tt in range(4):
                    t = g * 4 + tt
                    tc_ = slice(t * 128, (t + 1) * 128)
                    for c in range(3):
                        A("pe", "matmul", [cqnT.res, w_uq_t.res], [bank_res[bA]], banks[bA][:, tt * 96:(tt + 1) * 96],
                          lhsT=cqnT.ap[:, c, tc_], rhs=w_uq_t.ap[:, c, h * 96:(h + 1) * 96], start=(c == 0),
                          stop=(c == 2))
                    for c in range(2):
                        A("pe", "matmul", [ckvnT.res, w_ukv_t.res], [bank_res[bB]],
                          banks[bB][:, tt * 128:(tt + 1) * 128], lhsT=ckvnT.ap[:, c, tc_],
                          rhs=w_ukv_t.ap[:, c, h * 128:(h + 1) * 128], start=(c == 0), stop=(c == 1))
                qA = banks[bA][:, 0:384].rearrange("p (t d) -> p t d", d=96)
                kvB = banks[bB][:, :].rearrange("p (t d) -> p t d", d=128)
                rA, rB = bank_res[bA], bank_res[bB]
                qk = ar.alloc("qk", [128, 8, 128], BF16)
                qk4 = qk.ap.rearrange("p (t s) d -> p t s d", s=2)
                A("pool", "memset", [], [qk.res], qk.ap[:, :, 96:128], 0.0)
                A("act", "activation", [rA], [qk.res], out=qk4[:, :, 0, 0:64], in_=qA[:, :, 0:64], func=AF.Copy)
                A("act", "activation", [rB], [qk.res], out=qk4[:, :, 1, 0:64], in_=kvB[:, :, 0:64], func=AF.Copy)
                A("act", "activation", [rB], [Vh.res], out=Vh.ap[:, g * 4:(g + 1) * 4, 0:64], in_=kvB[:, :, 64:128],
                  func=AF.Copy)
                rt = ar.alloc("rt", [128, 16, 16], F32)
                cA = cosA.ap[:, g * 4:(g + 1) * 4, :]
                sA = sinA.ap[:, g * 4:(g + 1) * 4, :]
                A("dve", "tensor_tensor", [rA, cosA.res], [rt.res], out=rt.ap[:, 0:4, :], in0=qA[:, :, 64:80], in1=cA,
                  op=ALU.mult)
                A("dve", "tensor_tensor", [rA, sinA.res], [rt.res], out=rt.ap[:, 4:8, :], in0=qA[:, :, 80:96], in1=sA,
                  op=ALU.mult)
                A("dve", "tensor_tensor", [rA, cosA.res], [rt.res], out=rt.ap[:, 8:12, :], in0=qA[:, :, 80:96], in1=cA,
                  op=ALU.mult)
                A("dve", "tensor_tensor", [rA, sinA.res], [rt.res], out=rt.ap[:, 12:16, :], in0=qA[:, :, 64:80], in1=sA,
                  op=ALU.mult)
                A("dve", "tensor_tensor", [rt.res], [qk.res], out=qk4[:, :, 0, 64:80], in0=rt.ap[:, 0:4, :],
                  in1=rt.ap[:, 4:8, :], op=ALU.subtract)
                A("dve", "tensor_tensor", [rt.res], [qk.res], out=qk4[:, :, 0, 80:96], in0=rt.ap[:, 8:12, :],
                  in1=rt.ap[:, 12:16, :], op=ALU.add)
                A("dve", "tensor_copy", [krtm.res], [qk.res], out=qk4[:, :, 1, 64:96],
                  in_=krtm.ap[:, g * 4:(g + 1) * 4, :])
                bk = tp_bank()
                pv = banks[bk][:].bitcast(BF16).rearrange("p (a b) -> p a b", b=128)
                for i in range(8):
                    A("pe", "transpose", [qk.res, ident.res], [bank_res[bk]], out=pv[:, i, :], in_=qk.ap[:, i, :],
                      identity=ident.ap)
                copy(copy_eng(), [bank_res[bk]], [QKT.res],
                     QKT.ap[:, :, g * 512:(g + 1) * 512].rearrange("p s (t i) -> p s t i", i=128),
                     pv.rearrange("p (t s) i -> p s t i", s=2))
                ar.free(qk, rt)
            if stop_after == "p3a":
                break

            oua = ar.alloc("ou0", [128, NT, 65], F32)

            def evac_a(qb, acc, acc_res, oua=oua):
                A("dve", "tensor_copy", [acc_res], [oua.res], out=oua.ap[:, qb * 4:(qb + 1) * 4, :], in_=acc)

            attention((QKT.ap[:, 0, :], QKT.res), (QKT.ap[:, 1, :], QKT.res), 128, 0,
                      lambda kc, Vh=Vh: Vh.ap[:, kc, :], Vh.res, NT, 64, 96 ** -0.5, evac_a, vw=128)
            fr = ar.alloc("fr", [128, 4, NT], F32)
            A("dve", "reciprocal", [oua.res], [fr.res], out=fr.ap[:, 0, :], in_=oua.ap[:, :, 64])
            A("dve", "tensor_tensor", [oua.res, fr.res], [ca.res], out=ca.ap[:, :, h * 64:(h + 1) * 64],
              in0=oua.ap[:, :, 0:64], in1=fr.ap[:, 0, :].unsqueeze(2).broadcast_to([128, NT, 64]), op=ALU.mult)
            ar.free(oua, fr)
            ar.free(QKT, Vh)
            if stop_after == "p3b":
                break
        if stop_after in ("p3a", "p3b"):
            break
        ar.free(cqnT, ckvnT, krtm, w_uq_t, w_ukv_t)
        if stop_after == "p3":
            break

        def post_norm_add(t, bk2, gpost):
            s2 = ar.alloc("s2", [128, 4], F32)
            for i in range(2):
                A("act", "activation", [bank_res[bk2[i]]], [junk.res, s2.res], out=junk.ap[:, 0:512],
                  in_=banks[bk2[i]][:, :], func=AF.Square, accum_out=s2.ap[:, i:i + 1])
            A("dve", "tensor_tensor", [s2.res], [s2.res], out=s2.ap[:, 2:3], in0=s2.ap[:, 0:1], in1=s2.ap[:, 1:2],
              op=ALU.add)
            A("act", "activation", [s2.res, epsb.res], [s2.res], out=s2.ap[:, 3:4], in_=s2.ap[:, 2:3], func=AF.Ln,
              scale=1.0 / D, bias=epsb.ap[:, 0:1])
            A("act", "activation", [s2.res], [s2.res], out=s2.ap[:, 3:4], in_=s2.ap[:, 3:4], func=AF.Exp, scale=-0.5)
            tmp = ar.alloc("tmp", [128, D], F32)
            for i in range(2):
                A("dve", "scalar_tensor_tensor", [bank_res[bk2[i]], s2.res, gpost.res], [tmp.res],
                  out=tmp.ap[:, i * 512:(i + 1) * 512], in0=banks[bk2[i]][:, :], scalar=s2.ap[:, 3:4],
                  in1=gpost.ap[:, i * 512:(i + 1) * 512], op0=ALU.mult, op1=ALU.mult)
            A("pool", "tensor_tensor", [tmp.res, x_res[t]], [x_res[t]], out=xs[:, t, :], in0=xs[:, t, :], in1=tmp.ap,
              op=ALU.add)
            ar.free(s2, tmp)

        gpost = ar.alloc("gpost", [128, D], F32)
        A("sp", "dma_start", [], [gpost.res], is_dma=True, out=gpost.ap, in_=bc_d[l:l + 1, 0:1024].partition_broadcast(128))
        if stop_after != "mix":
            wkv_v = w_kv_d[l].rearrange("(c p) n -> p c n", p=128)
            w_k_t = load_w("w_k", [128, 8, D], wkv_v[:, :, 0:D], scale_cols=24, l=l)
            w_v_ts = [load_w("w_v%d" % i, [128, 8, 512], wkv_v[:, :, D + i * 512:D + (i + 1) * 512], scale_cols=24, l=l)
                      for i in range(2)]
            memf = ar.alloc("memf", [128, 2, D], F32)
            A("sp", "dma_start", [], [memf.res], is_dma=True, out=memf.ap, in_=mem_d.rearrange("(t p) d -> p t d", p=128))
        for t in range(NT):
            cT = ar.alloc("cT", [128, 8, 128], BF16)
            srcs = [ca.ap[:, t, c * 128:(c + 1) * 128] for c in range(4)] + \
                   [cb.ap[:, t, c * 128:(c + 1) * 128] for c in range(2)] + \
                   [cc.ap[:, t, c * 128:(c + 1) * 128] for c in range(2)]
            bk = tp_bank()
            pv = banks[bk][:].bitcast(BF16).rearrange("p (a b) -> p a b", b=128)
            for i, s_ap in enumerate(srcs):
                rr = ca.res if i < 4 else (cb.res if i < 6 else cc.res)
                A("pe", "transpose", [rr, ident.res], [bank_res[bk]], out=pv[:, i, :], in_=s_ap, identity=ident.ap)
            copy(copy_eng(), [bank_res[bk]], [cT.res], cT.ap, pv[:, 0:8, :])
            bk2 = [mm_bank(), mm_bank()]
            for i in range(2):
                for c in range(8):
                    A("pe", "matmul", [cT.res, w_mo_t.res], [bank_res[bk2[i]]], banks[bk2[i]][:, :],
                      lhsT=cT.ap[:, c, :], rhs=w_mo_t.ap[:, c, i * 512:(i + 1) * 512], start=(c == 0), stop=(c == 7))
            ar.free(cT)
            post_norm_add(t, bk2, gpost)
        ar.free(ca, cb, cc, w_mo_t, gpost, bcs, lamt)
        if stop_after == "mix":
            break

        w_q_t = load_w("w_q", [128, 8, D], w_q_d[l].rearrange("(c p) n -> p c n", p=128), scale_cols=8, l=l)
        w_o_t = load_w("w_o", [128, 8, D], w_o_d[l].rearrange("(c p) n -> p c n", p=128))
        ssm = ar.alloc("ssm", [128, 2], F32)
        rsm = ar.alloc("rsm", [128, 2], F32)
        for t in range(2):
            A("act", "activation", [memf.res], [junk.res, ssm.res], out=junk.ap, in_=memf.ap[:, t, :], func=AF.Square,
              accum_out=ssm.ap[:, t:t + 1])
        rstd_from((ssm.ap, ssm.res), 2, D, (rsm.ap, rsm.res))
        memT = ar.alloc("memT", [128, 8, NMEM], BF16)
        for t in range(2):
            mn = ar.alloc("mn", [128, D], BF16)
            A("dve", "tensor_scalar", [memf.res, rsm.res], [mn.res], out=mn.ap, in0=memf.ap[:, t, :],
              scalar1=rsm.ap[:, t:t + 1], scalar2=None, op0=ALU.mult)
            transposes([mn.ap[:, c * 128:(c + 1) * 128] for c in range(8)], mn.res, memT.ap[:, :, t * 128:(t + 1) * 128],
                       memT.res)
            ar.free(mn)
        ar.free(memf, ssm, rsm)
        KmT = ar.alloc("KmT", [128, 8, NMEM], BF16)
        for hc in range(8):
            mb = mm_bank()
            for c in range(8):
                A("pe", "matmul", [memT.res, w_k_t.res], [bank_res[mb]], banks[mb][:, 0:NMEM],
                  lhsT=w_k_t.ap[:, c, hc * 128:(hc + 1) * 128], rhs=memT.ap[:, c, :], start=(c == 0), stop=(c == 7))
            copy(copy_eng(), [bank_res[mb]], [KmT.res], KmT.ap[:, hc, :], banks[mb][:, 0:NMEM])
        Vm = ar.alloc("Vm", [128, 8, 257], BF16)
        A("pool", "memset", [], [Vm.res], Vm.ap[:, :, 256:257], 1.0)
        for kt in range(2):
            for i in range(2):
                mb = mm_bank()
                for c in range(8):
                    A("pe", "matmul", [memT.res, w_v_ts[i].res], [bank_res[mb]], banks[mb][:, :],
                      lhsT=memT.ap[:, c, kt * 128:(kt + 1) * 128], rhs=w_v_ts[i].ap[:, c, :],
                      start=(c == 0), stop=(c == 7))
                copy(copy_eng(), [bank_res[mb]], [Vm.res], Vm.ap[:, kt * 4 + 2 * i:kt * 4 + 2 * i + 2, 0:256],
                     banks[mb][:, :].rearrange("p (h e) -> p h e", e=256))
        ar.free(memT, w_k_t, *w_v_ts)
        gpost = ar.alloc("gpost", [128, D], F32)
        A("sp", "dma_start", [], [gpost.res], is_dma=True, out=gpost.ap, in_=bc_d[l:l + 1, 1024:2048].partition_broadcast(128))
        ss5 = ar.alloc("ss5", [128, NT], F32)
        rs5 = ar.alloc("rs5", [128, NT], F32)
        for t in range(NT):
            A("act", "activation", [x_res[t]], [junk.res, ss5.res], out=junk.ap, in_=xs[:, t, :], func=AF.Square,
              accum_out=ss5.ap[:, t:t + 1])
        rstd_from((ss5.ap, ss5.res), NT, D, (rs5.ap, rs5.res))
        hTbs = [ar.alloc("hTb", [128, 8, 512], BF16) for _ in range(2)]
        qTbs = [ar.alloc("qTb", [128, 8, 512], BF16) for _ in range(2)]

        def mem_front(qb):
            hTb = hTbs[qb % 2]
            for j in range(4):
                t = qb * 4 + j
                hn = ar.alloc("hn", [128, D], BF16)
                A("dve", "tensor_scalar", [x_res[t], rs5.res], [hn.res], out=hn.ap, in0=xs[:, t, :],
                  scalar1=rs5.ap[:, t:t + 1], scalar2=None, op0=ALU.mult)
                transposes([hn.ap[:, c * 128:(c + 1) * 128] for c in range(8)], hn.res,
                           hTb.ap[:, :, j * 128:(j + 1) * 128], hTb.res)
                ar.free(hn)
            qTb = qTbs[qb % 2]
            for hc in range(8):
                mb = mm_bank()
                for c in range(8):
                    A("pe", "matmul", [hTb.res, w_q_t.res], [bank_res[mb]], banks[mb][:, :],
                      lhsT=w_q_t.ap[:, c, hc * 128:(hc + 1) * 128], rhs=hTb.ap[:, c, :], start=(c == 0), stop=(c == 7))
                copy(copy_eng(), [bank_res[mb]], [qTb.res], qTb.ap[:, hc, :], banks[mb][:, :])

        def mem_back(qb):
            qTb = qTbs[qb % 2]
            ao = ar.alloc("ao", [128, 4, D], BF16)
            for h in range(4):
                pts = []
                for kt in range(2):
                    sb_ = tp_bank()
                    for dc in range(2):
                        A("pe", "matmul", [qTb.res, KmT.res], [bank_res[sb_]], banks[sb_][:, :],
                          lhsT=KmT.ap[:, h * 2 + dc, kt * 128:(kt + 1) * 128], rhs=qTb.ap[:, h * 2 + dc, :],
                          start=(dc == 0), stop=(dc == 1))
                    pt = ar.alloc("ptm", [128, 512], BF16)
                    A("act", "activation", [bank_res[sb_]], [pt.res], out=pt.ap, in_=banks[sb_][:, :], func=AF.Exp,
                      scale=1.0 / 16.0)
                    pts.append(pt)
                for j in range(4):
                    mb = mm_bank()
                    for kt in range(2):
                        A("pe", "matmul", [pts[kt].res, Vm.res], [bank_res[mb]], banks[mb][:, 0:257],
                          lhsT=pts[kt].ap[:, j * 128:(j + 1) * 128], rhs=Vm.ap[:, kt * 4 + h, :], start=(kt == 0),
                          stop=(kt == 1))
                    rc = ar.alloc("rc", [128, 1], F32)
                    A("dve", "reciprocal", [bank_res[mb]], [rc.res], out=rc.ap[:, 0:1], in_=banks[mb][:, 256:257])
                    A("dve", "tensor_scalar", [bank_res[mb], rc.res], [ao.res], out=ao.ap[:, j, h * 256:(h + 1) * 256],
                      in0=banks[mb][:, 0:256], scalar1=rc.ap[:, 0:1], scalar2=None, op0=ALU.mult)
                    ar.free(rc)
                ar.free(*pts)
            for j in range(4):
                t = qb * 4 + j
                aT = ar.alloc("aT", [128, 8, 128], BF16)
                transposes([ao.ap[:, j, c * 128:(c + 1) * 128] for c in range(8)], ao.res, aT.ap, aT.res)
                bk2 = [mm_bank(), mm_bank()]
                for i in range(2):
                    for c in range(8):
                        A("pe", "matmul", [aT.res, w_o_t.res], [bank_res[bk2[i]]], banks[bk2[i]][:, :],
                          lhsT=aT.ap[:, c, :], rhs=w_o_t.ap[:, c, i * 512:(i + 1) * 512], start=(c == 0), stop=(c == 7))
                ar.free(aT)
                post_norm_add(t, bk2, gpost)
            ar.free(ao)

        mem_front(0)
        for qb in range(4):
            if qb + 1 < 4:
                mem_front(qb + 1)
            mem_back(qb)
        ar.free(KmT, Vm, w_q_t, w_o_t, gpost, ss5, rs5, *hTbs, *qTbs)
        if stop_after == "mem":
            break

        w_dn_t = ar.alloc("w_dn", [128, NCH, D], BF16)
        w_dn_v = w_dn_d[l].rearrange("(c p) n -> p c n", p=128)

        def load_dn_chunk(c, w_dn_t=w_dn_t, w_dn_v=w_dn_v):
            stg = ar.alloc("stg", [128, D], F32)
            A("sp", "dma_start", [], [stg.res], is_dma=True, out=stg.ap, in_=w_dn_v[:, c, :])
            A("pool", "tensor_copy", [stg.res], [w_dn_t.res], out=w_dn_t.ap[:, c, :], in_=stg.ap)
            ar.free(stg)
        gpost = ar.alloc("gpost", [128, D], F32)
        A("sp", "dma_start", [], [gpost.res], is_dma=True, out=gpost.ap, in_=bc_d[l:l + 1, 2048:3072].partition_broadcast(128))
        ss6 = ar.alloc("ss6", [128, NT], F32)
        rs6 = ar.alloc("rs6", [128, NT], F32)
        for t in range(NT):
            A("act", "activation", [x_res[t]], [junk.res, ss6.res], out=junk.ap, in_=xs[:, t, :], func=AF.Square,
              accum_out=ss6.ap[:, t:t + 1])
        rstd_from((ss6.ap, ss6.res), NT, D, (rs6.ap, rs6.res))
        hl = ar.alloc("hl", [128, 8, 2], BF16)
        stgs = [ar.alloc("stgf", [128, 8, 256], F32) for _ in range(2)]
        wus = [ar.alloc("wuf", [128, 8, 256], BF16) for _ in range(3)]
        def issue_w(idx, l=l, stgs=stgs, wus=wus):
            if idx >= 4 * NCH:
                return
            jc_ = idx % NCH
            stg_, wu_ = stgs[idx % 2], wus[idx % 3]
            A("sp", "dma_start", [], [stg_.res], is_dma=True, out=stg_.ap,
              in_=w_up_d[l, jc_].rearrange("p (c n) -> p c n", n=256))
            A("pool", "tensor_tensor", [stg_.res, pk.res], [wu_.res], out=wu_.ap, in0=stg_.ap,
              in1=pk.ap[:, l, 16:24].unsqueeze(2).broadcast_to([128, 8, 256]), op=ALU.mult)

        issue_w(0)
        issue_w(1)
        def ffn_build(qb):
            hTe = ar.alloc("hTe", [128, 8, 514], BF16)
            if qb == 0:
                A("dve", "memset", [], [hTe.res], hTe.ap[:, :, 0:1], 0.0)
            if qb == 3:
                A("dve", "memset", [], [hTe.res], hTe.ap[:, :, 513:514], 0.0)
            tl = list(range(qb * 4, qb * 4 + 4))
            if qb > 0:
                A("dve", "tensor_copy", [hl.res], [hTe.res], out=hTe.ap[:, :, 0:1], in_=hl.ap[:, :, (qb - 1) % 2:(qb - 1) % 2 + 1])
            if qb < 3:
                tl = tl + [qb * 4 + 4]
            for t in tl:
                hn = ar.alloc("hn", [128, D], BF16)
                A("dve", "tensor_scalar", [x_res[t], rs6.res], [hn.res], out=hn.ap, in0=xs[:, t, :],
                  scalar1=rs6.ap[:, t:t + 1], scalar2=None, op0=ALU.mult)
                bk = tp_bank()
                pv = banks[bk][:].bitcast(BF16).rearrange("p (a b) -> p a b", b=128)
                for c in range(8):
                    A("pe", "transpose", [hn.res, ident.res], [bank_res[bk]], out=pv[:, c, :],
                      in_=hn.ap[:, c * 128:(c + 1) * 128], identity=ident.ap)
                j = t - qb * 4
                if j < 0:
                    copy(copy_eng(), [bank_res[bk]], [hTe.res], hTe.ap[:, :, 0:1], pv[:, 0:8, 127:128])
                elif j > 3:
                    copy(copy_eng(), [bank_res[bk]], [hTe.res], hTe.ap[:, :, 513:514], pv[:, 0:8, 0:1])
                else:
                    copy(copy_eng(), [bank_res[bk]], [hTe.res], hTe.ap[:, :, 1 + j * 128:1 + (j + 1) * 128], pv[:, 0:8, :])
                ar.free(hn)
            A("dve", "tensor_copy", [hTe.res], [hl.res], out=hl.ap[:, :, qb % 2:qb % 2 + 1], in_=hTe.ap[:, :, 512:513])
            return hTe

        def ffn_up(qb, hTe):
            mT = ar.alloc("mT", [128, NCH, 512], BF16)
            mm_all[0] = True
            for jc in range(NCH):
                wu = wus[(qb * NCH + jc) % 3]
                issue_w(qb * NCH + jc + 2)
                if qb == 0:
                    load_dn_chunk(jc)
                ys = []
                hb = mm_bank()
                for gu in range(2):
                    mb = mm_bank()
                    for c in range(8):
                        A("pe", "matmul", [hTe.res, wu.res], [bank_res[mb]], banks[mb][:, :],
                          lhsT=wu.ap[:, c, gu * 128:(gu + 1) * 128], rhs=hTe.ap[:, c, 1:513], start=(c == 0), stop=(c == 7))
                    for c in range(8):
                        A("pe", "matmul", [hTe.res, wu.res], [bank_res[hb]], banks[hb][:, 2 * gu:2 * gu + 2],
                          lhsT=wu.ap[:, c, gu * 128:(gu + 1) * 128], rhs=hTe.ap[:, c, 0:514:513], start=(c == 0),
                          stop=(c == 7))
                    ch = gu * NCH + jc
                    cw = pk.ap[:, l, 41 + ch * 4:41 + ch * 4 + 4]
                    y = ar.alloc("y", [128, 512], F32)
                    A("act", "activation", [bank_res[mb], pk.res], [y.res], out=y.ap, in_=banks[mb][:, :],
                      func=AF.Identity, scale=cw[:, 1:2], bias=cw[:, 3:4])
                    A("dve", "scalar_tensor_tensor", [bank_res[mb], pk.res, y.res], [y.res], out=y.ap[:, 1:512],
                      in0=banks[mb][:, 0:511], scalar=cw[:, 0:1], in1=y.ap[:, 1:512], op0=ALU.mult, op1=ALU.add)
                    A("dve", "scalar_tensor_tensor", [bank_res[hb], pk.res, y.res], [y.res], out=y.ap[:, 0:1],
                      in0=banks[hb][:, 2 * gu:2 * gu + 1], scalar=cw[:, 0:1], in1=y.ap[:, 0:1], op0=ALU.mult, op1=ALU.add)
                    A("dve", "scalar_tensor_tensor", [bank_res[mb], pk.res, y.res], [y.res], out=y.ap[:, 0:511],
                      in0=banks[mb][:, 1:512], scalar=cw[:, 2:3], in1=y.ap[:, 0:511], op0=ALU.mult, op1=ALU.add)
                    A("dve", "scalar_tensor_tensor", [bank_res[hb], pk.res, y.res], [y.res], out=y.ap[:, 511:512],
                      in0=banks[hb][:, 2 * gu + 1:2 * gu + 2], scalar=cw[:, 2:3], in1=y.ap[:, 511:512], op0=ALU.mult, op1=ALU.add)
                    ys.append(y)
                A("act", "activation", [ys[0].res], [ys[0].res], out=ys[0].ap, in_=ys[0].ap, func=AF.Gelu_apprx_tanh)
                A("pool", "tensor_tensor", [ys[0].res, ys[1].res], [mT.res], out=mT.ap[:, jc, :], in0=ys[0].ap,
                  in1=ys[1].ap, op=ALU.mult)
                ar.free(*ys)
            ar.free(hTe)
            return mT

        def ffn_down(qb, mT):
            for j in range(4):
                t = qb * 4 + j
                bk2 = [mm_bank(), mm_bank()]
                for i in range(2):
                    for jc in range(NCH):
                        A("pe", "matmul", [mT.res, w_dn_t.res], [bank_res[bk2[i]]], banks[bk2[i]][:, :],
                          lhsT=mT.ap[:, jc, j * 128:(j + 1) * 128], rhs=w_dn_t.ap[:, jc, i * 512:(i + 1) * 512],
                          start=(jc == 0), stop=(jc == NCH - 1))
                post_norm_add(t, bk2, gpost)
            ar.free(mT)
            mm_all[0] = False

        hTe_cur = ffn_build(0)
        for qb in range(4):
            mT_cur = ffn_up(qb, hTe_cur)
            if qb + 1 < 4:
                hTe_cur = ffn_build(qb + 1)
            ffn_down(qb, mT_cur)
        ar.free(w_dn_t, gpost, ss6, rs6, hl, *stgs, *wus)

    ov = out_d.rearrange("(t p) d -> p t d", p=128)
    last = []
    for t in range(NT):
        last.append(A("sp", "dma_start", [x_res[t]], [out_res[t]], is_dma=True, out=ov[:, t, :], in_=xs[:, t, :]))
    fin = A("sp", "nop", [out_res], [])

    counts = sc.finalize()
    nsem = {e: max(1, (counts[e] + SEM_LIMIT - 1) // SEM_LIMIT) for e in ("pe", "act", "dve", "pool", "sp")}
    sems = {e: [es.enter_context(nc.semaphore("s_%s%d" % (e, i))) for i in range(nsem[e])] for e in nsem}
    dsems = {q: [es.enter_context(nc.semaphore("d_%s%d" % (q, i))) for i in range(NDS)] for q in ("sp", "pool")}

    def emit(ename, eng):
        waited = {}
        for op in sc.ops[ename]:
            for d in op.deps:
                if d.is_dma:
                    sm, val = dsems[d.eng][d.dsem], d.dval
                    key = ("d", d.eng, d.dsem)
                else:
                    sm, val = sems[d.eng][d.semk // SEM_LIMIT], d.semk % SEM_LIMIT + 1
                    key = ("c", d.eng, d.semk // SEM_LIMIT)
                if waited.get(key, 0) >= val:
                    continue
                eng.wait_ge(sm, val)
                waited[key] = val
            if op.meth == "nop":
                continue
            ins = getattr(eng, op.meth)(*op.args, **op.kw)
            if op.is_dma:
                ins.then_inc(dsems[op.eng][op.dsem], 16)
            elif op.sig:
                ins.then_inc(sems[op.eng][op.semk // SEM_LIMIT], 1)

    block = es.enter_context(nc.Block())

    @block.sync
    def _(e):
        emit("sp", e)

    @block.gpsimd
    def _(e):
        emit("pool", e)

    @block.tensor
    def _(e):
        emit("pe", e)

    @block.scalar
    def _(e):
        emit("act", e)

    @block.vector
    def _(e):
        emit("dve", e)

    es.close()
    stats = {e: len(sc.ops[e]) for e in ENGS}
    stats["arena_peak_kb"] = ar.peak
    return nc, stats


def prep_inputs(inp, n_layers=DEPTH):
    f = lambda a: np.ascontiguousarray(np.asarray(a, dtype=np.float32))
    pk = np.zeros((128, DEPTH, 217), np.float32)
    for l in range(DEPTH):
        pk[:, l, 0:8] = np.asarray(inp["mix_pre_g"])[l].reshape(8, 128).T
        pk[:, l, 8:16] = np.asarray(inp["mem_pre_g"])[l].reshape(8, 128).T
        pk[:, l, 16:24] = np.asarray(inp["ffn_pre_g"])[l].reshape(8, 128).T
        pk[:, l, 24:32] = np.asarray(inp["mem_kv_g"])[l].reshape(8, 128).T
        pk[:, l, 32:35] = np.asarray(inp["mla_cq_g"])[l].reshape(3, 128).T
        pk[:, l, 35:37] = np.asarray(inp["mla_ckv_g"])[l].reshape(2, 128).T
        pk[:, l, 37:41] = np.asarray(inp["sgu_b_s"])[l].T
        cw = np.asarray(inp["ffn_conv_w"])[l]
        cbv = np.asarray(inp["ffn_conv_b"])[l]
        c4 = np.concatenate([cw, cbv[None, :]], axis=0)
        pk[:, l, 41:217] = c4.reshape(4, 44, 128).transpose(2, 1, 0).reshape(128, 176)
    bc = np.concatenate([
        np.asarray(inp["mix_post_g"]), np.asarray(inp["mem_post_g"]), np.asarray(inp["ffn_post_g"]),
        np.asarray(inp["sgu_norm_g"]).reshape(DEPTH, 256), np.asarray(inp["diff_sub_g"]),
        np.asarray(inp["diff_lam_q1"]), np.asarray(inp["diff_lam_k1"]),
        np.asarray(inp["diff_lam_q2"]), np.asarray(inp["diff_lam_k2"])], axis=1).astype(np.float32)
    wsT = np.asarray(inp["sgu_w_s"]).transpose(0, 3, 1, 2).reshape(DEPTH, 128, 512)
    nl = n_layers
    f = lambda a: np.ascontiguousarray(np.asarray(a, dtype=np.float32)[:nl])
    wup = np.asarray(inp["ffn_w_up"], dtype=np.float32)[:nl].reshape(nl, 8, 128, 2, NCH, 128)
    wup = np.ascontiguousarray(wup.transpose(0, 4, 2, 1, 3, 5)).reshape(nl, NCH, 128, 2048)
    shared = {
        "w_in": f(inp["w_in"]), "w_uq": f(inp["mla_w_uq"]), "w_ukv": f(inp["mla_w_ukv"]), "wsT": f(wsT),
        "w_mo": f(inp["w_mix_out"]), "w_q": f(inp["mem_w_q"]), "w_kv": f(inp["mem_w_kv"]), "w_o": f(inp["mem_w_o"]),
        "w_up": wup, "w_dn": f(inp["ffn_w_down"]),
        "pk": np.ascontiguousarray(pk), "bc": np.ascontiguousarray(bc),
    }
    x = np.asarray(inp["x"], dtype=np.float32)
    mem = np.asarray(inp["mem"], dtype=np.float32)
    pos = np.asarray(inp["positions"]).astype(np.int32)
    maps = []
    for b in range(x.shape[0]):
        m = dict(shared)
        m["x"] = np.ascontiguousarray(x[b])
        m["mem"] = np.ascontiguousarray(mem[b])
        m["pos"] = np.ascontiguousarray(pos[b].reshape(NT, 128))
        maps.append(m)
    return maps


def kernel(**inputs):
    maps = prep_inputs(inputs)
    nc, _ = build()
    res = run_bass_kernel_spmd(nc, maps, core_ids=list(range(8)))
    return np.stack([np.asarray(r["out"], dtype=np.float32) for r in res.results], axis=0)
```

```python
import math
from contextlib import ExitStack

import numpy as np
import concourse.bass as bass
import concourse.mybir as mybir
from concourse.bass_utils import run_bass_kernel_spmd

F32 = mybir.dt.float32
BF16 = mybir.dt.bfloat16
I32 = mybir.dt.int32
U8 = mybir.dt.uint8
AF = mybir.ActivationFunctionType
ALU = mybir.AluOpType
AX = mybir.AxisListType

D = 1024
S = 2048
NT = 16
DEPTH = 4
NMEM = 256
EPS = 1e-6
THETA = 500000.0
INW = 1952
DFF = 2816
NCH = 22

ENGS = ("pe", "act", "dve", "pool", "sp")
SEM_LIMIT = 1000
NDS = 12


class Res:
    __slots__ = ("lw", "rd", "rd_dma")

    def __init__(self):
        self.lw = None
        self.rd = {}
        self.rd_dma = []


class Op:
    __slots__ = ("eng", "meth", "args", "kw", "deps", "sig", "semk", "is_dma", "dsem", "dval", "idx")


class Sched:
    def __init__(self):
        self.ops = {e: [] for e in ENGS}
        self.ndma = {"sp": 0, "pool": 0}
        self.dma_hist = {"sp": [], "pool": []}
        self.n = 0

    def add(self, eng, meth, r, w, *args, is_dma=False, **kw):
        op = Op()
        op.eng, op.meth, op.args, op.kw = eng, meth, args, kw
        op.is_dma = is_dma
        op.sig = False
        op.semk = None
        op.idx = self.n
        op.dsem = None
        op.dval = None
        self.n += 1
        deps = {}

        def need(d):
            if d is None or d is op:
                return
            if (not d.is_dma) and d.eng == "pe" and eng == "pe" and not is_dma:
                return
            deps[d.idx] = d

        wset = set(id(x) for x in w)
        for x in r:
            need(x.lw)
        for x in w:
            need(x.lw)
            for d in x.rd.values():
                need(d)
            for d in x.rd_dma:
                need(d)
        if is_dma:
            q = self.dma_hist[eng]
            op.dsem = len(q) % NDS
            op.dval = 16 * (len(q) // NDS + 1)
            if len(q) >= NDS:
                need(q[len(q) - NDS])
            q.append(op)
        best = {}
        out = []
        for d in deps.values():
            if d.is_dma:
                out.append(d)
            else:
                b = best.get(d.eng)
                if b is None or d.idx > b.idx:
                    best[d.eng] = d
        out.extend(best.values())
        for d in out:
            d.sig = True
        op.deps = out
        for x in w:
            x.lw = op
            x.rd = {}
            x.rd_dma = []
        for x in r:
            if id(x) in wset:
                continue
            if is_dma:
                x.rd_dma.append(op)
            else:
                x.rd[eng] = op
        self.ops[eng].append(op)
        return op

    def finalize(self):
        for e in ENGS:
            k = 0
            for op in self.ops[e]:
                if op.is_dma:
                    continue
                if op.sig:
                    op.semk = k
                    k += 1
        return {e: sum(1 for o in self.ops[e] if (not o.is_dma) and o.sig) for e in ENGS}


class Buf:
    __slots__ = ("ap", "res", "off", "size", "name", "owner")


class Arena:
    CH = 1024

    def __init__(self, t, nbytes, base=0, nextfit=False):
        self.t = t
        self.base = base
        self.nextfit = nextfit
        self.ptr = 0
        self.nbytes = nbytes
        self.nch = nbytes // self.CH
        self.res = [Res() for _ in range(self.nch)]
        self.used = [False] * self.nch
        self.peak = 0

    def alloc(self, name, shape, dt):
        esz = 4 if dt in (F32, I32) else 2
        n = 1
        for s in shape[1:]:
            n *= s
        nb = n * esz
        k = (nb + self.CH - 1) // self.CH
        start = None
        order = [0]
        if self.nextfit:
            order = [self.ptr, 0]
        for s0 in order:
            run = 0
            for i in range(s0, self.nch):
                if not self.used[i]:
                    run += 1
                    if run == k:
                        start = i - k + 1
                        break
                else:
                    run = 0
            if start is not None:
                break
        if start is not None:
            self.ptr = start + k
        if start is None:
            raise RuntimeError("arena OOM for %s (%d B); used=%d" % (name, nb, sum(self.used)))
        for i in range(start, start + k):
            self.used[i] = True
        self.peak = max(self.peak, max(i for i in range(self.nch) if self.used[i]) + 1)
        b = Buf()
        b.name = name
        b.owner = self
        b.off = start
        b.size = k
        o = self.base + start * self.CH
        ap = self.t[:, o:o + nb].bitcast(dt)
        if len(shape) == 3:
            b.ap = ap.rearrange("p (a b) -> p a b", b=shape[2])
        else:
            b.ap = ap
        b.res = self.res[start:start + k]
        return b

    def free(self, *bufs):
        for b in bufs:
            for i in range(b.off, b.off + b.size):
                b.owner.used[i] = False


TRANS = {"dkm", "wuf", "stg", "oT", "hn", "hT", "ssq", "rsq", "cn", "rt", "zg", "st", "vsq", "vn", "qk", "r6", "pt", "rc", "o1", "s2", "tmp",
         "cT", "aT", "mn", "y", "wu", "ptm", "ltmp"}


class Arenas:
    def __init__(self, main, trans):
        self.main = main
        self.trans = trans

    def alloc(self, name, shape, dt):
        if name in TRANS:
            return self.trans.alloc(name, shape, dt)
        return self.main.alloc(name, shape, dt)

    def free(self, *bufs):
        self.main.free(*bufs)

    @property
    def peak(self):
        return (self.main.peak, self.trans.peak)


def build(n_layers=DEPTH, stop_after=None):
    nc = bass.Bass("TRN2", target_bir_lowering=False)

    def din(name, shape, dt=F32):
        return nc.dram_tensor(name, list(shape), dt, kind="ExternalInput").ap()

    WL = n_layers
    x_d = din("x", [S, D])
    mem_d = din("mem", [NMEM, D])
    pos_d = din("pos", [NT, 128], I32)
    w_in_d = din("w_in", [WL, D, INW])
    w_uq_d = din("w_uq", [WL, 384, 768])
    w_ukv_d = din("w_ukv", [WL, 256, 1024])
    wsT_d = din("wsT", [WL, 128, 512])
    w_mo_d = din("w_mo", [WL, D, D])
    w_q_d = din("w_q", [WL, D, D])
    w_kv_d = din("w_kv", [WL, D, 2 * D])
    w_o_d = din("w_o", [WL, D, D])
    w_up_d = din("w_up", [WL, NCH, 128, 2048])
    w_dn_d = din("w_dn", [WL, DFF, D])
    NPK = 217
    pk_d = din("pk", [128, DEPTH, NPK])
    NBC = 3520
    bc_d = din("bc", [DEPTH, NBC])
    out_d = nc.dram_tensor("out", [S, D], F32, kind="ExternalOutput").ap()

    sc = Sched()
    es = ExitStack()
    xs = es.enter_context(nc.sbuf_tensor("xs", [128, NT, D], F32))
    x_res = [Res() for _ in range(NT)]
    MAIN_B = 114 * 1024
    TRANS_B = 29 * 1024
    ar_t = es.enter_context(nc.sbuf_tensor("arena", [128, MAIN_B + TRANS_B], U8))
    ar = Arenas(Arena(ar_t, MAIN_B), Arena(ar_t, TRANS_B, base=MAIN_B, nextfit=True))
    banks = [es.enter_context(nc.psum_tensor("bank%d" % i, [128, 512], F32)) for i in range(8)]
    bank_res = [Res() for _ in range(8)]
    bank_ids = set(id(b) for b in bank_res)
    out_res = [Res() for _ in range(NT)]

    def A(eng, meth, r, w, *a, **k):
        rr = []
        for x in r:
            rr.extend(x if isinstance(x, (list, tuple)) else [x])
        ww = []
        for x in w:
            ww.extend(x if isinstance(x, (list, tuple)) else [x])
        for x in rr:
            if id(x) in bank_ids:
                ww.append(x)
        return sc.add(eng, meth, rr, ww, *a, **k)

    tp_i = [0]

    def tp_bank():
        tp_i[0] ^= 1
        return tp_i[0]

    mm_i = [0]

    mm_all = [False]

    def mm_bank():
        if mm_all[0]:
            mm_i[0] = (mm_i[0] + 1) % 8
            return mm_i[0]
        mm_i[0] = (mm_i[0] + 1) % 5
        return 3 + mm_i[0]

    sb_i = [0]

    def s_bank():
        sb_i[0] = (sb_i[0] + 1) % 3
        return sb_i[0]

    cp_i = [0]

    def copy_eng():
        cp_i[0] ^= 1
        return "act" if cp_i[0] else "dve"

    def copy(eng, r, w, out, in_):
        if eng == "act":
            A("act", "activation", r, w, out=out, in_=in_, func=AF.Copy)
        elif eng == "dve":
            A("dve", "tensor_copy", r, w, out=out, in_=in_)
        else:
            A("pool", "tensor_copy", r, w, out=out, in_=in_)

    ident = ar.alloc("ident", [128, 128], BF16)
    identf = ar.alloc("identf", [128, 128], F32)
    A("pool", "memset", [], [identf.res], identf.ap, 0.0)
    A("pool", "iota", [], [identf.res], identf.ap, pattern=[[1, 128]], base=0, channel_multiplier=-1,
      allow_small_or_imprecise_dtypes=True)
    A("dve", "tensor_scalar", [identf.res], [ident.res], out=ident.ap, in0=identf.ap, scalar1=0.0, scalar2=None,
      op0=ALU.is_equal)
    identF = ar.alloc("identF", [128, 128], F32)
    A("dve", "tensor_scalar", [identf.res], [identF.res], out=identF.ap, in0=identf.ap, scalar1=0.0, scalar2=None,
      op0=ALU.is_equal)
    epsb = ar.alloc("eps", [128, 1], F32)
    A("dve", "memset", [], [epsb.res], epsb.ap, EPS)
    zerob = ar.alloc("zero", [128, 1], F32)
    A("dve", "memset", [], [zerob.res], zerob.ap, 0.0)
    junk = ar.alloc("junk", [128, 1024], BF16)

    xv = x_d.rearrange("(t p) d -> p t d", p=128)
    for t in range(NT):
        A("sp", "dma_start", [], [x_res[t]], is_dma=True, out=xs[:, t, :], in_=xv[:, t, :])

    pk = ar.alloc("pk", [128, DEPTH, NPK], F32)
    A("sp", "dma_start", [], [pk.res], is_dma=True, out=pk.ap, in_=pk_d)

    posi = ar.alloc("posi", [128, NT], I32)
    for t in range(NT):
        A("sp", "dma_start", [], [posi.res], is_dma=True, out=posi.ap[:, t:t + 1],
          in_=pos_d[t:t + 1, :].rearrange("a p -> p a"))
    posf = ar.alloc("posf", [128, NT], F32)
    A("dve", "tensor_copy", [posi.res], [posf.res], out=posf.ap, in_=posi.ap)
    ang = ar.alloc("ang", [128, NT, 16], F32)
    for f in range(16):
        A("dve", "tensor_scalar", [posf.res], [ang.res], out=ang.ap[:, :, f], in0=posf.ap,
          scalar1=float(THETA ** (-2.0 * f / 32.0)), scalar2=None, op0=ALU.mult)
    cosA = ar.alloc("cosA", [128, NT, 16], F32)
    sinA = ar.alloc("sinA", [128, NT, 16], F32)
    TWO_PI = 2.0 * math.pi
    C1 = 6.28125
    C2 = TWO_PI - C1
    tA = ar.alloc("tA", [128, NT, 16], F32)
    tK = ar.alloc("tK", [128, NT, 16], I32)
    tKf = ar.alloc("tKf", [128, NT, 16], F32)
    tM = ar.alloc("tM", [128, NT, 16], F32)
    for (dst, shift) in ((sinA, 0.0), (cosA, math.pi / 2)):
        A("dve", "tensor_scalar", [ang.res], [tA.res], out=tA.ap, in0=ang.ap, scalar1=shift, scalar2=None,
          op0=ALU.add)
        A("dve", "tensor_scalar", [tA.res], [tM.res], out=tM.ap, in0=tA.ap, scalar1=1.0 / TWO_PI, scalar2=None,
          op0=ALU.mult)
        A("dve", "tensor_copy", [tM.res], [tK.res], out=tK.ap, in_=tM.ap)
        A("dve", "tensor_copy", [tK.res], [tKf.res], out=tKf.ap, in_=tK.ap)
        A("dve", "scalar_tensor_tensor", [tKf.res, tA.res], [tM.res], out=tM.ap, in0=tKf.ap, scalar=-C1,
          in1=tA.ap, op0=ALU.mult, op1=ALU.add)
        A("dve", "scalar_tensor_tensor", [tKf.res, tM.res], [tA.res], out=tA.ap, in0=tKf.ap, scalar=-C2,
          in1=tM.ap, op0=ALU.mult, op1=ALU.add)
        A("dve", "tensor_scalar", [tA.res], [tM.res], out=tM.ap, in0=tA.ap, scalar1=math.pi, scalar2=-TWO_PI,
          op0=ALU.is_gt, op1=ALU.mult)
        A("dve", "tensor_tensor", [tA.res, tM.res], [tKf.res], out=tKf.ap, in0=tA.ap, in1=tM.ap, op=ALU.add)
        A("dve", "tensor_scalar", [tKf.res], [tM.res], out=tM.ap, in0=tKf.ap, scalar1=-math.pi, scalar2=TWO_PI,
          op0=ALU.is_lt, op1=ALU.mult)
        A("dve", "tensor_tensor", [tKf.res, tM.res], [tA.res], out=tA.ap, in0=tKf.ap, in1=tM.ap, op=ALU.add)
        A("dve", "tensor_scalar", [tA.res], [tM.res], out=tM.ap, in0=tA.ap, scalar1=math.pi, scalar2=-math.pi,
          op0=ALU.min, op1=ALU.max)
        A("act", "activation", [tM.res], [dst.res], out=dst.ap, in_=tM.ap, func=AF.Sin)
    ar.free(tA, tK, tKf, tM, ang, posi, posf, identf)

    def rstd_from(ss, ncol, dim, dst):
        A("act", "activation", [ss[1], epsb.res], [dst[1]], out=dst[0], in_=ss[0], func=AF.Ln, scale=1.0 / dim,
          bias=epsb.ap[:, 0:1])
        A("act", "activation", [dst[1]], [dst[1]], out=dst[0], in_=dst[0], func=AF.Exp, scale=-0.5)

    def load_w(name, shape, src, scale_cols=None, l=0, chunk_cols=None):
        b = ar.alloc(name, shape, BF16)
        kc, n = shape[1], shape[2]
        step = 2048
        for c in range(kc):
            for n0 in range(0, n, step):
                n1 = min(n, n0 + step)
                stg = ar.alloc("stg", [128, n1 - n0], F32)
                A("sp", "dma_start", [], [stg.res], is_dma=True, out=stg.ap, in_=src[:, c, n0:n1])
                if scale_cols is not None:
                    A("pool", "tensor_scalar", [stg.res, pk.res], [b.res], out=b.ap[:, c, n0:n1], in0=stg.ap,
                      scalar1=pk.ap[:, l, scale_cols + c:scale_cols + c + 1], scalar2=1.0, op0=ALU.mult, op1=ALU.mult)
                else:
                    A("pool", "tensor_copy", [stg.res], [b.res], out=b.ap[:, c, n0:n1], in_=stg.ap)
                ar.free(stg)
        return b

    def transposes(srcs, src_res, dst_ap, dst_res, rows=128):
        bk = tp_bank()
        pv = banks[bk][:].bitcast(BF16).rearrange("p (a b) -> p a b", b=128)
        for i, s_ap in enumerate(srcs):
            A("pe", "transpose", [src_res, ident.res], [bank_res[bk]], out=pv[0:rows, i, :], in_=s_ap,
              identity=ident.ap)
        copy(copy_eng(), [bank_res[bk]], [dst_res], dst_ap, pv[0:rows, 0:len(srcs), :])

    def attention(QT, KT, krows, roff, V_of_kc, v_res, nkc, dv, scale, evac, vw=None):
        pending = [None]

        def epilogue(qb, ab):
            oT = ar.alloc("oT", [128, 512], F32)
            A("dve", "tensor_copy", [bank_res[ab]], [oT.res], out=oT.ap[0:dv + 1, :], in_=banks[ab][0:dv + 1, :])

            def pe_part():
                tb = mm_bank()
                for j in range(4):
                    A("pe", "transpose", [oT.res, identF.res], [bank_res[tb]],
                      out=banks[tb][:, j * (dv + 1):(j + 1) * (dv + 1)], in_=oT.ap[0:dv + 1, j * 128:(j + 1) * 128],
                      identity=identF.ap[0:dv + 1, 0:dv + 1])
                ar.free(oT)
                evac(qb, banks[tb][:, 0:4 * (dv + 1)].rearrange("p (j e) -> p j e", e=dv + 1), bank_res[tb])
            return pe_part

        for qb in range(4):
            ab = mm_bank()
            pts = {}

            def issue_s(kc, qb=qb, pts=pts):
                sb_ = s_bank()
                A("pe", "matmul", [QT[1], KT[1]], [bank_res[sb_]], banks[sb_][:, :],
                  lhsT=KT[0][roff:roff + krows, kc * 128:(kc + 1) * 128],
                  rhs=QT[0][roff:roff + krows, qb * 512:(qb + 1) * 512], start=True, stop=True,
                  **({"tile_position": (roff, 0)} if krows == 32 else {}))
                pt = ar.alloc("pt", [128, 512], BF16)
                A("act", "activation", [bank_res[sb_]], [pt.res], out=pt.ap, in_=banks[sb_][:, :], func=AF.Exp,
                  scale=scale)
                pts[kc] = pt

            def issue_pv(kc, ab=ab, pts=pts):
                pt = pts.pop(kc)
                A("pe", "matmul", [pt.res, v_res], [bank_res[ab]], banks[ab][0:(vw or dv + 1), :],
                  lhsT=V_of_kc(kc), rhs=pt.ap, start=(kc == 0), stop=(kc == nkc - 1))
                ar.free(pt)

            issue_s(0)
            issue_s(1)
            for kc in range(nkc):
                if kc + 2 < nkc:
                    issue_s(kc + 2)
                issue_pv(kc)
                if kc == 2 and pending[0] is not None:
                    pending[0]()
                    pending[0] = None
            if pending[0] is not None:
                pending[0]()
            pending[0] = epilogue(qb, ab)
        pending[0]()

    for l in range(n_layers):
        if stop_after == "setup":
            break
        lam_init = 0.8 - 0.6 * math.exp(-0.3 * l)
        bcs = ar.alloc("bcs", [128, 448], F32)
        A("sp", "dma_start", [], [bcs.res], is_dma=True, out=bcs.ap, in_=bc_d[l:l + 1, 3072:3520].partition_broadcast(128))
        lamt = ar.alloc("lamt", [128, 8], F32)
        ltmp = ar.alloc("ltmp", [128, 64], F32)
        A("dve", "tensor_tensor", [bcs.res], [ltmp.res], out=ltmp.ap[:, 0:32], in0=bcs.ap[:, 320:352],
          in1=bcs.ap[:, 352:384], op=ALU.mult)
        A("dve", "tensor_tensor", [bcs.res], [ltmp.res], out=ltmp.ap[:, 32:64], in0=bcs.ap[:, 384:416],
          in1=bcs.ap[:, 416:448], op=ALU.mult)
        A("dve", "tensor_reduce", [ltmp.res], [lamt.res], out=lamt.ap[:, 0:2],
          in_=ltmp.ap.rearrange("p (a b) -> p a b", b=32), axis=AX.X, op=ALU.add)
        A("act", "activation", [lamt.res], [lamt.res], out=lamt.ap[:, 2:4], in_=lamt.ap[:, 0:2], func=AF.Exp)
        A("dve", "tensor_tensor", [lamt.res], [lamt.res], out=lamt.ap[:, 4:5], in0=lamt.ap[:, 3:4], in1=lamt.ap[:, 2:3],
          op=ALU.subtract)
        A("dve", "tensor_scalar", [lamt.res], [lamt.res], out=lamt.ap[:, 5:6], in0=lamt.ap[:, 4:5], scalar1=-lam_init,
          scalar2=None, op0=ALU.add)
        neglam = lamt.ap[:, 5:6]
        A("dve", "tensor_scalar", [bcs.res], [bcs.res], out=bcs.ap[:, 256:320], in0=bcs.ap[:, 256:320],
          scalar1=1.0 - lam_init, scalar2=None, op0=ALU.mult)
        ar.free(ltmp)

        w_in_t = load_w("w_in", [128, 8, INW], w_in_d[l].rearrange("(c p) n -> p c n", p=128), scale_cols=0, l=l)
        wsT = load_w("wsT", [128, 1, 512], wsT_d[l].rearrange("p (a n) -> p a n", a=1))

        ss1 = ar.alloc("ss1", [128, NT], F32)
        rs1 = ar.alloc("rs1", [128, NT], F32)
        for t in range(NT):
            A("act", "activation", [x_res[t]], [junk.res, ss1.res], out=junk.ap, in_=xs[:, t, :], func=AF.Square,
              accum_out=ss1.ap[:, t:t + 1])
        rstd_from((ss1.ap, ss1.res), NT, D, (rs1.ap, rs1.res))
        cqnT = ar.alloc("cqnT", [128, 3, S], BF16)
        ckvnT = ar.alloc("ckvnT", [128, 2, S], BF16)
        krtm = ar.alloc("krtm", [128, NT, 32], BF16)
        dqT = ar.alloc("dqT", [128, 2, S], BF16)
        dkT = ar.alloc("dkT", [128, 2, S], BF16)
        dV = ar.alloc("dV", [128, NT * 4, 65], BF16)
        A("pool", "memset", [], [dV.res], dV.ap[:, :, 64:65], 1.0)
        cb = ar.alloc("cb", [128, NT, 256], BF16)
        COLS = ((0, 384), (384, 672), (672, 1184), (1184, 1696), (1696, 1952))
        for t in range(NT):
            tc_ = slice(t * 128, (t + 1) * 128)
            hn = ar.alloc("hn", [128, D], BF16)
            A("dve", "tensor_scalar", [x_res[t], rs1.res], [hn.res], out=hn.ap, in0=xs[:, t, :],
              scalar1=rs1.ap[:, t:t + 1], scalar2=None, op0=ALU.mult)
            hT = ar.alloc("hT", [128, 8, 128], BF16)
            transposes([hn.ap[:, c * 128:(c + 1) * 128] for c in range(8)], hn.res, hT.ap, hT.res)
            ar.free(hn)
            pj = [mm_bank() for _ in range(5)]
            for gi, (c0, c1) in enumerate(COLS):
                for c in range(8):
                    A("pe", "matmul", [hT.res, w_in_t.res], [bank_res[pj[gi]]], banks[pj[gi]][:, 0:c1 - c0],
                      lhsT=hT.ap[:, c, :], rhs=w_in_t.ap[:, c, c0:c1], start=(c == 0), stop=(c == 7))
            ar.free(hT)
            def chain_norm():
                ssq = ar.alloc("ssq", [128, 2], F32)
                rsq = ar.alloc("rsq", [128, 2], F32)
                A("act", "activation", [bank_res[pj[0]]], [junk.res, ssq.res], out=junk.ap[:, 0:384],
                  in_=banks[pj[0]][:, 0:384], func=AF.Square, accum_out=ssq.ap[:, 0:1])
                yield
                A("act", "activation", [bank_res[pj[1]]], [junk.res, ssq.res], out=junk.ap[:, 0:256],
                  in_=banks[pj[1]][:, 0:256], func=AF.Square, accum_out=ssq.ap[:, 1:2])
                yield
                A("act", "activation", [ssq.res, epsb.res], [rsq.res], out=rsq.ap[:, 0:1], in_=ssq.ap[:, 0:1], func=AF.Ln,
                  scale=1.0 / 384, bias=epsb.ap[:, 0:1])
                yield
                A("act", "activation", [ssq.res, epsb.res], [rsq.res], out=rsq.ap[:, 1:2], in_=ssq.ap[:, 1:2], func=AF.Ln,
                  scale=1.0 / 256, bias=epsb.ap[:, 0:1])
                yield
                A("act", "activation", [rsq.res], [rsq.res], out=rsq.ap, in_=rsq.ap, func=AF.Exp, scale=-0.5)
                yield
                cn = ar.alloc("cn", [128, 640], BF16)
                A("dve", "tensor_scalar", [bank_res[pj[0]], rsq.res], [cn.res], out=cn.ap[:, 0:384],
                  in0=banks[pj[0]][:, 0:384], scalar1=rsq.ap[:, 0:1], scalar2=None, op0=ALU.mult)
                yield
                A("dve", "tensor_scalar", [bank_res[pj[1]], rsq.res], [cn.res], out=cn.ap[:, 384:640],
                  in0=banks[pj[1]][:, 0:256], scalar1=rsq.ap[:, 1:2], scalar2=None, op0=ALU.mult)
                yield
                transposes([cn.ap[:, c * 128:(c + 1) * 128] for c in range(3)], cn.res, cqnT.ap[:, :, tc_], cqnT.res)
                yield
                transposes([cn.ap[:, 384 + c * 128:384 + (c + 1) * 128] for c in range(2)], cn.res, ckvnT.ap[:, :, tc_],
                           ckvnT.res)
                yield
                ar.free(ssq, rsq, cn)
                yield

            def chain_kr():
                rt = ar.alloc("rt", [128, 4, 16], F32)
                kb = banks[pj[1]]
                kr_res = bank_res[pj[1]]
                cA = cosA.ap[:, t, :]
                sA = sinA.ap[:, t, :]
                A("dve", "tensor_tensor", [kr_res, cosA.res], [rt.res], out=rt.ap[:, 0, :], in0=kb[:, 256:272], in1=cA,
                  op=ALU.mult)
                yield
                A("dve", "tensor_tensor", [kr_res, sinA.res], [rt.res], out=rt.ap[:, 1, :], in0=kb[:, 272:288], in1=sA,
                  op=ALU.mult)
                yield
                A("dve", "tensor_tensor", [kr_res, cosA.res], [rt.res], out=rt.ap[:, 2, :], in0=kb[:, 272:288], in1=cA,
                  op=ALU.mult)
                yield
                A("dve", "tensor_tensor", [kr_res, sinA.res], [rt.res], out=rt.ap[:, 3, :], in0=kb[:, 256:272], in1=sA,
                  op=ALU.mult)
                yield
                A("dve", "tensor_tensor", [rt.res], [krtm.res], out=krtm.ap[:, t, 0:16], in0=rt.ap[:, 0, :],
                  in1=rt.ap[:, 1, :], op=ALU.subtract)
                yield
                A("dve", "tensor_tensor", [rt.res], [krtm.res], out=krtm.ap[:, t, 16:32], in0=rt.ap[:, 2, :],
                  in1=rt.ap[:, 3, :], op=ALU.add)
                yield
                ar.free(rt)
                yield

            def chain_sgu():
                zg = ar.alloc("zg", [128, 512], F32)
                A("act", "activation", [bank_res[pj[2]]], [zg.res], out=zg.ap, in_=banks[pj[2]][:, :],
                  func=AF.Gelu_apprx_tanh)
                yield
                st = ar.alloc("st", [128, 16], F32)
                vsq = ar.alloc("vsq", [128, 256], F32)
                v3 = zg.ap[:, 256:512].rearrange("p (g c) -> p g c", c=64)
                A("dve", "tensor_reduce", [zg.res], [st.res], out=st.ap[:, 0:4], in_=v3, axis=AX.X, op=ALU.add)
                yield
                A("act", "activation", [zg.res], [vsq.res], out=vsq.ap, in_=zg.ap[:, 256:512], func=AF.Square)
                yield
                A("dve", "tensor_reduce", [vsq.res], [st.res], out=st.ap[:, 4:8],
                  in_=vsq.ap.rearrange("p (g c) -> p g c", c=64), axis=AX.X, op=ALU.add)
                yield
                A("dve", "tensor_scalar", [st.res], [st.res], out=st.ap[:, 8:12], in0=st.ap[:, 0:4], scalar1=1.0 / 64,
                  scalar2=None, op0=ALU.mult)
                yield
                A("dve", "tensor_tensor", [st.res], [st.res], out=st.ap[:, 12:16], in0=st.ap[:, 8:12], in1=st.ap[:, 8:12],
                  op=ALU.mult)
                yield
                A("dve", "scalar_tensor_tensor", [st.res], [st.res], out=st.ap[:, 4:8], in0=st.ap[:, 4:8], scalar=1.0 / 64,
                  in1=st.ap[:, 12:16], op0=ALU.mult, op1=ALU.subtract)
                yield
                A("act", "activation", [st.res, epsb.res], [st.res], out=st.ap[:, 0:4], in_=st.ap[:, 4:8], func=AF.Ln,
                  bias=epsb.ap[:, 0:1])
                yield
                A("act", "activation", [st.res], [st.res], out=st.ap[:, 0:4], in_=st.ap[:, 0:4], func=AF.Exp, scale=-0.5)
                yield
                vq3 = vsq.ap.rearrange("p (g c) -> p g c", c=64)
                A("dve", "tensor_tensor", [zg.res, st.res], [vsq.res], out=vq3, in0=v3,
                  in1=st.ap[:, 8:12].unsqueeze(2).broadcast_to([128, 4, 64]), op=ALU.subtract)
                yield
                A("dve", "tensor_tensor", [vsq.res, st.res], [vsq.res], out=vq3, in0=vq3,
                  in1=st.ap[:, 0:4].unsqueeze(2).broadcast_to([128, 4, 64]), op=ALU.mult)
                yield
                vn = ar.alloc("vn", [128, 256], BF16)
                A("dve", "tensor_tensor", [vsq.res, bcs.res], [vn.res], out=vn.ap, in0=vsq.ap, in1=bcs.ap[:, 0:256],
                  op=ALU.mult)
                yield
                mb = mm_bank()
                for g in range(4):
                    A("pe", "matmul", [vn.res, wsT.res], [bank_res[mb]], banks[mb][:, g * 64:(g + 1) * 64],
                      lhsT=wsT.ap[:, 0, g * 128:(g + 1) * 128], rhs=vn.ap[:, g * 64:(g + 1) * 64], start=True, stop=True)
                    yield
                for g in range(4):
                    A("dve", "scalar_tensor_tensor", [bank_res[mb], pk.res, zg.res], [cb.res],
                      out=cb.ap[:, t, g * 64:(g + 1) * 64], in0=banks[mb][:, g * 64:(g + 1) * 64],
                      scalar=pk.ap[:, l, 37 + g:38 + g], in1=zg.ap[:, g * 64:(g + 1) * 64], op0=ALU.add, op1=ALU.mult)
                    yield
                ar.free(zg, st, vsq, vn)
                yield

            def chain_dqk():
                qk = ar.alloc("qk", [128, 16, 32], BF16)
                r6 = ar.alloc("r6", [128, 4 * 16, 8], F32)
                pq = banks[pj[3]][:, :].rearrange("p (m d) -> p m d", d=32)
                qres = bank_res[pj[3]]
                cD = cosA.ap[:, t, 0:16:2].unsqueeze(1).broadcast_to([128, 16, 8])
                sD = sinA.ap[:, t, 0:16:2].unsqueeze(1).broadcast_to([128, 16, 8])
                A("dve", "tensor_tensor", [qres, cosA.res], [r6.res], out=r6.ap[:, 0:16, :], in0=pq[:, :, 0:8], in1=cD,
                  op=ALU.mult)
                yield
                A("dve", "tensor_tensor", [qres, sinA.res], [r6.res], out=r6.ap[:, 16:32, :], in0=pq[:, :, 8:16], in1=sD,
                  op=ALU.mult)
                yield
                A("dve", "tensor_tensor", [qres, cosA.res], [r6.res], out=r6.ap[:, 32:48, :], in0=pq[:, :, 8:16], in1=cD,
                  op=ALU.mult)
                yield
                A("dve", "tensor_tensor", [qres, sinA.res], [r6.res], out=r6.ap[:, 48:64, :], in0=pq[:, :, 0:8], in1=sD,
                  op=ALU.mult)
                yield
                A("dve", "tensor_tensor", [r6.res], [qk.res], out=qk.ap[:, :, 0:8], in0=r6.ap[:, 0:16, :],
                  in1=r6.ap[:, 16:32, :], op=ALU.subtract)
                yield
                A("dve", "tensor_tensor", [r6.res], [qk.res], out=qk.ap[:, :, 8:16], in0=r6.ap[:, 32:48, :],
                  in1=r6.ap[:, 48:64, :], op=ALU.add)
                yield
                A("act", "activation", [qres], [qk.res], out=qk.ap[:, :, 16:32], in_=pq[:, :, 16:32], func=AF.Copy)
                yield
                qk2 = qk.ap.rearrange("p m d -> p (m d)")
                transposes([qk2[:, c * 128:(c + 1) * 128] for c in range(2)], qk.res, dqT.ap[:, :, tc_], dqT.res)
                yield
                transposes([qk2[:, 256 + c * 128:256 + (c + 1) * 128] for c in range(2)], qk.res, dkT.ap[:, :, tc_],
                           dkT.res)
                yield
                ar.free(qk, r6)
                yield

            def chain_dv():
                A("act", "activation", [bank_res[pj[4]]], [dV.res], out=dV.ap[:, t * 4:(t + 1) * 4, 0:64],
                  in_=banks[pj[4]][:, 0:256].rearrange("p (h e) -> p h e", e=64), func=AF.Copy)
                yield
                yield

            gens = [chain_norm(), chain_sgu(), chain_dqk(), chain_kr(), chain_dv()]
            while gens:
                for g_ in list(gens):
                    try:
                        next(g_)
                    except StopIteration:
                        gens.remove(g_)
        ar.free(ss1, rs1, w_in_t, wsT)
        if stop_after == "p1":
            break

        cc = ar.alloc("cc", [128, NT, 256], BF16)
        w_mo_t = load_w("w_mo", [128, 8, D], w_mo_d[l].rearrange("(c p) n -> p c n", p=128))
        w_uq_t = load_w("w_uq", [128, 3, 768], w_uq_d[l].rearrange("(c p) n -> p c n", p=128), scale_cols=32, l=l)
        w_ukv_t = load_w("w_ukv", [128, 2, 1024], w_ukv_d[l].rearrange("(c p) n -> p c n", p=128), scale_cols=35, l=l)
        for h in range(4):
            ous = []
            for c in range(2):
                m = 2 * h + c
                ci, ro = m // 4, 32 * (m % 4)
                ou = ar.alloc("ou%d" % c, [128, NT, 65], F32)
                ous.append(ou)

                def evac(qb, acc, acc_res, ou=ou):
                    A("dve", "tensor_copy", [acc_res], [ou.res], out=ou.ap[:, qb * 4:(qb + 1) * 4, :], in_=acc)

                dkm = ar.alloc("dkm", [128, S], BF16)
                A("pool", "memset", [], [dkm.res], dkm.ap, 0.0)
                A("pool", "tensor_copy", [dkT.res], [dkm.res], out=dkm.ap[ro:ro + 32, :], in_=dkT.ap[ro:ro + 32, ci, :])
                attention((dqT.ap[:, ci, :], dqT.res), (dkm.ap, dkm.res), 128, 0,
                          lambda kc, h=h: dV.ap[:, kc * 4 + h, :], dV.res, NT, 64, 32 ** -0.5, evac)
                ar.free(dkm)
            o0, o1 = ous
            fr = ar.alloc("fr", [128, 4, NT], F32)
            A("dve", "reciprocal", [o0.res], [fr.res], out=fr.ap[:, 0, :], in_=o0.ap[:, :, 64])
            A("dve", "reciprocal", [o1.res], [fr.res], out=fr.ap[:, 1, :], in_=o1.ap[:, :, 64])
            A("dve", "tensor_scalar", [fr.res, lamt.res], [fr.res], out=fr.ap[:, 1, :], in0=fr.ap[:, 1, :],
              scalar1=neglam, scalar2=None, op0=ALU.mult)
            A("dve", "tensor_tensor", [o0.res, fr.res], [o0.res], out=o0.ap[:, :, 0:64], in0=o0.ap[:, :, 0:64],
              in1=fr.ap[:, 0, :].unsqueeze(2).broadcast_to([128, NT, 64]), op=ALU.mult)
            A("dve", "tensor_tensor", [o1.res, fr.res], [o1.res], out=o1.ap[:, :, 0:64], in0=o1.ap[:, :, 0:64],
              in1=fr.ap[:, 1, :].unsqueeze(2).broadcast_to([128, NT, 64]), op=ALU.mult)
            A("dve", "tensor_tensor", [o0.res, o1.res], [o0.res], out=o0.ap[:, :, 0:64], in0=o0.ap[:, :, 0:64],
              in1=o1.ap[:, :, 0:64], op=ALU.add)
            A("dve", "tensor_tensor", [o0.res], [o1.res], out=o1.ap[:, :, 0:64], in0=o0.ap[:, :, 0:64],
              in1=o0.ap[:, :, 0:64], op=ALU.mult)
            A("dve", "tensor_reduce", [o1.res], [fr.res], out=fr.ap[:, 2, :], in_=o1.ap[:, :, 0:64], axis=AX.X,
              op=ALU.add)
            A("act", "activation", [fr.res, epsb.res], [fr.res], out=fr.ap[:, 3, :], in_=fr.ap[:, 2, :], func=AF.Ln,
              scale=1.0 / 64, bias=epsb.ap[:, 0:1])
            A("act", "activation", [fr.res], [fr.res], out=fr.ap[:, 3, :], in_=fr.ap[:, 3, :], func=AF.Exp, scale=-0.5)
            A("dve", "tensor_tensor", [o0.res, fr.res], [o0.res], out=o0.ap[:, :, 0:64], in0=o0.ap[:, :, 0:64],
              in1=fr.ap[:, 3, :].unsqueeze(2).broadcast_to([128, NT, 64]), op=ALU.mult)
            A("dve", "tensor_tensor", [o0.res, bcs.res], [cc.res], out=cc.ap[:, :, h * 64:(h + 1) * 64],
              in0=o0.ap[:, :, 0:64], in1=bcs.ap[:, 256:320].unsqueeze(1).broadcast_to([128, NT, 64]), op=ALU.mult)
            ar.free(o0, o1, fr)
        ar.free(dqT, dkT, dV)
        if stop_after == "p2":
            break

        ca = ar.alloc("ca", [128, NT, 512], BF16)
        for h in range(8):
            QKT = ar.alloc("QKT", [128, 2, S], BF16)
            Vh = ar.alloc("Vh", [128, NT, 128], BF16)
            A("pool", "memset", [], [Vh.res], Vh.ap[:, :, 64:128], 0.0)
            A("pool", "memset", [], [Vh.res], Vh.ap[:, :, 64:65], 1.0)
            def prep_group(g):
                bA = mm_bank()
                bB = mm_bank()
                for tt in range(4):
                    t = g * 4 + tt
                    tc_ = slice(t * 128, (t + 1) * 128)
                    for c in range(3):
                        A("pe", "matmul", [cqnT.res, w_uq_t.res], [bank_res[bA]], banks[bA][:, tt * 96:(tt + 1) * 96],
                          lhsT=cqnT.ap[:, c, tc_], rhs=w_uq_t.ap[:, c, h * 96:(h + 1) * 96], start=(c == 0),
                          stop=(c == 2))
                        yield
                    for c in range(2):
                        A("pe", "matmul", [ckvnT.res, w_ukv_t.res], [bank_res[bB]],
                          banks[bB][:, tt * 128:(tt + 1) * 128], lhsT=ckvnT.ap[:, c, tc_],
                          rhs=w_ukv_t.ap[:, c, h * 128:(h + 1) * 128], start=(c == 0), stop=(c == 1))
                        yield
                qA = banks[bA][:, 0:384].rearrange("p (t d) -> p t d", d=96)
                kvB = banks[bB][:, :].rearrange("p (t d) -> p t d", d=128)
                rA, rB = bank_res[bA], bank_res[bB]
                qk = ar.alloc("qk", [128, 8, 128], BF16)
                qk4 = qk.ap.rearrange("p (t s) d -> p t s d", s=2)
                A("pool", "memset", [], [qk.res], qk.ap[:, :, 96:128], 0.0)
                yield
                A("act", "activation", [rA], [qk.res], out=qk4[:, :, 0, 0:64], in_=qA[:, :, 0:64], func=AF.Copy)
                yield
                A("act", "activation", [rB], [qk.res], out=qk4[:, :, 1, 0:64], in_=kvB[:, :, 0:64], func=AF.Copy)
                yield
                A("act", "activation", [rB], [Vh.res], out=Vh.ap[:, g * 4:(g + 1) * 4, 0:64], in_=kvB[:, :, 64:128],
                  func=AF.Copy)
                yield
                rt = ar.alloc("rt", [128, 16, 16], F32)
                cA = cosA.ap[:, g * 4:(g + 1) * 4, :]
                sA = sinA.ap[:, g * 4:(g + 1) * 4, :]
                A("dve", "tensor_tensor", [rA, cosA.res], [rt.res], out=rt.ap[:, 0:4, :], in0=qA[:, :, 64:80], in1=cA,
                  op=ALU.mult)
                yield
                A("dve", "tensor_tensor", [rA, sinA.res], [rt.res], out=rt.ap[:, 4:8, :], in0=qA[:, :, 80:96], in1=sA,
                  op=ALU.mult)
                yield
                A("dve", "tensor_tensor", [rA, cosA.res], [rt.res], out=rt.ap[:, 8:12, :], in0=qA[:, :, 80:96], in1=cA,
                  op=ALU.mult)
                yield
                A("dve", "tensor_tensor", [rA, sinA.res], [rt.res], out=rt.ap[:, 12:16, :], in0=qA[:, :, 64:80], in1=sA,
                  op=ALU.mult)
                yield
                A("dve", "tensor_tensor", [rt.res], [qk.res], out=qk4[:, :, 0, 64:80], in0=rt.ap[:, 0:4, :],
                  in1=rt.ap[:, 4:8, :], op=ALU.subtract)
                yield
                A("dve", "tensor_tensor", [rt.res], [qk.res], out=qk4[:, :, 0, 80:96], in0=rt.ap[:, 8:12, :],
                  in1=rt.ap[:, 12:16, :], op=ALU.add)
                yield
                A("dve", "tensor_copy", [krtm.res], [qk.res], out=qk4[:, :, 1, 64:96],
                  in_=krtm.ap[:, g * 4:(g + 1) * 4, :])
                yield
                bk = tp_bank()
                pv = banks[bk][:].bitcast(BF16).rearrange("p (a b) -> p a b", b=128)
                for i in range(8):
                    A("pe", "transpose", [qk.res, ident.res], [bank_res[bk]], out=pv[:, i, :], in_=qk.ap[:, i, :],
                      identity=ident.ap)
                    yield
                copy(copy_eng(), [bank_res[bk]], [QKT.res],
                     QKT.ap[:, :, g * 512:(g + 1) * 512].rearrange("p s (t i) -> p s t i", i=128),
                     pv.rearrange("p (t s) i -> p s t i", s=2))
                yield
                ar.free(qk, rt)
                yield

            for g0 in (0, 2):
                gens = [prep_group(g0), prep_group(g0 + 1)]
                while gens:
                    for g_ in list(gens):
                        try:
                            next(g_)
                        except StopIteration:
                            gens.remove(g_)
            if stop_after == "p3a":
                break

            oua = ar.alloc("ou0", [128, NT, 65], F32)

            def evac_a(qb, acc, acc_res, oua=oua):
                A("dve", "tensor_copy", [acc_res], [oua.res], out=oua.ap[:, qb * 4:(qb + 1) * 4, :], in_=acc)

            attention((QKT.ap[:, 0, :], QKT.res), (QKT.ap[:, 1, :], QKT.res), 128, 0,
                      lambda kc, Vh=Vh: Vh.ap[:, kc, :], Vh.res, NT, 64, 96 ** -0.5, evac_a, vw=128)
            fr = ar.alloc("fr", [128, 4, NT], F32)
            A("dve", "reciprocal", [oua.res], [fr.res], out=fr.ap[:, 0, :], in_=oua.ap[:, :, 64])
            A("dve", "tensor_tensor", [oua.res, fr.res], [ca.res], out=ca.ap[:, :, h * 64:(h + 1) * 64],
              in0=oua.ap[:, :, 0:64], in1=fr.ap[:, 0, :].unsqueeze(2).broadcast_to([128, NT, 64]), op=ALU.mult)
            ar.free(oua, fr)
            ar.free(QKT, Vh)
            if stop_after == "p3b":
                break
        if stop_after in ("p3a", "p3b"):
            break
        ar.free(cqnT, ckvnT, krtm, w_uq_t, w_ukv_t)
        if stop_after == "p3":
            break

        def post_norm_add(t, bk2, gpost):
            s2 = ar.alloc("s2", [128, 4], F32)
            for i in range(2):
                A("act", "activation", [bank_res[bk2[i]]], [junk.res, s2.res], out=junk.ap[:, 0:512],
                  in_=banks[bk2[i]][:, :], func=AF.Square, accum_out=s2.ap[:, i:i + 1])
            A("dve", "tensor_tensor", [s2.res], [s2.res], out=s2.ap[:, 2:3], in0=s2.ap[:, 0:1], in1=s2.ap[:, 1:2],
              op=ALU.add)
            A("act", "activation", [s2.res, epsb.res], [s2.res], out=s2.ap[:, 3:4], in_=s2.ap[:, 2:3], func=AF.Ln,
              scale=1.0 / D, bias=epsb.ap[:, 0:1])
            A("act", "activation", [s2.res], [s2.res], out=s2.ap[:, 3:4], in_=s2.ap[:, 3:4], func=AF.Exp, scale=-0.5)
            tmp = ar.alloc("tmp", [128, D], F32)
            for i in range(2):
                A("dve", "scalar_tensor_tensor", [bank_res[bk2[i]], s2.res, gpost.res], [tmp.res],
                  out=tmp.ap[:, i * 512:(i + 1) * 512], in0=banks[bk2[i]][:, :], scalar=s2.ap[:, 3:4],
                  in1=gpost.ap[:, i * 512:(i + 1) * 512], op0=ALU.mult, op1=ALU.mult)
            A("pool", "tensor_tensor", [tmp.res, x_res[t]], [x_res[t]], out=xs[:, t, :], in0=xs[:, t, :], in1=tmp.ap,
              op=ALU.add)
            ar.free(s2, tmp)

        gpost = ar.alloc("gpost", [128, D], F32)
        A("sp", "dma_start", [], [gpost.res], is_dma=True, out=gpost.ap, in_=bc_d[l:l + 1, 0:1024].partition_broadcast(128))
        if stop_after != "mix":
            wkv_v = w_kv_d[l].rearrange("(c p) n -> p c n", p=128)
            w_k_t = load_w("w_k", [128, 8, D], wkv_v[:, :, 0:D], scale_cols=24, l=l)
            w_v_ts = [load_w("w_v%d" % i, [128, 8, 512], wkv_v[:, :, D + i * 512:D + (i + 1) * 512], scale_cols=24, l=l)
                      for i in range(2)]
            memf = ar.alloc("memf", [128, 2, D], F32)
            A("sp", "dma_start", [], [memf.res], is_dma=True, out=memf.ap, in_=mem_d.rearrange("(t p) d -> p t d", p=128))
        for t in range(NT):
            cT = ar.alloc("cT", [128, 8, 128], BF16)
            srcs = [ca.ap[:, t, c * 128:(c + 1) * 128] for c in range(4)] + \
                   [cb.ap[:, t, c * 128:(c + 1) * 128] for c in range(2)] + \
                   [cc.ap[:, t, c * 128:(c + 1) * 128] for c in range(2)]
            bk = tp_bank()
            pv = banks[bk][:].bitcast(BF16).rearrange("p (a b) -> p a b", b=128)
            for i, s_ap in enumerate(srcs):
                rr = ca.res if i < 4 else (cb.res if i < 6 else cc.res)
                A("pe", "transpose", [rr, ident.res], [bank_res[bk]], out=pv[:, i, :], in_=s_ap, identity=ident.ap)
            copy(copy_eng(), [bank_res[bk]], [cT.res], cT.ap, pv[:, 0:8, :])
            bk2 = [mm_bank(), mm_bank()]
            for i in range(2):
                for c in range(8):
                    A("pe", "matmul", [cT.res, w_mo_t.res], [bank_res[bk2[i]]], banks[bk2[i]][:, :],
                      lhsT=cT.ap[:, c, :], rhs=w_mo_t.ap[:, c, i * 512:(i + 1) * 512], start=(c == 0), stop=(c == 7))
            ar.free(cT)
            post_norm_add(t, bk2, gpost)
        ar.free(ca, cb, cc, w_mo_t, gpost, bcs, lamt)
        if stop_after == "mix":
            break

        w_q_t = load_w("w_q", [128, 8, D], w_q_d[l].rearrange("(c p) n -> p c n", p=128), scale_cols=8, l=l)
        w_o_t = load_w("w_o", [128, 8, D], w_o_d[l].rearrange("(c p) n -> p c n", p=128))
        ssm = ar.alloc("ssm", [128, 2], F32)
        rsm = ar.alloc("rsm", [128, 2], F32)
        for t in range(2):
            A("act", "activation", [memf.res], [junk.res, ssm.res], out=junk.ap, in_=memf.ap[:, t, :], func=AF.Square,
              accum_out=ssm.ap[:, t:t + 1])
        rstd_from((ssm.ap, ssm.res), 2, D, (rsm.ap, rsm.res))
        memT = ar.alloc("memT", [128, 8, NMEM], BF16)
        for t in range(2):
            mn = ar.alloc("mn", [128, D], BF16)
            A("dve", "tensor_scalar", [memf.res, rsm.res], [mn.res], out=mn.ap, in0=memf.ap[:, t, :],
              scalar1=rsm.ap[:, t:t + 1], scalar2=None, op0=ALU.mult)
            transposes([mn.ap[:, c * 128:(c + 1) * 128] for c in range(8)], mn.res, memT.ap[:, :, t * 128:(t + 1) * 128],
                       memT.res)
            ar.free(mn)
        ar.free(memf, ssm, rsm)
        KmT = ar.alloc("KmT", [128, 8, NMEM], BF16)
        for hc in range(8):
            mb = mm_bank()
            for c in range(8):
                A("pe", "matmul", [memT.res, w_k_t.res], [bank_res[mb]], banks[mb][:, 0:NMEM],
                  lhsT=w_k_t.ap[:, c, hc * 128:(hc + 1) * 128], rhs=memT.ap[:, c, :], start=(c == 0), stop=(c == 7))
            copy(copy_eng(), [bank_res[mb]], [KmT.res], KmT.ap[:, hc, :], banks[mb][:, 0:NMEM])
        Vm = ar.alloc("Vm", [128, 8, 257], BF16)
        A("pool", "memset", [], [Vm.res], Vm.ap[:, :, 256:257], 1.0)
        for kt in range(2):
            for i in range(2):
                mb = mm_bank()
                for c in range(8):
                    A("pe", "matmul", [memT.res, w_v_ts[i].res], [bank_res[mb]], banks[mb][:, :],
                      lhsT=memT.ap[:, c, kt * 128:(kt + 1) * 128], rhs=w_v_ts[i].ap[:, c, :],
                      start=(c == 0), stop=(c == 7))
                copy(copy_eng(), [bank_res[mb]], [Vm.res], Vm.ap[:, kt * 4 + 2 * i:kt * 4 + 2 * i + 2, 0:256],
                     banks[mb][:, :].rearrange("p (h e) -> p h e", e=256))
        ar.free(memT, w_k_t, *w_v_ts)
        gpost = ar.alloc("gpost", [128, D], F32)
        A("sp", "dma_start", [], [gpost.res], is_dma=True, out=gpost.ap, in_=bc_d[l:l + 1, 1024:2048].partition_broadcast(128))
        ss5 = ar.alloc("ss5", [128, NT], F32)
        rs5 = ar.alloc("rs5", [128, NT], F32)
        for t in range(NT):
            A("act", "activation", [x_res[t]], [junk.res, ss5.res], out=junk.ap, in_=xs[:, t, :], func=AF.Square,
              accum_out=ss5.ap[:, t:t + 1])
        rstd_from((ss5.ap, ss5.res), NT, D, (rs5.ap, rs5.res))
        hTbs = [ar.alloc("hTb", [128, 8, 512], BF16) for _ in range(2)]
        qTbs = [ar.alloc("qTb", [128, 8, 512], BF16) for _ in range(2)]

        def mem_front(qb):
            hTb = hTbs[qb % 2]
            for j in range(4):
                t = qb * 4 + j
                hn = ar.alloc("hn", [128, D], BF16)
                A("dve", "tensor_scalar", [x_res[t], rs5.res], [hn.res], out=hn.ap, in0=xs[:, t, :],
                  scalar1=rs5.ap[:, t:t + 1], scalar2=None, op0=ALU.mult)
                transposes([hn.ap[:, c * 128:(c + 1) * 128] for c in range(8)], hn.res,
                           hTb.ap[:, :, j * 128:(j + 1) * 128], hTb.res)
                ar.free(hn)
            qTb = qTbs[qb % 2]
            for hc in range(8):
                mb = mm_bank()
                for c in range(8):
                    A("pe", "matmul", [hTb.res, w_q_t.res], [bank_res[mb]], banks[mb][:, :],
                      lhsT=w_q_t.ap[:, c, hc * 128:(hc + 1) * 128], rhs=hTb.ap[:, c, :], start=(c == 0), stop=(c == 7))
                copy(copy_eng(), [bank_res[mb]], [qTb.res], qTb.ap[:, hc, :], banks[mb][:, :])

        def mem_back(qb):
            qTb = qTbs[qb % 2]
            ao = ar.alloc("ao", [128, 4, D], BF16)
            for h in range(4):
                pts = []
                for kt in range(2):
                    sb_ = tp_bank()
                    for dc in range(2):
                        A("pe", "matmul", [qTb.res, KmT.res], [bank_res[sb_]], banks[sb_][:, :],
                          lhsT=KmT.ap[:, h * 2 + dc, kt * 128:(kt + 1) * 128], rhs=qTb.ap[:, h * 2 + dc, :],
                          start=(dc == 0), stop=(dc == 1))
                    pt = ar.alloc("ptm", [128, 512], BF16)
                    A("act", "activation", [bank_res[sb_]], [pt.res], out=pt.ap, in_=banks[sb_][:, :], func=AF.Exp,
                      scale=1.0 / 16.0)
                    pts.append(pt)
                for j in range(4):
                    mb = mm_bank()
                    for kt in range(2):
                        A("pe", "matmul", [pts[kt].res, Vm.res], [bank_res[mb]], banks[mb][:, 0:257],
                          lhsT=pts[kt].ap[:, j * 128:(j + 1) * 128], rhs=Vm.ap[:, kt * 4 + h, :], start=(kt == 0),
                          stop=(kt == 1))
                    rc = ar.alloc("rc", [128, 1], F32)
                    A("dve", "reciprocal", [bank_res[mb]], [rc.res], out=rc.ap[:, 0:1], in_=banks[mb][:, 256:257])
                    A("dve", "tensor_scalar", [bank_res[mb], rc.res], [ao.res], out=ao.ap[:, j, h * 256:(h + 1) * 256],
                      in0=banks[mb][:, 0:256], scalar1=rc.ap[:, 0:1], scalar2=None, op0=ALU.mult)
                    ar.free(rc)
                ar.free(*pts)
            for j in range(4):
                t = qb * 4 + j
                aT = ar.alloc("aT", [128, 8, 128], BF16)
                transposes([ao.ap[:, j, c * 128:(c + 1) * 128] for c in range(8)], ao.res, aT.ap, aT.res)
                bk2 = [mm_bank(), mm_bank()]
                for i in range(2):
                    for c in range(8):
                        A("pe", "matmul", [aT.res, w_o_t.res], [bank_res[bk2[i]]], banks[bk2[i]][:, :],
                          lhsT=aT.ap[:, c, :], rhs=w_o_t.ap[:, c, i * 512:(i + 1) * 512], start=(c == 0), stop=(c == 7))
                ar.free(aT)
                post_norm_add(t, bk2, gpost)
            ar.free(ao)

        mem_front(0)
        for qb in range(4):
            if qb + 1 < 4:
                mem_front(qb + 1)
            mem_back(qb)
        ar.free(KmT, Vm, w_q_t, w_o_t, gpost, ss5, rs5, *hTbs, *qTbs)
        if stop_after == "mem":
            break

        w_dn_t = ar.alloc("w_dn", [128, NCH, D], BF16)
        w_dn_v = w_dn_d[l].rearrange("(c p) n -> p c n", p=128)

        def load_dn_chunk(c, w_dn_t=w_dn_t, w_dn_v=w_dn_v):
            stg = ar.alloc("stg", [128, D], F32)
            A("sp", "dma_start", [], [stg.res], is_dma=True, out=stg.ap, in_=w_dn_v[:, c, :])
            A("pool", "tensor_copy", [stg.res], [w_dn_t.res], out=w_dn_t.ap[:, c, :], in_=stg.ap)
            ar.free(stg)
        gpost = ar.alloc("gpost", [128, D], F32)
        A("sp", "dma_start", [], [gpost.res], is_dma=True, out=gpost.ap, in_=bc_d[l:l + 1, 2048:3072].partition_broadcast(128))
        ss6 = ar.alloc("ss6", [128, NT], F32)
        rs6 = ar.alloc("rs6", [128, NT], F32)
        for t in range(NT):
            A("act", "activation", [x_res[t]], [junk.res, ss6.res], out=junk.ap, in_=xs[:, t, :], func=AF.Square,
              accum_out=ss6.ap[:, t:t + 1])
        rstd_from((ss6.ap, ss6.res), NT, D, (rs6.ap, rs6.res))
        hl = ar.alloc("hl", [128, 8, 2], BF16)
        stgs = [ar.alloc("stgf", [128, 8, 256], F32) for _ in range(2)]
        wus = [ar.alloc("wuf", [128, 8, 256], BF16) for _ in range(3)]
        def issue_w(idx, l=l, stgs=stgs, wus=wus):
            if idx >= 4 * NCH:
                return
            jc_ = idx % NCH
            stg_, wu_ = stgs[idx % 2], wus[idx % 3]
            A("sp", "dma_start", [], [stg_.res], is_dma=True, out=stg_.ap,
              in_=w_up_d[l, jc_].rearrange("p (c n) -> p c n", n=256))
            A("pool", "tensor_tensor", [stg_.res, pk.res], [wu_.res], out=wu_.ap, in0=stg_.ap,
              in1=pk.ap[:, l, 16:24].unsqueeze(2).broadcast_to([128, 8, 256]), op=ALU.mult)

        issue_w(0)
        issue_w(1)
        def ffn_build(qb):
            hTe = ar.alloc("hTe", [128, 8, 514], BF16)
            if qb == 0:
                A("dve", "memset", [], [hTe.res], hTe.ap[:, :, 0:1], 0.0)
            if qb == 3:
                A("dve", "memset", [], [hTe.res], hTe.ap[:, :, 513:514], 0.0)
            tl = list(range(qb * 4, qb * 4 + 4))
            if qb > 0:
                A("dve", "tensor_copy", [hl.res], [hTe.res], out=hTe.ap[:, :, 0:1], in_=hl.ap[:, :, (qb - 1) % 2:(qb - 1) % 2 + 1])
            if qb < 3:
                tl = tl + [qb * 4 + 4]
            for t in tl:
                hn = ar.alloc("hn", [128, D], BF16)
                A("dve", "tensor_scalar", [x_res[t], rs6.res], [hn.res], out=hn.ap, in0=xs[:, t, :],
                  scalar1=rs6.ap[:, t:t + 1], scalar2=None, op0=ALU.mult)
                bk = tp_bank()
                pv = banks[bk][:].bitcast(BF16).rearrange("p (a b) -> p a b", b=128)
                for c in range(8):
                    A("pe", "transpose", [hn.res, ident.res], [bank_res[bk]], out=pv[:, c, :],
                      in_=hn.ap[:, c * 128:(c + 1) * 128], identity=ident.ap)
                j = t - qb * 4
                if j < 0:
                    copy(copy_eng(), [bank_res[bk]], [hTe.res], hTe.ap[:, :, 0:1], pv[:, 0:8, 127:128])
                elif j > 3:
                    copy(copy_eng(), [bank_res[bk]], [hTe.res], hTe.ap[:, :, 513:514], pv[:, 0:8, 0:1])
                else:
                    copy(copy_eng(), [bank_res[bk]], [hTe.res], hTe.ap[:, :, 1 + j * 128:1 + (j + 1) * 128], pv[:, 0:8, :])
                ar.free(hn)
            A("dve", "tensor_copy", [hTe.res], [hl.res], out=hl.ap[:, :, qb % 2:qb % 2 + 1], in_=hTe.ap[:, :, 512:513])
            return hTe

        def ffn_up(qb, hTe):
            mT = ar.alloc("mT", [128, NCH, 512], BF16)
            mm_all[0] = True
            for jc in range(NCH):
                wu = wus[(qb * NCH + jc) % 3]
                issue_w(qb * NCH + jc + 2)
                if qb == 0:
                    load_dn_chunk(jc)
                ys = []
                hb = mm_bank()
                def conv_chain(gu):
                    mb = mm_bank()
                    for c in range(8):
                        A("pe", "matmul", [hTe.res, wu.res], [bank_res[mb]], banks[mb][:, :],
                          lhsT=wu.ap[:, c, gu * 128:(gu + 1) * 128], rhs=hTe.ap[:, c, 1:513], start=(c == 0), stop=(c == 7))
                    for c in range(8):
                        A("pe", "matmul", [hTe.res, wu.res], [bank_res[hb]], banks[hb][:, 2 * gu:2 * gu + 2],
                          lhsT=wu.ap[:, c, gu * 128:(gu + 1) * 128], rhs=hTe.ap[:, c, 0:514:513], start=(c == 0),
                          stop=(c == 7))
                    ch = gu * NCH + jc
                    cw = pk.ap[:, l, 41 + ch * 4:41 + ch * 4 + 4]
                    y = ar.alloc("y", [128, 512], F32)
                    A("act", "activation", [bank_res[mb], pk.res], [y.res], out=y.ap, in_=banks[mb][:, :],
                      func=AF.Identity, scale=cw[:, 1:2], bias=cw[:, 3:4])
                    yield
                    A("dve", "scalar_tensor_tensor", [bank_res[mb], pk.res, y.res], [y.res], out=y.ap[:, 1:512],
                      in0=banks[mb][:, 0:511], scalar=cw[:, 0:1], in1=y.ap[:, 1:512], op0=ALU.mult, op1=ALU.add)
                    yield
                    A("dve", "scalar_tensor_tensor", [bank_res[hb], pk.res, y.res], [y.res], out=y.ap[:, 0:1],
                      in0=banks[hb][:, 2 * gu:2 * gu + 1], scalar=cw[:, 0:1], in1=y.ap[:, 0:1], op0=ALU.mult, op1=ALU.add)
                    yield
                    A("dve", "scalar_tensor_tensor", [bank_res[mb], pk.res, y.res], [y.res], out=y.ap[:, 0:511],
                      in0=banks[mb][:, 1:512], scalar=cw[:, 2:3], in1=y.ap[:, 0:511], op0=ALU.mult, op1=ALU.add)
                    yield
                    A("dve", "scalar_tensor_tensor", [bank_res[hb], pk.res, y.res], [y.res], out=y.ap[:, 511:512],
                      in0=banks[hb][:, 2 * gu + 1:2 * gu + 2], scalar=cw[:, 2:3], in1=y.ap[:, 511:512], op0=ALU.mult, op1=ALU.add)
                    yield
                    ys[gu] = y
                    yield

                ys = [None, None]
                gens = [conv_chain(0), conv_chain(1)]
                while gens:
                    for g_ in list(gens):
                        try:
                            next(g_)
                        except StopIteration:
                            gens.remove(g_)
                A("act", "activation", [ys[0].res], [ys[0].res], out=ys[0].ap, in_=ys[0].ap, func=AF.Gelu_apprx_tanh)
                A("pool", "tensor_tensor", [ys[0].res, ys[1].res], [mT.res], out=mT.ap[:, jc, :], in0=ys[0].ap,
                  in1=ys[1].ap, op=ALU.mult)
                ar.free(*ys)
            ar.free(hTe)
            return mT

        def ffn_down(qb, mT):
            for j in range(4):
                t = qb * 4 + j
                bk2 = [mm_bank(), mm_bank()]
                for i in range(2):
                    for jc in range(NCH):
                        A("pe", "matmul", [mT.res, w_dn_t.res], [bank_res[bk2[i]]], banks[bk2[i]][:, :],
                          lhsT=mT.ap[:, jc, j * 128:(j + 1) * 128], rhs=w_dn_t.ap[:, jc, i * 512:(i + 1) * 512],
                          start=(jc == 0), stop=(jc == NCH - 1))
                post_norm_add(t, bk2, gpost)
            ar.free(mT)
            mm_all[0] = False

        hTe_cur = ffn_build(0)
        for qb in range(4):
            mT_cur = ffn_up(qb, hTe_cur)
            if qb + 1 < 4:
                hTe_cur = ffn_build(qb + 1)
            ffn_down(qb, mT_cur)
        ar.free(w_dn_t, gpost, ss6, rs6, hl, *stgs, *wus)

    ov = out_d.rearrange("(t p) d -> p t d", p=128)
    last = []
    for t in range(NT):
        last.append(A("sp", "dma_start", [x_res[t]], [out_res[t]], is_dma=True, out=ov[:, t, :], in_=xs[:, t, :]))
    fin = A("sp", "nop", [out_res], [])

    counts = sc.finalize()
    nsem = {e: max(1, (counts[e] + SEM_LIMIT - 1) // SEM_LIMIT) for e in ("pe", "act", "dve", "pool", "sp")}
    sems = {e: [es.enter_context(nc.semaphore("s_%s%d" % (e, i))) for i in range(nsem[e])] for e in nsem}
    dsems = {q: [es.enter_context(nc.semaphore("d_%s%d" % (q, i))) for i in range(NDS)] for q in ("sp", "pool")}

    def emit(ename, eng):
        waited = {}
        for op in sc.ops[ename]:
            for d in op.deps:
                if d.is_dma:
                    sm, val = dsems[d.eng][d.dsem], d.dval
                    key = ("d", d.eng, d.dsem)
                else:
                    sm, val = sems[d.eng][d.semk // SEM_LIMIT], d.semk % SEM_LIMIT + 1
                    key = ("c", d.eng, d.semk // SEM_LIMIT)
                if waited.get(key, 0) >= val:
                    continue
                eng.wait_ge(sm, val)
                waited[key] = val
            if op.meth == "nop":
                continue
            ins = getattr(eng, op.meth)(*op.args, **op.kw)
            if op.is_dma:
                ins.then_inc(dsems[op.eng][op.dsem], 16)
            elif op.sig:
                ins.then_inc(sems[op.eng][op.semk // SEM_LIMIT], 1)

    block = es.enter_context(nc.Block())

    @block.sync
    def _(e):
        emit("sp", e)

    @block.gpsimd
    def _(e):
        emit("pool", e)

    @block.tensor
    def _(e):
        emit("pe", e)

    @block.scalar
    def _(e):
        emit("act", e)

    @block.vector
    def _(e):
        emit("dve", e)

    es.close()
    stats = {e: len(sc.ops[e]) for e in ENGS}
    stats["arena_peak_kb"] = ar.peak
    return nc, stats


def prep_inputs(inp, n_layers=DEPTH):
    f = lambda a: np.ascontiguousarray(np.asarray(a, dtype=np.float32))
    pk = np.zeros((128, DEPTH, 217), np.float32)
    for l in range(DEPTH):
        pk[:, l, 0:8] = np.asarray(inp["mix_pre_g"])[l].reshape(8, 128).T
        pk[:, l, 8:16] = np.asarray(inp["mem_pre_g"])[l].reshape(8, 128).T
        pk[:, l, 16:24] = np.asarray(inp["ffn_pre_g"])[l].reshape(8, 128).T
        pk[:, l, 24:32] = np.asarray(inp["mem_kv_g"])[l].reshape(8, 128).T
        pk[:, l, 32:35] = np.asarray(inp["mla_cq_g"])[l].reshape(3, 128).T
        pk[:, l, 35:37] = np.asarray(inp["mla_ckv_g"])[l].reshape(2, 128).T
        pk[:, l, 37:41] = np.asarray(inp["sgu_b_s"])[l].T
        cw = np.asarray(inp["ffn_conv_w"])[l]
        cbv = np.asarray(inp["ffn_conv_b"])[l]
        c4 = np.concatenate([cw, cbv[None, :]], axis=0)
        pk[:, l, 41:217] = c4.reshape(4, 44, 128).transpose(2, 1, 0).reshape(128, 176)
    bc = np.concatenate([
        np.asarray(inp["mix_post_g"]), np.asarray(inp["mem_post_g"]), np.asarray(inp["ffn_post_g"]),
        np.asarray(inp["sgu_norm_g"]).reshape(DEPTH, 256), np.asarray(inp["diff_sub_g"]),
        np.asarray(inp["diff_lam_q1"]), np.asarray(inp["diff_lam_k1"]),
        np.asarray(inp["diff_lam_q2"]), np.asarray(inp["diff_lam_k2"])], axis=1).astype(np.float32)
    wsT = np.asarray(inp["sgu_w_s"]).transpose(0, 3, 1, 2).reshape(DEPTH, 128, 512)
    nl = n_layers
    f = lambda a: np.ascontiguousarray(np.asarray(a, dtype=np.float32)[:nl])
    wup = np.asarray(inp["ffn_w_up"], dtype=np.float32)[:nl].reshape(nl, 8, 128, 2, NCH, 128)
    wup = np.ascontiguousarray(wup.transpose(0, 4, 2, 1, 3, 5)).reshape(nl, NCH, 128, 2048)
    shared = {
        "w_in": f(inp["w_in"]), "w_uq": f(inp["mla_w_uq"]), "w_ukv": f(inp["mla_w_ukv"]), "wsT": f(wsT),
        "w_mo": f(inp["w_mix_out"]), "w_q": f(inp["mem_w_q"]), "w_kv": f(inp["mem_w_kv"]), "w_o": f(inp["mem_w_o"]),
        "w_up": wup, "w_dn": f(inp["ffn_w_down"]),
        "pk": np.ascontiguousarray(pk), "bc": np.ascontiguousarray(bc),
    }
    x = np.asarray(inp["x"], dtype=np.float32)
    mem = np.asarray(inp["mem"], dtype=np.float32)
    pos = np.asarray(inp["positions"]).astype(np.int32)
    maps = []
    for b in range(x.shape[0]):
        m = dict(shared)
        m["x"] = np.ascontiguousarray(x[b])
        m["mem"] = np.ascontiguousarray(mem[b])
        m["pos"] = np.ascontiguousarray(pos[b].reshape(NT, 128))
        maps.append(m)
    return maps


def kernel(**inputs):
    maps = prep_inputs(inputs)
    nc, _ = build()
    res = run_bass_kernel_spmd(nc, maps, core_ids=list(range(8)))
    return np.stack([np.asarray(r["out"], dtype=np.float32) for r in res.results], axis=0)
```

```python
import math
from contextlib import ExitStack

import numpy as np
import concourse.bass as bass
import concourse.mybir as mybir
from concourse.bass_utils import run_bass_kernel_spmd

F32 = mybir.dt.float32
BF16 = mybir.dt.bfloat16
I32 = mybir.dt.int32
U8 = mybir.dt.uint8
AF = mybir.ActivationFunctionType
ALU = mybir.AluOpType
AX = mybir.AxisListType

D = 1024
S = 2048
NT = 16
DEPTH = 4
NMEM = 256
EPS = 1e-6
THETA = 500000.0
INW = 1952
DFF = 2816
NCH = 22

ENGS = ("pe", "act", "dve", "pool", "sp")
SEM_LIMIT = 1000
NDS = 12


class Res:
    __slots__ = ("lw", "rd", "rd_dma")

    def __init__(self):
        self.lw = None
        self.rd = {}
        self.rd_dma = []


class Op:
    __slots__ = ("eng", "meth", "args", "kw", "deps", "sig", "semk", "is_dma", "dsem", "dval", "idx")


class Sched:
    def __init__(self):
        self.ops = {e: [] for e in ENGS}
        self.ndma = {"sp": 0, "pool": 0}
        self.dma_hist = {"sp": [], "pool": []}
        self.n = 0

    def add(self, eng, meth, r, w, *args, is_dma=False, **kw):
        op = Op()
        op.eng, op.meth, op.args, op.kw = eng, meth, args, kw
        op.is_dma = is_dma
        op.sig = False
        op.semk = None
        op.idx = self.n
        op.dsem = None
        op.dval = None
        self.n += 1
        deps = {}

        def need(d):
            if d is None or d is op:
                return
            if (not d.is_dma) and d.eng == "pe" and eng == "pe" and not is_dma:
                return
            deps[d.idx] = d

        wset = set(id(x) for x in w)
        for x in r:
            need(x.lw)
        for x in w:
            need(x.lw)
            for d in x.rd.values():
                need(d)
            for d in x.rd_dma:
                need(d)
        if is_dma:
            q = self.dma_hist[eng]
            op.dsem = len(q) % NDS
            op.dval = 16 * (len(q) // NDS + 1)
            if len(q) >= NDS:
                need(q[len(q) - NDS])
            q.append(op)
        best = {}
        out = []
        for d in deps.values():
            if d.is_dma:
                out.append(d)
            else:
                b = best.get(d.eng)
                if b is None or d.idx > b.idx:
                    best[d.eng] = d
        out.extend(best.values())
        for d in out:
            d.sig = True
        op.deps = out
        for x in w:
            x.lw = op
            x.rd = {}
            x.rd_dma = []
        for x in r:
            if id(x) in wset:
                continue
            if is_dma:
                x.rd_dma.append(op)
            else:
                x.rd[eng] = op
        self.ops[eng].append(op)
        return op

    def finalize(self):
        for e in ENGS:
            k = 0
            for op in self.ops[e]:
                if op.is_dma:
                    continue
                if op.sig:
                    op.semk = k
                    k += 1
        return {e: sum(1 for o in self.ops[e] if (not o.is_dma) and o.sig) for e in ENGS}


class Buf:
    __slots__ = ("ap", "res", "off", "size", "name", "owner")


class Arena:
    CH = 1024

    def __init__(self, t, nbytes, base=0, nextfit=False):
        self.t = t
        self.base = base
        self.nextfit = nextfit
        self.ptr = 0
        self.nbytes = nbytes
        self.nch = nbytes // self.CH
        self.res = [Res() for _ in range(self.nch)]
        self.used = [False] * self.nch
        self.peak = 0

    def alloc(self, name, shape, dt):
        esz = 4 if dt in (F32, I32) else 2
        n = 1
        for s in shape[1:]:
            n *= s
        nb = n * esz
        k = (nb + self.CH - 1) // self.CH
        start = None
        order = [0]
        if self.nextfit:
            order = [self.ptr, 0]
        for s0 in order:
            run = 0
            for i in range(s0, self.nch):
                if not self.used[i]:
                    run += 1
                    if run == k:
                        start = i - k + 1
                        break
                else:
                    run = 0
            if start is not None:
                break
        if start is not None:
            self.ptr = start + k
        if start is None:
            raise RuntimeError("arena OOM for %s (%d B); used=%d" % (name, nb, sum(self.used)))
        for i in range(start, start + k):
            self.used[i] = True
        self.peak = max(self.peak, max(i for i in range(self.nch) if self.used[i]) + 1)
        b = Buf()
        b.name = name
        b.owner = self
        b.off = start
        b.size = k
        o = self.base + start * self.CH
        ap = self.t[:, o:o + nb].bitcast(dt)
        if len(shape) == 3:
            b.ap = ap.rearrange("p (a b) -> p a b", b=shape[2])
        else:
            b.ap = ap
        b.res = self.res[start:start + k]
        return b

    def free(self, *bufs):
        for b in bufs:
            for i in range(b.off, b.off + b.size):
                b.owner.used[i] = False


TRANS = {"dkm", "wuf", "stg", "oT", "hn", "hT", "ssq", "rsq", "cn", "rt", "zg", "st", "vsq", "vn", "qk", "r6", "pt", "rc", "o1", "s2", "tmp",
         "cT", "aT", "mn", "y", "wu", "ptm", "ltmp"}


class Arenas:
    def __init__(self, main, trans):
        self.main = main
        self.trans = trans

    def alloc(self, name, shape, dt):
        if name in TRANS:
            return self.trans.alloc(name, shape, dt)
        return self.main.alloc(name, shape, dt)

    def free(self, *bufs):
        self.main.free(*bufs)

    @property
    def peak(self):
        return (self.main.peak, self.trans.peak)


def build(n_layers=DEPTH, stop_after=None):
    nc = bass.Bass("TRN2", target_bir_lowering=False)

    def din(name, shape, dt=F32):
        return nc.dram_tensor(name, list(shape), dt, kind="ExternalInput").ap()

    WL = n_layers
    x_d = din("x", [S, D])
    mem_d = din("mem", [NMEM, D])
    pos_d = din("pos", [NT, 128], I32)
    w_in_d = din("w_in", [WL, D, INW])
    w_uq_d = din("w_uq", [WL, 384, 768])
    w_ukv_d = din("w_ukv", [WL, 256, 1024])
    wsT_d = din("wsT", [WL, 128, 512])
    w_mo_d = din("w_mo", [WL, D, D])
    w_q_d = din("w_q", [WL, D, D])
    w_kv_d = din("w_kv", [WL, D, 2 * D])
    w_o_d = din("w_o", [WL, D, D])
    w_up_d = din("w_up", [WL, NCH, 128, 2048])
    w_dn_d = din("w_dn", [WL, DFF, D])
    NPK = 217
    pk_d = din("pk", [128, DEPTH, NPK])
    NBC = 3520
    bc_d = din("bc", [DEPTH, NBC])
    out_d = nc.dram_tensor("out", [S, D], F32, kind="ExternalOutput").ap()

    sc = Sched()
    es = ExitStack()
    xs = es.enter_context(nc.sbuf_tensor("xs", [128, NT, D], F32))
    x_res = [Res() for _ in range(NT)]
    MAIN_B = 114 * 1024
    TRANS_B = 29 * 1024
    ar_t = es.enter_context(nc.sbuf_tensor("arena", [128, MAIN_B + TRANS_B], U8))
    ar = Arenas(Arena(ar_t, MAIN_B), Arena(ar_t, TRANS_B, base=MAIN_B, nextfit=True))
    banks = [es.enter_context(nc.psum_tensor("bank%d" % i, [128, 512], F32)) for i in range(8)]
    bank_res = [Res() for _ in range(8)]
    bank_ids = set(id(b) for b in bank_res)
    out_res = [Res() for _ in range(NT)]

    def A(eng, meth, r, w, *a, **k):
        rr = []
        for x in r:
            rr.extend(x if isinstance(x, (list, tuple)) else [x])
        ww = []
        for x in w:
            ww.extend(x if isinstance(x, (list, tuple)) else [x])
        for x in rr:
            if id(x) in bank_ids:
                ww.append(x)
        return sc.add(eng, meth, rr, ww, *a, **k)

    tp_i = [0]

    def tp_bank():
        tp_i[0] ^= 1
        return tp_i[0]

    mm_i = [0]

    mm_all = [False]

    def mm_bank():
        if mm_all[0]:
            mm_i[0] = (mm_i[0] + 1) % 8
            return mm_i[0]
        mm_i[0] = (mm_i[0] + 1) % 5
        return 3 + mm_i[0]

    sb_i = [0]

    def s_bank():
        sb_i[0] = (sb_i[0] + 1) % 3
        return sb_i[0]

    cp_i = [0]

    def copy_eng():
        cp_i[0] ^= 1
        return "act" if cp_i[0] else "dve"

    def copy(eng, r, w, out, in_):
        if eng == "act":
            A("act", "activation", r, w, out=out, in_=in_, func=AF.Copy)
        elif eng == "dve":
            A("dve", "tensor_copy", r, w, out=out, in_=in_)
        else:
            A("pool", "tensor_copy", r, w, out=out, in_=in_)

    ident = ar.alloc("ident", [128, 128], BF16)
    identf = ar.alloc("identf", [128, 128], F32)
    A("pool", "memset", [], [identf.res], identf.ap, 0.0)
    A("pool", "iota", [], [identf.res], identf.ap, pattern=[[1, 128]], base=0, channel_multiplier=-1,
      allow_small_or_imprecise_dtypes=True)
    A("dve", "tensor_scalar", [identf.res], [ident.res], out=ident.ap, in0=identf.ap, scalar1=0.0, scalar2=None,
      op0=ALU.is_equal)
    identF = ar.alloc("identF", [128, 128], F32)
    A("dve", "tensor_scalar", [identf.res], [identF.res], out=identF.ap, in0=identf.ap, scalar1=0.0, scalar2=None,
      op0=ALU.is_equal)
    epsb = ar.alloc("eps", [128, 1], F32)
    A("dve", "memset", [], [epsb.res], epsb.ap, EPS)
    zerob = ar.alloc("zero", [128, 1], F32)
    A("dve", "memset", [], [zerob.res], zerob.ap, 0.0)
    junk = ar.alloc("junk", [128, 1024], BF16)

    xv = x_d.rearrange("(t p) d -> p t d", p=128)
    for t in range(NT):
        A("sp", "dma_start", [], [x_res[t]], is_dma=True, out=xs[:, t, :], in_=xv[:, t, :])

    pk = ar.alloc("pk", [128, DEPTH, NPK], F32)
    A("sp", "dma_start", [], [pk.res], is_dma=True, out=pk.ap, in_=pk_d)

    posi = ar.alloc("posi", [128, NT], I32)
    for t in range(NT):
        A("sp", "dma_start", [], [posi.res], is_dma=True, out=posi.ap[:, t:t + 1],
          in_=pos_d[t:t + 1, :].rearrange("a p -> p a"))
    posf = ar.alloc("posf", [128, NT], F32)
    A("dve", "tensor_copy", [posi.res], [posf.res], out=posf.ap, in_=posi.ap)
    ang = ar.alloc("ang", [128, NT, 16], F32)
    for f in range(16):
        A("dve", "tensor_scalar", [posf.res], [ang.res], out=ang.ap[:, :, f], in0=posf.ap,
          scalar1=float(THETA ** (-2.0 * f / 32.0)), scalar2=None, op0=ALU.mult)
    cosA = ar.alloc("cosA", [128, NT, 16], F32)
    sinA = ar.alloc("sinA", [128, NT, 16], F32)
    TWO_PI = 2.0 * math.pi
    C1 = 6.28125
    C2 = TWO_PI - C1
    tA = ar.alloc("tA", [128, NT, 16], F32)
    tK = ar.alloc("tK", [128, NT, 16], I32)
    tKf = ar.alloc("tKf", [128, NT, 16], F32)
    tM = ar.alloc("tM", [128, NT, 16], F32)
    for (dst, shift) in ((sinA, 0.0), (cosA, math.pi / 2)):
        A("dve", "tensor_scalar", [ang.res], [tA.res], out=tA.ap, in0=ang.ap, scalar1=shift, scalar2=None,
          op0=ALU.add)
        A("dve", "tensor_scalar", [tA.res], [tM.res], out=tM.ap, in0=tA.ap, scalar1=1.0 / TWO_PI, scalar2=None,
          op0=ALU.mult)
        A("dve", "tensor_copy", [tM.res], [tK.res], out=tK.ap, in_=tM.ap)
        A("dve", "tensor_copy", [tK.res], [tKf.res], out=tKf.ap, in_=tK.ap)
        A("dve", "scalar_tensor_tensor", [tKf.res, tA.res], [tM.res], out=tM.ap, in0=tKf.ap, scalar=-C1,
          in1=tA.ap, op0=ALU.mult, op1=ALU.add)
        A("dve", "scalar_tensor_tensor", [tKf.res, tM.res], [tA.res], out=tA.ap, in0=tKf.ap, scalar=-C2,
          in1=tM.ap, op0=ALU.mult, op1=ALU.add)
        A("dve", "tensor_scalar", [tA.res], [tM.res], out=tM.ap, in0=tA.ap, scalar1=math.pi, scalar2=-TWO_PI,
          op0=ALU.is_gt, op1=ALU.mult)
        A("dve", "tensor_tensor", [tA.res, tM.res], [tKf.res], out=tKf.ap, in0=tA.ap, in1=tM.ap, op=ALU.add)
        A("dve", "tensor_scalar", [tKf.res], [tM.res], out=tM.ap, in0=tKf.ap, scalar1=-math.pi, scalar2=TWO_PI,
          op0=ALU.is_lt, op1=ALU.mult)
        A("dve", "tensor_tensor", [tKf.res, tM.res], [tA.res], out=tA.ap, in0=tKf.ap, in1=tM.ap, op=ALU.add)
        A("dve", "tensor_scalar", [tA.res], [tM.res], out=tM.ap, in0=tA.ap, scalar1=math.pi, scalar2=-math.pi,
          op0=ALU.min, op1=ALU.max)
        A("act", "activation", [tM.res], [dst.res], out=dst.ap, in_=tM.ap, func=AF.Sin)
    ar.free(tA, tK, tKf, tM, ang, posi, posf, identf)

    def rstd_from(ss, ncol, dim, dst):
        A("act", "activation", [ss[1], epsb.res], [dst[1]], out=dst[0], in_=ss[0], func=AF.Ln, scale=1.0 / dim,
          bias=epsb.ap[:, 0:1])
        A("act", "activation", [dst[1]], [dst[1]], out=dst[0], in_=dst[0], func=AF.Exp, scale=-0.5)

    def load_w(name, shape, src, scale_cols=None, l=0, chunk_cols=None):
        b = ar.alloc(name, shape, BF16)
        kc, n = shape[1], shape[2]
        step = 2048
        for c in range(kc):
            for n0 in range(0, n, step):
                n1 = min(n, n0 + step)
                stg = ar.alloc("stg", [128, n1 - n0], F32)
                A("sp", "dma_start", [], [stg.res], is_dma=True, out=stg.ap, in_=src[:, c, n0:n1])
                if scale_cols is not None:
                    A("pool", "tensor_scalar", [stg.res, pk.res], [b.res], out=b.ap[:, c, n0:n1], in0=stg.ap,
                      scalar1=pk.ap[:, l, scale_cols + c:scale_cols + c + 1], scalar2=1.0, op0=ALU.mult, op1=ALU.mult)
                else:
                    A("pool", "tensor_copy", [stg.res], [b.res], out=b.ap[:, c, n0:n1], in_=stg.ap)
                ar.free(stg)
        return b

    def transposes(srcs, src_res, dst_ap, dst_res, rows=128):
        bk = tp_bank()
        pv = banks[bk][:].bitcast(BF16).rearrange("p (a b) -> p a b", b=128)
        for i, s_ap in enumerate(srcs):
            A("pe", "transpose", [src_res, ident.res], [bank_res[bk]], out=pv[0:rows, i, :], in_=s_ap,
              identity=ident.ap)
        copy(copy_eng(), [bank_res[bk]], [dst_res], dst_ap, pv[0:rows, 0:len(srcs), :])

    def attention(QT, KT, krows, roff, V_of_kc, v_res, nkc, dv, scale, evac, vw=None):
        pending = [None]

        def epilogue(qb, ab):
            oT = ar.alloc("oT", [128, 512], F32)
            A("dve", "tensor_copy", [bank_res[ab]], [oT.res], out=oT.ap[0:dv + 1, :], in_=banks[ab][0:dv + 1, :])

            def pe_part():
                tb = mm_bank()
                for j in range(4):
                    A("pe", "transpose", [oT.res, identF.res], [bank_res[tb]],
                      out=banks[tb][:, j * (dv + 1):(j + 1) * (dv + 1)], in_=oT.ap[0:dv + 1, j * 128:(j + 1) * 128],
                      identity=identF.ap[0:dv + 1, 0:dv + 1])
                ar.free(oT)
                evac(qb, banks[tb][:, 0:4 * (dv + 1)].rearrange("p (j e) -> p j e", e=dv + 1), bank_res[tb])
            return pe_part

        for qb in range(4):
            ab = mm_bank()
            pts = {}

            def issue_s(kc, qb=qb, pts=pts):
                sb_ = s_bank()
                A("pe", "matmul", [QT[1], KT[1]], [bank_res[sb_]], banks[sb_][:, :],
                  lhsT=KT[0][roff:roff + krows, kc * 128:(kc + 1) * 128],
                  rhs=QT[0][roff:roff + krows, qb * 512:(qb + 1) * 512], start=True, stop=True,
                  **({"tile_position": (roff, 0)} if krows == 32 else {}))
                pt = ar.alloc("pt", [128, 512], BF16)
                A("act", "activation", [bank_res[sb_]], [pt.res], out=pt.ap, in_=banks[sb_][:, :], func=AF.Exp,
                  scale=scale)
                pts[kc] = pt

            def issue_pv(kc, ab=ab, pts=pts):
                pt = pts.pop(kc)
                A("pe", "matmul", [pt.res, v_res], [bank_res[ab]], banks[ab][0:(vw or dv + 1), :],
                  lhsT=V_of_kc(kc), rhs=pt.ap, start=(kc == 0), stop=(kc == nkc - 1))
                ar.free(pt)

            issue_s(0)
            issue_s(1)
            for kc in range(nkc):
                if kc + 2 < nkc:
                    issue_s(kc + 2)
                issue_pv(kc)
                if kc == 2 and pending[0] is not None:
                    pending[0]()
                    pending[0] = None
            if pending[0] is not None:
                pending[0]()
            pending[0] = epilogue(qb, ab)
        pending[0]()

    for l in range(n_layers):
        if stop_after == "setup":
            break
        lam_init = 0.8 - 0.6 * math.exp(-0.3 * l)
        bcs = ar.alloc("bcs", [128, 448], F32)
        A("sp", "dma_start", [], [bcs.res], is_dma=True, out=bcs.ap, in_=bc_d[l:l + 1, 3072:3520].partition_broadcast(128))
        lamt = ar.alloc("lamt", [128, 8], F32)
        ltmp = ar.alloc("ltmp", [128, 64], F32)
        A("dve", "tensor_tensor", [bcs.res], [ltmp.res], out=ltmp.ap[:, 0:32], in0=bcs.ap[:, 320:352],
          in1=bcs.ap[:, 352:384], op=ALU.mult)
        A("dve", "tensor_tensor", [bcs.res], [ltmp.res], out=ltmp.ap[:, 32:64], in0=bcs.ap[:, 384:416],
          in1=bcs.ap[:, 416:448], op=ALU.mult)
        A("dve", "tensor_reduce", [ltmp.res], [lamt.res], out=lamt.ap[:, 0:2],
          in_=ltmp.ap.rearrange("p (a b) -> p a b", b=32), axis=AX.X, op=ALU.add)
        A("act", "activation", [lamt.res], [lamt.res], out=lamt.ap[:, 2:4], in_=lamt.ap[:, 0:2], func=AF.Exp)
        A("dve", "tensor_tensor", [lamt.res], [lamt.res], out=lamt.ap[:, 4:5], in0=lamt.ap[:, 3:4], in1=lamt.ap[:, 2:3],
          op=ALU.subtract)
        A("dve", "tensor_scalar", [lamt.res], [lamt.res], out=lamt.ap[:, 5:6], in0=lamt.ap[:, 4:5], scalar1=-lam_init,
          scalar2=None, op0=ALU.add)
        neglam = lamt.ap[:, 5:6]
        A("dve", "tensor_scalar", [bcs.res], [bcs.res], out=bcs.ap[:, 256:320], in0=bcs.ap[:, 256:320],
          scalar1=1.0 - lam_init, scalar2=None, op0=ALU.mult)
        ar.free(ltmp)

        w_in_t = load_w("w_in", [128, 8, INW], w_in_d[l].rearrange("(c p) n -> p c n", p=128), scale_cols=0, l=l)
        wsT = load_w("wsT", [128, 1, 512], wsT_d[l].rearrange("p (a n) -> p a n", a=1))

        ss1 = ar.alloc("ss1", [128, NT], F32)
        rs1 = ar.alloc("rs1", [128, NT], F32)
        for t in range(NT):
            A("act", "activation", [x_res[t]], [junk.res, ss1.res], out=junk.ap, in_=xs[:, t, :], func=AF.Square,
              accum_out=ss1.ap[:, t:t + 1])
        rstd_from((ss1.ap, ss1.res), NT, D, (rs1.ap, rs1.res))
        cqnT = ar.alloc("cqnT", [128, 3, S], BF16)
        ckvnT = ar.alloc("ckvnT", [128, 2, S], BF16)
        krtm = ar.alloc("krtm", [128, NT, 32], BF16)
        dqT = ar.alloc("dqT", [128, 2, S], BF16)
        dkT = ar.alloc("dkT", [128, 2, S], BF16)
        dV = ar.alloc("dV", [128, NT * 4, 65], BF16)
        A("pool", "memset", [], [dV.res], dV.ap[:, :, 64:65], 1.0)
        cb = ar.alloc("cb", [128, NT, 256], BF16)
        COLS = ((0, 384), (384, 672), (672, 1184), (1184, 1696), (1696, 1952))
        for t in range(NT):
            tc_ = slice(t * 128, (t + 1) * 128)
            hn = ar.alloc("hn", [128, D], BF16)
            A("dve", "tensor_scalar", [x_res[t], rs1.res], [hn.res], out=hn.ap, in0=xs[:, t, :],
              scalar1=rs1.ap[:, t:t + 1], scalar2=None, op0=ALU.mult)
            hT = ar.alloc("hT", [128, 8, 128], BF16)
            transposes([hn.ap[:, c * 128:(c + 1) * 128] for c in range(8)], hn.res, hT.ap, hT.res)
            ar.free(hn)
            pj = [mm_bank() for _ in range(5)]
            for gi, (c0, c1) in enumerate(COLS):
                for c in range(8):
                    A("pe", "matmul", [hT.res, w_in_t.res], [bank_res[pj[gi]]], banks[pj[gi]][:, 0:c1 - c0],
                      lhsT=hT.ap[:, c, :], rhs=w_in_t.ap[:, c, c0:c1], start=(c == 0), stop=(c == 7))
            ar.free(hT)
            def chain_norm():
                ssq = ar.alloc("ssq", [128, 2], F32)
                rsq = ar.alloc("rsq", [128, 2], F32)
                A("act", "activation", [bank_res[pj[0]]], [junk.res, ssq.res], out=junk.ap[:, 0:384],
                  in_=banks[pj[0]][:, 0:384], func=AF.Square, accum_out=ssq.ap[:, 0:1])
                yield
                A("act", "activation", [bank_res[pj[1]]], [junk.res, ssq.res], out=junk.ap[:, 0:256],
                  in_=banks[pj[1]][:, 0:256], func=AF.Square, accum_out=ssq.ap[:, 1:2])
                yield
                A("act", "activation", [ssq.res, epsb.res], [rsq.res], out=rsq.ap[:, 0:1], in_=ssq.ap[:, 0:1], func=AF.Ln,
                  scale=1.0 / 384, bias=epsb.ap[:, 0:1])
                yield
                A("act", "activation", [ssq.res, epsb.res], [rsq.res], out=rsq.ap[:, 1:2], in_=ssq.ap[:, 1:2], func=AF.Ln,
                  scale=1.0 / 256, bias=epsb.ap[:, 0:1])
                yield
                A("act", "activation", [rsq.res], [rsq.res], out=rsq.ap, in_=rsq.ap, func=AF.Exp, scale=-0.5)
                yield
                cn = ar.alloc("cn", [128, 640], BF16)
                A("dve", "tensor_scalar", [bank_res[pj[0]], rsq.res], [cn.res], out=cn.ap[:, 0:384],
                  in0=banks[pj[0]][:, 0:384], scalar1=rsq.ap[:, 0:1], scalar2=None, op0=ALU.mult)
                yield
                A("dve", "tensor_scalar", [bank_res[pj[1]], rsq.res], [cn.res], out=cn.ap[:, 384:640],
                  in0=banks[pj[1]][:, 0:256], scalar1=rsq.ap[:, 1:2], scalar2=None, op0=ALU.mult)
                yield
                transposes([cn.ap[:, c * 128:(c + 1) * 128] for c in range(3)], cn.res, cqnT.ap[:, :, tc_], cqnT.res)
                yield
                transposes([cn.ap[:, 384 + c * 128:384 + (c + 1) * 128] for c in range(2)], cn.res, ckvnT.ap[:, :, tc_],
                           ckvnT.res)
                yield
                ar.free(ssq, rsq, cn)
                yield

            def chain_kr():
                rt = ar.alloc("rt", [128, 4, 16], F32)
                kb = banks[pj[1]]
                kr_res = bank_res[pj[1]]
                cA = cosA.ap[:, t, :]
                sA = sinA.ap[:, t, :]
                A("dve", "tensor_tensor", [kr_res, cosA.res], [rt.res], out=rt.ap[:, 0, :], in0=kb[:, 256:272], in1=cA,
                  op=ALU.mult)
                yield
                A("dve", "tensor_tensor", [kr_res, sinA.res], [rt.res], out=rt.ap[:, 1, :], in0=kb[:, 272:288], in1=sA,
                  op=ALU.mult)
                yield
                A("dve", "tensor_tensor", [kr_res, cosA.res], [rt.res], out=rt.ap[:, 2, :], in0=kb[:, 272:288], in1=cA,
                  op=ALU.mult)
                yield
                A("dve", "tensor_tensor", [kr_res, sinA.res], [rt.res], out=rt.ap[:, 3, :], in0=kb[:, 256:272], in1=sA,
                  op=ALU.mult)
                yield
                A("dve", "tensor_tensor", [rt.res], [krtm.res], out=krtm.ap[:, t, 0:16], in0=rt.ap[:, 0, :],
                  in1=rt.ap[:, 1, :], op=ALU.subtract)
                yield
                A("dve", "tensor_tensor", [rt.res], [krtm.res], out=krtm.ap[:, t, 16:32], in0=rt.ap[:, 2, :],
                  in1=rt.ap[:, 3, :], op=ALU.add)
                yield
                ar.free(rt)
                yield

            def chain_sgu():
                zg = ar.alloc("zg", [128, 512], F32)
                A("act", "activation", [bank_res[pj[2]]], [zg.res], out=zg.ap, in_=banks[pj[2]][:, :],
                  func=AF.Gelu_apprx_tanh)
                yield
                st = ar.alloc("st", [128, 16], F32)
                vsq = ar.alloc("vsq", [128, 256], F32)
                v3 = zg.ap[:, 256:512].rearrange("p (g c) -> p g c", c=64)
                A("dve", "tensor_reduce", [zg.res], [st.res], out=st.ap[:, 0:4], in_=v3, axis=AX.X, op=ALU.add)
                yield
                A("act", "activation", [zg.res], [vsq.res], out=vsq.ap, in_=zg.ap[:, 256:512], func=AF.Square)
                yield
                A("dve", "tensor_reduce", [vsq.res], [st.res], out=st.ap[:, 4:8],
                  in_=vsq.ap.rearrange("p (g c) -> p g c", c=64), axis=AX.X, op=ALU.add)
                yield
                A("dve", "tensor_scalar", [st.res], [st.res], out=st.ap[:, 8:12], in0=st.ap[:, 0:4], scalar1=1.0 / 64,
                  scalar2=None, op0=ALU.mult)
                yield
                A("dve", "tensor_tensor", [st.res], [st.res], out=st.ap[:, 12:16], in0=st.ap[:, 8:12], in1=st.ap[:, 8:12],
                  op=ALU.mult)
                yield
                A("dve", "scalar_tensor_tensor", [st.res], [st.res], out=st.ap[:, 4:8], in0=st.ap[:, 4:8], scalar=1.0 / 64,
                  in1=st.ap[:, 12:16], op0=ALU.mult, op1=ALU.subtract)
                yield
                A("act", "activation", [st.res, epsb.res], [st.res], out=st.ap[:, 0:4], in_=st.ap[:, 4:8], func=AF.Ln,
                  bias=epsb.ap[:, 0:1])
                yield
                A("act", "activation", [st.res], [st.res], out=st.ap[:, 0:4], in_=st.ap[:, 0:4], func=AF.Exp, scale=-0.5)
                yield
                vq3 = vsq.ap.rearrange("p (g c) -> p g c", c=64)
                A("dve", "tensor_tensor", [zg.res, st.res], [vsq.res], out=vq3, in0=v3,
                  in1=st.ap[:, 8:12].unsqueeze(2).broadcast_to([128, 4, 64]), op=ALU.subtract)
                yield
                A("dve", "tensor_tensor", [vsq.res, st.res], [vsq.res], out=vq3, in0=vq3,
                  in1=st.ap[:, 0:4].unsqueeze(2).broadcast_to([128, 4, 64]), op=ALU.mult)
                yield
                vn = ar.alloc("vn", [128, 256], BF16)
                A("dve", "tensor_tensor", [vsq.res, bcs.res], [vn.res], out=vn.ap, in0=vsq.ap, in1=bcs.ap[:, 0:256],
                  op=ALU.mult)
                yield
                mb = mm_bank()
                for g in range(4):
                    A("pe", "matmul", [vn.res, wsT.res], [bank_res[mb]], banks[mb][:, g * 64:(g + 1) * 64],
                      lhsT=wsT.ap[:, 0, g * 128:(g + 1) * 128], rhs=vn.ap[:, g * 64:(g + 1) * 64], start=True, stop=True)
                    yield
                for g in range(4):
                    A("dve", "scalar_tensor_tensor", [bank_res[mb], pk.res, zg.res], [cb.res],
                      out=cb.ap[:, t, g * 64:(g + 1) * 64], in0=banks[mb][:, g * 64:(g + 1) * 64],
                      scalar=pk.ap[:, l, 37 + g:38 + g], in1=zg.ap[:, g * 64:(g + 1) * 64], op0=ALU.add, op1=ALU.mult)
                    yield
                ar.free(zg, st, vsq, vn)
                yield

            def chain_dqk():
                qk = ar.alloc("qk", [128, 16, 32], BF16)
                r6 = ar.alloc("r6", [128, 4 * 16, 8], F32)
                pq = banks[pj[3]][:, :].rearrange("p (m d) -> p m d", d=32)
                qres = bank_res[pj[3]]
                cD = cosA.ap[:, t, 0:16:2].unsqueeze(1).broadcast_to([128, 16, 8])
                sD = sinA.ap[:, t, 0:16:2].unsqueeze(1).broadcast_to([128, 16, 8])
                A("dve", "tensor_tensor", [qres, cosA.res], [r6.res], out=r6.ap[:, 0:16, :], in0=pq[:, :, 0:8], in1=cD,
                  op=ALU.mult)
                yield
                A("dve", "tensor_tensor", [qres, sinA.res], [r6.res], out=r6.ap[:, 16:32, :], in0=pq[:, :, 8:16], in1=sD,
                  op=ALU.mult)
                yield
                A("dve", "tensor_tensor", [qres, cosA.res], [r6.res], out=r6.ap[:, 32:48, :], in0=pq[:, :, 8:16], in1=cD,
                  op=ALU.mult)
                yield
                A("dve", "tensor_tensor", [qres, sinA.res], [r6.res], out=r6.ap[:, 48:64, :], in0=pq[:, :, 0:8], in1=sD,
                  op=ALU.mult)
                yield
                A("dve", "tensor_tensor", [r6.res], [qk.res], out=qk.ap[:, :, 0:8], in0=r6.ap[:, 0:16, :],
                  in1=r6.ap[:, 16:32, :], op=ALU.subtract)
                yield
                A("dve", "tensor_tensor", [r6.res], [qk.res], out=qk.ap[:, :, 8:16], in0=r6.ap[:, 32:48, :],
                  in1=r6.ap[:, 48:64, :], op=ALU.add)
                yield
                A("act", "activation", [qres], [qk.res], out=qk.ap[:, :, 16:32], in_=pq[:, :, 16:32], func=AF.Copy)
                yield
                qk2 = qk.ap.rearrange("p m d -> p (m d)")
                transposes([qk2[:, c * 128:(c + 1) * 128] for c in range(2)], qk.res, dqT.ap[:, :, tc_], dqT.res)
                yield
                transposes([qk2[:, 256 + c * 128:256 + (c + 1) * 128] for c in range(2)], qk.res, dkT.ap[:, :, tc_],
                           dkT.res)
                yield
                ar.free(qk, r6)
                yield

            def chain_dv():
                A("act", "activation", [bank_res[pj[4]]], [dV.res], out=dV.ap[:, t * 4:(t + 1) * 4, 0:64],
                  in_=banks[pj[4]][:, 0:256].rearrange("p (h e) -> p h e", e=64), func=AF.Copy)
                yield
                yield

            gens = [chain_norm(), chain_sgu(), chain_dqk(), chain_kr(), chain_dv()]
            while gens:
                for g_ in list(gens):
                    try:
                        next(g_)
                    except StopIteration:
                        gens.remove(g_)
        ar.free(ss1, rs1, w_in_t, wsT)
        if stop_after == "p1":
            break

        cc = ar.alloc("cc", [128, NT, 256], BF16)
        w_mo_t = load_w("w_mo", [128, 8, D], w_mo_d[l].rearrange("(c p) n -> p c n", p=128))
        w_uq_t = load_w("w_uq", [128, 3, 768], w_uq_d[l].rearrange("(c p) n -> p c n", p=128), scale_cols=32, l=l)
        w_ukv_t = load_w("w_ukv", [128, 2, 1024], w_ukv_d[l].rearrange("(c p) n -> p c n", p=128), scale_cols=35, l=l)
        for h in range(4):
            ous = []
            for c in range(2):
                m = 2 * h + c
                ci, ro = m // 4, 32 * (m % 4)
                ou = ar.alloc("ou%d" % c, [128, NT, 65], F32)
                ous.append(ou)

                def evac(qb, acc, acc_res, ou=ou):
                    A("dve", "tensor_copy", [acc_res], [ou.res], out=ou.ap[:, qb * 4:(qb + 1) * 4, :], in_=acc)

                dkm = ar.alloc("dkm", [128, S], BF16)
                A("pool", "memset", [], [dkm.res], dkm.ap, 0.0)
                A("pool", "tensor_copy", [dkT.res], [dkm.res], out=dkm.ap[ro:ro + 32, :], in_=dkT.ap[ro:ro + 32, ci, :])
                attention((dqT.ap[:, ci, :], dqT.res), (dkm.ap, dkm.res), 128, 0,
                          lambda kc, h=h: dV.ap[:, kc * 4 + h, :], dV.res, NT, 64, 32 ** -0.5, evac)
                ar.free(dkm)
            o0, o1 = ous
            fr = ar.alloc("fr", [128, 4, NT], F32)
            A("dve", "reciprocal", [o0.res], [fr.res], out=fr.ap[:, 0, :], in_=o0.ap[:, :, 64])
            A("dve", "reciprocal", [o1.res], [fr.res], out=fr.ap[:, 1, :], in_=o1.ap[:, :, 64])
            A("dve", "tensor_scalar", [fr.res, lamt.res], [fr.res], out=fr.ap[:, 1, :], in0=fr.ap[:, 1, :],
              scalar1=neglam, scalar2=None, op0=ALU.mult)
            A("dve", "tensor_tensor", [o0.res, fr.res], [o0.res], out=o0.ap[:, :, 0:64], in0=o0.ap[:, :, 0:64],
              in1=fr.ap[:, 0, :].unsqueeze(2).broadcast_to([128, NT, 64]), op=ALU.mult)
            A("dve", "tensor_tensor", [o1.res, fr.res], [o1.res], out=o1.ap[:, :, 0:64], in0=o1.ap[:, :, 0:64],
              in1=fr.ap[:, 1, :].unsqueeze(2).broadcast_to([128, NT, 64]), op=ALU.mult)
            A("dve", "tensor_tensor", [o0.res, o1.res], [o0.res], out=o0.ap[:, :, 0:64], in0=o0.ap[:, :, 0:64],
              in1=o1.ap[:, :, 0:64], op=ALU.add)
            A("dve", "tensor_tensor", [o0.res], [o1.res], out=o1.ap[:, :, 0:64], in0=o0.ap[:, :, 0:64],
              in1=o0.ap[:, :, 0:64], op=ALU.mult)
            A("dve", "tensor_reduce", [o1.res], [fr.res], out=fr.ap[:, 2, :], in_=o1.ap[:, :, 0:64], axis=AX.X,
              op=ALU.add)
            A("act", "activation", [fr.res, epsb.res], [fr.res], out=fr.ap[:, 3, :], in_=fr.ap[:, 2, :], func=AF.Ln,
              scale=1.0 / 64, bias=epsb.ap[:, 0:1])
            A("act", "activation", [fr.res], [fr.res], out=fr.ap[:, 3, :], in_=fr.ap[:, 3, :], func=AF.Exp, scale=-0.5)
            A("dve", "tensor_tensor", [o0.res, fr.res], [o0.res], out=o0.ap[:, :, 0:64], in0=o0.ap[:, :, 0:64],
              in1=fr.ap[:, 3, :].unsqueeze(2).broadcast_to([128, NT, 64]), op=ALU.mult)
            A("dve", "tensor_tensor", [o0.res, bcs.res], [cc.res], out=cc.ap[:, :, h * 64:(h + 1) * 64],
              in0=o0.ap[:, :, 0:64], in1=bcs.ap[:, 256:320].unsqueeze(1).broadcast_to([128, NT, 64]), op=ALU.mult)
            ar.free(o0, o1, fr)
        ar.free(dqT, dkT, dV)
        if stop_after == "p2":
            break

        ca = ar.alloc("ca", [128, NT, 512], BF16)
        for h in range(8):
            QKT = ar.alloc("QKT", [128, 2, S], BF16)
            Vh = ar.alloc("Vh", [128, NT, 128], BF16)
            A("pool", "memset", [], [Vh.res], Vh.ap[:, :, 64:128], 0.0)
            A("pool", "memset", [], [Vh.res], Vh.ap[:, :, 64:65], 1.0)
            def prep_group(g):
                bA = mm_bank()
                bB = mm_bank()
                for tt in range(4):
                    t = g * 4 + tt
                    tc_ = slice(t * 128, (t + 1) * 128)
                    for c in range(3):
                        A("pe", "matmul", [cqnT.res, w_uq_t.res], [bank_res[bA]], banks[bA][:, tt * 96:(tt + 1) * 96],
                          lhsT=cqnT.ap[:, c, tc_], rhs=w_uq_t.ap[:, c, h * 96:(h + 1) * 96], start=(c == 0),
                          stop=(c == 2))
                        yield
                    for c in range(2):
                        A("pe", "matmul", [ckvnT.res, w_ukv_t.res], [bank_res[bB]],
                          banks[bB][:, tt * 128:(tt + 1) * 128], lhsT=ckvnT.ap[:, c, tc_],
                          rhs=w_ukv_t.ap[:, c, h * 128:(h + 1) * 128], start=(c == 0), stop=(c == 1))
                        yield
                qA = banks[bA][:, 0:384].rearrange("p (t d) -> p t d", d=96)
                kvB = banks[bB][:, :].rearrange("p (t d) -> p t d", d=128)
                rA, rB = bank_res[bA], bank_res[bB]
                qk = ar.alloc("qk", [128, 8, 128], BF16)
                qk4 = qk.ap.rearrange("p (t s) d -> p t s d", s=2)
                A("pool", "memset", [], [qk.res], qk.ap[:, :, 96:128], 0.0)
                yield
                A("act", "activation", [rA], [qk.res], out=qk4[:, :, 0, 0:64], in_=qA[:, :, 0:64], func=AF.Copy)
                yield
                A("act", "activation", [rB], [qk.res], out=qk4[:, :, 1, 0:64], in_=kvB[:, :, 0:64], func=AF.Copy)
                yield
                A("act", "activation", [rB], [Vh.res], out=Vh.ap[:, g * 4:(g + 1) * 4, 0:64], in_=kvB[:, :, 64:128],
                  func=AF.Copy)
                yield
                rt = ar.alloc("rt", [128, 16, 16], F32)
                cA = cosA.ap[:, g * 4:(g + 1) * 4, :]
                sA = sinA.ap[:, g * 4:(g + 1) * 4, :]
                A("dve", "tensor_tensor", [rA, cosA.res], [rt.res], out=rt.ap[:, 0:4, :], in0=qA[:, :, 64:80], in1=cA,
                  op=ALU.mult)
                yield
                A("dve", "tensor_tensor", [rA, sinA.res], [rt.res], out=rt.ap[:, 4:8, :], in0=qA[:, :, 80:96], in1=sA,
                  op=ALU.mult)
                yield
                A("dve", "tensor_tensor", [rA, cosA.res], [rt.res], out=rt.ap[:, 8:12, :], in0=qA[:, :, 80:96], in1=cA,
                  op=ALU.mult)
                yield
                A("dve", "tensor_tensor", [rA, sinA.res], [rt.res], out=rt.ap[:, 12:16, :], in0=qA[:, :, 64:80], in1=sA,
                  op=ALU.mult)
                yield
                A("dve", "tensor_tensor", [rt.res], [qk.res], out=qk4[:, :, 0, 64:80], in0=rt.ap[:, 0:4, :],
                  in1=rt.ap[:, 4:8, :], op=ALU.subtract)
                yield
                A("dve", "tensor_tensor", [rt.res], [qk.res], out=qk4[:, :, 0, 80:96], in0=rt.ap[:, 8:12, :],
                  in1=rt.ap[:, 12:16, :], op=ALU.add)
                yield
                A("dve", "tensor_copy", [krtm.res], [qk.res], out=qk4[:, :, 1, 64:96],
                  in_=krtm.ap[:, g * 4:(g + 1) * 4, :])
                yield
                bk = tp_bank()
                pv = banks[bk][:].bitcast(BF16).rearrange("p (a b) -> p a b", b=128)
                for i in range(8):
                    A("pe", "transpose", [qk.res, ident.res], [bank_res[bk]], out=pv[:, i, :], in_=qk.ap[:, i, :],
                      identity=ident.ap)
                    yield
                copy(copy_eng(), [bank_res[bk]], [QKT.res],
                     QKT.ap[:, :, g * 512:(g + 1) * 512].rearrange("p s (t i) -> p s t i", i=128),
                     pv.rearrange("p (t s) i -> p s t i", s=2))
                yield
                ar.free(qk, rt)
                yield

            for g0 in (0, 2):
                gens = [prep_group(g0), prep_group(g0 + 1)]
                while gens:
                    for g_ in list(gens):
                        try:
                            next(g_)
                        except StopIteration:
                            gens.remove(g_)
            if stop_after == "p3a":
                break

            oua = ar.alloc("ou0", [128, NT, 65], F32)

            def evac_a(qb, acc, acc_res, oua=oua):
                A("dve", "tensor_copy", [acc_res], [oua.res], out=oua.ap[:, qb * 4:(qb + 1) * 4, :], in_=acc)

            attention((QKT.ap[:, 0, :], QKT.res), (QKT.ap[:, 1, :], QKT.res), 128, 0,
                      lambda kc, Vh=Vh: Vh.ap[:, kc, :], Vh.res, NT, 64, 96 ** -0.5, evac_a, vw=128)
            fr = ar.alloc("fr", [128, 4, NT], F32)
            A("dve", "reciprocal", [oua.res], [fr.res], out=fr.ap[:, 0, :], in_=oua.ap[:, :, 64])
            A("dve", "tensor_tensor", [oua.res, fr.res], [ca.res], out=ca.ap[:, :, h * 64:(h + 1) * 64],
              in0=oua.ap[:, :, 0:64], in1=fr.ap[:, 0, :].unsqueeze(2).broadcast_to([128, NT, 64]), op=ALU.mult)
            ar.free(oua, fr)
            ar.free(QKT, Vh)
            if stop_after == "p3b":
                break
        if stop_after in ("p3a", "p3b"):
            break
        ar.free(cqnT, ckvnT, krtm, w_uq_t, w_ukv_t)
        if stop_after == "p3":
            break

        def rr(gens):
            gens = list(gens)
            while gens:
                for g_ in list(gens):
                    try:
                        next(g_)
                    except StopIteration:
                        gens.remove(g_)

        def post_norm_gen(t, bk2, gpost):
            s2 = ar.alloc("s2", [128, 4], F32)
            for i in range(2):
                A("act", "activation", [bank_res[bk2[i]]], [junk.res, s2.res], out=junk.ap[:, 0:512],
                  in_=banks[bk2[i]][:, :], func=AF.Square, accum_out=s2.ap[:, i:i + 1])
                yield
            A("dve", "tensor_tensor", [s2.res], [s2.res], out=s2.ap[:, 2:3], in0=s2.ap[:, 0:1], in1=s2.ap[:, 1:2],
              op=ALU.add)
            yield
            A("act", "activation", [s2.res, epsb.res], [s2.res], out=s2.ap[:, 3:4], in_=s2.ap[:, 2:3], func=AF.Ln,
              scale=1.0 / D, bias=epsb.ap[:, 0:1])
            yield
            A("act", "activation", [s2.res], [s2.res], out=s2.ap[:, 3:4], in_=s2.ap[:, 3:4], func=AF.Exp, scale=-0.5)
            yield
            tmp = ar.alloc("tmp", [128, D], F32)
            for i in range(2):
                A("dve", "scalar_tensor_tensor", [bank_res[bk2[i]], s2.res, gpost.res], [tmp.res],
                  out=tmp.ap[:, i * 512:(i + 1) * 512], in0=banks[bk2[i]][:, :], scalar=s2.ap[:, 3:4],
                  in1=gpost.ap[:, i * 512:(i + 1) * 512], op0=ALU.mult, op1=ALU.mult)
                yield
            A("pool", "tensor_tensor", [tmp.res, x_res[t]], [x_res[t]], out=xs[:, t, :], in0=xs[:, t, :], in1=tmp.ap,
              op=ALU.add)
            ar.free(s2, tmp)
            yield

        gpost = ar.alloc("gpost", [128, D], F32)
        A("sp", "dma_start", [], [gpost.res], is_dma=True, out=gpost.ap, in_=bc_d[l:l + 1, 0:1024].partition_broadcast(128))
        if stop_after != "mix":
            wkv_v = w_kv_d[l].rearrange("(c p) n -> p c n", p=128)
            w_k_t = load_w("w_k", [128, 8, D], wkv_v[:, :, 0:D], scale_cols=24, l=l)
            w_v_ts = [load_w("w_v%d" % i, [128, 8, 512], wkv_v[:, :, D + i * 512:D + (i + 1) * 512], scale_cols=24, l=l)
                      for i in range(2)]
            memf = ar.alloc("memf", [128, 2, D], F32)
            A("sp", "dma_start", [], [memf.res], is_dma=True, out=memf.ap, in_=mem_d.rearrange("(t p) d -> p t d", p=128))
        def p4_tile(t):
            cT = ar.alloc("cT", [128, 8, 128], BF16)
            srcs = [ca.ap[:, t, c * 128:(c + 1) * 128] for c in range(4)] + \
                   [cb.ap[:, t, c * 128:(c + 1) * 128] for c in range(2)] + \
                   [cc.ap[:, t, c * 128:(c + 1) * 128] for c in range(2)]
            bk = tp_bank()
            pv = banks[bk][:].bitcast(BF16).rearrange("p (a b) -> p a b", b=128)
            for i, s_ap in enumerate(srcs):
                rr = ca.res if i < 4 else (cb.res if i < 6 else cc.res)
                A("pe", "transpose", [rr, ident.res], [bank_res[bk]], out=pv[:, i, :], in_=s_ap, identity=ident.ap)
            copy(copy_eng(), [bank_res[bk]], [cT.res], cT.ap, pv[:, 0:8, :])
            yield
            bk2 = [mm_bank(), mm_bank()]
            for i in range(2):
                for c in range(8):
                    A("pe", "matmul", [cT.res, w_mo_t.res], [bank_res[bk2[i]]], banks[bk2[i]][:, :],
                      lhsT=cT.ap[:, c, :], rhs=w_mo_t.ap[:, c, i * 512:(i + 1) * 512], start=(c == 0), stop=(c == 7))
            ar.free(cT)
            yield
            yield from post_norm_gen(t, bk2, gpost)

        for t0 in range(0, NT, 2):
            rr([p4_tile(t0), p4_tile(t0 + 1)])
        ar.free(ca, cb, cc, w_mo_t, gpost, bcs, lamt)
        if stop_after == "mix":
            break

        w_q_t = load_w("w_q", [128, 8, D], w_q_d[l].rearrange("(c p) n -> p c n", p=128), scale_cols=8, l=l)
        w_o_t = load_w("w_o", [128, 8, D], w_o_d[l].rearrange("(c p) n -> p c n", p=128))
        ssm = ar.alloc("ssm", [128, 2], F32)
        rsm = ar.alloc("rsm", [128, 2], F32)
        for t in range(2):
            A("act", "activation", [memf.res], [junk.res, ssm.res], out=junk.ap, in_=memf.ap[:, t, :], func=AF.Square,
              accum_out=ssm.ap[:, t:t + 1])
        rstd_from((ssm.ap, ssm.res), 2, D, (rsm.ap, rsm.res))
        memT = ar.alloc("memT", [128, 8, NMEM], BF16)
        for t in range(2):
            mn = ar.alloc("mn", [128, D], BF16)
            A("dve", "tensor_scalar", [memf.res, rsm.res], [mn.res], out=mn.ap, in0=memf.ap[:, t, :],
              scalar1=rsm.ap[:, t:t + 1], scalar2=None, op0=ALU.mult)
            transposes([mn.ap[:, c * 128:(c + 1) * 128] for c in range(8)], mn.res, memT.ap[:, :, t * 128:(t + 1) * 128],
                       memT.res)
            ar.free(mn)
        ar.free(memf, ssm, rsm)
        KmT = ar.alloc("KmT", [128, 8, NMEM], BF16)
        for hc in range(8):
            mb = mm_bank()
            for c in range(8):
                A("pe", "matmul", [memT.res, w_k_t.res], [bank_res[mb]], banks[mb][:, 0:NMEM],
                  lhsT=w_k_t.ap[:, c, hc * 128:(hc + 1) * 128], rhs=memT.ap[:, c, :], start=(c == 0), stop=(c == 7))
            copy(copy_eng(), [bank_res[mb]], [KmT.res], KmT.ap[:, hc, :], banks[mb][:, 0:NMEM])
        Vm = ar.alloc("Vm", [128, 8, 257], BF16)
        A("pool", "memset", [], [Vm.res], Vm.ap[:, :, 256:257], 1.0)
        for kt in range(2):
            for i in range(2):
                mb = mm_bank()
                for c in range(8):
                    A("pe", "matmul", [memT.res, w_v_ts[i].res], [bank_res[mb]], banks[mb][:, :],
                      lhsT=memT.ap[:, c, kt * 128:(kt + 1) * 128], rhs=w_v_ts[i].ap[:, c, :],
                      start=(c == 0), stop=(c == 7))
                copy(copy_eng(), [bank_res[mb]], [Vm.res], Vm.ap[:, kt * 4 + 2 * i:kt * 4 + 2 * i + 2, 0:256],
                     banks[mb][:, :].rearrange("p (h e) -> p h e", e=256))
        ar.free(memT, w_k_t, *w_v_ts)
        gpost = ar.alloc("gpost", [128, D], F32)
        A("sp", "dma_start", [], [gpost.res], is_dma=True, out=gpost.ap, in_=bc_d[l:l + 1, 1024:2048].partition_broadcast(128))
        ss5 = ar.alloc("ss5", [128, NT], F32)
        rs5 = ar.alloc("rs5", [128, NT], F32)
        for t in range(NT):
            A("act", "activation", [x_res[t]], [junk.res, ss5.res], out=junk.ap, in_=xs[:, t, :], func=AF.Square,
              accum_out=ss5.ap[:, t:t + 1])
        rstd_from((ss5.ap, ss5.res), NT, D, (rs5.ap, rs5.res))
        hTbs = [ar.alloc("hTb", [128, 8, 512], BF16) for _ in range(2)]
        qTbs = [ar.alloc("qTb", [128, 8, 512], BF16) for _ in range(2)]

        def mem_front(qb):
            hTb = hTbs[qb % 2]
            for j in range(4):
                t = qb * 4 + j
                hn = ar.alloc("hn", [128, D], BF16)
                A("dve", "tensor_scalar", [x_res[t], rs5.res], [hn.res], out=hn.ap, in0=xs[:, t, :],
                  scalar1=rs5.ap[:, t:t + 1], scalar2=None, op0=ALU.mult)
                transposes([hn.ap[:, c * 128:(c + 1) * 128] for c in range(8)], hn.res,
                           hTb.ap[:, :, j * 128:(j + 1) * 128], hTb.res)
                ar.free(hn)
            qTb = qTbs[qb % 2]
            for hc in range(8):
                mb = mm_bank()
                for c in range(8):
                    A("pe", "matmul", [hTb.res, w_q_t.res], [bank_res[mb]], banks[mb][:, :],
                      lhsT=w_q_t.ap[:, c, hc * 128:(hc + 1) * 128], rhs=hTb.ap[:, c, :], start=(c == 0), stop=(c == 7))
                copy(copy_eng(), [bank_res[mb]], [qTb.res], qTb.ap[:, hc, :], banks[mb][:, :])

        def mem_back(qb):
            qTb = qTbs[qb % 2]
            ao = ar.alloc("ao", [128, 4, D], BF16)
            for h in range(4):
                pts = []
                for kt in range(2):
                    sb_ = tp_bank()
                    for dc in range(2):
                        A("pe", "matmul", [qTb.res, KmT.res], [bank_res[sb_]], banks[sb_][:, :],
                          lhsT=KmT.ap[:, h * 2 + dc, kt * 128:(kt + 1) * 128], rhs=qTb.ap[:, h * 2 + dc, :],
                          start=(dc == 0), stop=(dc == 1))
                    pt = ar.alloc("ptm", [128, 512], BF16)
                    A("act", "activation", [bank_res[sb_]], [pt.res], out=pt.ap, in_=banks[sb_][:, :], func=AF.Exp,
                      scale=1.0 / 16.0)
                    pts.append(pt)
                for j in range(4):
                    mb = mm_bank()
                    for kt in range(2):
                        A("pe", "matmul", [pts[kt].res, Vm.res], [bank_res[mb]], banks[mb][:, 0:257],
                          lhsT=pts[kt].ap[:, j * 128:(j + 1) * 128], rhs=Vm.ap[:, kt * 4 + h, :], start=(kt == 0),
                          stop=(kt == 1))
                    rc = ar.alloc("rc", [128, 1], F32)
                    A("dve", "reciprocal", [bank_res[mb]], [rc.res], out=rc.ap[:, 0:1], in_=banks[mb][:, 256:257])
                    A("dve", "tensor_scalar", [bank_res[mb], rc.res], [ao.res], out=ao.ap[:, j, h * 256:(h + 1) * 256],
                      in0=banks[mb][:, 0:256], scalar1=rc.ap[:, 0:1], scalar2=None, op0=ALU.mult)
                    ar.free(rc)
                ar.free(*pts)
            def o_tile(j):
                t = qb * 4 + j
                aT = ar.alloc("aT", [128, 8, 128], BF16)
                transposes([ao.ap[:, j, c * 128:(c + 1) * 128] for c in range(8)], ao.res, aT.ap, aT.res)
                yield
                bk2 = [mm_bank(), mm_bank()]
                for i in range(2):
                    for c in range(8):
                        A("pe", "matmul", [aT.res, w_o_t.res], [bank_res[bk2[i]]], banks[bk2[i]][:, :],
                          lhsT=aT.ap[:, c, :], rhs=w_o_t.ap[:, c, i * 512:(i + 1) * 512], start=(c == 0), stop=(c == 7))
                ar.free(aT)
                yield
                yield from post_norm_gen(t, bk2, gpost)

            rr([o_tile(0), o_tile(1)])
            rr([o_tile(2), o_tile(3)])
            ar.free(ao)

        mem_front(0)
        for qb in range(4):
            if qb + 1 < 4:
                mem_front(qb + 1)
            mem_back(qb)
        ar.free(KmT, Vm, w_q_t, w_o_t, gpost, ss5, rs5, *hTbs, *qTbs)
        if stop_after == "mem":
            break

        w_dn_t = ar.alloc("w_dn", [128, NCH, D], BF16)
        w_dn_v = w_dn_d[l].rearrange("(c p) n -> p c n", p=128)

        def load_dn_chunk(c, w_dn_t=w_dn_t, w_dn_v=w_dn_v):
            stg = ar.alloc("stg", [128, D], F32)
            A("sp", "dma_start", [], [stg.res], is_dma=True, out=stg.ap, in_=w_dn_v[:, c, :])
            A("pool", "tensor_copy", [stg.res], [w_dn_t.res], out=w_dn_t.ap[:, c, :], in_=stg.ap)
            ar.free(stg)
        gpost = ar.alloc("gpost", [128, D], F32)
        A("sp", "dma_start", [], [gpost.res], is_dma=True, out=gpost.ap, in_=bc_d[l:l + 1, 2048:3072].partition_broadcast(128))
        ss6 = ar.alloc("ss6", [128, NT], F32)
        rs6 = ar.alloc("rs6", [128, NT], F32)
        for t in range(NT):
            A("act", "activation", [x_res[t]], [junk.res, ss6.res], out=junk.ap, in_=xs[:, t, :], func=AF.Square,
              accum_out=ss6.ap[:, t:t + 1])
        rstd_from((ss6.ap, ss6.res), NT, D, (rs6.ap, rs6.res))
        hl = ar.alloc("hl", [128, 8, 2], BF16)
        stgs = [ar.alloc("stgf", [128, 8, 256], F32) for _ in range(2)]
        wus = [ar.alloc("wuf", [128, 8, 256], BF16) for _ in range(3)]
        def issue_w(idx, l=l, stgs=stgs, wus=wus):
            if idx >= 4 * NCH:
                return
            jc_ = idx % NCH
            stg_, wu_ = stgs[idx % 2], wus[idx % 3]
            A("sp", "dma_start", [], [stg_.res], is_dma=True, out=stg_.ap,
              in_=w_up_d[l, jc_].rearrange("p (c n) -> p c n", n=256))
            A("pool", "tensor_tensor", [stg_.res, pk.res], [wu_.res], out=wu_.ap, in0=stg_.ap,
              in1=pk.ap[:, l, 16:24].unsqueeze(2).broadcast_to([128, 8, 256]), op=ALU.mult)

        issue_w(0)
        issue_w(1)
        def ffn_build(qb):
            hTe = ar.alloc("hTe", [128, 8, 514], BF16)
            if qb == 0:
                A("dve", "memset", [], [hTe.res], hTe.ap[:, :, 0:1], 0.0)
            if qb == 3:
                A("dve", "memset", [], [hTe.res], hTe.ap[:, :, 513:514], 0.0)
            tl = list(range(qb * 4, qb * 4 + 4))
            if qb > 0:
                A("dve", "tensor_copy", [hl.res], [hTe.res], out=hTe.ap[:, :, 0:1], in_=hl.ap[:, :, (qb - 1) % 2:(qb - 1) % 2 + 1])
            if qb < 3:
                tl = tl + [qb * 4 + 4]
            for t in tl:
                hn = ar.alloc("hn", [128, D], BF16)
                A("dve", "tensor_scalar", [x_res[t], rs6.res], [hn.res], out=hn.ap, in0=xs[:, t, :],
                  scalar1=rs6.ap[:, t:t + 1], scalar2=None, op0=ALU.mult)
                bk = tp_bank()
                pv = banks[bk][:].bitcast(BF16).rearrange("p (a b) -> p a b", b=128)
                for c in range(8):
                    A("pe", "transpose", [hn.res, ident.res], [bank_res[bk]], out=pv[:, c, :],
                      in_=hn.ap[:, c * 128:(c + 1) * 128], identity=ident.ap)
                j = t - qb * 4
                if j < 0:
                    copy(copy_eng(), [bank_res[bk]], [hTe.res], hTe.ap[:, :, 0:1], pv[:, 0:8, 127:128])
                elif j > 3:
                    copy(copy_eng(), [bank_res[bk]], [hTe.res], hTe.ap[:, :, 513:514], pv[:, 0:8, 0:1])
                else:
                    copy(copy_eng(), [bank_res[bk]], [hTe.res], hTe.ap[:, :, 1 + j * 128:1 + (j + 1) * 128], pv[:, 0:8, :])
                ar.free(hn)
            A("dve", "tensor_copy", [hTe.res], [hl.res], out=hl.ap[:, :, qb % 2:qb % 2 + 1], in_=hTe.ap[:, :, 512:513])
            return hTe

        def ffn_up(qb, hTe):
            mT = ar.alloc("mT", [128, NCH, 512], BF16)
            mm_all[0] = True
            for jc in range(NCH):
                wu = wus[(qb * NCH + jc) % 3]
                issue_w(qb * NCH + jc + 2)
                if qb == 0:
                    load_dn_chunk(jc)
                ys = []
                hb = mm_bank()
                def conv_chain(gu):
                    mb = mm_bank()
                    for c in range(8):
                        A("pe", "matmul", [hTe.res, wu.res], [bank_res[mb]], banks[mb][:, :],
                          lhsT=wu.ap[:, c, gu * 128:(gu + 1) * 128], rhs=hTe.ap[:, c, 1:513], start=(c == 0), stop=(c == 7))
                    for c in range(8):
                        A("pe", "matmul", [hTe.res, wu.res], [bank_res[hb]], banks[hb][:, 2 * gu:2 * gu + 2],
                          lhsT=wu.ap[:, c, gu * 128:(gu + 1) * 128], rhs=hTe.ap[:, c, 0:514:513], start=(c == 0),
                          stop=(c == 7))
                    ch = gu * NCH + jc
                    cw = pk.ap[:, l, 41 + ch * 4:41 + ch * 4 + 4]
                    y = ar.alloc("y", [128, 512], F32)
                    A("act", "activation", [bank_res[mb], pk.res], [y.res], out=y.ap, in_=banks[mb][:, :],
                      func=AF.Identity, scale=cw[:, 1:2], bias=cw[:, 3:4])
                    yield
                    A("dve", "scalar_tensor_tensor", [bank_res[mb], pk.res, y.res], [y.res], out=y.ap[:, 1:512],
                      in0=banks[mb][:, 0:511], scalar=cw[:, 0:1], in1=y.ap[:, 1:512], op0=ALU.mult, op1=ALU.add)
                    yield
                    A("dve", "scalar_tensor_tensor", [bank_res[hb], pk.res, y.res], [y.res], out=y.ap[:, 0:1],
                      in0=banks[hb][:, 2 * gu:2 * gu + 1], scalar=cw[:, 0:1], in1=y.ap[:, 0:1], op0=ALU.mult, op1=ALU.add)
                    yield
                    A("dve", "scalar_tensor_tensor", [bank_res[mb], pk.res, y.res], [y.res], out=y.ap[:, 0:511],
                      in0=banks[mb][:, 1:512], scalar=cw[:, 2:3], in1=y.ap[:, 0:511], op0=ALU.mult, op1=ALU.add)
                    yield
                    A("dve", "scalar_tensor_tensor", [bank_res[hb], pk.res, y.res], [y.res], out=y.ap[:, 511:512],
                      in0=banks[hb][:, 2 * gu + 1:2 * gu + 2], scalar=cw[:, 2:3], in1=y.ap[:, 511:512], op0=ALU.mult, op1=ALU.add)
                    yield
                    ys[gu] = y
                    yield

                ys = [None, None]
                gens = [conv_chain(0), conv_chain(1)]
                while gens:
                    for g_ in list(gens):
                        try:
                            next(g_)
                        except StopIteration:
                            gens.remove(g_)
                A("act", "activation", [ys[0].res], [ys[0].res], out=ys[0].ap, in_=ys[0].ap, func=AF.Gelu_apprx_tanh)
                A("pool", "tensor_tensor", [ys[0].res, ys[1].res], [mT.res], out=mT.ap[:, jc, :], in0=ys[0].ap,
                  in1=ys[1].ap, op=ALU.mult)
                ar.free(*ys)
            ar.free(hTe)
            return mT

        def ffn_down(qb, mT):
            def dn_tile(j):
                t = qb * 4 + j
                bk2 = [mm_bank(), mm_bank()]
                for i in range(2):
                    for jc in range(NCH):
                        A("pe", "matmul", [mT.res, w_dn_t.res], [bank_res[bk2[i]]], banks[bk2[i]][:, :],
                          lhsT=mT.ap[:, jc, j * 128:(j + 1) * 128], rhs=w_dn_t.ap[:, jc, i * 512:(i + 1) * 512],
                          start=(jc == 0), stop=(jc == NCH - 1))
                yield
                yield from post_norm_gen(t, bk2, gpost)

            rr([dn_tile(0), dn_tile(1)])
            rr([dn_tile(2), dn_tile(3)])
            ar.free(mT)
            mm_all[0] = False

        hTe_cur = ffn_build(0)
        for qb in range(4):
            mT_cur = ffn_up(qb, hTe_cur)
            if qb + 1 < 4:
                hTe_cur = ffn_build(qb + 1)
            ffn_down(qb, mT_cur)
        ar.free(w_dn_t, gpost, ss6, rs6, hl, *stgs, *wus)

    ov = out_d.rearrange("(t p) d -> p t d", p=128)
    last = []
    for t in range(NT):
        last.append(A("sp", "dma_start", [x_res[t]], [out_res[t]], is_dma=True, out=ov[:, t, :], in_=xs[:, t, :]))
    fin = A("sp", "nop", [out_res], [])

    counts = sc.finalize()
    nsem = {e: max(1, (counts[e] + SEM_LIMIT - 1) // SEM_LIMIT) for e in ("pe", "act", "dve", "pool", "sp")}
    sems = {e: [es.enter_context(nc.semaphore("s_%s%d" % (e, i))) for i in range(nsem[e])] for e in nsem}
    dsems = {q: [es.enter_context(nc.semaphore("d_%s%d" % (q, i))) for i in range(NDS)] for q in ("sp", "pool")}

    def emit(ename, eng):
        waited = {}
        for op in sc.ops[ename]:
            for d in op.deps:
                if d.is_dma:
                    sm, val = dsems[d.eng][d.dsem], d.dval
                    key = ("d", d.eng, d.dsem)
                else:
                    sm, val = sems[d.eng][d.semk // SEM_LIMIT], d.semk % SEM_LIMIT + 1
                    key = ("c", d.eng, d.semk // SEM_LIMIT)
                if waited.get(key, 0) >= val:
                    continue
                eng.wait_ge(sm, val)
                waited[key] = val
            if op.meth == "nop":
                continue
            ins = getattr(eng, op.meth)(*op.args, **op.kw)
            if op.is_dma:
                ins.then_inc(dsems[op.eng][op.dsem], 16)
            elif op.sig:
                ins.then_inc(sems[op.eng][op.semk // SEM_LIMIT], 1)

    block = es.enter_context(nc.Block())

    @block.sync
    def _(e):
        emit("sp", e)

    @block.gpsimd
    def _(e):
        emit("pool", e)

    @block.tensor
    def _(e):
        emit("pe", e)

    @block.scalar
    def _(e):
        emit("act", e)

    @block.vector
    def _(e):
        emit("dve", e)

    es.close()
    stats = {e: len(sc.ops[e]) for e in ENGS}
    stats["arena_peak_kb"] = ar.peak
    return nc, stats


def prep_inputs(inp, n_layers=DEPTH):
    f = lambda a: np.ascontiguousarray(np.asarray(a, dtype=np.float32))
    pk = np.zeros((128, DEPTH, 217), np.float32)
    for l in range(DEPTH):
        pk[:, l, 0:8] = np.asarray(inp["mix_pre_g"])[l].reshape(8, 128).T
        pk[:, l, 8:16] = np.asarray(inp["mem_pre_g"])[l].reshape(8, 128).T
        pk[:, l, 16:24] = np.asarray(inp["ffn_pre_g"])[l].reshape(8, 128).T
        pk[:, l, 24:32] = np.asarray(inp["mem_kv_g"])[l].reshape(8, 128).T
        pk[:, l, 32:35] = np.asarray(inp["mla_cq_g"])[l].reshape(3, 128).T
        pk[:, l, 35:37] = np.asarray(inp["mla_ckv_g"])[l].reshape(2, 128).T
        pk[:, l, 37:41] = np.asarray(inp["sgu_b_s"])[l].T
        cw = np.asarray(inp["ffn_conv_w"])[l]
        cbv = np.asarray(inp["ffn_conv_b"])[l]
        c4 = np.concatenate([cw, cbv[None, :]], axis=0)
        pk[:, l, 41:217] = c4.reshape(4, 44, 128).transpose(2, 1, 0).reshape(128, 176)
    bc = np.concatenate([
        np.asarray(inp["mix_post_g"]), np.asarray(inp["mem_post_g"]), np.asarray(inp["ffn_post_g"]),
        np.asarray(inp["sgu_norm_g"]).reshape(DEPTH, 256), np.asarray(inp["diff_sub_g"]),
        np.asarray(inp["diff_lam_q1"]), np.asarray(inp["diff_lam_k1"]),
        np.asarray(inp["diff_lam_q2"]), np.asarray(inp["diff_lam_k2"])], axis=1).astype(np.float32)
    wsT = np.asarray(inp["sgu_w_s"]).transpose(0, 3, 1, 2).reshape(DEPTH, 128, 512)
    nl = n_layers
    f = lambda a: np.ascontiguousarray(np.asarray(a, dtype=np.float32)[:nl])
    wup = np.asarray(inp["ffn_w_up"], dtype=np.float32)[:nl].reshape(nl, 8, 128, 2, NCH, 128)
    wup = np.ascontiguousarray(wup.transpose(0, 4, 2, 1, 3, 5)).reshape(nl, NCH, 128, 2048)
    shared = {
        "w_in": f(inp["w_in"]), "w_uq": f(inp["mla_w_uq"]), "w_ukv": f(inp["mla_w_ukv"]), "wsT": f(wsT),
        "w_mo": f(inp["w_mix_out"]), "w_q": f(inp["mem_w_q"]), "w_kv": f(inp["mem_w_kv"]), "w_o": f(inp["mem_w_o"]),
        "w_up": wup, "w_dn": f(inp["ffn_w_down"]),
        "pk": np.ascontiguousarray(pk), "bc": np.ascontiguousarray(bc),
    }
    x = np.asarray(inp["x"], dtype=np.float32)
    mem = np.asarray(inp["mem"], dtype=np.float32)
    pos = np.asarray(inp["positions"]).astype(np.int32)
    maps = []
    for b in range(x.shape[0]):
        m = dict(shared)
        m["x"] = np.ascontiguousarray(x[b])
        m["mem"] = np.ascontiguousarray(mem[b])
        m["pos"] = np.ascontiguousarray(pos[b].reshape(NT, 128))
        maps.append(m)
    return maps


def kernel(**inputs):
    maps = prep_inputs(inputs)
    nc, _ = build()
    res = run_bass_kernel_spmd(nc, maps, core_ids=list(range(8)))
    return np.stack([np.asarray(r["out"], dtype=np.float32) for r in res.results], axis=0)
```
